# Optimizing a Trainium2 kernel written in Bass

```python
import jax, jax.numpy as jnp
from jax import lax
import numpy as np

D_MODEL = 1024
BATCH = 2
SEQ = 8192
DEPTH = 2
DEC_BATCH = 4
DEC_SEQ = 8192
PAST_LEN = 128

GRID_W = 64
Q_BLOCK = 128
N_MEM = 256
MLA_HEADS = 8
MLA_Q_RANK = 384
MLA_KV_RANK = 256
MLA_NOPE = 64
MLA_ROPE = 32
MLA_V = 64
GQA_HEADS = 8
GQA_KV_HEADS = 2
GQA_HEAD_DIM = 64
RWKV_HEADS = 8
RWKV_HEAD = 64
RWKV_DIM = RWKV_HEADS * RWKV_HEAD
W_LORA = 64
A_LORA = 64
G_LORA = 128
N_BRANCH = 3
BRANCH_DIM = 512
CROSS_HEADS = 4
CROSS_HEAD_DIM = 128
CROSS_DIM = CROSS_HEADS * CROSS_HEAD_DIM
D_FF = 4 * D_MODEL
ROPE_THETA = 10000.0
NORM_EPS = 1e-6
GN_EPS = 64e-5

GQA_Q_COLS = GQA_HEADS * GQA_HEAD_DIM
GQA_KV_COLS = GQA_KV_HEADS * GQA_HEAD_DIM
RWKV_COLS = 3 * RWKV_DIM + 2 * W_LORA + 2 * A_LORA + G_LORA
GATE_COLS = N_BRANCH * D_MODEL
IN_SPLIT = (MLA_Q_RANK, MLA_KV_RANK, MLA_ROPE, GQA_Q_COLS, GQA_KV_COLS, GQA_KV_COLS, RWKV_COLS, GATE_COLS)
IN_COLS = MLA_Q_RANK + MLA_KV_RANK + MLA_ROPE + GQA_Q_COLS + 2 * GQA_KV_COLS + RWKV_COLS + GATE_COLS
RWKV_SPLIT = (RWKV_DIM, RWKV_DIM, RWKV_DIM, 2 * W_LORA, 2 * A_LORA, G_LORA)

kernel_name = 'hybrid_mla_gqa_rwkv7_encoder'


def _split(z, sizes):
    offs = []
    o = 0
    for s in sizes[:-1]:
        o += s
        offs.append(o)
    return jnp.split(z, offs, axis=-1)


def _rms_norm(x, g):
    xf = x.astype(jnp.float32)
    y = xf * lax.rsqrt(jnp.mean(xf * xf, axis=-1, keepdims=True) + NORM_EPS)
    return (y * g.astype(jnp.float32)).astype(x.dtype)


def _rope_tables(pos, dim):
    inv = ROPE_THETA ** (-jnp.arange(0, dim, 2, dtype=jnp.float32) / dim)
    ang = pos.astype(jnp.float32)[:, None] * inv[None, :]
    return jnp.cos(ang), jnp.sin(ang)


def _apply_rope(x, cos, sin):
    xf = x.astype(jnp.float32)
    x1, x2 = jnp.split(xf, 2, axis=-1)
    c = cos[None, :, None, :]
    s = sin[None, :, None, :]
    return jnp.concatenate([x1 * c - x2 * s, x1 * s + x2 * c], axis=-1).astype(x.dtype)


def _block_attention(q, k, v):
    B, S, H, Dq = q.shape
    Hk = k.shape[2]
    G = H // Hk
    Dv = v.shape[-1]
    nb = S // Q_BLOCK
    qb = jnp.moveaxis(q.astype(jnp.float32).reshape(B, nb, Q_BLOCK, Hk, G, Dq), 1, 0)
    kf = k.astype(jnp.float32)
    vf = v.astype(jnp.float32)
    scale = Dq ** -0.5

    def one(qi):
        s = jnp.einsum('bqhgd,bkhd->bhgqk', qi, kf) * scale
        p = jax.nn.softmax(s, axis=-1)
        return jnp.einsum('bhgqk,bkhd->bqhgd', p, vf)

    o = lax.map(one, qb)
    return jnp.moveaxis(o, 0, 1).reshape(B, S, H, Dv).astype(v.dtype)


def _mla_branch(c_q, c_kv, k_rope, q_norm, w_uq, kv_norm, w_ukv, cos, sin):
    B, S, _ = c_q.shape
    q = (_rms_norm(c_q, q_norm) @ w_uq).reshape(B, S, MLA_HEADS, MLA_NOPE + MLA_ROPE)
    q = jnp.concatenate([q[..., :MLA_NOPE], _apply_rope(q[..., MLA_NOPE:], cos, sin)], axis=-1)
    kv = (_rms_norm(c_kv, kv_norm) @ w_ukv).reshape(B, S, MLA_HEADS, MLA_NOPE + MLA_V)
    kr = jnp.broadcast_to(_apply_rope(k_rope[:, :, None, :], cos, sin), (B, S, MLA_HEADS, MLA_ROPE))
    k = jnp.concatenate([kv[..., :MLA_NOPE], kr], axis=-1)
    o = _block_attention(q, k, kv[..., MLA_NOPE:])
    return o.reshape(B, S, MLA_HEADS * MLA_V)


def _gqa_branch(gq, gk, gv, q_norm, k_norm, cos_r, sin_r, cos_c, sin_c):
    B, S, _ = gq.shape
    half = GQA_HEAD_DIM // 2

    def axial(t):
        return jnp.concatenate([_apply_rope(t[..., :half], cos_r, sin_r),
                                _apply_rope(t[..., half:], cos_c, sin_c)], axis=-1)

    q = axial(_rms_norm(gq.reshape(B, S, GQA_HEADS, GQA_HEAD_DIM), q_norm))
    k = axial(_rms_norm(gk.reshape(B, S, GQA_KV_HEADS, GQA_HEAD_DIM), k_norm))
    v = gv.reshape(B, S, GQA_KV_HEADS, GQA_HEAD_DIM)
    return _block_attention(q, k, v).reshape(B, S, GQA_HEADS * GQA_HEAD_DIM)


def _wkv_scan(r, w, k, v, a, b):
    Z, S, H, N = r.shape
    xs = tuple(jnp.moveaxis(t, 1, 0) for t in (r, w, k, v, a, b))

    def step(state, inp):
        rt, wt, kt, vt, at, bt = inp
        sa = jnp.einsum('zhvk,zhk->zhv', state, at)
        state = state * wt[:, :, None, :] + sa[..., None] * bt[:, :, None, :] + vt[..., None] * kt[:, :, None, :]
        return state, jnp.einsum('zhvk,zhk->zhv', state, rt)

    s0 = jnp.zeros((Z, H, N, N), jnp.float32)
    _, y = lax.scan(step, s0, xs)
    return jnp.moveaxis(y, 0, 1)


def _rwkv_branch(z, mu, w0, w2, a0, a2, g2, k_k, k_a, r_k, ln_w, ln_b):
    B, S, _ = z.shape
    H, N, C = RWKV_HEADS, RWKV_HEAD, RWKV_DIM
    z = z.astype(jnp.float32)
    prev = jnp.pad(z, ((0, 0), (1, 0), (0, 0)))[:, :-1]
    nxt = jnp.pad(z, ((0, 0), (0, 1), (0, 0)))[:, 1:]
    z = z + mu * (0.5 * (prev + nxt) - z)
    r, k, v, wl, al, gl = _split(z, RWKV_SPLIT)
    wl = wl.reshape(B, S, 2, W_LORA)
    al = al.reshape(B, S, 2, A_LORA)
    w_log = -jax.nn.softplus(-(w0 + jnp.einsum('bsdr,drc->bsdc', jnp.tanh(wl), w2))) - 0.5
    decay = jnp.exp(-jnp.exp(w_log))
    a = jax.nn.sigmoid(a0 + jnp.einsum('bsdr,drc->bsdc', al, a2))
    g = jax.nn.sigmoid(gl) @ g2
    kk = (k * k_k).reshape(B, S, H, N)
    kk = (kk / jnp.maximum(jnp.sqrt(jnp.sum(kk * kk, axis=-1, keepdims=True)), 1e-12)).reshape(B, S, C)
    kt = k[:, :, None, :] * (1.0 + (a - 1.0) * k_a)
    bb = kk[:, :, None, :] * a

    def dirs(fw, bw):
        return jnp.concatenate([fw, jnp.flip(bw, 1)], axis=0).reshape(2 * B, S, H, N)

    y = _wkv_scan(dirs(r, r), dirs(decay[:, :, 0], decay[:, :, 1]), dirs(kt[:, :, 0], kt[:, :, 1]),
                  dirs(v, v), dirs(-kk, -kk), dirs(bb[:, :, 0], bb[:, :, 1]))
    wkv = y[:B] + jnp.flip(y[B:], 1)
    mean = jnp.mean(wkv, axis=-1, keepdims=True)
    var = jnp.mean(jnp.square(wkv - mean), axis=-1, keepdims=True)
    gn = ((wkv - mean) * lax.rsqrt(var + GN_EPS)).reshape(B, S, C) * ln_w + ln_b
    bonus = jnp.einsum('bshn,bsdhn,hn->bsh', r.reshape(B, S, H, N), kt.reshape(B, S, 2, H, N), r_k)[..., None] \
        * v.reshape(B, S, H, N)
    return (gn + bonus.reshape(B, S, C)) * g


def _cross_attn(h, mem_h, wq, wkv, wo):
    B, S, _ = h.shape
    M = mem_h.shape[1]
    q = (h @ wq).reshape(B, S, CROSS_HEADS, CROSS_HEAD_DIM).astype(jnp.float32)
    kv = (mem_h @ wkv).reshape(B, M, 2, CROSS_HEADS, CROSS_HEAD_DIM).astype(jnp.float32)
    s = jnp.einsum('bqhd,bkhd->bhqk', q, kv[:, :, 0]) * (CROSS_HEAD_DIM ** -0.5)
    p = jax.nn.softmax(s, axis=-1)
    o = jnp.einsum('bhqk,bkhd->bqhd', p, kv[:, :, 1]).reshape(B, S, CROSS_DIM).astype(h.dtype)
    return o @ wo


def setup_inputs(seed: int = 0) -> dict:
    key = jax.random.key(seed)
    ks = iter(list(jax.random.split(key, 64)))
    L = DEPTH

    def nrm(shape, scale):
        return scale * jax.random.normal(next(ks), shape, jnp.float32)

    def gain(shape):
        return 1.0 + 0.02 * jax.random.normal(next(ks), shape, jnp.float32)

    return {
        'x_prompt': nrm((BATCH, SEQ, D_MODEL), 1.0),
        'x_sample': nrm((DEC_BATCH, DEC_SEQ, D_MODEL), 1.0),
        'mem_prompt': nrm((BATCH, N_MEM, D_MODEL), 1.0),
        'mem_sample': nrm((DEC_BATCH, N_MEM, D_MODEL), 1.0),
        'norm_mix': gain((L, D_MODEL)),
        'w_in': nrm((L, D_MODEL, IN_COLS), D_MODEL ** -0.5),
        'mla_q_norm': gain((L, MLA_Q_RANK)),
        'mla_w_uq': nrm((L, MLA_Q_RANK, MLA_HEADS * (MLA_NOPE + MLA_ROPE)), MLA_Q_RANK ** -0.5),
        'mla_kv_norm': gain((L, MLA_KV_RANK)),
        'mla_w_ukv': nrm((L, MLA_KV_RANK, MLA_HEADS * (MLA_NOPE + MLA_V)), MLA_KV_RANK ** -0.5),
        'gqa_q_norm': gain((L, GQA_HEAD_DIM)),
        'gqa_k_norm': gain((L, GQA_HEAD_DIM)),
        'rwkv_mu': jax.random.uniform(next(ks), (L, RWKV_COLS), jnp.float32),
        'rwkv_w0': nrm((L, 2, RWKV_DIM), 0.5),
        'rwkv_w2': nrm((L, 2, W_LORA, RWKV_DIM), W_LORA ** -0.5),
        'rwkv_a0': nrm((L, 2, RWKV_DIM), 0.1),
        'rwkv_a2': nrm((L, 2, A_LORA, RWKV_DIM), A_LORA ** -0.5),
        'rwkv_g2': nrm((L, G_LORA, RWKV_DIM), G_LORA ** -0.5),
        'rwkv_k_k': 0.85 + nrm((L, RWKV_DIM), 0.02),
        'rwkv_k_a': gain((L, RWKV_DIM)),
        'rwkv_r_k': nrm((L, RWKV_HEADS, RWKV_HEAD), 0.1),
        'rwkv_ln_w': gain((L, RWKV_DIM)),
        'rwkv_ln_b': nrm((L, RWKV_DIM), 0.02),
        'w_branch': nrm((L, N_BRANCH, BRANCH_DIM, D_MODEL), BRANCH_DIM ** -0.5),
        'b_gate': nrm((L, N_BRANCH, D_MODEL), 0.02),
        'w_out': nrm((L, D_MODEL, D_MODEL), D_MODEL ** -0.5),
        'norm_cross': gain((L, D_MODEL)),
        'norm_mem': gain((L, D_MODEL)),
        'cross_wq': nrm((L, D_MODEL, CROSS_DIM), D_MODEL ** -0.5),
        'cross_wkv': nrm((L, D_MODEL, 2 * CROSS_DIM), D_MODEL ** -0.5),
        'cross_wo': nrm((L, CROSS_DIM, D_MODEL), CROSS_DIM ** -0.5),
        'norm_mlp': gain((L, D_MODEL)),
        'mlp_w1': nrm((L, D_MODEL, D_FF), D_MODEL ** -0.5),
        'mlp_w2': nrm((L, D_FF, D_MODEL), D_FF ** -0.5),
        'norm_final': gain((D_MODEL,)),
    }


def reference(x_prompt, x_sample, mem_prompt, mem_sample, norm_mix, w_in, mla_q_norm, mla_w_uq,
              mla_kv_norm, mla_w_ukv, gqa_q_norm, gqa_k_norm, rwkv_mu, rwkv_w0, rwkv_w2, rwkv_a0,
              rwkv_a2, rwkv_g2, rwkv_k_k, rwkv_k_a, rwkv_r_k, rwkv_ln_w, rwkv_ln_b, w_branch, b_gate,
              w_out, norm_cross, norm_mem, cross_wq, cross_wkv, cross_wo, norm_mlp, mlp_w1, mlp_w2,
              norm_final):
    def trunk(x, mem):
        B, S, _ = x.shape
        rows = S // GRID_W
        cos1, sin1 = _rope_tables(jnp.arange(S), MLA_ROPE)
        row_pos = jnp.repeat(jnp.arange(rows), GRID_W, total_repeat_length=S)
        col_pos = jnp.tile(jnp.arange(GRID_W), rows)
        cos_r, sin_r = _rope_tables(row_pos, GQA_HEAD_DIM // 2)
        cos_c, sin_c = _rope_tables(col_pos, GQA_HEAD_DIM // 2)
        for l in range(DEPTH):
            h = _rms_norm(x, norm_mix[l])
            z = h @ w_in[l]
            c_q, c_kv, k_rope, gq, gk, gv, zr, zg = _split(z, IN_SPLIT)
            o_mla = _mla_branch(c_q, c_kv, k_rope, mla_q_norm[l], mla_w_uq[l], mla_kv_norm[l],
                                mla_w_ukv[l], cos1, sin1)
            o_gqa = _gqa_branch(gq, gk, gv, gqa_q_norm[l], gqa_k_norm[l], cos_r, sin_r, cos_c, sin_c)
            o_rwkv = _rwkv_branch(zr, rwkv_mu[l], rwkv_w0[l], rwkv_w2[l], rwkv_a0[l], rwkv_a2[l],
                                  rwkv_g2[l], rwkv_k_k[l], rwkv_k_a[l], rwkv_r_k[l], rwkv_ln_w[l],
                                  rwkv_ln_b[l]).astype(x.dtype)
            br = jnp.einsum('bsgc,gcd->bsgd', jnp.stack([o_mla, o_gqa, o_rwkv], axis=2), w_branch[l])
            gate = jax.nn.sigmoid(zg.reshape(B, S, N_BRANCH, D_MODEL) + b_gate[l])
            x = x + jnp.sum(gate * br, axis=2) @ w_out[l]
            x = x + _cross_attn(_rms_norm(x, norm_cross[l]), _rms_norm(mem, norm_mem[l]),
                                cross_wq[l], cross_wkv[l], cross_wo[l])
            hm = _rms_norm(x, norm_mlp[l])
            x = x + jnp.square(jax.nn.relu(hm @ mlp_w1[l])) @ mlp_w2[l]
        return _rms_norm(x, norm_final)

    y_prompt = trunk(x_prompt, mem_prompt)
    y_sample = trunk(x_sample, mem_sample)
    return (y_prompt, y_sample)
```

```python
import contextlib
import math
import numpy as np
import ml_dtypes
import concourse.bass as bass
import concourse.mybir as mybir
from concourse.bass_utils import run_bass_kernel_spmd

F32 = mybir.dt.float32
BF16 = mybir.dt.bfloat16
ALU = mybir.AluOpType
AF = mybir.ActivationFunctionType
AX = mybir.AxisListType

D = 1024
NMEM = 256
IN_COLS = 6432
EPS = 1e-6
GN_EPS = 64e-5
DECAY_C = math.exp(-0.5)

import os
MAXOPS = int(os.environ.get("MAXOPS", "100000000"))
ENGS = ("pe", "act", "dve", "pool", "sp")
N_DMA_SEMS = 16
READ_KEYS = ("in_", "in0", "in1", "lhsT", "rhs", "scalar", "scalar1", "scalar2", "bias", "scale",
             "identity", "data0", "data1", "initial")
WRITE_KEYS = ("out", "accum_out", "ap")


class Buf:
    __slots__ = ("name", "w", "r")

    def __init__(self, name):
        self.name = name
        self.w = None
        self.r = {}


class Prog:
    def __init__(self, nc):
        self.nc = nc
        self.es = contextlib.ExitStack()
        self.eobj = {"pe": nc.tensor, "act": nc.scalar, "dve": nc.vector, "pool": nc.gpsimd, "sp": nc.sync}
        self.sem = {}
        self.cnt = {}
        for e in ENGS:
            self.sem[e] = self.es.enter_context(nc.semaphore("s_" + e))
            self.cnt[e] = 0
        self.dq = {}
        for q in ("sp", "act", "pool"):
            sems = []
            for i in range(N_DMA_SEMS):
                k = "d_%s_%d" % (q, i)
                self.sem[k] = self.es.enter_context(nc.semaphore(k))
                self.cnt[k] = 0
                sems.append(k)
            self.dq[q] = [sems, 0]
        self.seen = {e: {} for e in ENGS}
        self.ops = {e: [] for e in ENGS}
        self.bufs = {}
        self.nops = 0

    def buf_of(self, ap):
        n = ap.name
        b = self.bufs.get(n)
        if b is None:
            b = self.bufs[n] = Buf(n)
        return b

    def _need(self, e, key, val, waits):
        if self.seen[e].get(key, 0) >= val:
            return
        self.seen[e][key] = val
        waits.append((key, val))

    def _collect(self, kw, extra_r, extra_w):
        reads, writes = [], []
        for k in READ_KEYS:
            v = kw.get(k)
            if v is not None and hasattr(v, "name") and hasattr(v, "ap"):
                reads.append(self.buf_of(v))
        for k in WRITE_KEYS:
            v = kw.get(k)
            if v is not None and hasattr(v, "name") and hasattr(v, "ap"):
                writes.append(self.buf_of(v))
        for v in extra_r:
            reads.append(v if isinstance(v, Buf) else self.buf_of(v))
        for v in extra_w:
            writes.append(v if isinstance(v, Buf) else self.buf_of(v))
        return reads, writes

    def _issue(self, e, inckey, incv, name, kw, reads, writes, accum):
        self.nissued = getattr(self, "nissued", 0) + 1
        if self.nissued > MAXOPS:
            return
        if self.nissued == MAXOPS:
            print("LAST OP:", e, name, {a: (str(b.name) + str(b.shape) if hasattr(b, "ap") else b) for a, b in kw.items()})
        need = {}
        for b in reads:
            if b.w is not None and need.get(b.w[0], 0) < b.w[1]:
                need[b.w[0]] = b.w[1]
        for b in writes:
            if b.w is not None and not (accum and b.w[0] == "pe") and need.get(b.w[0], 0) < b.w[1]:
                need[b.w[0]] = b.w[1]
            for k, v in b.r.items():
                if need.get(k, 0) < v:
                    need[k] = v
        waits = []
        for k, v in need.items():
            self._need(e, k, v, waits)
        self.cnt[inckey] += incv
        v = self.cnt[inckey]
        self.ops[e].append((waits, name, kw, inckey, incv))
        self.nops += 1 + len(waits)
        for b in reads:
            if b.r.get(inckey, 0) < v:
                b.r[inckey] = v
        for b in writes:
            b.w = (inckey, v)
            b.r = {}

    def op(self, e, name, R=(), W=(), accum=False, **kw):
        reads, writes = self._collect(kw, R, W)
        self._issue(e, e, 1, name, kw, reads, writes, accum)

    def V(self, name, **kw):
        self.op("dve", name, **kw)

    def A(self, name, **kw):
        self.op("act", name, **kw)

    def G(self, name, **kw):
        self.op("pool", name, **kw)

    def M(self, name="matmul", **kw):
        self.op("pe", name, **kw)

    def dma(self, q, out, in_, R=(), W=()):
        kw = dict(out=out, in_=in_)
        reads, writes = self._collect(kw, R, W)
        sems, idx = self.dq[q]
        key = sems[idx % len(sems)]
        self.dq[q][1] = idx + 1
        if self.cnt[key] > 0:
            w = []
            self._need(q, key, self.cnt[key], w)
            pre = w
        else:
            pre = []
        n0 = len(self.ops[q])
        self._issue(q, key, 16, "dma_start", kw, reads, writes, False)
        if pre and len(self.ops[q]) > n0:
            waits, name, kw2, ik, iv = self.ops[q][n0]
            self.ops[q][n0] = (pre + waits, name, kw2, ik, iv)

    def barrier(self):
        for e in ENGS:
            waits = []
            for key, v in self.cnt.items():
                if v > 0:
                    self._need(e, key, v, waits)
            if waits:
                self.ops[e].append((waits, None, None, None, None))
                self.nops += len(waits)

    def wait_bufs(self, e, aps):
        waits = []
        for a in aps:
            b = a if isinstance(a, Buf) else self.buf_of(a)
            if b.w is not None:
                self._need(e, b.w[0], b.w[1], waits)
        self.ops[e].append((waits, None, None, None, None))

    def emit(self):
        nc = self.nc
        with nc.Block() as block:
            def run(e):
                def body(engine):
                    for waits, name, kw, ik, iv in self.ops[e]:
                        for k, v in waits:
                            engine.wait_ge(self.sem[k], v)
                        if name is None:
                            continue
                        inst = getattr(engine, name)(**kw)
                        inst.then_inc(self.sem[ik], iv)
                return body
            block.sync(run("sp"))
            block.scalar(run("act"))
            block.vector(run("dve"))
            block.gpsimd(run("pool"))
            block.tensor(run("pe"))

    def close(self):
        self.es.close()


W_SPECS = [
    ("norm_mix", (D,)), ("w_in", (D, IN_COLS)), ("mla_q_norm", (384,)), ("mla_w_uq", (384, 768)),
    ("mla_kv_norm", (256,)), ("mla_w_ukv", (256, 1024)), ("gqa_q_norm", (64,)), ("gqa_k_norm", (64,)),
    ("rwkv_mu", (1920,)), ("rwkv_w0", (2, 512)), ("rwkv_w2", (2, 64, 512)), ("rwkv_a0", (2, 512)),
    ("rwkv_a2", (2, 64, 512)), ("rwkv_g2", (128, 512)), ("rwkv_k_k", (512,)), ("rwkv_k_a", (512,)),
    ("rwkv_r_k", (8, 64)), ("rwkv_ln_w", (512,)), ("rwkv_ln_b", (512,)), ("w_branch", (3, 512, D)),
    ("b_gate", (3, D)), ("w_out", (D, D)), ("norm_cross", (D,)), ("norm_mem", (D,)),
    ("cross_wq", (D, 512)), ("cross_wkv", (D, 1024)), ("cross_wo", (512, D)), ("norm_mlp", (D,)),
    ("mlp_w1", (D, 4096)), ("mlp_w2", (4096, D)),
]


class K:
    pass


def build(S, depth, dbg=(), phases=("p1", "p2", "rf", "rb", "p3a", "p3b")):
    NT = S // 128
    nc = bass.Bass("TRN2", target_bir_lowering=False)
    P = Prog(nc)
    k = K()
    k.nc, k.P, k.S, k.NT, k.depth, k.dbg = nc, P, S, NT, depth, dbg
    k.phases = phases

    def din(name, shape):
        return nc.dram_tensor(name, list(shape), F32, kind="ExternalInput").ap()

    def dscr(name, shape, dt):
        kind = "ExternalOutput" if name in dbg else "Internal"
        return nc.dram_tensor(name, list(shape), dt, kind=kind).ap()

    k.x = din("x", (S, D))
    k.mem = din("mem", (NMEM, D))
    k.w = {}
    for name, shp in W_SPECS:
        k.w[name] = din(name, (depth,) + shp)
    k.w["norm_final"] = din("norm_final", (1, D))
    k.tab1 = din("tab1", (S, 2, 16))
    k.tab2 = din("tab2", (S, 2, 2, 16))
    k.ident_d = din("ident", (128, 128))
    k.msk_d = din("msk", (2, 2, 128, 128))
    k.bdm_d = din("bdm", (128, 128))
    k.y = nc.dram_tensor("y", [S, D], F32, kind="ExternalOutput").ap()

    k.h_tm = dscr("h_tm", (S, D), BF16)
    k.hT_t = dscr("hT_t", (NT, 128, 8, 128), BF16)
    k.qT_mla = dscr("qT_mla", (8, 96, S), BF16)
    k.kT_mla = dscr("kT_mla", (8, 96, S), BF16)
    k.v_mla = dscr("v_mla", (S, 512), BF16)
    k.qT_gqa = dscr("qT_gqa", (512, S), BF16)
    k.kT_gqa = dscr("kT_gqa", (128, S), BF16)
    k.v_gqa = dscr("v_gqa", (S, 128), BF16)
    k.o_mla = dscr("o_mla", (S, 512), BF16)
    k.o_gqa = dscr("o_gqa", (S, 512), BF16)
    k.y_fw = dscr("y_fw", (S, 512), F32)
    k.z_r = dscr("z_r", (S, 1920), F32)
    k.zmix = dscr("zmix", (S, 1920), F32)
    k.oT_rw = dscr("oT_rw", (NT, 128, 4, 128), BF16)
    k.x2 = dscr("x2", (S, D), F32)
    k.xs = [dscr("xs0", (S, D), F32), dscr("xs1", (S, D), F32)]

    with contextlib.ExitStack() as gst:
        k.gst = gst
        k.rot = [0]
        k.ident = gst.enter_context(nc.sbuf_tensor("ident_b", [128, 128], BF16))
        P.dma("pool", out=k.ident[:, :], in_=k.ident_d)
        k.identf = gst.enter_context(nc.sbuf_tensor("ident_f", [128, 128], F32))
        P.dma("sp", out=k.identf[:, :], in_=k.ident_d)
        for l in range(depth):
            xin = k.x if l == 0 else k.xs[(l - 1) % 2]
            xout = k.xs[l % 2]
            if "p1" in phases:
                phase_p1(k, l, xin)
                P.barrier()
            if "p2" in phases:
                phase_attn(k, l)
                P.barrier()
            if "rf" in phases:
                phase_rwkv(k, l, 0)
                P.barrier()
            if "rb" in phases:
                phase_rwkv(k, l, 1)
                P.barrier()
            if "p3a" in phases:
                phase_p3a(k, l, xin)
                P.barrier()
            if "p3b" in phases:
                phase_p3b(k, l, xout, last=(l == depth - 1))
                P.barrier()
        outs = [k.y]
        for name in dbg:
            outs.append(P.bufs[name]) if name in P.bufs else None
        P.wait_bufs("sp", outs)
        P.emit()
    P.close()
    return nc


_UID = [0]


def un(name):
    _UID[0] += 1
    return "%s_u%d" % (name, _UID[0])


def alloc_banks(k, st, n=8):
    k.banks = [st.enter_context(k.nc.psum_tensor(un("bank%d" % i), [128, 512], F32)) for i in range(n)]


def bank(k, lo=0, hi=None):
    if hi is None:
        hi = len(k.banks)
    n = hi - lo
    i = k.rot[0] % n
    k.rot[0] += 1
    return k.banks[lo + i]


def bfv(b, ncols=1024):
    return b[:, :].bitcast(BF16)


def rstd_from_ss(k, out, ss, t, n, eps):
    P = k.P
    if eps is not None and eps != 0.0:
        P.V("tensor_scalar", out=t, in0=ss, scalar1=1.0 / n, scalar2=float(eps), op0=ALU.mult, op1=ALU.add)
        P.A("activation", out=t, in_=t, func=AF.Ln)
    else:
        P.A("activation", out=t, in_=ss, func=AF.Ln, scale=1.0 / n)
    P.A("activation", out=out, in_=t, func=AF.Exp, scale=-0.5)


def load_col(k, q, dst, src1d, n):
    k.P.dma(q, out=dst, in_=src1d.rearrange("(p o) -> p o", o=1))


def bc(ap, shape, axis):
    return ap.unsqueeze(axis).broadcast_to(list(shape))


def phase_p1(k, l, xin):
    nc, P, S, NT = k.nc, k.P, k.S, k.NT
    w = k.w
    with contextlib.ExitStack() as st:
        def SB(name, shape, dt):
            return st.enter_context(nc.sbuf_tensor(un("p1_" + name), list(shape), dt))
        alloc_banks(k, st)
        w_att = SB("watt", [128, 8, 1440], BF16)
        w_r = SB("wr", [128, 8, 1920], BF16)
        w_uq = SB("wuq", [128, 3, 768], BF16)
        w_ukv = SB("wukv", [128, 2, 1024], BF16)
        stg = SB("stg", [128, 2304], F32)
        g_bc = SB("gbc", [128, 1024], F32)
        gq_bc = SB("gqbc", [128, 64], F32)
        gk_bc = SB("gkbc", [128, 64], F32)
        gcol = SB("gcol", [128, 5], F32)
        win = w["w_in"][l].rearrange("(c p) n -> p c n", p=128)
        for c in range(8):
            P.dma("pool", out=w_att[:, c, :], in_=win[:, c, 0:1440])
        for c in range(8):
            P.dma("pool", out=w_r[:, c, :], in_=win[:, c, 1440:3360])
        P.dma("sp", out=g_bc[:, :], in_=w["norm_mix"][l:l + 1, :].broadcast_to([128, 1024]))
        P.dma("sp", out=gq_bc[:, :], in_=w["gqa_q_norm"][l:l + 1, :].broadcast_to([128, 64]))
        P.dma("sp", out=gk_bc[:, :], in_=w["gqa_k_norm"][l:l + 1, :].broadcast_to([128, 64]))
        for c in range(3):
            load_col(k, "sp", gcol[:, c:c + 1], w["mla_q_norm"][l, c * 128:(c + 1) * 128], 128)
        for c in range(2):
            load_col(k, "sp", gcol[:, 3 + c:4 + c], w["mla_kv_norm"][l, c * 128:(c + 1) * 128], 128)
        P.dma("sp", out=stg[:, 0:2304].rearrange("p (c n) -> p c n", c=3),
              in_=w["mla_w_uq"][l].rearrange("(c p) n -> p c n", p=128))
        for c in range(3):
            P.V("tensor_scalar", out=w_uq[:, c, :], in0=stg[:, c * 768:(c + 1) * 768],
                scalar1=gcol[:, c:c + 1], scalar2=None, op0=ALU.mult)
        P.dma("sp", out=stg[:, 0:2048].rearrange("p (c n) -> p c n", c=2),
              in_=w["mla_w_ukv"][l].rearrange("(c p) n -> p c n", p=128))
        for c in range(2):
            P.V("tensor_scalar", out=w_ukv[:, c, :], in0=stg[:, c * 1024:(c + 1) * 1024],
                scalar1=gcol[:, 3 + c:4 + c], scalar2=None, op0=ALU.mult)

        sets = []
        for s in range(2):
            d = {}
            for name, shape, dt in [
                ("x", [128, 1024], F32), ("junk", [128, 1024], BF16), ("hb", [128, 1024], BF16),
                ("hT", [128, 8, 128], BF16), ("tb1", [128, 2, 16], F32), ("tb2", [128, 2, 2, 16], F32),
                ("st", [128, 16], F32), ("cqb", [128, 384], BF16), ("cqT", [128, 3, 128], BF16),
                ("q32", [128, 8, 96], F32), ("qrot", [128, 8, 96], BF16), ("qT", [96, 8, 128], BF16),
                ("ta", [128, 8, 2, 16], F32), ("tb", [128, 8, 2, 16], F32),
                ("ckvb", [128, 256], BF16), ("ckvT", [128, 2, 128], BF16), ("kt", [128, 8, 96], BF16),
                ("vt", [128, 8, 64], BF16), ("kr", [128, 32], F32), ("kT", [96, 8, 128], BF16),
                ("sq", [128, 512], F32), ("gst", [128, 24], F32), ("qn", [128, 8, 64], F32),
                ("gqr", [128, 8, 64], BF16), ("gqT", [128, 4, 128], BF16),
                ("kn", [128, 2, 64], F32), ("gkr", [128, 2, 64], BF16), ("gkT", [128, 128], BF16),
                ("gv", [128, 128], BF16), ("zr", [128, 1920], F32),
            ]:
                d[name] = SB("%s%d" % (name, s), shape, dt)
            sets.append(d)

        ident = k.ident
        for i in range(NT):
            d = sets[i % 2]
            t0 = i * 128
            x, hb, hT, stt = d["x"], d["hb"], d["hT"], d["st"]
            P.dma("sp", out=x[:, :], in_=xin[t0:t0 + 128, :])
            P.dma("sp", out=d["tb1"][:, :, :], in_=k.tab1[t0:t0 + 128])
            P.dma("sp", out=d["tb2"][:, :, :, :], in_=k.tab2[t0:t0 + 128])
            P.A("activation", out=d["junk"][:, :], in_=x[:, :], func=AF.Square, accum_out=stt[:, 0:1])
            rstd_from_ss(k, stt[:, 1:2], stt[:, 0:1], stt[:, 2:3], 1024.0, EPS)
            P.V("scalar_tensor_tensor", out=hb[:, :], in0=x[:, :], scalar=stt[:, 1:2], in1=g_bc[:, :],
                op0=ALU.mult, op1=ALU.mult)
            P.dma("pool", out=k.h_tm[t0:t0 + 128, :], in_=hb[:, :])
            pb = bank(k)
            pv = bfv(pb)
            for c in range(8):
                P.M("transpose", out=pv[:, c * 128:(c + 1) * 128], in_=hb[:, c * 128:(c + 1) * 128],
                    identity=ident[:, :], accum=(c > 0))
            P.A("activation", out=hT[:, :, :].rearrange("p c t -> p (c t)"), in_=pv[:, 0:1024], func=AF.Copy)
            P.dma("pool", out=k.hT_t[i], in_=hT[:, :, :])

            def proj(c0, n):
                b = bank(k)
                for c in range(8):
                    P.M("matmul", out=b[:, 0:n], lhsT=hT[:, c, :], rhs=w_att[:, c, c0:c0 + n],
                        start=(c == 0), stop=(c == 7), accum=(c > 0))
                return b

            for ci, (c0, n) in enumerate(((0, 512), (512, 512), (1024, 512), (1536, 384))):
                b = bank(k)
                for c in range(8):
                    P.M("matmul", out=b[:, 0:n], lhsT=hT[:, c, :], rhs=w_r[:, c, c0:c0 + n],
                        start=(c == 0), stop=(c == 7), accum=(c > 0))
                if ci % 2 == 0:
                    P.A("activation", out=d["zr"][:, c0:c0 + n], in_=b[:, 0:n], func=AF.Copy)
                else:
                    P.V("tensor_copy", out=d["zr"][:, c0:c0 + n], in_=b[:, 0:n])
            P.dma("pool", out=k.z_r[t0:t0 + 128, :], in_=d["zr"][:, :])
            cos1 = bc(d["tb1"][:, 0, :], [128, 8, 16], 1)
            sin1 = bc(d["tb1"][:, 1, :], [128, 8, 16], 1)
            pq = proj(0, 384)
            P.A("activation", out=d["junk"][:, 0:384], in_=pq[:, 0:384], func=AF.Square, accum_out=stt[:, 3:4])
            rstd_from_ss(k, stt[:, 4:5], stt[:, 3:4], stt[:, 5:6], 384.0, EPS)
            P.V("tensor_copy", out=d["cqb"][:, :], in_=pq[:, 0:384])
            pb = bank(k)
            pv = bfv(pb)
            for c in range(3):
                P.M("transpose", out=pv[:, c * 128:(c + 1) * 128], in_=d["cqb"][:, c * 128:(c + 1) * 128],
                    identity=ident[:, :], accum=(c > 0))
            P.A("activation", out=d["cqT"][:, :, :].rearrange("p c t -> p (c t)"), in_=pv[:, 0:384], func=AF.Copy)
            q32 = d["q32"]
            q32f = q32[:, :, :].rearrange("p h e -> p (h e)")
            for (c0, n) in ((0, 512), (512, 256)):
                b = bank(k)
                for c in range(3):
                    P.M("matmul", out=b[:, 0:n], lhsT=d["cqT"][:, c, :], rhs=w_uq[:, c, c0:c0 + n],
                        start=(c == 0), stop=(c == 2), accum=(c > 0))
                P.V("tensor_scalar", out=q32f[:, c0:c0 + n], in0=b[:, 0:n], scalar1=stt[:, 4:5], scalar2=None,
                    op0=ALU.mult)
            qrot = d["qrot"]
            ta = d["ta"][:, :, 0, :]
            tb = d["tb"][:, :, 0, :]
            P.G("tensor_copy", out=qrot[:, :, 0:64], in_=q32[:, :, 0:64])
            P.V("tensor_tensor", out=ta, in0=q32[:, :, 64:80], in1=cos1, op=ALU.mult)
            P.V("tensor_tensor", out=tb, in0=q32[:, :, 80:96], in1=sin1, op=ALU.mult)
            P.V("tensor_tensor", out=qrot[:, :, 64:80], in0=ta, in1=tb, op=ALU.subtract)
            P.V("tensor_tensor", out=ta, in0=q32[:, :, 64:80], in1=sin1, op=ALU.mult)
            P.V("tensor_tensor", out=tb, in0=q32[:, :, 80:96], in1=cos1, op=ALU.mult)
            P.V("tensor_tensor", out=qrot[:, :, 80:96], in0=ta, in1=tb, op=ALU.add)
            pb = bank(k)
            pv = bfv(pb)
            for h in range(8):
                P.M("transpose", out=pv[0:96, h * 128:(h + 1) * 128], in_=qrot[:, h, :], identity=ident[:, :],
                    accum=(h > 0))
            P.A("activation", out=d["qT"][:, :, :].rearrange("p h t -> p (h t)"), in_=pv[0:96, 0:1024], func=AF.Copy)
            P.dma("pool", out=k.qT_mla[:, :, t0:t0 + 128].rearrange("h e s -> e h s"), in_=d["qT"][:, :, :])
            pkv = proj(384, 288)
            P.A("activation", out=d["junk"][:, 0:256], in_=pkv[:, 0:256], func=AF.Square, accum_out=stt[:, 6:7])
            rstd_from_ss(k, stt[:, 7:8], stt[:, 6:7], stt[:, 8:9], 256.0, EPS)
            P.V("tensor_copy", out=d["ckvb"][:, :], in_=pkv[:, 0:256])
            kr = d["kr"]
            c1 = d["tb1"][:, 0, :]
            s1 = d["tb1"][:, 1, :]
            t2a = d["ta"][:, 0, 1, :]
            t2b = d["tb"][:, 0, 1, :]
            P.V("tensor_tensor", out=t2a, in0=pkv[:, 256:272], in1=c1, op=ALU.mult)
            P.V("tensor_tensor", out=t2b, in0=pkv[:, 272:288], in1=s1, op=ALU.mult)
            P.V("tensor_tensor", out=kr[:, 0:16], in0=t2a, in1=t2b, op=ALU.subtract)
            P.V("tensor_tensor", out=t2a, in0=pkv[:, 256:272], in1=s1, op=ALU.mult)
            P.V("tensor_tensor", out=t2b, in0=pkv[:, 272:288], in1=c1, op=ALU.mult)
            P.V("tensor_tensor", out=kr[:, 16:32], in0=t2a, in1=t2b, op=ALU.add)
            pb = bank(k)
            pv = bfv(pb)
            for c in range(2):
                P.M("transpose", out=pv[:, c * 128:(c + 1) * 128], in_=d["ckvb"][:, c * 128:(c + 1) * 128],
                    identity=ident[:, :], accum=(c > 0))
            P.A("activation", out=d["ckvT"][:, :, :].rearrange("p c t -> p (c t)"), in_=pv[:, 0:256], func=AF.Copy)
            kt, vt = d["kt"], d["vt"]
            for half in range(2):
                b = bank(k)
                for c in range(2):
                    P.M("matmul", out=b[:, 0:512], lhsT=d["ckvT"][:, c, :], rhs=w_ukv[:, c, half * 512:(half + 1) * 512],
                        start=(c == 0), stop=(c == 1), accum=(c > 0))
                b3 = b[:, 0:512].rearrange("p (h e) -> p h e", h=4)
                P.V("tensor_scalar", out=kt[:, half * 4:(half + 1) * 4, 0:64], in0=b3[:, :, 0:64], scalar1=stt[:, 7:8],
                    scalar2=None, op0=ALU.mult)
                P.V("tensor_scalar", out=vt[:, half * 4:(half + 1) * 4, :], in0=b3[:, :, 64:128], scalar1=stt[:, 7:8],
                    scalar2=None, op0=ALU.mult)
            P.G("tensor_copy", out=kt[:, :, 64:96], in_=bc(kr[:, :], [128, 8, 32], 1))
            pb = bank(k)
            pv = bfv(pb)
            for h in range(8):
                P.M("transpose", out=pv[0:96, h * 128:(h + 1) * 128], in_=kt[:, h, :], identity=ident[:, :],
                    accum=(h > 0))
            P.A("activation", out=d["kT"][:, :, :].rearrange("p h t -> p (h t)"), in_=pv[0:96, 0:1024], func=AF.Copy)
            P.dma("pool", out=k.kT_mla[:, :, t0:t0 + 128].rearrange("h e s -> e h s"), in_=d["kT"][:, :, :])
            P.dma("pool", out=k.v_mla[t0:t0 + 128, :], in_=vt[:, :, :].rearrange("p h e -> p (h e)"))

            def qknorm_rope(pb_ap, nh, gbc, n32, rot, soff):
                gs = d["gst"]
                P.A("activation", out=d["sq"][:, 0:nh * 64], in_=pb_ap, func=AF.Square)
                P.V("tensor_reduce", out=gs[:, soff:soff + nh],
                    in_=d["sq"][:, 0:nh * 64].rearrange("p (h e) -> p h e", h=nh), axis=AX.X, op=ALU.add)
                rstd_from_ss(k, gs[:, soff + 8:soff + 8 + nh], gs[:, soff:soff + nh], gs[:, soff + 16:soff + 16 + nh],
                             64.0, EPS)
                P.V("tensor_tensor", out=n32[:, :, :], in0=pb_ap.rearrange("p (h e) -> p h e", h=nh),
                    in1=bc(gs[:, soff + 8:soff + 8 + nh], [128, nh, 64], 2), op=ALU.mult)
                P.G("tensor_tensor", out=n32[:, :, :], in0=n32[:, :, :], in1=bc(gbc[:, :], [128, nh, 64], 1),
                    op=ALU.mult)
                v5 = n32[:, :, :].rearrange("p h (a b e) -> p h a b e", a=2, b=2)
                r5 = rot[:, :, :].rearrange("p h (a b e) -> p h a b e", a=2, b=2)
                x1, x2 = v5[:, :, :, 0, :], v5[:, :, :, 1, :]
                cos2 = bc(d["tb2"][:, 0, :, :], [128, nh, 2, 16], 1)
                sin2 = bc(d["tb2"][:, 1, :, :], [128, nh, 2, 16], 1)
                ta4 = d["ta"][:, 0:nh, :, :]
                tb4 = d["tb"][:, 0:nh, :, :]
                P.V("tensor_tensor", out=ta4, in0=x1, in1=cos2, op=ALU.mult)
                P.V("tensor_tensor", out=tb4, in0=x2, in1=sin2, op=ALU.mult)
                P.V("tensor_tensor", out=r5[:, :, :, 0, :], in0=ta4, in1=tb4, op=ALU.subtract)
                P.V("tensor_tensor", out=ta4, in0=x1, in1=sin2, op=ALU.mult)
                P.V("tensor_tensor", out=tb4, in0=x2, in1=cos2, op=ALU.mult)
                P.V("tensor_tensor", out=r5[:, :, :, 1, :], in0=ta4, in1=tb4, op=ALU.add)

            pgq = proj(672, 512)
            qknorm_rope(pgq[:, 0:512], 8, gq_bc, d["qn"], d["gqr"], 0)
            pb = bank(k)
            pv = bfv(pb)
            gqf = d["gqr"][:, :, :].rearrange("p h e -> p (h e)")
            for c in range(4):
                P.M("transpose", out=pv[:, c * 128:(c + 1) * 128], in_=gqf[:, c * 128:(c + 1) * 128],
                    identity=ident[:, :], accum=(c > 0))
            P.A("activation", out=d["gqT"][:, :, :].rearrange("p c t -> p (c t)"), in_=pv[:, 0:512], func=AF.Copy)
            P.dma("pool", out=k.qT_gqa[:, t0:t0 + 128].rearrange("(c p) s -> p c s", p=128), in_=d["gqT"][:, :, :])
            pgk = proj(1184, 256)
            P.A("activation", out=d["gv"][:, :], in_=pgk[:, 128:256], func=AF.Copy)
            P.dma("pool", out=k.v_gqa[t0:t0 + 128, :], in_=d["gv"][:, :])
            qknorm_rope(pgk[:, 0:128], 2, gk_bc, d["kn"], d["gkr"], 2)
            pb = bank(k)
            pv = bfv(pb)
            P.M("transpose", out=pv[:, 0:128], in_=d["gkr"][:, :, :].rearrange("p h e -> p (h e)"),
                identity=ident[:, :])
            P.A("activation", out=d["gkT"][:, :], in_=pv[:, 0:128], func=AF.Copy)
            P.dma("pool", out=k.kT_gqa[:, t0:t0 + 128], in_=d["gkT"][:, :])


def phase_attn(k, l):
    nc, P, S, NT = k.nc, k.P, k.S, k.NT
    QB = 512
    NQB = S // QB
    NG = NT // 2
    with contextlib.ExitStack() as st:
        def SB(name, shape, dt):
            return st.enter_context(nc.sbuf_tensor(un("p2_" + name), list(shape), dt))
        pps = [st.enter_context(nc.psum_tensor(un("pp%d" % i), [128, 1024], F32)) for i in range(3)]
        obs = [st.enter_context(nc.psum_tensor(un("ob%d" % i), [128, 512], F32)) for i in range(2)]
        kts = [SB("kt%d" % i, [96, S], BF16) for i in range(2)]
        qts = [SB("qt%d" % i, [96, S], BF16) for i in range(2)]
        vxs = [SB("vx%d" % i, [128, NT, 65], BF16) for i in range(2)]
        pts = [SB("pt%d" % i, [128, 2 * QB], BF16) for i in range(3)]
        o32 = [SB("o32_%d" % i, [65, QB], F32) for i in range(2)]
        osb = [SB("o%d" % i, [128, 4, 64], BF16) for i in range(2)]
        rsb = [SB("rs%d" % i, [128, 4], F32) for i in range(2)]
        for vx in vxs:
            P.G("memset", ap=vx[:, :, 64:65], constant=1.0, W=[vx[:, :, :]])
        heads = [("mla", h) for h in range(8)] + [("gqa", h) for h in range(8)]
        kvi = -1
        npp = 0
        nqb = 0
        for hi, (kind, h) in enumerate(heads):
            qt = qts[hi % 2]
            if kind == "mla":
                dq = 96
                kvi += 1
                kt, vx = kts[kvi % 2], vxs[kvi % 2]
                P.dma("sp", out=kt[0:96, :], in_=k.kT_mla[h])
                vsrc = k.v_mla[:, h * 64:(h + 1) * 64].rearrange("(c p) e -> p c e", p=128)
                for c0 in range(0, NT, 8):
                    c1 = min(NT, c0 + 8)
                    P.dma("sp", out=vx[:, c0:c1, 0:64], in_=vsrc[:, c0:c1, :])
                P.dma("sp", out=qt[0:96, :], in_=k.qT_mla[h])
                oscr = k.o_mla
            else:
                dq = 64
                if h % 4 == 0:
                    kvi += 1
                    kvh = h // 4
                    kt, vx = kts[kvi % 2], vxs[kvi % 2]
                    P.dma("sp", out=kt[0:64, :], in_=k.kT_gqa[kvh * 64:(kvh + 1) * 64, :])
                    vsrc = k.v_gqa[:, kvh * 64:(kvh + 1) * 64].rearrange("(c p) e -> p c e", p=128)
                    for c0 in range(0, NT, 8):
                        c1 = min(NT, c0 + 8)
                        P.dma("sp", out=vx[:, c0:c1, 0:64], in_=vsrc[:, c0:c1, :])
                P.dma("sp", out=qt[0:64, :], in_=k.qT_gqa[h * 64:(h + 1) * 64, :])
                oscr = k.o_gqa
            scale = float(dq) ** -0.5
            for qb in range(NQB):
                ob = obs[nqb % 2]
                o3, o_s, r_s = o32[nqb % 2], osb[nqb % 2], rsb[nqb % 2]
                nqb += 1
                for g in range(NG):
                    pp = pps[npp % 3]
                    pt = pts[npp % 3]
                    npp += 1
                    for u in range(2):
                        kc = 2 * g + u
                        P.M("matmul", out=pp[:, u * QB:(u + 1) * QB], lhsT=kt[0:dq, kc * 128:(kc + 1) * 128],
                            rhs=qt[0:dq, qb * QB:(qb + 1) * QB], start=True, stop=True, accum=(u > 0))
                    P.A("activation", out=pt[:, :], in_=pp[:, :], func=AF.Exp, scale=scale)
                    for u in range(2):
                        kc = 2 * g + u
                        P.M("matmul", out=ob[0:65, 0:QB], lhsT=vx[:, kc, 0:65], rhs=pt[:, u * QB:(u + 1) * QB],
                            start=(kc == 0), stop=(kc == NT - 1), accum=(kc > 0))
                P.A("activation", out=o3[:, :], in_=ob[0:65, 0:QB], func=AF.Copy)
                pp = pps[npp % 3]
                npp += 1
                for j in range(4):
                    P.M("transpose", out=pp[:, j * 128:j * 128 + 65], in_=o3[0:65, j * 128:(j + 1) * 128],
                        identity=k.identf[0:65, 0:65], accum=(j > 0))
                tb3 = pp[:, 0:512].rearrange("p (j e) -> p j e", j=4)
                P.V("reciprocal", out=r_s[:, :], in_=tb3[:, :, 64])
                P.V("tensor_tensor", out=o_s[:, :, :], in0=tb3[:, :, 0:64], in1=bc(r_s[:, :], [128, 4, 64], 2),
                    op=ALU.mult)
                P.dma("pool", out=oscr[qb * QB:(qb + 1) * QB, h * 64:(h + 1) * 64].rearrange("(j p) e -> p j e", p=128),
                      in_=o_s[:, :, :])


def phase_rwkv(k, l, d):
    nc, P, S, NT = k.nc, k.P, k.S, k.NT
    w = k.w
    C0 = DECAY_C
    with contextlib.ExitStack() as st:
        def SB(name, shape, dt):
            return st.enter_context(nc.sbuf_tensor(un("rw_" + name), list(shape), dt))
        alloc_banks(k, st)
        bcn = {}

        def load_bc(name, src_row, n=512):
            t = SB(name, [128, n], F32)
            P.dma("sp", out=t[:, :], in_=src_row.broadcast_to([128, n]))
            bcn[name] = t
            return t
        w0b = load_bc("w0", w["rwkv_w0"][l, d:d + 1, :])
        a0b = load_bc("a0", w["rwkv_a0"][l, d:d + 1, :])
        kkb = load_bc("kk", w["rwkv_k_k"][l:l + 1, :])
        kab = load_bc("ka", w["rwkv_k_a"][l:l + 1, :])
        if d == 0:
            mub = load_bc("mu", w["rwkv_mu"][l:l + 1, :], 1920)
        else:
            a0o = load_bc("a0o", w["rwkv_a0"][l, 0:1, :])
            rkb = load_bc("rk", w["rwkv_r_k"][l:l + 1].rearrange("o h n -> o (h n)"))
            lnw = load_bc("lnw", w["rwkv_ln_w"][l:l + 1, :])
            lnb = load_bc("lnb", w["rwkv_ln_b"][l:l + 1, :])
            g2s = SB("g2", [128, 512], BF16)
            P.dma("pool", out=g2s[:, :], in_=w["rwkv_g2"][l])
        w2s = SB("w2", [128, 512], BF16)
        a2s = SB("a2", [128, 512], BF16)
        P.dma("pool", out=w2s[:, :], in_=w["rwkv_w2"][l].rearrange("d r c -> (d r) c"))
        P.dma("pool", out=a2s[:, :], in_=w["rwkv_a2"][l].rearrange("d r c -> (d r) c"))
        m2 = SB("m2", [128, 2, 128], F32)
        mT = SB("mT", [128, 128], F32)
        bdm = SB("bdm", [128, 128], F32)
        onec = SB("onec", [128, 1], F32)
        P.dma("sp", out=m2[:, 0, :], in_=k.msk_d[d, 0])
        P.dma("sp", out=m2[:, 1, :], in_=k.msk_d[d, 1])
        P.dma("sp", out=mT[:, :], in_=k.msk_d[1 - d, 0])
        P.dma("sp", out=bdm[:, :], in_=k.bdm_d)
        P.G("memset", ap=onec[:, :], constant=1.0)
        H32 = SB("H32", [128, 4, 128], F32)
        Hb = SB("Hb", [128, 4, 128], BF16)
        P.G("memset", ap=H32[:, :, :], constant=0.0)
        P.G("memset", ap=Hb[:, :, :], constant=0.0)
        if d == 0:
            zin = [SB("z", [128, 1920], F32), SB("zp", [128, 1920], F32), SB("zn", [128, 1920], F32),
                   SB("zt", [128, 1920], F32)]
        sets = []
        for s_ in range(2):
            dd = {}
            lst = [("zm", [128, 1920], F32), ("lor", [128, 384], BF16), ("lorT", [128, 3, 128], BF16),
                   ("tok4", [128, 4, 512], BF16), ("vb", [128, 512], BF16), ("TTs", [128, 4, 4, 128], BF16),
                   ("gC", [128, 4], F32), ("sm", [128, 64], F32), ("Y32", [128, 512], F32), ("hT_", [128, 128], F32)]
            for t in range(8):
                lst.append(("T%d" % t, [128, 512], F32))
            for p in range(4):
                lst += [("XL%d" % p, [128, 2, 2, 128], BF16), ("LT%d" % p, [128, 2, 128], BF16),
                        ("ARB%d" % p, [128, 2, 128], BF16), ("AK%d" % p, [128, 2, 2, 128], BF16),
                        ("PT%d" % p, [128, 128], BF16), ("AKV%d" % p, [128, 128], BF16), ("Ub%d" % p, [128, 128], BF16)]
            if d == 1:
                lst += [("yfw", [128, 512], F32), ("ob", [128, 512], BF16), ("oT", [128, 4, 128], BF16)]
            for name, shape, dt in lst:
                dd[name] = SB("%s_%d" % (name, s_), shape, dt)
            sets.append(dd)

        order = list(range(NT)) if d == 0 else list(range(NT - 1, -1, -1))
        for it, i in enumerate(order):
            D_ = sets[it % 2]
            t0 = i * 128
            zm = D_["zm"]
            T = [D_["T%d" % t] for t in range(8)]
            sm = D_["sm"]
            if d == 0:
                z, zp, zn, zt = zin
                P.dma("sp", out=z[:, :], in_=k.z_r[t0:t0 + 128, :])
                if i == 0:
                    P.G("memset", ap=zp[0:1, :], constant=0.0)
                    P.dma("sp", out=zp[1:128, :], in_=k.z_r[0:127, :])
                else:
                    P.dma("sp", out=zp[:, :], in_=k.z_r[t0 - 1:t0 + 127, :])
                if i == NT - 1:
                    P.G("memset", ap=zn[:, :], constant=0.0)
                    P.dma("sp", out=zn[0:127, :], in_=k.z_r[t0 + 1:t0 + 128, :])
                else:
                    P.dma("sp", out=zn[:, :], in_=k.z_r[t0 + 1:t0 + 129, :])
                P.G("tensor_tensor", out=zt[:, :], in0=zp[:, :], in1=zn[:, :], op=ALU.add)
                P.V("scalar_tensor_tensor", out=zt[:, :], in0=zt[:, :], scalar=0.5, in1=z[:, :], op0=ALU.mult,
                    op1=ALU.subtract)
                P.G("tensor_tensor", out=zt[:, :], in0=zt[:, :], in1=mub[:, :], op=ALU.mult)
                P.V("tensor_tensor", out=zm[:, :], in0=z[:, :], in1=zt[:, :], op=ALU.add)
                P.dma("pool", out=k.zmix[t0:t0 + 128, :], in_=zm[:, :])
            else:
                P.dma("sp", out=zm[:, :], in_=k.zmix[t0:t0 + 128, :])
                P.dma("sp", out=D_["yfw"][:, :], in_=k.y_fw[t0:t0 + 128, :])
            r_ = zm[:, 0:512]
            kx = zm[:, 512:1024]
            v_ = zm[:, 1024:1536]
            lor, lorT = D_["lor"], D_["lorT"]
            P.A("activation", out=lor[:, 0:128], in_=zm[:, 1536:1664], func=AF.Tanh)
            P.V("tensor_copy", out=lor[:, 128:256], in_=zm[:, 1664:1792])
            nl = 2
            if d == 1:
                P.A("activation", out=lor[:, 256:384], in_=zm[:, 1792:1920], func=AF.Sigmoid)
                nl = 3
            pb = bank(k)
            pv = bfv(pb)
            for c in range(nl):
                P.M("transpose", out=pv[:, c * 128:(c + 1) * 128], in_=lor[:, c * 128:(c + 1) * 128],
                    identity=k.ident[:, :], accum=(c > 0))
            P.A("activation", out=lorT[:, 0:nl, :].rearrange("p c t -> p (c t)"), in_=pv[:, 0:nl * 128], func=AF.Copy)
            ds = slice(d * 64, (d + 1) * 64)
            sg, asig = T[0], T[1]
            pb = bank(k)
            P.M("matmul", out=pb[:, 0:512], lhsT=lorT[ds, 0, :], rhs=w2s[ds, :], start=True, stop=True)
            P.V("tensor_tensor", out=sg[:, :], in0=pb[:, 0:512], in1=w0b[:, :], op=ALU.add)
            P.A("activation", out=sg[:, :], in_=sg[:, :], func=AF.Sigmoid)
            pb = bank(k)
            P.M("matmul", out=pb[:, 0:512], lhsT=lorT[ds, 1, :], rhs=a2s[ds, :], start=True, stop=True)
            P.V("tensor_tensor", out=asig[:, :], in0=pb[:, 0:512], in1=a0b[:, :], op=ALU.add)
            P.A("activation", out=asig[:, :], in_=asig[:, :], func=AF.Sigmoid)
            eP, eN, eX = T[2], T[3], T[4]
            pc = bank(k)
            P.M("matmul", out=pc[:, 0:512], lhsT=m2[:, 1, :], rhs=sg[:, :], start=True, stop=True)
            pcx = bank(k)
            P.M("matmul", out=pcx[:, 0:512], lhsT=m2[:, 0, :], rhs=sg[:, :], start=True, stop=True)
            P.A("activation", out=eP[:, :], in_=pc[:, 0:512], func=AF.Exp, scale=-C0)
            P.A("activation", out=eN[:, :], in_=pc[:, 0:512], func=AF.Exp, scale=C0)
            P.A("activation", out=eX[:, :], in_=pcx[:, 0:512], func=AF.Exp, scale=-C0)
            pg = bank(k)
            for p in range(4):
                P.M("matmul", out=pg[:, p:p + 1], lhsT=sg[:, p * 128:(p + 1) * 128], rhs=onec[:, 0:1], start=True,
                    stop=True, accum=(p > 0))
            P.A("activation", out=D_["gC"][:, :], in_=pg[:, 0:4], func=AF.Exp, scale=-C0)
            kk, kt, bb = T[5], T[6], T[7]
            tok4, vb = D_["tok4"], D_["vb"]
            P.G("tensor_tensor", out=kk[:, :], in0=kx, in1=kkb[:, :], op=ALU.mult)
            P.A("activation", out=kt[:, :], in_=kk[:, :], func=AF.Square)
            P.V("tensor_reduce", out=sm[:, 0:8], in_=kt[:, :].rearrange("p (h e) -> p h e", h=8), axis=AX.X, op=ALU.add)
            P.V("tensor_scalar", out=sm[:, 0:8], in0=sm[:, 0:8], scalar1=1e-24, scalar2=None, op0=ALU.max)
            P.A("activation", out=sm[:, 8:16], in_=sm[:, 0:8], func=AF.Ln)
            P.A("activation", out=sm[:, 16:24], in_=sm[:, 8:16], func=AF.Exp, scale=-0.5)
            kk3 = kk[:, :].rearrange("p (h e) -> p h e", h=8)
            P.V("tensor_tensor", out=kk3, in0=kk3, in1=bc(sm[:, 16:24], [128, 8, 64], 2), op=ALU.mult)
            P.V("scalar_tensor_tensor", out=kt[:, :], in0=asig[:, :], scalar=-1.0, in1=kab[:, :], op0=ALU.add, op1=ALU.mult)
            P.V("scalar_tensor_tensor", out=kt[:, :], in0=kt[:, :], scalar=1.0, in1=kx, op0=ALU.add, op1=ALU.mult)
            P.G("tensor_tensor", out=bb[:, :], in0=kk[:, :], in1=asig[:, :], op=ALU.mult)
            P.V("scalar_tensor_tensor", out=tok4[:, 0, :], in0=kk[:, :], scalar=-1.0, in1=eX[:, :], op0=ALU.mult, op1=ALU.mult)
            P.G("tensor_tensor", out=tok4[:, 1, :], in0=r_, in1=eP[:, :], op=ALU.mult)
            P.G("tensor_tensor", out=tok4[:, 2, :], in0=bb[:, :], in1=eN[:, :], op=ALU.mult)
            P.V("tensor_tensor", out=tok4[:, 3, :], in0=kt[:, :], in1=eN[:, :], op=ALU.mult)
            P.A("activation", out=vb[:, :], in_=v_, func=AF.Copy)
            TTs = D_["TTs"]
            for p0 in (0, 2):
                pb = bank(k)
                pv = bfv(pb)
                for p in (p0, p0 + 1):
                    for q in range(4):
                        o0 = (p - p0) * 512 + q * 128
                        P.M("transpose", out=pv[:, o0:o0 + 128], in_=tok4[:, q, p * 128:(p + 1) * 128],
                            identity=k.ident[:, :], accum=not (p == p0 and q == 0))
                P.A("activation", out=TTs[:, p0:p0 + 2, :, :].rearrange("p a q t -> p (a q t)"), in_=pv[:, 0:1024],
                    func=AF.Copy)
            XL = [D_["XL%d" % p] for p in range(4)]
            LT = [D_["LT%d" % p] for p in range(4)]
            ARB = [D_["ARB%d" % p] for p in range(4)]
            AK = [D_["AK%d" % p] for p in range(4)]
            PT = [D_["PT%d" % p] for p in range(4)]
            AKV = [D_["AKV%d" % p] for p in range(4)]
            Ub = [D_["Ub%d" % p] for p in range(4)]
            m2b = bc(m2[:, :, :].rearrange("p a t -> p (a t)"), [128, 2, 256], 1)
            for p in range(4):
                bB, bK, bL = bank(k), bank(k), bank(k)
                for e in range(2):
                    bs = slice(e * 64, (e + 1) * 64)
                    ar = TTs[bs, p, 0:2, :].rearrange("p q t -> p (q t)")
                    P.M("matmul", out=bB[:, e * 256:(e + 1) * 256], lhsT=TTs[bs, p, 2, :], rhs=ar, start=True, stop=True,
                        accum=(e > 0))
                    P.M("matmul", out=bK[:, e * 256:(e + 1) * 256], lhsT=TTs[bs, p, 3, :], rhs=ar, start=True, stop=True,
                        accum=(e > 0))
                    P.M("matmul", out=bL[:, e * 128:(e + 1) * 128], lhsT=TTs[bs, p, 0, :], rhs=TTs[bs, p, 2, :], start=True,
                        stop=True, accum=(e > 0))
                bB4 = bB[:, 0:512].rearrange("p (e a t) -> p e a t", e=2, a=2)
                P.V("tensor_tensor", out=XL[p][:, :, 1, :], in0=bB4[:, :, 0, :], in1=bc(m2[:, 0, :], [128, 2, 128], 1),
                    op=ALU.mult)
                P.V("tensor_tensor", out=ARB[p][:, :, :], in0=bB4[:, :, 1, :], in1=bc(m2[:, 1, :], [128, 2, 128], 1),
                    op=ALU.mult)
                P.V("tensor_tensor", out=AK[p][:, :, :, :].rearrange("p e a t -> p e (a t)"),
                    in0=bK[:, 0:512].rearrange("p (e n) -> p e n", e=2), in1=m2b, op=ALU.mult)
                P.V("tensor_tensor", out=LT[p][:, :, :], in0=bL[:, 0:256].rearrange("p (e t) -> p e t", e=2),
                    in1=bc(mT[:, :], [128, 2, 128], 1), op=ALU.mult)
                P.G("tensor_copy", out=XL[p][:, :, 0, :], in_=bc(k.ident[:, :], [128, 2, 128], 1))
            for lev in range(7):
                last = (lev == 6)
                for p in range(4):
                    bb_ = bank(k)
                    for e in range(2):
                        if not last:
                            P.M("matmul", out=bb_[:, e * 256:(e + 1) * 256], lhsT=LT[p][:, e, :],
                                rhs=XL[p][:, e, :, :].rearrange("p a t -> p (a t)"), start=True, stop=True, accum=(e > 0))
                        else:
                            P.M("matmul", out=bb_[:, e * 256:e * 256 + 128], lhsT=LT[p][:, e, :],
                                rhs=XL[p][:, e, 0, :], start=True, stop=True, accum=(e > 0))
                    if not last:
                        ba_ = bank(k)
                        for e in range(2):
                            P.M("matmul", out=ba_[:, e * 128:(e + 1) * 128], lhsT=XL[p][:, e, 1, :], rhs=LT[p][:, e, :],
                                start=True, stop=True, accum=(e > 0))
                    b4 = bb_[:, 0:512].rearrange("p (e a t) -> p e a t", e=2, a=2)
                    P.V("tensor_tensor", out=XL[p][:, :, 0, :], in0=XL[p][:, :, 0, :], in1=b4[:, :, 0, :], op=ALU.add)
                    if not last:
                        P.A("activation", out=XL[p][:, :, 1, :], in_=b4[:, :, 1, :], func=AF.Copy)
                        P.A("activation", out=LT[p][:, :, :], in_=ba_[:, 0:256].rearrange("p (e t) -> p e t", e=2),
                            func=AF.Copy)
            for p in range(4):
                pb = bank(k)
                P.M("matmul", out=pb[:, 0:256].rearrange("p (e t) -> p e t", e=2), lhsT=tok4[:, 0, p * 128:(p + 1) * 128],
                    rhs=XL[p][:, :, 0, :], start=True, stop=True)
                P.A("activation", out=PT[p][0:64, :], in_=pb[0:64, 0:128], func=AF.Copy)
                P.V("tensor_copy", out=PT[p][64:128, :], in_=pb[64:128, 128:256])
                pb2 = bank(k)
                for e in range(2):
                    h = 2 * p + e
                    P.M("matmul", out=pb2[:, e * 64:(e + 1) * 64], lhsT=AK[p][:, e, 0, :], rhs=vb[:, h * 64:(h + 1) * 64],
                        start=True, stop=True, accum=(e > 0))
                P.A("activation", out=AKV[p][:, :], in_=pb2[:, 0:128], func=AF.Copy)
            Y32 = D_["Y32"]
            for p in range(4):
                ps = slice(p * 128, (p + 1) * 128)
                bU = bank(k)
                for e in range(2):
                    P.M("matmul", out=bU[:, e * 64:(e + 1) * 64], lhsT=XL[p][:, e, 0, :], rhs=AKV[p][:, e * 64:(e + 1) * 64],
                        start=(e == 0), stop=False, accum=(e > 0))
                P.M("matmul", out=bU[:, 0:128], lhsT=PT[p][:, :], rhs=Hb[:, p, :], start=False, stop=True, accum=True)
                P.A("activation", out=Ub[p][:, :], in_=bU[:, 0:128], func=AF.Copy)
                bY = bank(k)
                for e in range(2):
                    h = 2 * p + e
                    P.M("matmul", out=bY[:, e * 64:(e + 1) * 64], lhsT=AK[p][:, e, 1, :], rhs=vb[:, h * 64:(h + 1) * 64],
                        start=(e == 0), stop=False, accum=(e > 0))
                P.M("matmul", out=bY[:, 0:128], lhsT=TTs[:, p, 1, :], rhs=Hb[:, p, :], start=False, stop=False, accum=True)
                for e in range(2):
                    P.M("matmul", out=bY[:, e * 64:(e + 1) * 64], lhsT=ARB[p][:, e, :], rhs=Ub[p][:, e * 64:(e + 1) * 64],
                        start=False, stop=(e == 1), accum=True)
                P.V("tensor_copy", out=Y32[:, ps], in_=bY[:, 0:128])
                bH = bank(k)
                P.M("matmul", out=bH[:, 0:128], lhsT=tok4[:, 2, ps], rhs=Ub[p][:, :], start=True, stop=False)
                P.M("matmul", out=bH[:, 0:128], lhsT=tok4[:, 3, ps], rhs=vb[:, ps], start=False, stop=True, accum=True)
                hT_ = D_["hT_"]
                P.V("tensor_tensor", out=hT_[:, :], in0=bH[:, 0:128], in1=H32[:, p, :], op=ALU.add)
                P.V("scalar_tensor_tensor", out=H32[:, p, :], in0=hT_[:, :], scalar=D_["gC"][:, p:p + 1], in1=bdm[:, :],
                    op0=ALU.mult, op1=ALU.mult)
                P.A("activation", out=Hb[:, p, :], in_=H32[:, p, :], func=AF.Copy)
            if d == 0:
                P.dma("pool", out=k.y_fw[t0:t0 + 128, :], in_=Y32[:, :])
            else:
                wkv, sq, bon = T[3], T[4], T[2]
                pb = bank(k)
                P.M("matmul", out=pb[:, 0:512], lhsT=lorT[0:64, 1, :], rhs=a2s[0:64, :], start=True, stop=True)
                P.V("tensor_tensor", out=T[0][:, :], in0=pb[:, 0:512], in1=a0o[:, :], op=ALU.add)
                P.A("activation", out=T[0][:, :], in_=T[0][:, :], func=AF.Sigmoid)
                P.V("scalar_tensor_tensor", out=T[0][:, :], in0=T[0][:, :], scalar=-1.0, in1=kab[:, :], op0=ALU.add,
                    op1=ALU.mult)
                P.V("scalar_tensor_tensor", out=T[0][:, :], in0=T[0][:, :], scalar=1.0, in1=kx, op0=ALU.add, op1=ALU.mult)
                P.G("tensor_tensor", out=T[0][:, :], in0=T[0][:, :], in1=kt[:, :], op=ALU.add)
                P.G("tensor_tensor", out=T[0][:, :], in0=T[0][:, :], in1=r_, op=ALU.mult)
                P.G("tensor_tensor", out=T[0][:, :], in0=T[0][:, :], in1=rkb[:, :], op=ALU.mult)
                P.V("tensor_reduce", out=sm[:, 24:32], in_=T[0][:, :].rearrange("p (h e) -> p h e", h=8), axis=AX.X,
                    op=ALU.add)
                P.V("tensor_tensor", out=bon[:, :].rearrange("p (h e) -> p h e", h=8),
                    in0=v_.rearrange("p (h e) -> p h e", h=8), in1=bc(sm[:, 24:32], [128, 8, 64], 2), op=ALU.mult)
                P.G("tensor_tensor", out=wkv[:, :], in0=Y32[:, :], in1=D_["yfw"][:, :], op=ALU.add)
                w3 = wkv[:, :].rearrange("p (h e) -> p h e", h=8)
                P.V("tensor_reduce", out=sm[:, 32:40], in_=w3, axis=AX.X, op=ALU.add)
                P.V("tensor_scalar", out=sm[:, 32:40], in0=sm[:, 32:40], scalar1=-1.0 / 64, scalar2=None, op0=ALU.mult)
                P.V("tensor_tensor", out=w3, in0=w3, in1=bc(sm[:, 32:40], [128, 8, 64], 2), op=ALU.add)
                P.A("activation", out=sq[:, :], in_=wkv[:, :], func=AF.Square)
                P.V("tensor_reduce", out=sm[:, 40:48], in_=sq[:, :].rearrange("p (h e) -> p h e", h=8), axis=AX.X, op=ALU.add)
                rstd_from_ss(k, sm[:, 48:56], sm[:, 40:48], sm[:, 56:64], 64.0, GN_EPS)
                P.V("tensor_tensor", out=w3, in0=w3, in1=bc(sm[:, 48:56], [128, 8, 64], 2), op=ALU.mult)
                P.G("tensor_tensor", out=wkv[:, :], in0=wkv[:, :], in1=lnw[:, :], op=ALU.mult)
                P.G("tensor_tensor", out=wkv[:, :], in0=wkv[:, :], in1=lnb[:, :], op=ALU.add)
                P.G("tensor_tensor", out=wkv[:, :], in0=wkv[:, :], in1=bon[:, :], op=ALU.add)
                pb = bank(k)
                P.M("matmul", out=pb[:, 0:512], lhsT=lorT[:, 2, :], rhs=g2s[:, :], start=True, stop=True)
                P.V("tensor_tensor", out=D_["ob"][:, :], in0=wkv[:, :], in1=pb[:, 0:512], op=ALU.mult)
                pb = bank(k)
                pv = bfv(pb)
                for c in range(4):
                    P.M("transpose", out=pv[:, c * 128:(c + 1) * 128], in_=D_["ob"][:, c * 128:(c + 1) * 128],
                        identity=k.ident[:, :], accum=(c > 0))
                P.A("activation", out=D_["oT"][:, :, :].rearrange("p c t -> p (c t)"), in_=pv[:, 0:512], func=AF.Copy)
                P.dma("pool", out=k.oT_rw[i], in_=D_["oT"][:, :, :])


def phase_p3a(k, l, xin):
    nc, P, S, NT = k.nc, k.P, k.S, k.NT
    w = k.w
    with contextlib.ExitStack() as st:
        def SB(name, shape, dt):
            return st.enter_context(nc.sbuf_tensor(un("p3_" + name), list(shape), dt))
        alloc_banks(k, st, 6)
        pp2 = st.enter_context(nc.psum_tensor(un("p3_pp"), [128, 1024], F32))
        wg = SB("wg", [128, 8, 3072], BF16)
        wbr = SB("wbr", [128, 3, 4, 1024], BF16)
        wo = SB("wo", [128, 8, 1024], BF16)
        wq = SB("wq", [128, 8, 512], BF16)
        cwo = SB("cwo", [128, 4, 1024], BF16)
        bg = SB("bg", [1, 3072], BF16)
        ones = SB("ones", [1, 128], BF16)
        gcol = SB("gcol", [128, 16], F32)
        kmT = SB("kmT", [128, 4, 256], BF16)
        vm = SB("vm", [128, 2, 512], BF16)
        win = w["w_in"][l].rearrange("(c p) n -> p c n", p=128)
        for c in range(8):
            P.dma("pool", out=wg[:, c, :], in_=win[:, c, 3360:6432])
        for g in range(3):
            P.dma("pool", out=wbr[:, g, :, :], in_=w["w_branch"][l, g].rearrange("(c p) n -> p c n", p=128))
        P.dma("pool", out=wo[:, :, :], in_=w["w_out"][l].rearrange("(c p) n -> p c n", p=128))
        P.dma("pool", out=cwo[:, :, :], in_=w["cross_wo"][l].rearrange("(c p) n -> p c n", p=128))
        P.dma("pool", out=bg[0:1, :], in_=w["b_gate"][l:l + 1].rearrange("o g n -> o (g n)"))
        P.G("memset", ap=ones[0:1, :], constant=1.0)
        for c in range(8):
            load_col(k, "sp", gcol[:, c:c + 1], w["norm_cross"][l, c * 128:(c + 1) * 128], 128)
            load_col(k, "sp", gcol[:, 8 + c:9 + c], w["norm_mem"][l, c * 128:(c + 1) * 128], 128)
        with contextlib.ExitStack() as st2:
            stg = st2.enter_context(nc.sbuf_tensor(un("p3_stg"), [128, 8, 1024], F32))
            wkv = st2.enter_context(nc.sbuf_tensor(un("p3_wkv"), [128, 8, 1024], BF16))
            mx = st2.enter_context(nc.sbuf_tensor(un("p3_mx"), [128, 2, 1024], F32))
            mhb = st2.enter_context(nc.sbuf_tensor(un("p3_mhb"), [128, 2, 1024], BF16))
            mhT = st2.enter_context(nc.sbuf_tensor(un("p3_mhT"), [128, 8, 256], BF16))
            mjunk = st2.enter_context(nc.sbuf_tensor(un("p3_mjunk"), [128, 1024], BF16))
            mst = st2.enter_context(nc.sbuf_tensor(un("p3_mst"), [128, 8], F32))
            P.dma("sp", out=stg[:, :, 0:512], in_=w["cross_wq"][l].rearrange("(c p) n -> p c n", p=128))
            for c in range(8):
                P.V("tensor_scalar", out=wq[:, c, :], in0=stg[:, c, 0:512], scalar1=gcol[:, c:c + 1], scalar2=None,
                    op0=ALU.mult)
            P.dma("sp", out=stg[:, :, :], in_=w["cross_wkv"][l].rearrange("(c p) n -> p c n", p=128))
            for c in range(8):
                P.V("tensor_scalar", out=wkv[:, c, :], in0=stg[:, c, :], scalar1=gcol[:, 8 + c:9 + c], scalar2=None,
                    op0=ALU.mult)
            P.dma("sp", out=mx[:, :, :], in_=k.mem.rearrange("(j p) n -> p j n", p=128))
            for j in range(2):
                P.A("activation", out=mjunk[:, :], in_=mx[:, j, :], func=AF.Square, accum_out=mst[:, j:j + 1])
            rstd_from_ss(k, mst[:, 2:4], mst[:, 0:2], mst[:, 4:6], 1024.0, EPS)
            for j in range(2):
                P.V("tensor_scalar", out=mhb[:, j, :], in0=mx[:, j, :], scalar1=mst[:, 2 + j:3 + j], scalar2=None,
                    op0=ALU.mult)
                pb = bank(k)
                pv = bfv(pb)
                for c in range(8):
                    P.M("transpose", out=pv[:, c * 128:(c + 1) * 128], in_=mhb[:, j, c * 128:(c + 1) * 128],
                        identity=k.ident[:, :], accum=(c > 0))
                P.A("activation", out=mhT[:, :, j * 128:(j + 1) * 128],
                    in_=pv[:, 0:1024].rearrange("p (c t) -> p c t", c=8), func=AF.Copy)
            for h in range(4):
                pb = bank(k)
                for c in range(8):
                    P.M("matmul", out=pb[:, 0:256], lhsT=wkv[:, c, h * 128:(h + 1) * 128], rhs=mhT[:, c, :],
                        start=(c == 0), stop=(c == 7), accum=(c > 0))
                P.A("activation", out=kmT[:, h, :], in_=pb[:, 0:256], func=AF.Copy)
            for j in range(2):
                pb = bank(k)
                for c in range(8):
                    P.M("matmul", out=pb[:, 0:512], lhsT=mhT[:, c, j * 128:(j + 1) * 128], rhs=wkv[:, c, 512:1024],
                        start=(c == 0), stop=(c == 7), accum=(c > 0))
                P.A("activation", out=vm[:, j, :], in_=pb[:, 0:512], func=AF.Copy)
        P.barrier()

        sets = []
        for s_ in range(2):
            d = {}
            for name, shape, dt in [
                ("x", [128, 1024], F32), ("hT", [128, 8, 128], BF16), ("ob", [128, 2, 512], BF16),
                ("oT", [128, 3, 4, 128], BF16), ("gt", [128, 512], F32), ("tmp", [128, 512], F32),
                ("m", [128, 1024], F32), ("mb", [128, 1024], BF16), ("mT", [128, 8, 128], BF16),
                ("x1", [128, 1024], F32), ("junk", [128, 1024], BF16), ("st", [128, 16], F32),
                ("h2", [128, 1024], BF16), ("h2T", [128, 8, 128], BF16), ("qT", [128, 4, 128], BF16),
                ("p", [128, 4, 256], BF16), ("pn", [128, 4, 256], BF16), ("pT", [128, 8, 128], BF16),
                ("ocT", [128, 4, 128], BF16), ("x2", [128, 1024], F32),
            ]:
                d[name] = SB("%s%d" % (name, s_), shape, dt)
            sets.append(d)
        have_rw = "rb" in k.phases
        for i in range(NT):
            d = sets[i % 2]
            t0 = i * 128
            x, hT, oT, stt = d["x"], d["hT"], d["oT"], d["st"]
            P.dma("sp", out=x[:, :], in_=xin[t0:t0 + 128, :])
            P.dma("sp", out=hT[:, :, :], in_=k.hT_t[i])
            P.dma("sp", out=d["ob"][:, 0, :], in_=k.o_mla[t0:t0 + 128, :])
            P.dma("sp", out=d["ob"][:, 1, :], in_=k.o_gqa[t0:t0 + 128, :])
            if have_rw:
                P.dma("sp", out=oT[:, 2, :, :], in_=k.oT_rw[i])
            else:
                P.G("memset", ap=oT[:, 2, :, :], constant=0.0)
            pb = bank(k)
            pv = bfv(pb)
            obf = d["ob"][:, :, :].rearrange("p g n -> p (g n)")
            for c in range(8):
                P.M("transpose", out=pv[:, c * 128:(c + 1) * 128], in_=obf[:, c * 128:(c + 1) * 128],
                    identity=k.ident[:, :], accum=(c > 0))
            P.A("activation", out=oT[:, 0:2, :, :].rearrange("p g c t -> p (g c t)"), in_=pv[:, 0:1024], func=AF.Copy)
            for g in range(3):
                for n in range(2):
                    cs = slice(n * 512, (n + 1) * 512)
                    pz = bank(k)
                    for c in range(8):
                        P.M("matmul", out=pz[:, 0:512], lhsT=hT[:, c, :], rhs=wg[:, c, g * 1024 + n * 512:g * 1024 + (n + 1) * 512],
                            start=(c == 0), stop=False, accum=(c > 0))
                    P.M("matmul", out=pz[:, 0:512], lhsT=ones[0:1, :], rhs=bg[0:1, g * 1024 + n * 512:g * 1024 + (n + 1) * 512],
                        start=False, stop=True, accum=True)
                    P.A("activation", out=d["gt"][:, :], in_=pz[:, 0:512], func=AF.Sigmoid)
                    pbr = bank(k)
                    for c in range(4):
                        P.M("matmul", out=pbr[:, 0:512], lhsT=oT[:, g, c, :], rhs=wbr[:, g, c, cs],
                            start=(c == 0), stop=(c == 3), accum=(c > 0))
                    if g == 0:
                        P.V("tensor_tensor", out=d["m"][:, cs], in0=d["gt"][:, :], in1=pbr[:, 0:512], op=ALU.mult)
                    else:
                        P.V("tensor_tensor", out=d["tmp"][:, :], in0=d["gt"][:, :], in1=pbr[:, 0:512], op=ALU.mult)
                        if g == 1:
                            P.G("tensor_tensor", out=d["m"][:, cs], in0=d["m"][:, cs], in1=d["tmp"][:, :], op=ALU.add)
                        else:
                            P.G("tensor_tensor", out=d["mb"][:, cs], in0=d["m"][:, cs], in1=d["tmp"][:, :], op=ALU.add)
            pb = bank(k)
            pv = bfv(pb)
            for c in range(8):
                P.M("transpose", out=pv[:, c * 128:(c + 1) * 128], in_=d["mb"][:, c * 128:(c + 1) * 128],
                    identity=k.ident[:, :], accum=(c > 0))
            P.A("activation", out=d["mT"][:, :, :].rearrange("p c t -> p (c t)"), in_=pv[:, 0:1024], func=AF.Copy)
            for n in range(2):
                cs = slice(n * 512, (n + 1) * 512)
                pb = bank(k)
                for c in range(8):
                    P.M("matmul", out=pb[:, 0:512], lhsT=d["mT"][:, c, :], rhs=wo[:, c, cs],
                        start=(c == 0), stop=(c == 7), accum=(c > 0))
                P.V("tensor_tensor", out=d["x1"][:, cs], in0=x[:, cs], in1=pb[:, 0:512], op=ALU.add)
            x1 = d["x1"]
            P.A("activation", out=d["junk"][:, :], in_=x1[:, :], func=AF.Square, accum_out=stt[:, 0:1])
            rstd_from_ss(k, stt[:, 1:2], stt[:, 0:1], stt[:, 2:3], 1024.0, EPS)
            P.V("tensor_scalar", out=d["h2"][:, :], in0=x1[:, :], scalar1=stt[:, 1:2], scalar2=None, op0=ALU.mult)
            pb = bank(k)
            pv = bfv(pb)
            for c in range(8):
                P.M("transpose", out=pv[:, c * 128:(c + 1) * 128], in_=d["h2"][:, c * 128:(c + 1) * 128],
                    identity=k.ident[:, :], accum=(c > 0))
            P.A("activation", out=d["h2T"][:, :, :].rearrange("p c t -> p (c t)"), in_=pv[:, 0:1024], func=AF.Copy)
            pb = bank(k)
            for h in range(4):
                for c in range(8):
                    P.M("matmul", out=pb[:, h * 128:(h + 1) * 128], lhsT=wq[:, c, h * 128:(h + 1) * 128], rhs=d["h2T"][:, c, :],
                        start=(c == 0), stop=(c == 7), accum=(c > 0 or h > 0))
            P.A("activation", out=d["qT"][:, :, :].rearrange("p h t -> p (h t)"), in_=pb[:, 0:512], func=AF.Copy)
            for h in range(4):
                P.M("matmul", out=pp2[:, h * 256:(h + 1) * 256], lhsT=d["qT"][:, h, :], rhs=kmT[:, h, :],
                    start=True, stop=True, accum=(h > 0))
            for h in range(4):
                P.A("activation", out=d["p"][:, h, :], in_=pp2[:, h * 256:(h + 1) * 256], func=AF.Exp,
                    scale=128.0 ** -0.5, accum_out=stt[:, 4 + h:5 + h])
            P.V("reciprocal", out=stt[:, 8:12], in_=stt[:, 4:8])
            P.V("tensor_tensor", out=d["pn"][:, :, :], in0=d["p"][:, :, :], in1=bc(stt[:, 8:12], [128, 4, 256], 2),
                op=ALU.mult)
            pb = bank(k)
            pv = bfv(pb)
            pnf = d["pn"][:, :, :].rearrange("p h m -> p (h m)")
            for c in range(8):
                P.M("transpose", out=pv[:, c * 128:(c + 1) * 128], in_=pnf[:, c * 128:(c + 1) * 128],
                    identity=k.ident[:, :], accum=(c > 0))
            P.A("activation", out=d["pT"][:, :, :].rearrange("p c t -> p (c t)"), in_=pv[:, 0:1024], func=AF.Copy)
            pb = bank(k)
            for h in range(4):
                for mc in range(2):
                    P.M("matmul", out=pb[:, h * 128:(h + 1) * 128], lhsT=vm[:, mc, h * 128:(h + 1) * 128],
                        rhs=d["pT"][:, h * 2 + mc, :], start=(mc == 0), stop=(mc == 1), accum=(mc > 0 or h > 0))
            P.A("activation", out=d["ocT"][:, :, :].rearrange("p h t -> p (h t)"), in_=pb[:, 0:512], func=AF.Copy)
            for n in range(2):
                cs = slice(n * 512, (n + 1) * 512)
                pb = bank(k)
                for h in range(4):
                    P.M("matmul", out=pb[:, 0:512], lhsT=d["ocT"][:, h, :], rhs=cwo[:, h, cs],
                        start=(h == 0), stop=(h == 3), accum=(h > 0))
                P.V("tensor_tensor", out=d["x2"][:, cs], in0=x1[:, cs], in1=pb[:, 0:512], op=ALU.add)
            P.dma("pool", out=k.x2[t0:t0 + 128, :], in_=d["x2"][:, :])


def phase_p3b(k, l, xout, last):
    nc, P, S = k.nc, k.P, k.S
    w = k.w
    TT = 256
    NTT = S // TT
    with contextlib.ExitStack() as st:
        def SB(name, shape, dt):
            return st.enter_context(nc.sbuf_tensor(un("p4_" + name), list(shape), dt))
        alloc_banks(k, st)
        w1 = SB("w1", [128, 8, 4096], BF16)
        w2 = SB("w2", [128, 32, 1024], BF16)
        gcol = SB("gcol", [128, 8], F32)
        for c in range(8):
            load_col(k, "sp", gcol[:, c:c + 1], w["norm_mlp"][l, c * 128:(c + 1) * 128], 128)
        with contextlib.ExitStack() as st2:
            stgs = [st2.enter_context(nc.sbuf_tensor(un("p4_stg%d" % i), [128, 4096], F32)) for i in range(2)]
            w1v = w["mlp_w1"][l].rearrange("(c p) n -> p c n", p=128)
            for c in range(8):
                sg = stgs[c % 2]
                P.dma("sp", out=sg[:, :], in_=w1v[:, c, :])
                for hf in range(2):
                    P.V("tensor_scalar", out=w1[:, c, hf * 2048:(hf + 1) * 2048], in0=sg[:, hf * 2048:(hf + 1) * 2048],
                        scalar1=gcol[:, c:c + 1], scalar2=None, op0=ALU.mult)
        P.barrier()
        w2v = w["mlp_w2"][l].rearrange("(f p) n -> p f n", p=128)
        for f0 in range(0, 32, 4):
            P.dma("pool", out=w2[:, f0:f0 + 4, :], in_=w2v[:, f0:f0 + 4, :])
        if last:
            gf = SB("gf", [128, 1024], F32)
            P.dma("sp", out=gf[:, :], in_=w["norm_final"][0:1, :].broadcast_to([128, 1024]))
        uT = SB("uT", [128, 32, TT], BF16)
        rbuf = [SB("r%d" % i, [128, TT], BF16) for i in range(3)]
        junk = SB("junk", [128, 1024], BF16)
        hm = [SB("hm%d" % i, [128, 1024], BF16) for i in range(2)]
        sets = []
        for s_ in range(2):
            d = {}
            for name, shape, dt in [("x", [128, 2, 1024], F32), ("hmT", [128, 8, TT], BF16), ("st", [128, 16], F32),
                                    ("y", [128, 2, 1024], F32)]:
                if name == "y" and not last:
                    continue
                d[name] = SB("%s%d" % (name, s_), shape, dt)
            sets.append(d)
        nr = 0
        for i in range(NTT):
            d = sets[i % 2]
            t0 = i * TT
            x, hmT, stt = d["x"], d["hmT"], d["st"]
            P.dma("sp", out=x[:, :, :], in_=k.x2[t0:t0 + TT, :].rearrange("(j p) n -> p j n", p=128))
            for j in range(2):
                P.A("activation", out=junk[:, :], in_=x[:, j, :], func=AF.Square, accum_out=stt[:, j:j + 1])
            rstd_from_ss(k, stt[:, 2:4], stt[:, 0:2], stt[:, 4:6], 1024.0, EPS)
            for j in range(2):
                P.V("tensor_scalar", out=hm[j][:, :], in0=x[:, j, :], scalar1=stt[:, 2 + j:3 + j], scalar2=None,
                    op0=ALU.mult)
                pb = bank(k)
                pv = bfv(pb)
                for c in range(8):
                    P.M("transpose", out=pv[:, c * 128:(c + 1) * 128], in_=hm[j][:, c * 128:(c + 1) * 128],
                        identity=k.ident[:, :], accum=(c > 0))
                P.A("activation", out=hmT[:, :, j * 128:(j + 1) * 128],
                    in_=pv[:, 0:1024].rearrange("p (c t) -> p c t", c=8), func=AF.Copy)
            for f in range(32):
                pb = bank(k)
                for c in range(8):
                    P.M("matmul", out=pb[:, 0:TT], lhsT=w1[:, c, f * 128:(f + 1) * 128], rhs=hmT[:, c, :],
                        start=(c == 0), stop=(c == 7), accum=(c > 0))
                r = rbuf[nr % 3]
                nr += 1
                P.A("activation", out=r[:, :], in_=pb[:, 0:TT], func=AF.Relu)
                P.G("tensor_tensor", out=uT[:, f, :], in0=r[:, :], in1=r[:, :], op=ALU.mult)
            for j in range(2):
                for n in range(2):
                    cs = slice(n * 512, (n + 1) * 512)
                    pb = bank(k)
                    for f in range(32):
                        P.M("matmul", out=pb[:, 0:512], lhsT=uT[:, f, j * 128:(j + 1) * 128], rhs=w2[:, f, cs],
                            start=(f == 0), stop=(f == 31), accum=(f > 0))
                    P.V("tensor_tensor", out=x[:, j, cs], in0=x[:, j, cs], in1=pb[:, 0:512], op=ALU.add)
            if not last:
                P.dma("pool", out=xout[t0:t0 + TT, :].rearrange("(j p) n -> p j n", p=128), in_=x[:, :, :])
            else:
                for j in range(2):
                    P.A("activation", out=junk[:, :], in_=x[:, j, :], func=AF.Square, accum_out=stt[:, 8 + j:9 + j])
                rstd_from_ss(k, stt[:, 10:12], stt[:, 8:10], stt[:, 12:14], 1024.0, EPS)
                for j in range(2):
                    P.V("scalar_tensor_tensor", out=d["y"][:, j, :], in0=x[:, j, :], scalar=stt[:, 10 + j:11 + j],
                        in1=gf[:, :], op0=ALU.mult, op1=ALU.mult)
                P.dma("pool", out=k.y[t0:t0 + TT, :].rearrange("(j p) n -> p j n", p=128), in_=d["y"][:, :, :])
                if "xs0" in k.dbg or "xs1" in k.dbg:
                    P.dma("pool", out=xout[t0:t0 + TT, :].rearrange("(j p) n -> p j n", p=128), in_=x[:, :, :])


def rope_tables(pos, dim):
    inv = (10000.0 ** (-np.arange(0, dim, 2, dtype=np.float32) / dim)).astype(np.float32)
    ang = pos.astype(np.float32)[:, None] * inv[None, :]
    return np.cos(ang).astype(np.float32), np.sin(ang).astype(np.float32)


def const_inputs(S):
    pos = np.arange(S)
    c1, s1 = rope_tables(pos, 32)
    cr, sr = rope_tables(pos // 64, 32)
    cc, sc = rope_tables(pos % 64, 32)
    tab1 = np.stack([c1, s1], 1).astype(np.float32)
    tab2 = np.stack([np.stack([cr, cc], 1), np.stack([sr, sc], 1)], 1).astype(np.float32)
    s_idx = np.arange(128)[:, None]
    t_idx = np.arange(128)[None, :]
    msk = np.zeros((2, 2, 128, 128), np.float32)
    msk[0, 0] = s_idx < t_idx
    msk[0, 1] = s_idx <= t_idx
    msk[1, 0] = s_idx > t_idx
    msk[1, 1] = s_idx >= t_idx
    bdm = np.zeros((128, 128), np.float32)
    bdm[:64, :64] = 1
    bdm[64:, 64:] = 1
    return dict(tab1=tab1, tab2=tab2, ident=np.eye(128, dtype=np.float32), msk=msk, bdm=bdm)


_NC_CACHE = {}


def kernel(**inputs):
    S, depth = 8192, 2
    key = (S, depth)
    if key not in _NC_CACHE:
        _NC_CACHE[key] = build(S, depth)
    nc = _NC_CACHE[key]
    xp, xs = np.asarray(inputs["x_prompt"]), np.asarray(inputs["x_sample"])
    mp, ms = np.asarray(inputs["mem_prompt"]), np.asarray(inputs["mem_sample"])
    seqs = [(xp[b], mp[b]) for b in range(xp.shape[0])] + [(xs[b], ms[b]) for b in range(xs.shape[0])]
    n_seq = len(seqs)
    consts = const_inputs(S)
    wts = {name: np.ascontiguousarray(np.asarray(inputs[name], np.float32)) for name, _ in W_SPECS}
    wts["norm_final"] = np.ascontiguousarray(np.asarray(inputs["norm_final"], np.float32).reshape(1, D))
    in_maps = []
    for c in range(8):
        x, mem = seqs[c % n_seq]
        m = dict(x=np.ascontiguousarray(x, np.float32), mem=np.ascontiguousarray(mem, np.float32))
        m.update(wts)
        m.update(consts)
        in_maps.append(m)
    res = run_bass_kernel_spmd(nc, in_maps, core_ids=list(range(8)))
    ys = [np.asarray(res.results[c]["y"], np.float32) for c in range(n_seq)]
    y_prompt = np.stack(ys[:xp.shape[0]], 0)
    y_sample = np.stack(ys[xp.shape[0]:], 0)
    return (y_prompt, y_sample)
```

```python
import contextlib
import math
import numpy as np
import ml_dtypes
import concourse.bass as bass
import concourse.mybir as mybir
from concourse.bass_utils import run_bass_kernel_spmd

F32 = mybir.dt.float32
BF16 = mybir.dt.bfloat16
ALU = mybir.AluOpType
AF = mybir.ActivationFunctionType
AX = mybir.AxisListType

D = 1024
NMEM = 256
IN_COLS = 6432
EPS = 1e-6
GN_EPS = 64e-5
DECAY_C = math.exp(-0.5)

import os
MAXOPS = int(os.environ.get("MAXOPS", "100000000"))
ENGS = ("pe", "act", "dve", "pool", "sp")
N_DMA_SEMS = 16
READ_KEYS = ("in_", "in0", "in1", "lhsT", "rhs", "scalar", "scalar1", "scalar2", "bias", "scale",
             "identity", "data0", "data1", "initial")
WRITE_KEYS = ("out", "accum_out", "ap")


class Buf:
    __slots__ = ("name", "w", "r")

    def __init__(self, name):
        self.name = name
        self.w = None
        self.r = {}


class Prog:
    def __init__(self, nc):
        self.nc = nc
        self.es = contextlib.ExitStack()
        self.eobj = {"pe": nc.tensor, "act": nc.scalar, "dve": nc.vector, "pool": nc.gpsimd, "sp": nc.sync}
        self.sem = {}
        self.cnt = {}
        for e in ENGS:
            self.sem[e] = self.es.enter_context(nc.semaphore("s_" + e))
            self.cnt[e] = 0
        self.dq = {}
        for q in ("sp", "act", "pool"):
            sems = []
            for i in range(N_DMA_SEMS):
                k = "d_%s_%d" % (q, i)
                self.sem[k] = self.es.enter_context(nc.semaphore(k))
                self.cnt[k] = 0
                sems.append(k)
            self.dq[q] = [sems, 0]
        self.seen = {e: {} for e in ENGS}
        self.ops = {e: [] for e in ENGS}
        self.bufs = {}
        self.nops = 0

    def buf_of(self, ap):
        n = ap.name
        b = self.bufs.get(n)
        if b is None:
            b = self.bufs[n] = Buf(n)
        return b

    def _need(self, e, key, val, waits):
        if self.seen[e].get(key, 0) >= val:
            return
        self.seen[e][key] = val
        waits.append((key, val))

    def _collect(self, kw, extra_r, extra_w):
        reads, writes = [], []
        for k in READ_KEYS:
            v = kw.get(k)
            if v is not None and hasattr(v, "name") and hasattr(v, "ap"):
                reads.append(self.buf_of(v))
        for k in WRITE_KEYS:
            v = kw.get(k)
            if v is not None and hasattr(v, "name") and hasattr(v, "ap"):
                writes.append(self.buf_of(v))
        for v in extra_r:
            reads.append(v if isinstance(v, Buf) else self.buf_of(v))
        for v in extra_w:
            writes.append(v if isinstance(v, Buf) else self.buf_of(v))
        return reads, writes

    def _issue(self, e, inckey, incv, name, kw, reads, writes, accum):
        self.nissued = getattr(self, "nissued", 0) + 1
        if self.nissued > MAXOPS:
            return
        if self.nissued == MAXOPS:
            print("LAST OP:", e, name, {a: (str(b.name) + str(b.shape) if hasattr(b, "ap") else b) for a, b in kw.items()})
        need = {}
        for b in reads:
            if b.w is not None and need.get(b.w[0], 0) < b.w[1]:
                need[b.w[0]] = b.w[1]
        for b in writes:
            if b.w is not None and not (accum and b.w[0] == "pe") and need.get(b.w[0], 0) < b.w[1]:
                need[b.w[0]] = b.w[1]
            for k, v in b.r.items():
                if need.get(k, 0) < v:
                    need[k] = v
        waits = []
        for k, v in need.items():
            self._need(e, k, v, waits)
        self.cnt[inckey] += incv
        v = self.cnt[inckey]
        self.ops[e].append((waits, name, kw, inckey, incv))
        self.nops += 1 + len(waits)
        for b in reads:
            if b.r.get(inckey, 0) < v:
                b.r[inckey] = v
        for b in writes:
            b.w = (inckey, v)
            b.r = {}

    def op(self, e, name, R=(), W=(), accum=False, **kw):
        reads, writes = self._collect(kw, R, W)
        self._issue(e, e, 1, name, kw, reads, writes, accum)

    def V(self, name, **kw):
        self.op("dve", name, **kw)

    def A(self, name, **kw):
        self.op("act", name, **kw)

    def G(self, name, **kw):
        self.op("pool", name, **kw)

    def M(self, name="matmul", **kw):
        self.op("pe", name, **kw)

    def dma(self, q, out, in_, R=(), W=()):
        kw = dict(out=out, in_=in_)
        reads, writes = self._collect(kw, R, W)
        sems, idx = self.dq[q]
        key = sems[idx % len(sems)]
        self.dq[q][1] = idx + 1
        if self.cnt[key] > 0:
            w = []
            self._need(q, key, self.cnt[key], w)
            pre = w
        else:
            pre = []
        n0 = len(self.ops[q])
        self._issue(q, key, 16, "dma_start", kw, reads, writes, False)
        if pre and len(self.ops[q]) > n0:
            waits, name, kw2, ik, iv = self.ops[q][n0]
            self.ops[q][n0] = (pre + waits, name, kw2, ik, iv)

    def barrier(self):
        for e in ENGS:
            waits = []
            for key, v in self.cnt.items():
                if v > 0:
                    self._need(e, key, v, waits)
            if waits:
                self.ops[e].append((waits, None, None, None, None))
                self.nops += len(waits)

    def wait_bufs(self, e, aps):
        waits = []
        for a in aps:
            b = a if isinstance(a, Buf) else self.buf_of(a)
            if b.w is not None:
                self._need(e, b.w[0], b.w[1], waits)
        self.ops[e].append((waits, None, None, None, None))

    def emit(self):
        nc = self.nc
        with nc.Block() as block:
            def run(e):
                def body(engine):
                    for waits, name, kw, ik, iv in self.ops[e]:
                        for k, v in waits:
                            engine.wait_ge(self.sem[k], v)
                        if name is None:
                            continue
                        inst = getattr(engine, name)(**kw)
                        inst.then_inc(self.sem[ik], iv)
                return body
            block.sync(run("sp"))
            block.scalar(run("act"))
            block.vector(run("dve"))
            block.gpsimd(run("pool"))
            block.tensor(run("pe"))

    def close(self):
        self.es.close()


W_SPECS = [
    ("norm_mix", (D,)), ("w_in", (D, IN_COLS)), ("mla_q_norm", (384,)), ("mla_w_uq", (384, 768)),
    ("mla_kv_norm", (256,)), ("mla_w_ukv", (256, 1024)), ("gqa_q_norm", (64,)), ("gqa_k_norm", (64,)),
    ("rwkv_mu", (1920,)), ("rwkv_w0", (2, 512)), ("rwkv_w2", (2, 64, 512)), ("rwkv_a0", (2, 512)),
    ("rwkv_a2", (2, 64, 512)), ("rwkv_g2", (128, 512)), ("rwkv_k_k", (512,)), ("rwkv_k_a", (512,)),
    ("rwkv_r_k", (8, 64)), ("rwkv_ln_w", (512,)), ("rwkv_ln_b", (512,)), ("w_branch", (3, 512, D)),
    ("b_gate", (3, D)), ("w_out", (D, D)), ("norm_cross", (D,)), ("norm_mem", (D,)),
    ("cross_wq", (D, 512)), ("cross_wkv", (D, 1024)), ("cross_wo", (512, D)), ("norm_mlp", (D,)),
    ("mlp_w1", (D, 4096)), ("mlp_w2", (4096, D)),
]


class K:
    pass


def build(S, depth, dbg=(), phases=("p1", "p2", "rf", "rb", "p3a", "p3b")):
    NT = S // 128
    nc = bass.Bass("TRN2", target_bir_lowering=False)
    P = Prog(nc)
    k = K()
    k.nc, k.P, k.S, k.NT, k.depth, k.dbg = nc, P, S, NT, depth, dbg
    k.phases = phases

    def din(name, shape):
        return nc.dram_tensor(name, list(shape), F32, kind="ExternalInput").ap()

    def dscr(name, shape, dt):
        kind = "ExternalOutput" if name in dbg else "Internal"
        return nc.dram_tensor(name, list(shape), dt, kind=kind).ap()

    k.x = din("x", (S, D))
    k.mem = din("mem", (NMEM, D))
    k.w = {}
    for name, shp in W_SPECS:
        k.w[name] = din(name, (depth,) + shp)
    k.w["norm_final"] = din("norm_final", (1, D))
    k.tab1 = din("tab1", (S, 2, 16))
    k.tab2 = din("tab2", (S, 2, 2, 16))
    k.ident_d = din("ident", (128, 128))
    k.msk_d = din("msk", (2, 2, 128, 128))
    k.bdm_d = din("bdm", (128, 128))
    k.y = nc.dram_tensor("y", [S, D], F32, kind="ExternalOutput").ap()

    k.h_tm = dscr("h_tm", (S, D), BF16)
    k.hT_t = dscr("hT_t", (NT, 128, 8, 128), BF16)
    k.qT_mla = dscr("qT_mla", (8, 96, S), BF16)
    k.kT_mla = dscr("kT_mla", (8, 96, S), BF16)
    k.v_mla = dscr("v_mla", (S, 512), BF16)
    k.qT_gqa = dscr("qT_gqa", (512, S), BF16)
    k.kT_gqa = dscr("kT_gqa", (128, S), BF16)
    k.v_gqa = dscr("v_gqa", (S, 128), BF16)
    k.o_mla = dscr("o_mla", (S, 512), BF16)
    k.o_gqa = dscr("o_gqa", (S, 512), BF16)
    k.y_fw = dscr("y_fw", (S, 512), F32)
    k.z_r = dscr("z_r", (S, 1920), F32)
    k.zmix = dscr("zmix", (S, 1920), F32)
    k.oT_rw = dscr("oT_rw", (NT, 128, 4, 128), BF16)
    k.x2 = dscr("x2", (S, D), F32)
    k.xs = [dscr("xs0", (S, D), F32), dscr("xs1", (S, D), F32)]

    with contextlib.ExitStack() as gst:
        k.gst = gst
        k.rot = [0]
        k.ident = gst.enter_context(nc.sbuf_tensor("ident_b", [128, 128], BF16))
        P.dma("pool", out=k.ident[:, :], in_=k.ident_d)
        k.identf = gst.enter_context(nc.sbuf_tensor("ident_f", [128, 128], F32))
        P.dma("sp", out=k.identf[:, :], in_=k.ident_d)
        for l in range(depth):
            xin = k.x if l == 0 else k.xs[(l - 1) % 2]
            xout = k.xs[l % 2]
            if "p1" in phases:
                phase_p1(k, l, xin)
                P.barrier()
            if "p2" in phases:
                phase_attn(k, l)
                P.barrier()
            if "rf" in phases:
                phase_rwkv(k, l, 0)
                P.barrier()
            if "rb" in phases:
                phase_rwkv(k, l, 1)
                P.barrier()
            if "p3a" in phases:
                phase_p3a(k, l, xin)
                P.barrier()
            if "p3b" in phases:
                phase_p3b(k, l, xout, last=(l == depth - 1))
                P.barrier()
        outs = [k.y]
        for name in dbg:
            outs.append(P.bufs[name]) if name in P.bufs else None
        P.wait_bufs("sp", outs)
        P.emit()
    P.close()
    return nc


_UID = [0]


def un(name):
    _UID[0] += 1
    return "%s_u%d" % (name, _UID[0])


def alloc_banks(k, st, n=8):
    k.banks = [st.enter_context(k.nc.psum_tensor(un("bank%d" % i), [128, 512], F32)) for i in range(n)]


def bank(k, lo=0, hi=None):
    if hi is None:
        hi = len(k.banks)
    n = hi - lo
    i = k.rot[0] % n
    k.rot[0] += 1
    return k.banks[lo + i]


def bfv(b, ncols=1024):
    return b[:, :].bitcast(BF16)


def rstd_from_ss(k, out, ss, t, n, eps):
    P = k.P
    if eps is not None and eps != 0.0:
        P.V("tensor_scalar", out=t, in0=ss, scalar1=1.0 / n, scalar2=float(eps), op0=ALU.mult, op1=ALU.add)
        P.A("activation", out=t, in_=t, func=AF.Ln)
    else:
        P.A("activation", out=t, in_=ss, func=AF.Ln, scale=1.0 / n)
    P.A("activation", out=out, in_=t, func=AF.Exp, scale=-0.5)


def load_col(k, q, dst, src1d, n):
    k.P.dma(q, out=dst, in_=src1d.rearrange("(p o) -> p o", o=1))


def bc(ap, shape, axis):
    return ap.unsqueeze(axis).broadcast_to(list(shape))


def phase_p1(k, l, xin):
    nc, P, S, NT = k.nc, k.P, k.S, k.NT
    w = k.w
    with contextlib.ExitStack() as st:
        def SB(name, shape, dt):
            return st.enter_context(nc.sbuf_tensor(un("p1_" + name), list(shape), dt))
        alloc_banks(k, st)
        w_att = SB("watt", [128, 8, 1440], BF16)
        w_r = SB("wr", [128, 8, 1920], BF16)
        w_uq = SB("wuq", [128, 3, 768], BF16)
        w_ukv = SB("wukv", [128, 2, 1024], BF16)
        stg = SB("stg", [128, 2304], F32)
        g_bc = SB("gbc", [128, 1024], F32)
        gq_bc = SB("gqbc", [128, 64], F32)
        gk_bc = SB("gkbc", [128, 64], F32)
        gcol = SB("gcol", [128, 5], F32)
        win = w["w_in"][l].rearrange("(c p) n -> p c n", p=128)
        for c in range(8):
            P.dma("pool", out=w_att[:, c, :], in_=win[:, c, 0:1440])
        for c in range(8):
            P.dma("pool", out=w_r[:, c, :], in_=win[:, c, 1440:3360])
        P.dma("sp", out=g_bc[:, :], in_=w["norm_mix"][l:l + 1, :].broadcast_to([128, 1024]))
        P.dma("sp", out=gq_bc[:, :], in_=w["gqa_q_norm"][l:l + 1, :].broadcast_to([128, 64]))
        P.dma("sp", out=gk_bc[:, :], in_=w["gqa_k_norm"][l:l + 1, :].broadcast_to([128, 64]))
        for c in range(3):
            load_col(k, "sp", gcol[:, c:c + 1], w["mla_q_norm"][l, c * 128:(c + 1) * 128], 128)
        for c in range(2):
            load_col(k, "sp", gcol[:, 3 + c:4 + c], w["mla_kv_norm"][l, c * 128:(c + 1) * 128], 128)
        P.dma("sp", out=stg[:, 0:2304].rearrange("p (c n) -> p c n", c=3),
              in_=w["mla_w_uq"][l].rearrange("(c p) n -> p c n", p=128))
        for c in range(3):
            P.V("tensor_scalar", out=w_uq[:, c, :], in0=stg[:, c * 768:(c + 1) * 768],
                scalar1=gcol[:, c:c + 1], scalar2=None, op0=ALU.mult)
        P.dma("sp", out=stg[:, 0:2048].rearrange("p (c n) -> p c n", c=2),
              in_=w["mla_w_ukv"][l].rearrange("(c p) n -> p c n", p=128))
        for c in range(2):
            P.V("tensor_scalar", out=w_ukv[:, c, :], in0=stg[:, c * 1024:(c + 1) * 1024],
                scalar1=gcol[:, 3 + c:4 + c], scalar2=None, op0=ALU.mult)

        sets = []
        for s in range(2):
            d = {}
            for name, shape, dt in [
                ("x", [128, 1024], F32), ("junk", [128, 1024], BF16), ("hb", [128, 1024], BF16),
                ("hT", [128, 8, 128], BF16), ("tb1", [128, 2, 16], F32), ("tb2", [128, 2, 2, 16], F32),
                ("st", [128, 16], F32), ("cqb", [128, 384], BF16), ("cqT", [128, 3, 128], BF16),
                ("q32", [128, 8, 96], F32), ("qrot", [128, 8, 96], BF16), ("qT", [96, 8, 128], BF16),
                ("ta", [128, 8, 2, 16], F32), ("tb", [128, 8, 2, 16], F32),
                ("ckvb", [128, 256], BF16), ("ckvT", [128, 2, 128], BF16), ("kt", [128, 8, 96], BF16),
                ("vt", [128, 8, 64], BF16), ("kr", [128, 32], F32), ("kT", [96, 8, 128], BF16),
                ("sq", [128, 512], F32), ("gst", [128, 24], F32), ("qn", [128, 8, 64], F32),
                ("gqr", [128, 8, 64], BF16), ("gqT", [128, 4, 128], BF16),
                ("kn", [128, 2, 64], F32), ("gkr", [128, 2, 64], BF16), ("gkT", [128, 128], BF16),
                ("gv", [128, 128], BF16), ("zr", [128, 1920], F32),
            ]:
                d[name] = SB("%s%d" % (name, s), shape, dt)
            sets.append(d)

        ident = k.ident
        for i in range(NT):
            d = sets[i % 2]
            t0 = i * 128
            x, hb, hT, stt = d["x"], d["hb"], d["hT"], d["st"]
            P.dma("sp", out=x[:, :], in_=xin[t0:t0 + 128, :])
            P.dma("sp", out=d["tb1"][:, :, :], in_=k.tab1[t0:t0 + 128])
            P.dma("sp", out=d["tb2"][:, :, :, :], in_=k.tab2[t0:t0 + 128])
            P.A("activation", out=d["junk"][:, :], in_=x[:, :], func=AF.Square, accum_out=stt[:, 0:1])
            rstd_from_ss(k, stt[:, 1:2], stt[:, 0:1], stt[:, 2:3], 1024.0, EPS)
            P.V("scalar_tensor_tensor", out=hb[:, :], in0=x[:, :], scalar=stt[:, 1:2], in1=g_bc[:, :],
                op0=ALU.mult, op1=ALU.mult)
            P.dma("pool", out=k.h_tm[t0:t0 + 128, :], in_=hb[:, :])
            pb = bank(k)
            pv = bfv(pb)
            for c in range(8):
                P.M("transpose", out=pv[:, c * 128:(c + 1) * 128], in_=hb[:, c * 128:(c + 1) * 128],
                    identity=ident[:, :], accum=(c > 0))
            P.A("activation", out=hT[:, :, :].rearrange("p c t -> p (c t)"), in_=pv[:, 0:1024], func=AF.Copy)
            P.dma("pool", out=k.hT_t[i], in_=hT[:, :, :])

            def proj(c0, n):
                b = bank(k)
                for c in range(8):
                    P.M("matmul", out=b[:, 0:n], lhsT=hT[:, c, :], rhs=w_att[:, c, c0:c0 + n],
                        start=(c == 0), stop=(c == 7), accum=(c > 0))
                return b

            for ci, (c0, n) in enumerate(((0, 512), (512, 512), (1024, 512), (1536, 384))):
                b = bank(k)
                for c in range(8):
                    P.M("matmul", out=b[:, 0:n], lhsT=hT[:, c, :], rhs=w_r[:, c, c0:c0 + n],
                        start=(c == 0), stop=(c == 7), accum=(c > 0))
                if ci % 2 == 0:
                    P.A("activation", out=d["zr"][:, c0:c0 + n], in_=b[:, 0:n], func=AF.Copy)
                else:
                    P.V("tensor_copy", out=d["zr"][:, c0:c0 + n], in_=b[:, 0:n])
            P.dma("pool", out=k.z_r[t0:t0 + 128, :], in_=d["zr"][:, :])
            cos1 = bc(d["tb1"][:, 0, :], [128, 8, 16], 1)
            sin1 = bc(d["tb1"][:, 1, :], [128, 8, 16], 1)
            pq = proj(0, 384)
            P.A("activation", out=d["junk"][:, 0:384], in_=pq[:, 0:384], func=AF.Square, accum_out=stt[:, 3:4])
            rstd_from_ss(k, stt[:, 4:5], stt[:, 3:4], stt[:, 5:6], 384.0, EPS)
            P.V("tensor_copy", out=d["cqb"][:, :], in_=pq[:, 0:384])
            pb = bank(k)
            pv = bfv(pb)
            for c in range(3):
                P.M("transpose", out=pv[:, c * 128:(c + 1) * 128], in_=d["cqb"][:, c * 128:(c + 1) * 128],
                    identity=ident[:, :], accum=(c > 0))
            P.A("activation", out=d["cqT"][:, :, :].rearrange("p c t -> p (c t)"), in_=pv[:, 0:384], func=AF.Copy)
            q32 = d["q32"]
            q32f = q32[:, :, :].rearrange("p h e -> p (h e)")
            for (c0, n) in ((0, 512), (512, 256)):
                b = bank(k)
                for c in range(3):
                    P.M("matmul", out=b[:, 0:n], lhsT=d["cqT"][:, c, :], rhs=w_uq[:, c, c0:c0 + n],
                        start=(c == 0), stop=(c == 2), accum=(c > 0))
                P.V("tensor_scalar", out=q32f[:, c0:c0 + n], in0=b[:, 0:n], scalar1=stt[:, 4:5], scalar2=None,
                    op0=ALU.mult)
            qrot = d["qrot"]
            ta = d["ta"][:, :, 0, :]
            tb = d["tb"][:, :, 0, :]
            P.G("tensor_copy", out=qrot[:, :, 0:64], in_=q32[:, :, 0:64])
            P.V("tensor_tensor", out=ta, in0=q32[:, :, 64:80], in1=cos1, op=ALU.mult)
            P.V("tensor_tensor", out=tb, in0=q32[:, :, 80:96], in1=sin1, op=ALU.mult)
            P.V("tensor_tensor", out=qrot[:, :, 64:80], in0=ta, in1=tb, op=ALU.subtract)
            P.V("tensor_tensor", out=ta, in0=q32[:, :, 64:80], in1=sin1, op=ALU.mult)
            P.V("tensor_tensor", out=tb, in0=q32[:, :, 80:96], in1=cos1, op=ALU.mult)
            P.V("tensor_tensor", out=qrot[:, :, 80:96], in0=ta, in1=tb, op=ALU.add)
            pb = bank(k)
            pv = bfv(pb)
            for h in range(8):
                P.M("transpose", out=pv[0:96, h * 128:(h + 1) * 128], in_=qrot[:, h, :], identity=ident[:, :],
                    accum=(h > 0))
            P.A("activation", out=d["qT"][:, :, :].rearrange("p h t -> p (h t)"), in_=pv[0:96, 0:1024], func=AF.Copy)
            P.dma("pool", out=k.qT_mla[:, :, t0:t0 + 128].rearrange("h e s -> e h s"), in_=d["qT"][:, :, :])
            pkv = proj(384, 288)
            P.A("activation", out=d["junk"][:, 0:256], in_=pkv[:, 0:256], func=AF.Square, accum_out=stt[:, 6:7])
            rstd_from_ss(k, stt[:, 7:8], stt[:, 6:7], stt[:, 8:9], 256.0, EPS)
            P.V("tensor_copy", out=d["ckvb"][:, :], in_=pkv[:, 0:256])
            kr = d["kr"]
            c1 = d["tb1"][:, 0, :]
            s1 = d["tb1"][:, 1, :]
            t2a = d["ta"][:, 0, 1, :]
            t2b = d["tb"][:, 0, 1, :]
            P.V("tensor_tensor", out=t2a, in0=pkv[:, 256:272], in1=c1, op=ALU.mult)
            P.V("tensor_tensor", out=t2b, in0=pkv[:, 272:288], in1=s1, op=ALU.mult)
            P.V("tensor_tensor", out=kr[:, 0:16], in0=t2a, in1=t2b, op=ALU.subtract)
            P.V("tensor_tensor", out=t2a, in0=pkv[:, 256:272], in1=s1, op=ALU.mult)
            P.V("tensor_tensor", out=t2b, in0=pkv[:, 272:288], in1=c1, op=ALU.mult)
            P.V("tensor_tensor", out=kr[:, 16:32], in0=t2a, in1=t2b, op=ALU.add)
            pb = bank(k)
            pv = bfv(pb)
            for c in range(2):
                P.M("transpose", out=pv[:, c * 128:(c + 1) * 128], in_=d["ckvb"][:, c * 128:(c + 1) * 128],
                    identity=ident[:, :], accum=(c > 0))
            P.A("activation", out=d["ckvT"][:, :, :].rearrange("p c t -> p (c t)"), in_=pv[:, 0:256], func=AF.Copy)
            kt, vt = d["kt"], d["vt"]
            for half in range(2):
                b = bank(k)
                for c in range(2):
                    P.M("matmul", out=b[:, 0:512], lhsT=d["ckvT"][:, c, :], rhs=w_ukv[:, c, half * 512:(half + 1) * 512],
                        start=(c == 0), stop=(c == 1), accum=(c > 0))
                b3 = b[:, 0:512].rearrange("p (h e) -> p h e", h=4)
                P.V("tensor_scalar", out=kt[:, half * 4:(half + 1) * 4, 0:64], in0=b3[:, :, 0:64], scalar1=stt[:, 7:8],
                    scalar2=None, op0=ALU.mult)
                P.V("tensor_scalar", out=vt[:, half * 4:(half + 1) * 4, :], in0=b3[:, :, 64:128], scalar1=stt[:, 7:8],
                    scalar2=None, op0=ALU.mult)
            P.G("tensor_copy", out=kt[:, :, 64:96], in_=bc(kr[:, :], [128, 8, 32], 1))
            pb = bank(k)
            pv = bfv(pb)
            for h in range(8):
                P.M("transpose", out=pv[0:96, h * 128:(h + 1) * 128], in_=kt[:, h, :], identity=ident[:, :],
                    accum=(h > 0))
            P.A("activation", out=d["kT"][:, :, :].rearrange("p h t -> p (h t)"), in_=pv[0:96, 0:1024], func=AF.Copy)
            P.dma("pool", out=k.kT_mla[:, :, t0:t0 + 128].rearrange("h e s -> e h s"), in_=d["kT"][:, :, :])
            P.dma("pool", out=k.v_mla[t0:t0 + 128, :], in_=vt[:, :, :].rearrange("p h e -> p (h e)"))

            def qknorm_rope(pb_ap, nh, gbc, n32, rot, soff):
                gs = d["gst"]
                P.A("activation", out=d["sq"][:, 0:nh * 64], in_=pb_ap, func=AF.Square)
                P.V("tensor_reduce", out=gs[:, soff:soff + nh],
                    in_=d["sq"][:, 0:nh * 64].rearrange("p (h e) -> p h e", h=nh), axis=AX.X, op=ALU.add)
                rstd_from_ss(k, gs[:, soff + 8:soff + 8 + nh], gs[:, soff:soff + nh], gs[:, soff + 16:soff + 16 + nh],
                             64.0, EPS)
                P.V("tensor_tensor", out=n32[:, :, :], in0=pb_ap.rearrange("p (h e) -> p h e", h=nh),
                    in1=bc(gs[:, soff + 8:soff + 8 + nh], [128, nh, 64], 2), op=ALU.mult)
                P.G("tensor_tensor", out=n32[:, :, :], in0=n32[:, :, :], in1=bc(gbc[:, :], [128, nh, 64], 1),
                    op=ALU.mult)
                v5 = n32[:, :, :].rearrange("p h (a b e) -> p h a b e", a=2, b=2)
                r5 = rot[:, :, :].rearrange("p h (a b e) -> p h a b e", a=2, b=2)
                x1, x2 = v5[:, :, :, 0, :], v5[:, :, :, 1, :]
                cos2 = bc(d["tb2"][:, 0, :, :], [128, nh, 2, 16], 1)
                sin2 = bc(d["tb2"][:, 1, :, :], [128, nh, 2, 16], 1)
                ta4 = d["ta"][:, 0:nh, :, :]
                tb4 = d["tb"][:, 0:nh, :, :]
                P.V("tensor_tensor", out=ta4, in0=x1, in1=cos2, op=ALU.mult)
                P.V("tensor_tensor", out=tb4, in0=x2, in1=sin2, op=ALU.mult)
                P.V("tensor_tensor", out=r5[:, :, :, 0, :], in0=ta4, in1=tb4, op=ALU.subtract)
                P.V("tensor_tensor", out=ta4, in0=x1, in1=sin2, op=ALU.mult)
                P.V("tensor_tensor", out=tb4, in0=x2, in1=cos2, op=ALU.mult)
                P.V("tensor_tensor", out=r5[:, :, :, 1, :], in0=ta4, in1=tb4, op=ALU.add)

            pgq = proj(672, 512)
            qknorm_rope(pgq[:, 0:512], 8, gq_bc, d["qn"], d["gqr"], 0)
            pb = bank(k)
            pv = bfv(pb)
            gqf = d["gqr"][:, :, :].rearrange("p h e -> p (h e)")
            for c in range(4):
                P.M("transpose", out=pv[:, c * 128:(c + 1) * 128], in_=gqf[:, c * 128:(c + 1) * 128],
                    identity=ident[:, :], accum=(c > 0))
            P.A("activation", out=d["gqT"][:, :, :].rearrange("p c t -> p (c t)"), in_=pv[:, 0:512], func=AF.Copy)
            P.dma("pool", out=k.qT_gqa[:, t0:t0 + 128].rearrange("(c p) s -> p c s", p=128), in_=d["gqT"][:, :, :])
            pgk = proj(1184, 256)
            P.A("activation", out=d["gv"][:, :], in_=pgk[:, 128:256], func=AF.Copy)
            P.dma("pool", out=k.v_gqa[t0:t0 + 128, :], in_=d["gv"][:, :])
            qknorm_rope(pgk[:, 0:128], 2, gk_bc, d["kn"], d["gkr"], 2)
            pb = bank(k)
            pv = bfv(pb)
            P.M("transpose", out=pv[:, 0:128], in_=d["gkr"][:, :, :].rearrange("p h e -> p (h e)"),
                identity=ident[:, :])
            P.A("activation", out=d["gkT"][:, :], in_=pv[:, 0:128], func=AF.Copy)
            P.dma("pool", out=k.kT_gqa[:, t0:t0 + 128], in_=d["gkT"][:, :])


def phase_attn(k, l):
    nc, P, S, NT = k.nc, k.P, k.S, k.NT
    QB = 512
    NQB = S // QB
    NG = NT // 2
    with contextlib.ExitStack() as st:
        def SB(name, shape, dt):
            return st.enter_context(nc.sbuf_tensor(un("p2_" + name), list(shape), dt))
        pps = [st.enter_context(nc.psum_tensor(un("pp%d" % i), [128, 1024], F32)) for i in range(3)]
        obs = [st.enter_context(nc.psum_tensor(un("ob%d" % i), [128, 512], F32)) for i in range(2)]
        kts = [SB("kt%d" % i, [96, S], BF16) for i in range(2)]
        qts = [SB("qt%d" % i, [96, S], BF16) for i in range(2)]
        vxs = [SB("vx%d" % i, [128, NT, 65], BF16) for i in range(2)]
        pts = [SB("pt%d" % i, [128, 2 * QB], BF16) for i in range(3)]
        o32 = [SB("o32_%d" % i, [65, QB], F32) for i in range(2)]
        osb = [SB("o%d" % i, [128, 4, 64], BF16) for i in range(2)]
        rsb = [SB("rs%d" % i, [128, 4], F32) for i in range(2)]
        for vx in vxs:
            P.G("memset", ap=vx[:, :, 64:65], constant=1.0, W=[vx[:, :, :]])
        heads = [("mla", h) for h in range(8)] + [("gqa", h) for h in range(8)]
        hd = []
        kvi = -1
        for hi, (kind, h) in enumerate(heads):
            qt = qts[hi % 2]
            loads = []
            if kind == "mla":
                dq = 96
                kvi += 1
                kt, vx = kts[kvi % 2], vxs[kvi % 2]
                loads.append((kt[0:96, :], k.kT_mla[h]))
                vsrc = k.v_mla[:, h * 64:(h + 1) * 64].rearrange("(c p) e -> p c e", p=128)
                for c0 in range(0, NT, 8):
                    c1 = min(NT, c0 + 8)
                    loads.append((vx[:, c0:c1, 0:64], vsrc[:, c0:c1, :]))
                loads.append((qt[0:96, :], k.qT_mla[h]))
                oscr = k.o_mla
            else:
                dq = 64
                if h % 4 == 0:
                    kvi += 1
                    kvh = h // 4
                    kt, vx = kts[kvi % 2], vxs[kvi % 2]
                    loads.append((kt[0:64, :], k.kT_gqa[kvh * 64:(kvh + 1) * 64, :]))
                    vsrc = k.v_gqa[:, kvh * 64:(kvh + 1) * 64].rearrange("(c p) e -> p c e", p=128)
                    for c0 in range(0, NT, 8):
                        c1 = min(NT, c0 + 8)
                        loads.append((vx[:, c0:c1, 0:64], vsrc[:, c0:c1, :]))
                loads.append((qt[0:64, :], k.qT_gqa[h * 64:(h + 1) * 64, :]))
                oscr = k.o_gqa
            hd.append(dict(kt=kt, vx=vx, qt=qt, dq=dq, scale=float(dq) ** -0.5, h=h, oscr=oscr, loads=loads))
        items = []
        for hi in range(len(hd)):
            for qb in range(NQB):
                for g in range(NG):
                    items.append((hi, qb, g))
        state = {"npp": 0, "nqb": 0}

        def do_loads(hi):
            for (o_, i_) in hd[hi]["loads"]:
                P.dma("sp", out=o_, in_=i_)

        def emit_qk(hi, qb, g):
            H_ = hd[hi]
            pp = pps[state["npp"] % 3]
            pt = pts[state["npp"] % 3]
            state["npp"] += 1
            for u in range(2):
                kc = 2 * g + u
                P.M("matmul", out=pp[:, u * QB:(u + 1) * QB], lhsT=H_["kt"][0:H_["dq"], kc * 128:(kc + 1) * 128],
                    rhs=H_["qt"][0:H_["dq"], qb * QB:(qb + 1) * QB], start=True, stop=True, accum=(u > 0))
            P.A("activation", out=pt[:, :], in_=pp[:, :], func=AF.Exp, scale=H_["scale"])
            return pt

        def emit_pv(hi, qb, g, pt):
            H_ = hd[hi]
            ob = obs[state["nqb"] % 2]
            for u in range(2):
                kc = 2 * g + u
                P.M("matmul", out=ob[0:65, 0:QB], lhsT=H_["vx"][:, kc, 0:65], rhs=pt[:, u * QB:(u + 1) * QB],
                    start=(kc == 0), stop=(kc == NT - 1), accum=(kc > 0))
            if g != NG - 1:
                return None
            o3, o_s, r_s = o32[state["nqb"] % 2], osb[state["nqb"] % 2], rsb[state["nqb"] % 2]
            state["nqb"] += 1
            P.A("activation", out=o3[:, :], in_=ob[0:65, 0:QB], func=AF.Copy)

            def epilogue():
                pp = pps[state["npp"] % 3]
                state["npp"] += 1
                for j in range(4):
                    P.M("transpose", out=pp[:, j * 128:j * 128 + 65], in_=o3[0:65, j * 128:(j + 1) * 128],
                        identity=k.identf[0:65, 0:65], accum=(j > 0))
                tb3 = pp[:, 0:512].rearrange("p (j e) -> p j e", j=4)
                P.V("reciprocal", out=r_s[:, :], in_=tb3[:, :, 64])
                P.V("tensor_tensor", out=o_s[:, :, :], in0=tb3[:, :, 0:64], in1=bc(r_s[:, :], [128, 4, 64], 2),
                    op=ALU.mult)
                h = H_["h"]
                P.dma("pool", out=H_["oscr"][qb * QB:(qb + 1) * QB, h * 64:(h + 1) * 64].rearrange("(j p) e -> p j e", p=128),
                      in_=o_s[:, :, :])
            return epilogue

        n = len(items)
        per_head = NQB * NG
        do_loads(0)
        prev_pt = None
        pending = None
        for idx in range(n + 1):
            if idx < n:
                hi, qb, g = items[idx]
                if idx % per_head == min(2, per_head - 1) and hi + 1 < len(hd):
                    do_loads(hi + 1)
                cur_pt = emit_qk(hi, qb, g)
            if idx >= 1:
                hi0, qb0, g0 = items[idx - 1]
                ep = emit_pv(hi0, qb0, g0, prev_pt)
                if pending is not None:
                    pending()
                pending = ep
            prev_pt = cur_pt
        if pending is not None:
            pending()


def phase_rwkv(k, l, d):
    nc, P, S, NT = k.nc, k.P, k.S, k.NT
    w = k.w
    C0 = DECAY_C
    with contextlib.ExitStack() as st:
        def SB(name, shape, dt):
            return st.enter_context(nc.sbuf_tensor(un("rw_" + name), list(shape), dt))
        alloc_banks(k, st)
        bcn = {}

        def load_bc(name, src_row, n=512):
            t = SB(name, [128, n], F32)
            P.dma("sp", out=t[:, :], in_=src_row.broadcast_to([128, n]))
            bcn[name] = t
            return t
        w0b = load_bc("w0", w["rwkv_w0"][l, d:d + 1, :])
        a0b = load_bc("a0", w["rwkv_a0"][l, d:d + 1, :])
        kkb = load_bc("kk", w["rwkv_k_k"][l:l + 1, :])
        kab = load_bc("ka", w["rwkv_k_a"][l:l + 1, :])
        if d == 0:
            mub = load_bc("mu", w["rwkv_mu"][l:l + 1, :], 1920)
        else:
            a0o = load_bc("a0o", w["rwkv_a0"][l, 0:1, :])
            rkb = load_bc("rk", w["rwkv_r_k"][l:l + 1].rearrange("o h n -> o (h n)"))
            lnw = load_bc("lnw", w["rwkv_ln_w"][l:l + 1, :])
            lnb = load_bc("lnb", w["rwkv_ln_b"][l:l + 1, :])
            g2s = SB("g2", [128, 512], BF16)
            P.dma("pool", out=g2s[:, :], in_=w["rwkv_g2"][l])
        w2s = SB("w2", [128, 512], BF16)
        a2s = SB("a2", [128, 512], BF16)
        P.dma("pool", out=w2s[:, :], in_=w["rwkv_w2"][l].rearrange("d r c -> (d r) c"))
        P.dma("pool", out=a2s[:, :], in_=w["rwkv_a2"][l].rearrange("d r c -> (d r) c"))
        m2 = SB("m2", [128, 2, 128], F32)
        mT = SB("mT", [128, 128], F32)
        bdm = SB("bdm", [128, 128], F32)
        onec = SB("onec", [128, 1], F32)
        P.dma("sp", out=m2[:, 0, :], in_=k.msk_d[d, 0])
        P.dma("sp", out=m2[:, 1, :], in_=k.msk_d[d, 1])
        P.dma("sp", out=mT[:, :], in_=k.msk_d[1 - d, 0])
        P.dma("sp", out=bdm[:, :], in_=k.bdm_d)
        P.G("memset", ap=onec[:, :], constant=1.0)
        H32 = SB("H32", [128, 4, 128], F32)
        Hb = SB("Hb", [128, 4, 128], BF16)
        P.G("memset", ap=H32[:, :, :], constant=0.0)
        P.G("memset", ap=Hb[:, :, :], constant=0.0)
        if d == 0:
            zin = [SB("z", [128, 1920], F32), SB("zp", [128, 1920], F32), SB("zn", [128, 1920], F32),
                   SB("zt", [128, 1920], F32)]
        sets = []
        for s_ in range(2):
            dd = {}
            lst = [("zm", [128, 1920], F32), ("lor", [128, 384], BF16), ("lorT", [128, 3, 128], BF16),
                   ("tok4", [128, 4, 512], BF16), ("vb", [128, 512], BF16), ("TTs", [128, 4, 4, 128], BF16),
                   ("gC", [128, 4], F32), ("sm", [128, 64], F32), ("Y32", [128, 512], F32), ("hT_", [128, 128], F32)]
            for t in range(8):
                lst.append(("T%d" % t, [128, 512], F32))
            for p in range(4):
                lst += [("XL%d" % p, [128, 2, 2, 128], BF16), ("LT%d" % p, [128, 2, 128], BF16),
                        ("ARB%d" % p, [128, 2, 128], BF16), ("AK%d" % p, [128, 2, 2, 128], BF16),
                        ("PT%d" % p, [128, 128], BF16), ("AKV%d" % p, [128, 128], BF16), ("Ub%d" % p, [128, 128], BF16)]
            if d == 1:
                lst += [("yfw", [128, 512], F32), ("ob", [128, 512], BF16), ("oT", [128, 4, 128], BF16)]
            for name, shape, dt in lst:
                dd[name] = SB("%s_%d" % (name, s_), shape, dt)
            sets.append(dd)

        order = list(range(NT)) if d == 0 else list(range(NT - 1, -1, -1))
        for it, i in enumerate(order):
            D_ = sets[it % 2]
            t0 = i * 128
            zm = D_["zm"]
            T = [D_["T%d" % t] for t in range(8)]
            sm = D_["sm"]
            if d == 0:
                z, zp, zn, zt = zin
                P.dma("sp", out=z[:, :], in_=k.z_r[t0:t0 + 128, :])
                if i == 0:
                    P.G("memset", ap=zp[0:1, :], constant=0.0)
                    P.dma("sp", out=zp[1:128, :], in_=k.z_r[0:127, :])
                else:
                    P.dma("sp", out=zp[:, :], in_=k.z_r[t0 - 1:t0 + 127, :])
                if i == NT - 1:
                    P.G("memset", ap=zn[:, :], constant=0.0)
                    P.dma("sp", out=zn[0:127, :], in_=k.z_r[t0 + 1:t0 + 128, :])
                else:
                    P.dma("sp", out=zn[:, :], in_=k.z_r[t0 + 1:t0 + 129, :])
                P.G("tensor_tensor", out=zt[:, :], in0=zp[:, :], in1=zn[:, :], op=ALU.add)
                P.V("scalar_tensor_tensor", out=zt[:, :], in0=zt[:, :], scalar=0.5, in1=z[:, :], op0=ALU.mult,
                    op1=ALU.subtract)
                P.G("tensor_tensor", out=zt[:, :], in0=zt[:, :], in1=mub[:, :], op=ALU.mult)
                P.V("tensor_tensor", out=zm[:, :], in0=z[:, :], in1=zt[:, :], op=ALU.add)
                P.dma("pool", out=k.zmix[t0:t0 + 128, :], in_=zm[:, :])
            else:
                P.dma("sp", out=zm[:, :], in_=k.zmix[t0:t0 + 128, :])
                P.dma("sp", out=D_["yfw"][:, :], in_=k.y_fw[t0:t0 + 128, :])
            r_ = zm[:, 0:512]
            kx = zm[:, 512:1024]
            v_ = zm[:, 1024:1536]
            lor, lorT = D_["lor"], D_["lorT"]
            P.A("activation", out=lor[:, 0:128], in_=zm[:, 1536:1664], func=AF.Tanh)
            P.V("tensor_copy", out=lor[:, 128:256], in_=zm[:, 1664:1792])
            nl = 2
            if d == 1:
                P.A("activation", out=lor[:, 256:384], in_=zm[:, 1792:1920], func=AF.Sigmoid)
                nl = 3
            pb = bank(k)
            pv = bfv(pb)
            for c in range(nl):
                P.M("transpose", out=pv[:, c * 128:(c + 1) * 128], in_=lor[:, c * 128:(c + 1) * 128],
                    identity=k.ident[:, :], accum=(c > 0))
            P.A("activation", out=lorT[:, 0:nl, :].rearrange("p c t -> p (c t)"), in_=pv[:, 0:nl * 128], func=AF.Copy)
            ds = slice(d * 64, (d + 1) * 64)
            sg, asig = T[0], T[1]
            pb = bank(k)
            P.M("matmul", out=pb[:, 0:512], lhsT=lorT[ds, 0, :], rhs=w2s[ds, :], start=True, stop=True)
            P.V("tensor_tensor", out=sg[:, :], in0=pb[:, 0:512], in1=w0b[:, :], op=ALU.add)
            P.A("activation", out=sg[:, :], in_=sg[:, :], func=AF.Sigmoid)
            pb = bank(k)
            P.M("matmul", out=pb[:, 0:512], lhsT=lorT[ds, 1, :], rhs=a2s[ds, :], start=True, stop=True)
            P.V("tensor_tensor", out=asig[:, :], in0=pb[:, 0:512], in1=a0b[:, :], op=ALU.add)
            P.A("activation", out=asig[:, :], in_=asig[:, :], func=AF.Sigmoid)
            eP, eN, eX = T[2], T[3], T[4]
            pc = bank(k)
            P.M("matmul", out=pc[:, 0:512], lhsT=m2[:, 1, :], rhs=sg[:, :], start=True, stop=True)
            pcx = bank(k)
            P.M("matmul", out=pcx[:, 0:512], lhsT=m2[:, 0, :], rhs=sg[:, :], start=True, stop=True)
            P.A("activation", out=eP[:, :], in_=pc[:, 0:512], func=AF.Exp, scale=-C0)
            P.A("activation", out=eN[:, :], in_=pc[:, 0:512], func=AF.Exp, scale=C0)
            P.A("activation", out=eX[:, :], in_=pcx[:, 0:512], func=AF.Exp, scale=-C0)
            pg = bank(k)
            for p in range(4):
                P.M("matmul", out=pg[:, p:p + 1], lhsT=sg[:, p * 128:(p + 1) * 128], rhs=onec[:, 0:1], start=True,
                    stop=True, accum=(p > 0))
            P.A("activation", out=D_["gC"][:, :], in_=pg[:, 0:4], func=AF.Exp, scale=-C0)
            kk, kt, bb = T[5], T[6], T[7]
            tok4, vb = D_["tok4"], D_["vb"]
            P.G("tensor_tensor", out=kk[:, :], in0=kx, in1=kkb[:, :], op=ALU.mult)
            P.A("activation", out=kt[:, :], in_=kk[:, :], func=AF.Square)
            P.V("tensor_reduce", out=sm[:, 0:8], in_=kt[:, :].rearrange("p (h e) -> p h e", h=8), axis=AX.X, op=ALU.add)
            P.V("tensor_scalar", out=sm[:, 0:8], in0=sm[:, 0:8], scalar1=1e-24, scalar2=None, op0=ALU.max)
            P.A("activation", out=sm[:, 8:16], in_=sm[:, 0:8], func=AF.Ln)
            P.A("activation", out=sm[:, 16:24], in_=sm[:, 8:16], func=AF.Exp, scale=-0.5)
            kk3 = kk[:, :].rearrange("p (h e) -> p h e", h=8)
            P.V("tensor_tensor", out=kk3, in0=kk3, in1=bc(sm[:, 16:24], [128, 8, 64], 2), op=ALU.mult)
            P.V("scalar_tensor_tensor", out=kt[:, :], in0=asig[:, :], scalar=-1.0, in1=kab[:, :], op0=ALU.add, op1=ALU.mult)
            P.V("scalar_tensor_tensor", out=kt[:, :], in0=kt[:, :], scalar=1.0, in1=kx, op0=ALU.add, op1=ALU.mult)
            P.G("tensor_tensor", out=bb[:, :], in0=kk[:, :], in1=asig[:, :], op=ALU.mult)
            P.V("scalar_tensor_tensor", out=tok4[:, 0, :], in0=kk[:, :], scalar=-1.0, in1=eX[:, :], op0=ALU.mult, op1=ALU.mult)
            P.G("tensor_tensor", out=tok4[:, 1, :], in0=r_, in1=eP[:, :], op=ALU.mult)
            P.G("tensor_tensor", out=tok4[:, 2, :], in0=bb[:, :], in1=eN[:, :], op=ALU.mult)
            P.V("tensor_tensor", out=tok4[:, 3, :], in0=kt[:, :], in1=eN[:, :], op=ALU.mult)
            P.A("activation", out=vb[:, :], in_=v_, func=AF.Copy)
            TTs = D_["TTs"]
            for p0 in (0, 2):
                pb = bank(k)
                pv = bfv(pb)
                for p in (p0, p0 + 1):
                    for q in range(4):
                        o0 = (p - p0) * 512 + q * 128
                        P.M("transpose", out=pv[:, o0:o0 + 128], in_=tok4[:, q, p * 128:(p + 1) * 128],
                            identity=k.ident[:, :], accum=not (p == p0 and q == 0))
                P.A("activation", out=TTs[:, p0:p0 + 2, :, :].rearrange("p a q t -> p (a q t)"), in_=pv[:, 0:1024],
                    func=AF.Copy)
            XL = [D_["XL%d" % p] for p in range(4)]
            LT = [D_["LT%d" % p] for p in range(4)]
            ARB = [D_["ARB%d" % p] for p in range(4)]
            AK = [D_["AK%d" % p] for p in range(4)]
            PT = [D_["PT%d" % p] for p in range(4)]
            AKV = [D_["AKV%d" % p] for p in range(4)]
            Ub = [D_["Ub%d" % p] for p in range(4)]
            m2b = bc(m2[:, :, :].rearrange("p a t -> p (a t)"), [128, 2, 256], 1)
            for p in range(4):
                bB, bK, bL = bank(k), bank(k), bank(k)
                for e in range(2):
                    bs = slice(e * 64, (e + 1) * 64)
                    ar = TTs[bs, p, 0:2, :].rearrange("p q t -> p (q t)")
                    P.M("matmul", out=bB[:, e * 256:(e + 1) * 256], lhsT=TTs[bs, p, 2, :], rhs=ar, start=True, stop=True,
                        accum=(e > 0))
                    P.M("matmul", out=bK[:, e * 256:(e + 1) * 256], lhsT=TTs[bs, p, 3, :], rhs=ar, start=True, stop=True,
                        accum=(e > 0))
                    P.M("matmul", out=bL[:, e * 128:(e + 1) * 128], lhsT=TTs[bs, p, 0, :], rhs=TTs[bs, p, 2, :], start=True,
                        stop=True, accum=(e > 0))
                bB4 = bB[:, 0:512].rearrange("p (e a t) -> p e a t", e=2, a=2)
                P.V("tensor_tensor", out=XL[p][:, :, 1, :], in0=bB4[:, :, 0, :], in1=bc(m2[:, 0, :], [128, 2, 128], 1),
                    op=ALU.mult)
                P.V("tensor_tensor", out=ARB[p][:, :, :], in0=bB4[:, :, 1, :], in1=bc(m2[:, 1, :], [128, 2, 128], 1),
                    op=ALU.mult)
                P.V("tensor_tensor", out=AK[p][:, :, :, :].rearrange("p e a t -> p e (a t)"),
                    in0=bK[:, 0:512].rearrange("p (e n) -> p e n", e=2), in1=m2b, op=ALU.mult)
                P.V("tensor_tensor", out=LT[p][:, :, :], in0=bL[:, 0:256].rearrange("p (e t) -> p e t", e=2),
                    in1=bc(mT[:, :], [128, 2, 128], 1), op=ALU.mult)
                P.G("tensor_copy", out=XL[p][:, :, 0, :], in_=bc(k.ident[:, :], [128, 2, 128], 1))
            for lev in range(7):
                last = (lev == 6)
                for p in range(4):
                    bb_ = bank(k)
                    for e in range(2):
                        if not last:
                            P.M("matmul", out=bb_[:, e * 256:(e + 1) * 256], lhsT=LT[p][:, e, :],
                                rhs=XL[p][:, e, :, :].rearrange("p a t -> p (a t)"), start=True, stop=True, accum=(e > 0))
                        else:
                            P.M("matmul", out=bb_[:, e * 256:e * 256 + 128], lhsT=LT[p][:, e, :],
                                rhs=XL[p][:, e, 0, :], start=True, stop=True, accum=(e > 0))
                    if not last:
                        ba_ = bank(k)
                        for e in range(2):
                            P.M("matmul", out=ba_[:, e * 128:(e + 1) * 128], lhsT=XL[p][:, e, 1, :], rhs=LT[p][:, e, :],
                                start=True, stop=True, accum=(e > 0))
                    b4 = bb_[:, 0:512].rearrange("p (e a t) -> p e a t", e=2, a=2)
                    P.V("tensor_tensor", out=XL[p][:, :, 0, :], in0=XL[p][:, :, 0, :], in1=b4[:, :, 0, :], op=ALU.add)
                    if not last:
                        P.A("activation", out=XL[p][:, :, 1, :], in_=b4[:, :, 1, :], func=AF.Copy)
                        P.A("activation", out=LT[p][:, :, :], in_=ba_[:, 0:256].rearrange("p (e t) -> p e t", e=2),
                            func=AF.Copy)
            for p in range(4):
                pb = bank(k)
                P.M("matmul", out=pb[:, 0:256].rearrange("p (e t) -> p e t", e=2), lhsT=tok4[:, 0, p * 128:(p + 1) * 128],
                    rhs=XL[p][:, :, 0, :], start=True, stop=True)
                P.A("activation", out=PT[p][0:64, :], in_=pb[0:64, 0:128], func=AF.Copy)
                P.V("tensor_copy", out=PT[p][64:128, :], in_=pb[64:128, 128:256])
                pb2 = bank(k)
                for e in range(2):
                    h = 2 * p + e
                    P.M("matmul", out=pb2[:, e * 64:(e + 1) * 64], lhsT=AK[p][:, e, 0, :], rhs=vb[:, h * 64:(h + 1) * 64],
                        start=True, stop=True, accum=(e > 0))
                P.A("activation", out=AKV[p][:, :], in_=pb2[:, 0:128], func=AF.Copy)
            Y32 = D_["Y32"]
            for p in range(4):
                ps = slice(p * 128, (p + 1) * 128)
                bU = bank(k)
                for e in range(2):
                    P.M("matmul", out=bU[:, e * 64:(e + 1) * 64], lhsT=XL[p][:, e, 0, :], rhs=AKV[p][:, e * 64:(e + 1) * 64],
                        start=(e == 0), stop=False, accum=(e > 0))
                P.M("matmul", out=bU[:, 0:128], lhsT=PT[p][:, :], rhs=Hb[:, p, :], start=False, stop=True, accum=True)
                P.A("activation", out=Ub[p][:, :], in_=bU[:, 0:128], func=AF.Copy)
                bY = bank(k)
                for e in range(2):
                    h = 2 * p + e
                    P.M("matmul", out=bY[:, e * 64:(e + 1) * 64], lhsT=AK[p][:, e, 1, :], rhs=vb[:, h * 64:(h + 1) * 64],
                        start=(e == 0), stop=False, accum=(e > 0))
                P.M("matmul", out=bY[:, 0:128], lhsT=TTs[:, p, 1, :], rhs=Hb[:, p, :], start=False, stop=False, accum=True)
                for e in range(2):
                    P.M("matmul", out=bY[:, e * 64:(e + 1) * 64], lhsT=ARB[p][:, e, :], rhs=Ub[p][:, e * 64:(e + 1) * 64],
                        start=False, stop=(e == 1), accum=True)
                P.V("tensor_copy", out=Y32[:, ps], in_=bY[:, 0:128])
                bH = bank(k)
                P.M("matmul", out=bH[:, 0:128], lhsT=tok4[:, 2, ps], rhs=Ub[p][:, :], start=True, stop=False)
                P.M("matmul", out=bH[:, 0:128], lhsT=tok4[:, 3, ps], rhs=vb[:, ps], start=False, stop=True, accum=True)
                hT_ = D_["hT_"]
                P.V("tensor_tensor", out=hT_[:, :], in0=bH[:, 0:128], in1=H32[:, p, :], op=ALU.add)
                P.V("scalar_tensor_tensor", out=H32[:, p, :], in0=hT_[:, :], scalar=D_["gC"][:, p:p + 1], in1=bdm[:, :],
                    op0=ALU.mult, op1=ALU.mult)
                P.A("activation", out=Hb[:, p, :], in_=H32[:, p, :], func=AF.Copy)
            if d == 0:
                P.dma("pool", out=k.y_fw[t0:t0 + 128, :], in_=Y32[:, :])
            else:
                wkv, sq, bon = T[3], T[4], T[2]
                pb = bank(k)
                P.M("matmul", out=pb[:, 0:512], lhsT=lorT[0:64, 1, :], rhs=a2s[0:64, :], start=True, stop=True)
                P.V("tensor_tensor", out=T[0][:, :], in0=pb[:, 0:512], in1=a0o[:, :], op=ALU.add)
                P.A("activation", out=T[0][:, :], in_=T[0][:, :], func=AF.Sigmoid)
                P.V("scalar_tensor_tensor", out=T[0][:, :], in0=T[0][:, :], scalar=-1.0, in1=kab[:, :], op0=ALU.add,
                    op1=ALU.mult)
                P.V("scalar_tensor_tensor", out=T[0][:, :], in0=T[0][:, :], scalar=1.0, in1=kx, op0=ALU.add, op1=ALU.mult)
                P.G("tensor_tensor", out=T[0][:, :], in0=T[0][:, :], in1=kt[:, :], op=ALU.add)
                P.G("tensor_tensor", out=T[0][:, :], in0=T[0][:, :], in1=r_, op=ALU.mult)
                P.G("tensor_tensor", out=T[0][:, :], in0=T[0][:, :], in1=rkb[:, :], op=ALU.mult)
                P.V("tensor_reduce", out=sm[:, 24:32], in_=T[0][:, :].rearrange("p (h e) -> p h e", h=8), axis=AX.X,
                    op=ALU.add)
                P.V("tensor_tensor", out=bon[:, :].rearrange("p (h e) -> p h e", h=8),
                    in0=v_.rearrange("p (h e) -> p h e", h=8), in1=bc(sm[:, 24:32], [128, 8, 64], 2), op=ALU.mult)
                P.G("tensor_tensor", out=wkv[:, :], in0=Y32[:, :], in1=D_["yfw"][:, :], op=ALU.add)
                w3 = wkv[:, :].rearrange("p (h e) -> p h e", h=8)
                P.V("tensor_reduce", out=sm[:, 32:40], in_=w3, axis=AX.X, op=ALU.add)
                P.V("tensor_scalar", out=sm[:, 32:40], in0=sm[:, 32:40], scalar1=-1.0 / 64, scalar2=None, op0=ALU.mult)
                P.V("tensor_tensor", out=w3, in0=w3, in1=bc(sm[:, 32:40], [128, 8, 64], 2), op=ALU.add)
                P.A("activation", out=sq[:, :], in_=wkv[:, :], func=AF.Square)
                P.V("tensor_reduce", out=sm[:, 40:48], in_=sq[:, :].rearrange("p (h e) -> p h e", h=8), axis=AX.X, op=ALU.add)
                rstd_from_ss(k, sm[:, 48:56], sm[:, 40:48], sm[:, 56:64], 64.0, GN_EPS)
                P.V("tensor_tensor", out=w3, in0=w3, in1=bc(sm[:, 48:56], [128, 8, 64], 2), op=ALU.mult)
                P.G("tensor_tensor", out=wkv[:, :], in0=wkv[:, :], in1=lnw[:, :], op=ALU.mult)
                P.G("tensor_tensor", out=wkv[:, :], in0=wkv[:, :], in1=lnb[:, :], op=ALU.add)
                P.G("tensor_tensor", out=wkv[:, :], in0=wkv[:, :], in1=bon[:, :], op=ALU.add)
                pb = bank(k)
                P.M("matmul", out=pb[:, 0:512], lhsT=lorT[:, 2, :], rhs=g2s[:, :], start=True, stop=True)
                P.V("tensor_tensor", out=D_["ob"][:, :], in0=wkv[:, :], in1=pb[:, 0:512], op=ALU.mult)
                pb = bank(k)
                pv = bfv(pb)
                for c in range(4):
                    P.M("transpose", out=pv[:, c * 128:(c + 1) * 128], in_=D_["ob"][:, c * 128:(c + 1) * 128],
                        identity=k.ident[:, :], accum=(c > 0))
                P.A("activation", out=D_["oT"][:, :, :].rearrange("p c t -> p (c t)"), in_=pv[:, 0:512], func=AF.Copy)
                P.dma("pool", out=k.oT_rw[i], in_=D_["oT"][:, :, :])


def phase_p3a(k, l, xin):
    nc, P, S, NT = k.nc, k.P, k.S, k.NT
    w = k.w
    with contextlib.ExitStack() as st:
        def SB(name, shape, dt):
            return st.enter_context(nc.sbuf_tensor(un("p3_" + name), list(shape), dt))
        alloc_banks(k, st, 6)
        pp2 = st.enter_context(nc.psum_tensor(un("p3_pp"), [128, 1024], F32))
        wg = SB("wg", [128, 8, 3072], BF16)
        wbr = SB("wbr", [128, 3, 4, 1024], BF16)
        wo = SB("wo", [128, 8, 1024], BF16)
        wq = SB("wq", [128, 8, 512], BF16)
        cwo = SB("cwo", [128, 4, 1024], BF16)
        bg = SB("bg", [1, 3072], BF16)
        ones = SB("ones", [1, 128], BF16)
        gcol = SB("gcol", [128, 16], F32)
        kmT = SB("kmT", [128, 4, 256], BF16)
        vm = SB("vm", [128, 2, 512], BF16)
        win = w["w_in"][l].rearrange("(c p) n -> p c n", p=128)
        for c in range(8):
            P.dma("pool", out=wg[:, c, :], in_=win[:, c, 3360:6432])
        for g in range(3):
            P.dma("pool", out=wbr[:, g, :, :], in_=w["w_branch"][l, g].rearrange("(c p) n -> p c n", p=128))
        P.dma("pool", out=wo[:, :, :], in_=w["w_out"][l].rearrange("(c p) n -> p c n", p=128))
        P.dma("pool", out=cwo[:, :, :], in_=w["cross_wo"][l].rearrange("(c p) n -> p c n", p=128))
        P.dma("pool", out=bg[0:1, :], in_=w["b_gate"][l:l + 1].rearrange("o g n -> o (g n)"))
        P.G("memset", ap=ones[0:1, :], constant=1.0)
        for c in range(8):
            load_col(k, "sp", gcol[:, c:c + 1], w["norm_cross"][l, c * 128:(c + 1) * 128], 128)
            load_col(k, "sp", gcol[:, 8 + c:9 + c], w["norm_mem"][l, c * 128:(c + 1) * 128], 128)
        with contextlib.ExitStack() as st2:
            stg = st2.enter_context(nc.sbuf_tensor(un("p3_stg"), [128, 8, 1024], F32))
            wkv = st2.enter_context(nc.sbuf_tensor(un("p3_wkv"), [128, 8, 1024], BF16))
            mx = st2.enter_context(nc.sbuf_tensor(un("p3_mx"), [128, 2, 1024], F32))
            mhb = st2.enter_context(nc.sbuf_tensor(un("p3_mhb"), [128, 2, 1024], BF16))
            mhT = st2.enter_context(nc.sbuf_tensor(un("p3_mhT"), [128, 8, 256], BF16))
            mjunk = st2.enter_context(nc.sbuf_tensor(un("p3_mjunk"), [128, 1024], BF16))
            mst = st2.enter_context(nc.sbuf_tensor(un("p3_mst"), [128, 8], F32))
            P.dma("sp", out=stg[:, :, 0:512], in_=w["cross_wq"][l].rearrange("(c p) n -> p c n", p=128))
            for c in range(8):
                P.V("tensor_scalar", out=wq[:, c, :], in0=stg[:, c, 0:512], scalar1=gcol[:, c:c + 1], scalar2=None,
                    op0=ALU.mult)
            P.dma("sp", out=stg[:, :, :], in_=w["cross_wkv"][l].rearrange("(c p) n -> p c n", p=128))
            for c in range(8):
                P.V("tensor_scalar", out=wkv[:, c, :], in0=stg[:, c, :], scalar1=gcol[:, 8 + c:9 + c], scalar2=None,
                    op0=ALU.mult)
            P.dma("sp", out=mx[:, :, :], in_=k.mem.rearrange("(j p) n -> p j n", p=128))
            for j in range(2):
                P.A("activation", out=mjunk[:, :], in_=mx[:, j, :], func=AF.Square, accum_out=mst[:, j:j + 1])
            rstd_from_ss(k, mst[:, 2:4], mst[:, 0:2], mst[:, 4:6], 1024.0, EPS)
            for j in range(2):
                P.V("tensor_scalar", out=mhb[:, j, :], in0=mx[:, j, :], scalar1=mst[:, 2 + j:3 + j], scalar2=None,
                    op0=ALU.mult)
                pb = bank(k)
                pv = bfv(pb)
                for c in range(8):
                    P.M("transpose", out=pv[:, c * 128:(c + 1) * 128], in_=mhb[:, j, c * 128:(c + 1) * 128],
                        identity=k.ident[:, :], accum=(c > 0))
                P.A("activation", out=mhT[:, :, j * 128:(j + 1) * 128],
                    in_=pv[:, 0:1024].rearrange("p (c t) -> p c t", c=8), func=AF.Copy)
            for h in range(4):
                pb = bank(k)
                for c in range(8):
                    P.M("matmul", out=pb[:, 0:256], lhsT=wkv[:, c, h * 128:(h + 1) * 128], rhs=mhT[:, c, :],
                        start=(c == 0), stop=(c == 7), accum=(c > 0))
                P.A("activation", out=kmT[:, h, :], in_=pb[:, 0:256], func=AF.Copy)
            for j in range(2):
                pb = bank(k)
                for c in range(8):
                    P.M("matmul", out=pb[:, 0:512], lhsT=mhT[:, c, j * 128:(j + 1) * 128], rhs=wkv[:, c, 512:1024],
                        start=(c == 0), stop=(c == 7), accum=(c > 0))
                P.A("activation", out=vm[:, j, :], in_=pb[:, 0:512], func=AF.Copy)
        P.barrier()

        sets = []
        for s_ in range(2):
            d = {}
            for name, shape, dt in [
                ("x", [128, 1024], F32), ("hT", [128, 8, 128], BF16), ("ob", [128, 2, 512], BF16),
                ("oT", [128, 3, 4, 128], BF16), ("gt", [128, 512], F32), ("tmp", [128, 512], F32),
                ("m", [128, 1024], F32), ("mb", [128, 1024], BF16), ("mT", [128, 8, 128], BF16),
                ("x1", [128, 1024], F32), ("junk", [128, 1024], BF16), ("st", [128, 16], F32),
                ("h2", [128, 1024], BF16), ("h2T", [128, 8, 128], BF16), ("qT", [128, 4, 128], BF16),
                ("p", [128, 4, 256], BF16), ("pn", [128, 4, 256], BF16), ("pT", [128, 8, 128], BF16),
                ("ocT", [128, 4, 128], BF16), ("x2", [128, 1024], F32),
            ]:
                d[name] = SB("%s%d" % (name, s_), shape, dt)
            sets.append(d)
        have_rw = "rb" in k.phases
        for i in range(NT):
            d = sets[i % 2]
            t0 = i * 128
            x, hT, oT, stt = d["x"], d["hT"], d["oT"], d["st"]
            P.dma("sp", out=x[:, :], in_=xin[t0:t0 + 128, :])
            P.dma("sp", out=hT[:, :, :], in_=k.hT_t[i])
            P.dma("sp", out=d["ob"][:, 0, :], in_=k.o_mla[t0:t0 + 128, :])
            P.dma("sp", out=d["ob"][:, 1, :], in_=k.o_gqa[t0:t0 + 128, :])
            if have_rw:
                P.dma("sp", out=oT[:, 2, :, :], in_=k.oT_rw[i])
            else:
                P.G("memset", ap=oT[:, 2, :, :], constant=0.0)
            pb = bank(k)
            pv = bfv(pb)
            obf = d["ob"][:, :, :].rearrange("p g n -> p (g n)")
            for c in range(8):
                P.M("transpose", out=pv[:, c * 128:(c + 1) * 128], in_=obf[:, c * 128:(c + 1) * 128],
                    identity=k.ident[:, :], accum=(c > 0))
            P.A("activation", out=oT[:, 0:2, :, :].rearrange("p g c t -> p (g c t)"), in_=pv[:, 0:1024], func=AF.Copy)
            for g in range(3):
                for n in range(2):
                    cs = slice(n * 512, (n + 1) * 512)
                    pz = bank(k)
                    for c in range(8):
                        P.M("matmul", out=pz[:, 0:512], lhsT=hT[:, c, :], rhs=wg[:, c, g * 1024 + n * 512:g * 1024 + (n + 1) * 512],
                            start=(c == 0), stop=False, accum=(c > 0))
                    P.M("matmul", out=pz[:, 0:512], lhsT=ones[0:1, :], rhs=bg[0:1, g * 1024 + n * 512:g * 1024 + (n + 1) * 512],
                        start=False, stop=True, accum=True)
                    P.A("activation", out=d["gt"][:, :], in_=pz[:, 0:512], func=AF.Sigmoid)
                    pbr = bank(k)
                    for c in range(4):
                        P.M("matmul", out=pbr[:, 0:512], lhsT=oT[:, g, c, :], rhs=wbr[:, g, c, cs],
                            start=(c == 0), stop=(c == 3), accum=(c > 0))
                    if g == 0:
                        P.V("tensor_tensor", out=d["m"][:, cs], in0=d["gt"][:, :], in1=pbr[:, 0:512], op=ALU.mult)
                    else:
                        P.V("tensor_tensor", out=d["tmp"][:, :], in0=d["gt"][:, :], in1=pbr[:, 0:512], op=ALU.mult)
                        if g == 1:
                            P.G("tensor_tensor", out=d["m"][:, cs], in0=d["m"][:, cs], in1=d["tmp"][:, :], op=ALU.add)
                        else:
                            P.G("tensor_tensor", out=d["mb"][:, cs], in0=d["m"][:, cs], in1=d["tmp"][:, :], op=ALU.add)
            pb = bank(k)
            pv = bfv(pb)
            for c in range(8):
                P.M("transpose", out=pv[:, c * 128:(c + 1) * 128], in_=d["mb"][:, c * 128:(c + 1) * 128],
                    identity=k.ident[:, :], accum=(c > 0))
            P.A("activation", out=d["mT"][:, :, :].rearrange("p c t -> p (c t)"), in_=pv[:, 0:1024], func=AF.Copy)
            for n in range(2):
                cs = slice(n * 512, (n + 1) * 512)
                pb = bank(k)
                for c in range(8):
                    P.M("matmul", out=pb[:, 0:512], lhsT=d["mT"][:, c, :], rhs=wo[:, c, cs],
                        start=(c == 0), stop=(c == 7), accum=(c > 0))
                P.V("tensor_tensor", out=d["x1"][:, cs], in0=x[:, cs], in1=pb[:, 0:512], op=ALU.add)
            x1 = d["x1"]
            P.A("activation", out=d["junk"][:, :], in_=x1[:, :], func=AF.Square, accum_out=stt[:, 0:1])
            rstd_from_ss(k, stt[:, 1:2], stt[:, 0:1], stt[:, 2:3], 1024.0, EPS)
            P.V("tensor_scalar", out=d["h2"][:, :], in0=x1[:, :], scalar1=stt[:, 1:2], scalar2=None, op0=ALU.mult)
            pb = bank(k)
            pv = bfv(pb)
            for c in range(8):
                P.M("transpose", out=pv[:, c * 128:(c + 1) * 128], in_=d["h2"][:, c * 128:(c + 1) * 128],
                    identity=k.ident[:, :], accum=(c > 0))
            P.A("activation", out=d["h2T"][:, :, :].rearrange("p c t -> p (c t)"), in_=pv[:, 0:1024], func=AF.Copy)
            pb = bank(k)
            for h in range(4):
                for c in range(8):
                    P.M("matmul", out=pb[:, h * 128:(h + 1) * 128], lhsT=wq[:, c, h * 128:(h + 1) * 128], rhs=d["h2T"][:, c, :],
                        start=(c == 0), stop=(c == 7), accum=(c > 0 or h > 0))
            P.A("activation", out=d["qT"][:, :, :].rearrange("p h t -> p (h t)"), in_=pb[:, 0:512], func=AF.Copy)
            for h in range(4):
                P.M("matmul", out=pp2[:, h * 256:(h + 1) * 256], lhsT=d["qT"][:, h, :], rhs=kmT[:, h, :],
                    start=True, stop=True, accum=(h > 0))
            for h in range(4):
                P.A("activation", out=d["p"][:, h, :], in_=pp2[:, h * 256:(h + 1) * 256], func=AF.Exp,
                    scale=128.0 ** -0.5, accum_out=stt[:, 4 + h:5 + h])
            P.V("reciprocal", out=stt[:, 8:12], in_=stt[:, 4:8])
            P.V("tensor_tensor", out=d["pn"][:, :, :], in0=d["p"][:, :, :], in1=bc(stt[:, 8:12], [128, 4, 256], 2),
                op=ALU.mult)
            pb = bank(k)
            pv = bfv(pb)
            pnf = d["pn"][:, :, :].rearrange("p h m -> p (h m)")
            for c in range(8):
                P.M("transpose", out=pv[:, c * 128:(c + 1) * 128], in_=pnf[:, c * 128:(c + 1) * 128],
                    identity=k.ident[:, :], accum=(c > 0))
            P.A("activation", out=d["pT"][:, :, :].rearrange("p c t -> p (c t)"), in_=pv[:, 0:1024], func=AF.Copy)
            pb = bank(k)
            for h in range(4):
                for mc in range(2):
                    P.M("matmul", out=pb[:, h * 128:(h + 1) * 128], lhsT=vm[:, mc, h * 128:(h + 1) * 128],
                        rhs=d["pT"][:, h * 2 + mc, :], start=(mc == 0), stop=(mc == 1), accum=(mc > 0 or h > 0))
            P.A("activation", out=d["ocT"][:, :, :].rearrange("p h t -> p (h t)"), in_=pb[:, 0:512], func=AF.Copy)
            for n in range(2):
                cs = slice(n * 512, (n + 1) * 512)
                pb = bank(k)
                for h in range(4):
                    P.M("matmul", out=pb[:, 0:512], lhsT=d["ocT"][:, h, :], rhs=cwo[:, h, cs],
                        start=(h == 0), stop=(h == 3), accum=(h > 0))
                P.V("tensor_tensor", out=d["x2"][:, cs], in0=x1[:, cs], in1=pb[:, 0:512], op=ALU.add)
            P.dma("pool", out=k.x2[t0:t0 + 128, :], in_=d["x2"][:, :])


def phase_p3b(k, l, xout, last):
    nc, P, S = k.nc, k.P, k.S
    w = k.w
    TT = 256
    NTT = S // TT
    with contextlib.ExitStack() as st:
        def SB(name, shape, dt):
            return st.enter_context(nc.sbuf_tensor(un("p4_" + name), list(shape), dt))
        alloc_banks(k, st)
        w1 = SB("w1", [128, 8, 4096], BF16)
        w2 = SB("w2", [128, 32, 1024], BF16)
        gcol = SB("gcol", [128, 8], F32)
        for c in range(8):
            load_col(k, "sp", gcol[:, c:c + 1], w["norm_mlp"][l, c * 128:(c + 1) * 128], 128)
        with contextlib.ExitStack() as st2:
            stgs = [st2.enter_context(nc.sbuf_tensor(un("p4_stg%d" % i), [128, 4096], F32)) for i in range(2)]
            w1v = w["mlp_w1"][l].rearrange("(c p) n -> p c n", p=128)
            for c in range(8):
                sg = stgs[c % 2]
                P.dma("sp", out=sg[:, :], in_=w1v[:, c, :])
                for hf in range(2):
                    P.V("tensor_scalar", out=w1[:, c, hf * 2048:(hf + 1) * 2048], in0=sg[:, hf * 2048:(hf + 1) * 2048],
                        scalar1=gcol[:, c:c + 1], scalar2=None, op0=ALU.mult)
        P.barrier()
        w2v = w["mlp_w2"][l].rearrange("(f p) n -> p f n", p=128)
        for f0 in range(0, 32, 4):
            P.dma("pool", out=w2[:, f0:f0 + 4, :], in_=w2v[:, f0:f0 + 4, :])
        if last:
            gf = SB("gf", [128, 1024], F32)
            P.dma("sp", out=gf[:, :], in_=w["norm_final"][0:1, :].broadcast_to([128, 1024]))
        uT = SB("uT", [128, 32, TT], BF16)
        rbuf = [SB("r%d" % i, [128, TT], BF16) for i in range(3)]
        junk = SB("junk", [128, 1024], BF16)
        hm = [SB("hm%d" % i, [128, 1024], BF16) for i in range(2)]
        sets = []
        for s_ in range(2):
            d = {}
            for name, shape, dt in [("x", [128, 2, 1024], F32), ("hmT", [128, 8, TT], BF16), ("st", [128, 16], F32),
                                    ("y", [128, 2, 1024], F32)]:
                if name == "y" and not last:
                    continue
                d[name] = SB("%s%d" % (name, s_), shape, dt)
            sets.append(d)
        nr = 0
        for i in range(NTT):
            d = sets[i % 2]
            t0 = i * TT
            x, hmT, stt = d["x"], d["hmT"], d["st"]
            P.dma("sp", out=x[:, :, :], in_=k.x2[t0:t0 + TT, :].rearrange("(j p) n -> p j n", p=128))
            for j in range(2):
                P.A("activation", out=junk[:, :], in_=x[:, j, :], func=AF.Square, accum_out=stt[:, j:j + 1])
            rstd_from_ss(k, stt[:, 2:4], stt[:, 0:2], stt[:, 4:6], 1024.0, EPS)
            for j in range(2):
                P.V("tensor_scalar", out=hm[j][:, :], in0=x[:, j, :], scalar1=stt[:, 2 + j:3 + j], scalar2=None,
                    op0=ALU.mult)
                pb = bank(k)
                pv = bfv(pb)
                for c in range(8):
                    P.M("transpose", out=pv[:, c * 128:(c + 1) * 128], in_=hm[j][:, c * 128:(c + 1) * 128],
                        identity=k.ident[:, :], accum=(c > 0))
                P.A("activation", out=hmT[:, :, j * 128:(j + 1) * 128],
                    in_=pv[:, 0:1024].rearrange("p (c t) -> p c t", c=8), func=AF.Copy)
            for f in range(32):
                pb = bank(k)
                for c in range(8):
                    P.M("matmul", out=pb[:, 0:TT], lhsT=w1[:, c, f * 128:(f + 1) * 128], rhs=hmT[:, c, :],
                        start=(c == 0), stop=(c == 7), accum=(c > 0))
                r = rbuf[nr % 3]
                nr += 1
                P.A("activation", out=r[:, :], in_=pb[:, 0:TT], func=AF.Relu)
                P.G("tensor_tensor", out=uT[:, f, :], in0=r[:, :], in1=r[:, :], op=ALU.mult)
            for j in range(2):
                for n in range(2):
                    cs = slice(n * 512, (n + 1) * 512)
                    pb = bank(k)
                    for f in range(32):
                        P.M("matmul", out=pb[:, 0:512], lhsT=uT[:, f, j * 128:(j + 1) * 128], rhs=w2[:, f, cs],
                            start=(f == 0), stop=(f == 31), accum=(f > 0))
                    P.V("tensor_tensor", out=x[:, j, cs], in0=x[:, j, cs], in1=pb[:, 0:512], op=ALU.add)
            if not last:
                P.dma("pool", out=xout[t0:t0 + TT, :].rearrange("(j p) n -> p j n", p=128), in_=x[:, :, :])
            else:
                for j in range(2):
                    P.A("activation", out=junk[:, :], in_=x[:, j, :], func=AF.Square, accum_out=stt[:, 8 + j:9 + j])
                rstd_from_ss(k, stt[:, 10:12], stt[:, 8:10], stt[:, 12:14], 1024.0, EPS)
                for j in range(2):
                    P.V("scalar_tensor_tensor", out=d["y"][:, j, :], in0=x[:, j, :], scalar=stt[:, 10 + j:11 + j],
                        in1=gf[:, :], op0=ALU.mult, op1=ALU.mult)
                P.dma("pool", out=k.y[t0:t0 + TT, :].rearrange("(j p) n -> p j n", p=128), in_=d["y"][:, :, :])
                if "xs0" in k.dbg or "xs1" in k.dbg:
                    P.dma("pool", out=xout[t0:t0 + TT, :].rearrange("(j p) n -> p j n", p=128), in_=x[:, :, :])


def rope_tables(pos, dim):
    inv = (10000.0 ** (-np.arange(0, dim, 2, dtype=np.float32) / dim)).astype(np.float32)
    ang = pos.astype(np.float32)[:, None] * inv[None, :]
    return np.cos(ang).astype(np.float32), np.sin(ang).astype(np.float32)


def const_inputs(S):
    pos = np.arange(S)
    c1, s1 = rope_tables(pos, 32)
    cr, sr = rope_tables(pos // 64, 32)
    cc, sc = rope_tables(pos % 64, 32)
    tab1 = np.stack([c1, s1], 1).astype(np.float32)
    tab2 = np.stack([np.stack([cr, cc], 1), np.stack([sr, sc], 1)], 1).astype(np.float32)
    s_idx = np.arange(128)[:, None]
    t_idx = np.arange(128)[None, :]
    msk = np.zeros((2, 2, 128, 128), np.float32)
    msk[0, 0] = s_idx < t_idx
    msk[0, 1] = s_idx <= t_idx
    msk[1, 0] = s_idx > t_idx
    msk[1, 1] = s_idx >= t_idx
    bdm = np.zeros((128, 128), np.float32)
    bdm[:64, :64] = 1
    bdm[64:, 64:] = 1
    return dict(tab1=tab1, tab2=tab2, ident=np.eye(128, dtype=np.float32), msk=msk, bdm=bdm)


_NC_CACHE = {}


def kernel(**inputs):
    S, depth = 8192, 2
    key = (S, depth)
    if key not in _NC_CACHE:
        _NC_CACHE[key] = build(S, depth)
    nc = _NC_CACHE[key]
    xp, xs = np.asarray(inputs["x_prompt"]), np.asarray(inputs["x_sample"])
    mp, ms = np.asarray(inputs["mem_prompt"]), np.asarray(inputs["mem_sample"])
    seqs = [(xp[b], mp[b]) for b in range(xp.shape[0])] + [(xs[b], ms[b]) for b in range(xs.shape[0])]
    n_seq = len(seqs)
    consts = const_inputs(S)
    wts = {name: np.ascontiguousarray(np.asarray(inputs[name], np.float32)) for name, _ in W_SPECS}
    wts["norm_final"] = np.ascontiguousarray(np.asarray(inputs["norm_final"], np.float32).reshape(1, D))
    in_maps = []
    for c in range(8):
        x, mem = seqs[c % n_seq]
        m = dict(x=np.ascontiguousarray(x, np.float32), mem=np.ascontiguousarray(mem, np.float32))
        m.update(wts)
        m.update(consts)
        in_maps.append(m)
    res = run_bass_kernel_spmd(nc, in_maps, core_ids=list(range(8)))
    ys = [np.asarray(res.results[c]["y"], np.float32) for c in range(n_seq)]
    y_prompt = np.stack(ys[:xp.shape[0]], 0)
    y_sample = np.stack(ys[xp.shape[0]:], 0)
    return (y_prompt, y_sample)
```

```python
import contextlib
import math
import numpy as np
import ml_dtypes
import concourse.bass as bass
import concourse.mybir as mybir
from concourse.bass_utils import run_bass_kernel_spmd

F32 = mybir.dt.float32
BF16 = mybir.dt.bfloat16
ALU = mybir.AluOpType
AF = mybir.ActivationFunctionType
AX = mybir.AxisListType

D = 1024
NMEM = 256
IN_COLS = 6432
EPS = 1e-6
GN_EPS = 64e-5
DECAY_C = math.exp(-0.5)

import os
MAXOPS = int(os.environ.get("MAXOPS", "100000000"))
RW_RATIO = int(os.environ.get("RW_RATIO", "2"))
RW_DELAY = int(os.environ.get("RW_DELAY", "2"))
ENGS = ("pe", "act", "dve", "pool", "sp")
N_DMA_SEMS = 16
READ_KEYS = ("in_", "in0", "in1", "lhsT", "rhs", "scalar", "scalar1", "scalar2", "bias", "scale",
             "identity", "data0", "data1", "initial")
WRITE_KEYS = ("out", "accum_out", "ap")


class Buf:
    __slots__ = ("name", "w", "r")

    def __init__(self, name):
        self.name = name
        self.w = None
        self.r = {}


class Prog:
    def __init__(self, nc):
        self.nc = nc
        self.es = contextlib.ExitStack()
        self.eobj = {"pe": nc.tensor, "act": nc.scalar, "dve": nc.vector, "pool": nc.gpsimd, "sp": nc.sync}
        self.sem = {}
        self.cnt = {}
        for e in ENGS:
            self.sem[e] = self.es.enter_context(nc.semaphore("s_" + e))
            self.cnt[e] = 0
        self.dq = {}
        for q in ("sp", "act", "pool"):
            sems = []
            for i in range(N_DMA_SEMS):
                k = "d_%s_%d" % (q, i)
                self.sem[k] = self.es.enter_context(nc.semaphore(k))
                self.cnt[k] = 0
                sems.append(k)
            self.dq[q] = [sems, 0]
        self.seen = {e: {} for e in ENGS}
        self.ops = {e: [] for e in ENGS}
        self.bufs = {}
        self.nops = 0

    def buf_of(self, ap):
        n = ap.name
        b = self.bufs.get(n)
        if b is None:
            b = self.bufs[n] = Buf(n)
        return b

    def _need(self, e, key, val, waits):
        if self.seen[e].get(key, 0) >= val:
            return
        self.seen[e][key] = val
        waits.append((key, val))

    def _collect(self, kw, extra_r, extra_w):
        reads, writes = [], []
        for k in READ_KEYS:
            v = kw.get(k)
            if v is not None and hasattr(v, "name") and hasattr(v, "ap"):
                reads.append(self.buf_of(v))
        for k in WRITE_KEYS:
            v = kw.get(k)
            if v is not None and hasattr(v, "name") and hasattr(v, "ap"):
                writes.append(self.buf_of(v))
        for v in extra_r:
            reads.append(v if isinstance(v, Buf) else self.buf_of(v))
        for v in extra_w:
            writes.append(v if isinstance(v, Buf) else self.buf_of(v))
        return reads, writes

    def _issue(self, e, inckey, incv, name, kw, reads, writes, accum):
        self.nissued = getattr(self, "nissued", 0) + 1
        if self.nissued > MAXOPS:
            return
        if self.nissued == MAXOPS:
            print("LAST OP:", e, name, {a: (str(b.name) + str(b.shape) if hasattr(b, "ap") else b) for a, b in kw.items()})
        need = {}
        for b in reads:
            if b.w is not None and need.get(b.w[0], 0) < b.w[1]:
                need[b.w[0]] = b.w[1]
        for b in writes:
            if b.w is not None and not (e == "pe" and b.w[0] == "pe") and need.get(b.w[0], 0) < b.w[1]:
                need[b.w[0]] = b.w[1]
            for k, v in b.r.items():
                if need.get(k, 0) < v:
                    need[k] = v
        waits = []
        for k, v in need.items():
            self._need(e, k, v, waits)
        self.cnt[inckey] += incv
        v = self.cnt[inckey]
        self.ops[e].append((waits, name, kw, inckey, incv))
        self.nops += 1 + len(waits)
        for b in reads:
            if b.r.get(inckey, 0) < v:
                b.r[inckey] = v
        for b in writes:
            b.w = (inckey, v)
            b.r = {}

    def op(self, e, name, R=(), W=(), accum=False, drain=False, **kw):
        reads, writes = self._collect(kw, R, W)
        if drain and self.cnt[e] > 0:
            d_ = Buf("drain")
            d_.w = (e, self.cnt[e])
            reads = list(reads) + [d_]
        self._issue(e, e, 1, name, kw, reads, writes, accum)

    def V(self, name, **kw):
        self.op("dve", name, **kw)

    def A(self, name, **kw):
        self.op("act", name, **kw)

    def G(self, name, **kw):
        self.op("pool", name, **kw)

    def M(self, name="matmul", **kw):
        self.op("pe", name, **kw)

    def dma(self, q, out, in_, R=(), W=()):
        kw = dict(out=out, in_=in_)
        reads, writes = self._collect(kw, R, W)
        sems, idx = self.dq[q]
        key = sems[idx % len(sems)]
        self.dq[q][1] = idx + 1
        if self.cnt[key] > 0:
            w = []
            self._need(q, key, self.cnt[key], w)
            pre = w
        else:
            pre = []
        n0 = len(self.ops[q])
        self._issue(q, key, 16, "dma_start", kw, reads, writes, False)
        if pre and len(self.ops[q]) > n0:
            waits, name, kw2, ik, iv = self.ops[q][n0]
            self.ops[q][n0] = (pre + waits, name, kw2, ik, iv)

    def barrier(self):
        for e in ENGS:
            waits = []
            for key, v in self.cnt.items():
                if v > 0:
                    self._need(e, key, v, waits)
            if waits:
                self.ops[e].append((waits, None, None, None, None))
                self.nops += len(waits)

    def wait_bufs(self, e, aps):
        waits = []
        for a in aps:
            b = a if isinstance(a, Buf) else self.buf_of(a)
            if b.w is not None:
                self._need(e, b.w[0], b.w[1], waits)
        self.ops[e].append((waits, None, None, None, None))

    def emit(self):
        nc = self.nc
        with nc.Block() as block:
            def run(e):
                def body(engine):
                    for waits, name, kw, ik, iv in self.ops[e]:
                        for k, v in waits:
                            engine.wait_ge(self.sem[k], v)
                        if name is None:
                            continue
                        inst = getattr(engine, name)(**kw)
                        inst.then_inc(self.sem[ik], iv)
                return body
            block.sync(run("sp"))
            block.scalar(run("act"))
            block.vector(run("dve"))
            block.gpsimd(run("pool"))
            block.tensor(run("pe"))

    def close(self):
        self.es.close()


W_SPECS = [
    ("norm_mix", (D,)), ("w_in", (D, IN_COLS)), ("mla_q_norm", (384,)), ("mla_w_uq", (384, 768)),
    ("mla_kv_norm", (256,)), ("mla_w_ukv", (256, 1024)), ("gqa_q_norm", (64,)), ("gqa_k_norm", (64,)),
    ("rwkv_mu", (1920,)), ("rwkv_w0", (2, 512)), ("rwkv_w2", (2, 64, 512)), ("rwkv_a0", (2, 512)),
    ("rwkv_a2", (2, 64, 512)), ("rwkv_g2", (128, 512)), ("rwkv_k_k", (512,)), ("rwkv_k_a", (512,)),
    ("rwkv_r_k", (8, 64)), ("rwkv_ln_w", (512,)), ("rwkv_ln_b", (512,)), ("w_branch", (3, 512, D)),
    ("b_gate", (3, D)), ("w_out", (D, D)), ("norm_cross", (D,)), ("norm_mem", (D,)),
    ("cross_wq", (D, 512)), ("cross_wkv", (D, 1024)), ("cross_wo", (512, D)), ("norm_mlp", (D,)),
    ("mlp_w1", (D, 4096)), ("mlp_w2", (4096, D)),
]


class K:
    pass


def build(S, depth, dbg=(), phases=("p1", "p2", "rf", "rb", "p3a", "p3b")):
    NT = S // 128
    nc = bass.Bass("TRN2", target_bir_lowering=False)
    P = Prog(nc)
    k = K()
    k.nc, k.P, k.S, k.NT, k.depth, k.dbg = nc, P, S, NT, depth, dbg
    k.phases = phases

    def din(name, shape):
        return nc.dram_tensor(name, list(shape), F32, kind="ExternalInput").ap()

    def dscr(name, shape, dt):
        kind = "ExternalOutput" if name in dbg else "Internal"
        return nc.dram_tensor(name, list(shape), dt, kind=kind).ap()

    k.x = din("x", (S, D))
    k.mem = din("mem", (NMEM, D))
    k.w = {}
    for name, shp in W_SPECS:
        k.w[name] = din(name, (depth,) + shp)
    k.w["norm_final"] = din("norm_final", (1, D))
    k.tab1 = din("tab1", (S, 2, 16))
    k.tab2 = din("tab2", (S, 2, 2, 16))
    k.ident_d = din("ident", (128, 128))
    k.msk_d = din("msk", (2, 2, 128, 128))
    k.bdm_d = din("bdm", (128, 128))
    k.y = nc.dram_tensor("y", [S, D], F32, kind="ExternalOutput").ap()

    k.h_tm = dscr("h_tm", (S, D), BF16)
    k.hT_t = dscr("hT_t", (NT, 128, 8, 128), BF16)
    k.qT_mla = dscr("qT_mla", (8, 96, S), BF16)
    k.kT_mla = dscr("kT_mla", (8, 96, S), BF16)
    k.v_mla = dscr("v_mla", (S, 512), BF16)
    k.qT_gqa = dscr("qT_gqa", (512, S), BF16)
    k.kT_gqa = dscr("kT_gqa", (128, S), BF16)
    k.v_gqa = dscr("v_gqa", (S, 128), BF16)
    k.o_mla = dscr("o_mla", (S, 512), BF16)
    k.o_gqa = dscr("o_gqa", (S, 512), BF16)
    k.y_fw = dscr("y_fw", (S, 512), F32)
    k.z_r = dscr("z_r", (S, 1920), F32)
    k.zmix = dscr("zmix", (S, 1920), F32)
    k.oT_rw = dscr("oT_rw", (NT, 128, 4, 128), BF16)
    k.x2 = dscr("x2", (S, D), F32)
    k.xs = [dscr("xs0", (S, D), F32), dscr("xs1", (S, D), F32)]

    with contextlib.ExitStack() as gst:
        k.gst = gst
        k.rot = [0]
        k.ident = gst.enter_context(nc.sbuf_tensor("ident_b", [128, 128], BF16))
        P.dma("pool", out=k.ident[:, :], in_=k.ident_d)
        k.identf = gst.enter_context(nc.sbuf_tensor("ident_f", [128, 128], F32))
        P.dma("sp", out=k.identf[:, :], in_=k.ident_d)
        for l in range(depth):
            xin = k.x if l == 0 else k.xs[(l - 1) % 2]
            xout = k.xs[l % 2]
            if "p1" in phases:
                phase_p1(k, l, xin)
                P.barrier()
            if "p2" in phases:
                phase_attn(k, l)
                P.barrier()
            if "rf" in phases:
                phase_rwkv(k, l, 0)
                P.barrier()
            if "rb" in phases:
                phase_rwkv(k, l, 1)
                P.barrier()
            if "p3a" in phases:
                phase_p3a(k, l, xin)
                P.barrier()
            if "p3b" in phases:
                phase_p3b(k, l, xout, last=(l == depth - 1))
                P.barrier()
        outs = [k.y]
        for name in dbg:
            outs.append(P.bufs[name]) if name in P.bufs else None
        P.wait_bufs("sp", outs)
        P.emit()
    P.close()
    return nc


_UID = [0]


def un(name):
    _UID[0] += 1
    return "%s_u%d" % (name, _UID[0])


def alloc_banks(k, st, n=8):
    k.banks = [st.enter_context(k.nc.psum_tensor(un("bank%d" % i), [128, 512], F32)) for i in range(n)]


class BankPool:
    def __init__(self, banks):
        self.banks = banks
        self.i = 0

    def get(self):
        b = self.banks[self.i % len(self.banks)]
        self.i += 1
        return b


def run_skewed(gens, ratio=1):
    old = None
    for new in gens:
        new_mid = False
        while True:
            for _ in range(ratio):
                if old is not None:
                    try:
                        next(old)
                    except StopIteration:
                        old = None
            if not new_mid:
                try:
                    if next(new) == "mid":
                        new_mid = True
                except StopIteration:
                    new_mid = True
                    new = None
            if old is None and new_mid:
                break
        old = new
    while old is not None:
        try:
            next(old)
        except StopIteration:
            old = None


def bank(k, lo=0, hi=None):
    if hi is None:
        hi = len(k.banks)
    n = hi - lo
    i = k.rot[0] % n
    k.rot[0] += 1
    return k.banks[lo + i]


def bfv(b, ncols=1024):
    return b[:, :].bitcast(BF16)


def rstd_from_ss(k, out, ss, t, n, eps):
    P = k.P
    if eps is not None and eps != 0.0:
        P.V("tensor_scalar", out=t, in0=ss, scalar1=1.0 / n, scalar2=float(eps), op0=ALU.mult, op1=ALU.add)
        P.A("activation", out=t, in_=t, func=AF.Ln)
    else:
        P.A("activation", out=t, in_=ss, func=AF.Ln, scale=1.0 / n)
    P.A("activation", out=out, in_=t, func=AF.Exp, scale=-0.5)


def load_col(k, q, dst, src1d, n):
    k.P.dma(q, out=dst, in_=src1d.rearrange("(p o) -> p o", o=1))


def bc(ap, shape, axis):
    return ap.unsqueeze(axis).broadcast_to(list(shape))


def phase_p1(k, l, xin):
    nc, P, S, NT = k.nc, k.P, k.S, k.NT
    w = k.w
    with contextlib.ExitStack() as st:
        def SB(name, shape, dt):
            return st.enter_context(nc.sbuf_tensor(un("p1_" + name), list(shape), dt))
        alloc_banks(k, st)
        w_att = SB("watt", [128, 8, 1440], BF16)
        w_r = SB("wr", [128, 8, 1920], BF16)
        w_uq = SB("wuq", [128, 3, 768], BF16)
        w_ukv = SB("wukv", [128, 2, 1024], BF16)
        stg = SB("stg", [128, 2304], F32)
        g_bc = SB("gbc", [128, 1024], F32)
        gq_bc = SB("gqbc", [128, 64], F32)
        gk_bc = SB("gkbc", [128, 64], F32)
        gcol = SB("gcol", [128, 5], F32)
        win = w["w_in"][l].rearrange("(c p) n -> p c n", p=128)
        for c in range(8):
            P.dma("pool", out=w_att[:, c, :], in_=win[:, c, 0:1440])
        for c in range(8):
            P.dma("pool", out=w_r[:, c, :], in_=win[:, c, 1440:3360])
        P.dma("sp", out=g_bc[:, :], in_=w["norm_mix"][l:l + 1, :].broadcast_to([128, 1024]))
        P.dma("sp", out=gq_bc[:, :], in_=w["gqa_q_norm"][l:l + 1, :].broadcast_to([128, 64]))
        P.dma("sp", out=gk_bc[:, :], in_=w["gqa_k_norm"][l:l + 1, :].broadcast_to([128, 64]))
        for c in range(3):
            load_col(k, "sp", gcol[:, c:c + 1], w["mla_q_norm"][l, c * 128:(c + 1) * 128], 128)
        for c in range(2):
            load_col(k, "sp", gcol[:, 3 + c:4 + c], w["mla_kv_norm"][l, c * 128:(c + 1) * 128], 128)
        P.dma("sp", out=stg[:, 0:2304].rearrange("p (c n) -> p c n", c=3),
              in_=w["mla_w_uq"][l].rearrange("(c p) n -> p c n", p=128))
        for c in range(3):
            P.V("tensor_scalar", out=w_uq[:, c, :], in0=stg[:, c * 768:(c + 1) * 768],
                scalar1=gcol[:, c:c + 1], scalar2=None, op0=ALU.mult)
        P.dma("sp", out=stg[:, 0:2048].rearrange("p (c n) -> p c n", c=2),
              in_=w["mla_w_ukv"][l].rearrange("(c p) n -> p c n", p=128))
        for c in range(2):
            P.V("tensor_scalar", out=w_ukv[:, c, :], in0=stg[:, c * 1024:(c + 1) * 1024],
                scalar1=gcol[:, 3 + c:4 + c], scalar2=None, op0=ALU.mult)

        sets = []
        for s in range(2):
            d = {}
            for name, shape, dt in [
                ("x", [128, 1024], F32), ("junk", [128, 1024], BF16), ("hb", [128, 1024], BF16),
                ("hT", [128, 8, 128], BF16), ("tb1", [128, 2, 16], F32), ("tb2", [128, 2, 2, 16], F32),
                ("st", [128, 16], F32), ("cqb", [128, 384], BF16), ("cqT", [128, 3, 128], BF16),
                ("q32", [128, 8, 96], F32), ("qrot", [128, 8, 96], BF16), ("qT", [96, 8, 128], BF16),
                ("ta", [128, 8, 2, 16], F32), ("tb", [128, 8, 2, 16], F32),
                ("ckvb", [128, 256], BF16), ("ckvT", [128, 2, 128], BF16), ("kt", [128, 8, 96], BF16),
                ("vt", [128, 8, 64], BF16), ("kr", [128, 32], F32), ("kT", [96, 8, 128], BF16),
                ("sq", [128, 512], F32), ("gst", [128, 24], F32), ("qn", [128, 8, 64], F32),
                ("gqr", [128, 8, 64], BF16), ("gqT", [128, 4, 128], BF16),
                ("kn", [128, 2, 64], F32), ("gkr", [128, 2, 64], BF16), ("gkT", [128, 128], BF16),
                ("gv", [128, 128], BF16), ("zr", [128, 1920], F32),
            ]:
                d[name] = SB("%s%d" % (name, s), shape, dt)
            sets.append(d)

        ident = k.ident
        poolA = BankPool(k.banks[0:3])
        poolB = BankPool(k.banks[3:8])

        def tile_gen(i):
            pool = poolA
            d = sets[i % 2]
            t0 = i * 128
            x, hb, hT, stt = d["x"], d["hb"], d["hT"], d["st"]
            P.dma("sp", out=x[:, :], in_=xin[t0:t0 + 128, :])
            P.dma("sp", out=d["tb1"][:, :, :], in_=k.tab1[t0:t0 + 128])
            P.dma("sp", out=d["tb2"][:, :, :, :], in_=k.tab2[t0:t0 + 128])
            P.A("activation", out=d["junk"][:, :], in_=x[:, :], func=AF.Square, accum_out=stt[:, 0:1])
            rstd_from_ss(k, stt[:, 1:2], stt[:, 0:1], stt[:, 2:3], 1024.0, EPS)
            P.V("scalar_tensor_tensor", out=hb[:, :], in0=x[:, :], scalar=stt[:, 1:2], in1=g_bc[:, :],
                op0=ALU.mult, op1=ALU.mult)
            P.dma("pool", out=k.h_tm[t0:t0 + 128, :], in_=hb[:, :])
            pb = pool.get()
            pv = bfv(pb)
            for c in range(8):
                P.M("transpose", out=pv[:, c * 128:(c + 1) * 128], in_=hb[:, c * 128:(c + 1) * 128],
                    identity=ident[:, :], accum=(c > 0))
            P.A("activation", out=hT[:, :, :].rearrange("p c t -> p (c t)"), in_=pv[:, 0:1024], func=AF.Copy)
            P.dma("pool", out=k.hT_t[i], in_=hT[:, :, :])
            yield

            def proj(c0, n):
                nonlocal pool
                b = pool.get()
                for c in range(8):
                    P.M("matmul", out=b[:, 0:n], lhsT=hT[:, c, :], rhs=w_att[:, c, c0:c0 + n],
                        start=(c == 0), stop=(c == 7), accum=(c > 0))
                return b

            for ci, (c0, n) in enumerate(((0, 512), (512, 512), (1024, 512), (1536, 384))):
                b = pool.get()
                for c in range(8):
                    P.M("matmul", out=b[:, 0:n], lhsT=hT[:, c, :], rhs=w_r[:, c, c0:c0 + n],
                        start=(c == 0), stop=(c == 7), accum=(c > 0))
                if ci % 2 == 0:
                    P.A("activation", out=d["zr"][:, c0:c0 + n], in_=b[:, 0:n], func=AF.Copy)
                else:
                    P.V("tensor_copy", out=d["zr"][:, c0:c0 + n], in_=b[:, 0:n])
                yield
            P.dma("pool", out=k.z_r[t0:t0 + 128, :], in_=d["zr"][:, :])
            pool = poolB
            yield "mid"
            cos1 = bc(d["tb1"][:, 0, :], [128, 8, 16], 1)
            sin1 = bc(d["tb1"][:, 1, :], [128, 8, 16], 1)
            pq = proj(0, 384)
            P.A("activation", out=d["junk"][:, 0:384], in_=pq[:, 0:384], func=AF.Square, accum_out=stt[:, 3:4])
            rstd_from_ss(k, stt[:, 4:5], stt[:, 3:4], stt[:, 5:6], 384.0, EPS)
            P.V("tensor_copy", out=d["cqb"][:, :], in_=pq[:, 0:384])
            yield
            pb = pool.get()
            pv = bfv(pb)
            for c in range(3):
                P.M("transpose", out=pv[:, c * 128:(c + 1) * 128], in_=d["cqb"][:, c * 128:(c + 1) * 128],
                    identity=ident[:, :], accum=(c > 0))
            P.A("activation", out=d["cqT"][:, :, :].rearrange("p c t -> p (c t)"), in_=pv[:, 0:384], func=AF.Copy)
            q32 = d["q32"]
            q32f = q32[:, :, :].rearrange("p h e -> p (h e)")
            for (c0, n) in ((0, 512), (512, 256)):
                b = pool.get()
                for c in range(3):
                    P.M("matmul", out=b[:, 0:n], lhsT=d["cqT"][:, c, :], rhs=w_uq[:, c, c0:c0 + n],
                        start=(c == 0), stop=(c == 2), accum=(c > 0))
                P.V("tensor_scalar", out=q32f[:, c0:c0 + n], in0=b[:, 0:n], scalar1=stt[:, 4:5], scalar2=None,
                    op0=ALU.mult)
            yield
            qrot = d["qrot"]
            ta = d["ta"][:, :, 0, :]
            tb = d["tb"][:, :, 0, :]
            P.G("tensor_copy", out=qrot[:, :, 0:64], in_=q32[:, :, 0:64])
            P.V("tensor_tensor", out=ta, in0=q32[:, :, 64:80], in1=cos1, op=ALU.mult)
            P.V("tensor_tensor", out=tb, in0=q32[:, :, 80:96], in1=sin1, op=ALU.mult)
            P.V("tensor_tensor", out=qrot[:, :, 64:80], in0=ta, in1=tb, op=ALU.subtract)
            P.V("tensor_tensor", out=ta, in0=q32[:, :, 64:80], in1=sin1, op=ALU.mult)
            P.V("tensor_tensor", out=tb, in0=q32[:, :, 80:96], in1=cos1, op=ALU.mult)
            P.V("tensor_tensor", out=qrot[:, :, 80:96], in0=ta, in1=tb, op=ALU.add)
            pb = pool.get()
            pv = bfv(pb)
            for h in range(8):
                P.M("transpose", out=pv[0:96, h * 128:(h + 1) * 128], in_=qrot[:, h, :], identity=ident[:, :],
                    accum=(h > 0))
            P.A("activation", out=d["qT"][:, :, :].rearrange("p h t -> p (h t)"), in_=pv[0:96, 0:1024], func=AF.Copy)
            P.dma("pool", out=k.qT_mla[:, :, t0:t0 + 128].rearrange("h e s -> e h s"), in_=d["qT"][:, :, :])
            yield
            pkv = proj(384, 288)
            P.A("activation", out=d["junk"][:, 0:256], in_=pkv[:, 0:256], func=AF.Square, accum_out=stt[:, 6:7])
            rstd_from_ss(k, stt[:, 7:8], stt[:, 6:7], stt[:, 8:9], 256.0, EPS)
            P.V("tensor_copy", out=d["ckvb"][:, :], in_=pkv[:, 0:256])
            kr = d["kr"]
            c1 = d["tb1"][:, 0, :]
            s1 = d["tb1"][:, 1, :]
            t2a = d["ta"][:, 0, 1, :]
            t2b = d["tb"][:, 0, 1, :]
            P.V("tensor_tensor", out=t2a, in0=pkv[:, 256:272], in1=c1, op=ALU.mult)
            P.V("tensor_tensor", out=t2b, in0=pkv[:, 272:288], in1=s1, op=ALU.mult)
            P.V("tensor_tensor", out=kr[:, 0:16], in0=t2a, in1=t2b, op=ALU.subtract)
            P.V("tensor_tensor", out=t2a, in0=pkv[:, 256:272], in1=s1, op=ALU.mult)
            P.V("tensor_tensor", out=t2b, in0=pkv[:, 272:288], in1=c1, op=ALU.mult)
            P.V("tensor_tensor", out=kr[:, 16:32], in0=t2a, in1=t2b, op=ALU.add)
            pb = pool.get()
            pv = bfv(pb)
            for c in range(2):
                P.M("transpose", out=pv[:, c * 128:(c + 1) * 128], in_=d["ckvb"][:, c * 128:(c + 1) * 128],
                    identity=ident[:, :], accum=(c > 0))
            P.A("activation", out=d["ckvT"][:, :, :].rearrange("p c t -> p (c t)"), in_=pv[:, 0:256], func=AF.Copy)
            yield
            kt, vt = d["kt"], d["vt"]
            for half in range(2):
                b = pool.get()
                for c in range(2):
                    P.M("matmul", out=b[:, 0:512], lhsT=d["ckvT"][:, c, :], rhs=w_ukv[:, c, half * 512:(half + 1) * 512],
                        start=(c == 0), stop=(c == 1), accum=(c > 0))
                b3 = b[:, 0:512].rearrange("p (h e) -> p h e", h=4)
                P.V("tensor_scalar", out=kt[:, half * 4:(half + 1) * 4, 0:64], in0=b3[:, :, 0:64], scalar1=stt[:, 7:8],
                    scalar2=None, op0=ALU.mult)
                P.V("tensor_scalar", out=vt[:, half * 4:(half + 1) * 4, :], in0=b3[:, :, 64:128], scalar1=stt[:, 7:8],
                    scalar2=None, op0=ALU.mult)
            P.G("tensor_copy", out=kt[:, :, 64:96], in_=bc(kr[:, :], [128, 8, 32], 1))
            pb = pool.get()
            pv = bfv(pb)
            for h in range(8):
                P.M("transpose", out=pv[0:96, h * 128:(h + 1) * 128], in_=kt[:, h, :], identity=ident[:, :],
                    accum=(h > 0))
            P.A("activation", out=d["kT"][:, :, :].rearrange("p h t -> p (h t)"), in_=pv[0:96, 0:1024], func=AF.Copy)
            P.dma("pool", out=k.kT_mla[:, :, t0:t0 + 128].rearrange("h e s -> e h s"), in_=d["kT"][:, :, :])
            P.dma("pool", out=k.v_mla[t0:t0 + 128, :], in_=vt[:, :, :].rearrange("p h e -> p (h e)"))

            yield
            def qknorm_rope(pb_ap, nh, gbc, n32, rot, soff):
                gs = d["gst"]
                P.A("activation", out=d["sq"][:, 0:nh * 64], in_=pb_ap, func=AF.Square)
                P.V("tensor_reduce", out=gs[:, soff:soff + nh],
                    in_=d["sq"][:, 0:nh * 64].rearrange("p (h e) -> p h e", h=nh), axis=AX.X, op=ALU.add)
                rstd_from_ss(k, gs[:, soff + 8:soff + 8 + nh], gs[:, soff:soff + nh], gs[:, soff + 16:soff + 16 + nh],
                             64.0, EPS)
                P.V("tensor_tensor", out=n32[:, :, :], in0=pb_ap.rearrange("p (h e) -> p h e", h=nh),
                    in1=bc(gs[:, soff + 8:soff + 8 + nh], [128, nh, 64], 2), op=ALU.mult)
                P.G("tensor_tensor", out=n32[:, :, :], in0=n32[:, :, :], in1=bc(gbc[:, :], [128, nh, 64], 1),
                    op=ALU.mult)
                v5 = n32[:, :, :].rearrange("p h (a b e) -> p h a b e", a=2, b=2)
                r5 = rot[:, :, :].rearrange("p h (a b e) -> p h a b e", a=2, b=2)
                x1, x2 = v5[:, :, :, 0, :], v5[:, :, :, 1, :]
                cos2 = bc(d["tb2"][:, 0, :, :], [128, nh, 2, 16], 1)
                sin2 = bc(d["tb2"][:, 1, :, :], [128, nh, 2, 16], 1)
                ta4 = d["ta"][:, 0:nh, :, :]
                tb4 = d["tb"][:, 0:nh, :, :]
                P.V("tensor_tensor", out=ta4, in0=x1, in1=cos2, op=ALU.mult)
                P.V("tensor_tensor", out=tb4, in0=x2, in1=sin2, op=ALU.mult)
                P.V("tensor_tensor", out=r5[:, :, :, 0, :], in0=ta4, in1=tb4, op=ALU.subtract)
                P.V("tensor_tensor", out=ta4, in0=x1, in1=sin2, op=ALU.mult)
                P.V("tensor_tensor", out=tb4, in0=x2, in1=cos2, op=ALU.mult)
                P.V("tensor_tensor", out=r5[:, :, :, 1, :], in0=ta4, in1=tb4, op=ALU.add)

            pgq = proj(672, 512)
            yield
            qknorm_rope(pgq[:, 0:512], 8, gq_bc, d["qn"], d["gqr"], 0)
            pb = pool.get()
            pv = bfv(pb)
            gqf = d["gqr"][:, :, :].rearrange("p h e -> p (h e)")
            for c in range(4):
                P.M("transpose", out=pv[:, c * 128:(c + 1) * 128], in_=gqf[:, c * 128:(c + 1) * 128],
                    identity=ident[:, :], accum=(c > 0))
            P.A("activation", out=d["gqT"][:, :, :].rearrange("p c t -> p (c t)"), in_=pv[:, 0:512], func=AF.Copy)
            P.dma("pool", out=k.qT_gqa[:, t0:t0 + 128].rearrange("(c p) s -> p c s", p=128), in_=d["gqT"][:, :, :])
            yield
            pgk = proj(1184, 256)
            P.A("activation", out=d["gv"][:, :], in_=pgk[:, 128:256], func=AF.Copy)
            P.dma("pool", out=k.v_gqa[t0:t0 + 128, :], in_=d["gv"][:, :])
            qknorm_rope(pgk[:, 0:128], 2, gk_bc, d["kn"], d["gkr"], 2)
            pb = pool.get()
            pv = bfv(pb)
            P.M("transpose", out=pv[:, 0:128], in_=d["gkr"][:, :, :].rearrange("p h e -> p (h e)"),
                identity=ident[:, :])
            P.A("activation", out=d["gkT"][:, :], in_=pv[:, 0:128], func=AF.Copy)
            P.dma("pool", out=k.kT_gqa[:, t0:t0 + 128], in_=d["gkT"][:, :])

        run_skewed([tile_gen(i) for i in range(NT)], ratio=2)


def phase_attn(k, l):
    nc, P, S, NT = k.nc, k.P, k.S, k.NT
    QB = 512
    NQB = S // QB
    NG = NT // 2
    with contextlib.ExitStack() as st:
        def SB(name, shape, dt):
            return st.enter_context(nc.sbuf_tensor(un("p2_" + name), list(shape), dt))
        pps = [st.enter_context(nc.psum_tensor(un("pp%d" % i), [128, 1024], F32)) for i in range(3)]
        obs = [st.enter_context(nc.psum_tensor(un("ob%d" % i), [128, 512], F32)) for i in range(2)]
        kts = [SB("kt%d" % i, [96, S], BF16) for i in range(2)]
        qts = [SB("qt%d" % i, [96, S], BF16) for i in range(2)]
        vxs = [SB("vx%d" % i, [128, NT, 65], BF16) for i in range(2)]
        pts = [SB("pt%d" % i, [128, 2 * QB], BF16) for i in range(3)]
        o32 = [SB("o32_%d" % i, [65, QB], F32) for i in range(2)]
        osb = [SB("o%d" % i, [128, 4, 64], BF16) for i in range(2)]
        rsb = [SB("rs%d" % i, [128, 4], F32) for i in range(2)]
        for vx in vxs:
            P.G("memset", ap=vx[:, :, 64:65], constant=1.0, W=[vx[:, :, :]])
        heads = [("mla", h) for h in range(8)] + [("gqa", h) for h in range(8)]
        hd = []
        kvi = -1
        for hi, (kind, h) in enumerate(heads):
            qt = qts[hi % 2]
            loads = []
            if kind == "mla":
                dq = 96
                kvi += 1
                kt, vx = kts[kvi % 2], vxs[kvi % 2]
                loads.append((kt[0:96, :], k.kT_mla[h]))
                vsrc = k.v_mla[:, h * 64:(h + 1) * 64].rearrange("(c p) e -> p c e", p=128)
                for c0 in range(0, NT, 8):
                    c1 = min(NT, c0 + 8)
                    loads.append((vx[:, c0:c1, 0:64], vsrc[:, c0:c1, :]))
                loads.append((qt[0:96, :], k.qT_mla[h]))
                oscr = k.o_mla
            else:
                dq = 64
                if h % 4 == 0:
                    kvi += 1
                    kvh = h // 4
                    kt, vx = kts[kvi % 2], vxs[kvi % 2]
                    loads.append((kt[0:64, :], k.kT_gqa[kvh * 64:(kvh + 1) * 64, :]))
                    vsrc = k.v_gqa[:, kvh * 64:(kvh + 1) * 64].rearrange("(c p) e -> p c e", p=128)
                    for c0 in range(0, NT, 8):
                        c1 = min(NT, c0 + 8)
                        loads.append((vx[:, c0:c1, 0:64], vsrc[:, c0:c1, :]))
                loads.append((qt[0:64, :], k.qT_gqa[h * 64:(h + 1) * 64, :]))
                oscr = k.o_gqa
            hd.append(dict(kt=kt, vx=vx, qt=qt, dq=dq, scale=float(dq) ** -0.5, h=h, oscr=oscr, loads=loads))
        items = []
        for hi in range(len(hd)):
            for qb in range(NQB):
                for g in range(NG):
                    items.append((hi, qb, g))
        state = {"npp": 0, "nqb": 0}

        def do_loads(hi):
            for (o_, i_) in hd[hi]["loads"]:
                P.dma("sp", out=o_, in_=i_)

        def emit_qk(hi, qb, g):
            H_ = hd[hi]
            pp = pps[state["npp"] % 3]
            pt = pts[state["npp"] % 3]
            state["npp"] += 1
            for u in range(2):
                kc = 2 * g + u
                P.M("matmul", out=pp[:, u * QB:(u + 1) * QB], lhsT=H_["kt"][0:H_["dq"], kc * 128:(kc + 1) * 128],
                    rhs=H_["qt"][0:H_["dq"], qb * QB:(qb + 1) * QB], start=True, stop=True, accum=(u > 0))
            P.A("activation", out=pt[:, :], in_=pp[:, :], func=AF.Exp, scale=H_["scale"])
            return pt

        def emit_pv(hi, qb, g, pt):
            H_ = hd[hi]
            ob = obs[state["nqb"] % 2]
            for u in range(2):
                kc = 2 * g + u
                P.M("matmul", out=ob[0:65, 0:QB], lhsT=H_["vx"][:, kc, 0:65], rhs=pt[:, u * QB:(u + 1) * QB],
                    start=(kc == 0), stop=(kc == NT - 1), accum=(kc > 0))
            if g != NG - 1:
                return None
            o3, o_s, r_s = o32[state["nqb"] % 2], osb[state["nqb"] % 2], rsb[state["nqb"] % 2]
            state["nqb"] += 1
            P.A("activation", out=o3[:, :], in_=ob[0:65, 0:QB], func=AF.Copy)

            def epilogue():
                pp = pps[state["npp"] % 3]
                state["npp"] += 1
                for j in range(4):
                    P.M("transpose", out=pp[:, j * 128:j * 128 + 65], in_=o3[0:65, j * 128:(j + 1) * 128],
                        identity=k.identf[0:65, 0:65], accum=(j > 0))
                tb3 = pp[:, 0:512].rearrange("p (j e) -> p j e", j=4)
                P.V("reciprocal", out=r_s[:, :], in_=tb3[:, :, 64])
                P.V("tensor_tensor", out=o_s[:, :, :], in0=tb3[:, :, 0:64], in1=bc(r_s[:, :], [128, 4, 64], 2),
                    op=ALU.mult)
                h = H_["h"]
                P.dma("pool", out=H_["oscr"][qb * QB:(qb + 1) * QB, h * 64:(h + 1) * 64].rearrange("(j p) e -> p j e", p=128),
                      in_=o_s[:, :, :])
            return epilogue

        n = len(items)
        per_head = NQB * NG
        do_loads(0)
        prev_pt = None
        pending = None
        for idx in range(n + 1):
            if idx < n:
                hi, qb, g = items[idx]
                if idx % per_head == min(2, per_head - 1) and hi + 1 < len(hd):
                    do_loads(hi + 1)
                cur_pt = emit_qk(hi, qb, g)
            if idx >= 1:
                hi0, qb0, g0 = items[idx - 1]
                ep = emit_pv(hi0, qb0, g0, prev_pt)
                if pending is not None:
                    pending()
                pending = ep
            prev_pt = cur_pt
        if pending is not None:
            pending()


def phase_rwkv(k, l, d):
    nc, P, S, NT = k.nc, k.P, k.S, k.NT
    w = k.w
    C0 = DECAY_C
    with contextlib.ExitStack() as st:
        def SB(name, shape, dt):
            return st.enter_context(nc.sbuf_tensor(un("rw_" + name), list(shape), dt))
        alloc_banks(k, st)
        bcn = {}

        def load_bc(name, src_row, n=512):
            t = SB(name, [128, n], F32)
            P.dma("sp", out=t[:, :], in_=src_row.broadcast_to([128, n]))
            bcn[name] = t
            return t
        w0b = load_bc("w0", w["rwkv_w0"][l, d:d + 1, :])
        a0b = load_bc("a0", w["rwkv_a0"][l, d:d + 1, :])
        kkb = load_bc("kk", w["rwkv_k_k"][l:l + 1, :])
        kab = load_bc("ka", w["rwkv_k_a"][l:l + 1, :])
        if d == 0:
            mub = load_bc("mu", w["rwkv_mu"][l:l + 1, :], 1920)
        else:
            a0o = load_bc("a0o", w["rwkv_a0"][l, 0:1, :])
            rkb = load_bc("rk", w["rwkv_r_k"][l:l + 1].rearrange("o h n -> o (h n)"))
            lnw = load_bc("lnw", w["rwkv_ln_w"][l:l + 1, :])
            lnb = load_bc("lnb", w["rwkv_ln_b"][l:l + 1, :])
            g2s = SB("g2", [128, 512], BF16)
            P.dma("pool", out=g2s[:, :], in_=w["rwkv_g2"][l])
        w2s = SB("w2", [128, 512], BF16)
        a2s = SB("a2", [128, 512], BF16)
        P.dma("pool", out=w2s[:, :], in_=w["rwkv_w2"][l].rearrange("d r c -> (d r) c"))
        P.dma("pool", out=a2s[:, :], in_=w["rwkv_a2"][l].rearrange("d r c -> (d r) c"))
        m2 = SB("m2", [128, 2, 128], F32)
        mT = SB("mT", [128, 128], F32)
        bdm = SB("bdm", [128, 128], F32)
        onec = SB("onec", [128, 1], F32)
        P.dma("sp", out=m2[:, 0, :], in_=k.msk_d[d, 0])
        P.dma("sp", out=m2[:, 1, :], in_=k.msk_d[d, 1])
        P.dma("sp", out=mT[:, :], in_=k.msk_d[1 - d, 0])
        P.dma("sp", out=bdm[:, :], in_=k.bdm_d)
        P.G("memset", ap=onec[:, :], constant=1.0)
        H32 = SB("H32", [128, 4, 128], F32)
        Hb = SB("Hb", [128, 4, 128], BF16)
        P.G("memset", ap=H32[:, :, :], constant=0.0)
        P.G("memset", ap=Hb[:, :, :], constant=0.0)
        if d == 0:
            zin = [SB("z", [128, 1920], F32), SB("zp", [128, 1920], F32), SB("zn", [128, 1920], F32),
                   SB("zt", [128, 1920], F32)]
        sets = []
        for s_ in range(2):
            dd = {}
            lst = [("zm", [128, 1920], F32), ("lor", [128, 384], BF16), ("lorT", [128, 3, 128], BF16),
                   ("tok4", [128, 4, 512], BF16), ("vb", [128, 512], BF16), ("TTs", [128, 4, 4, 128], BF16),
                   ("gC", [128, 4], F32), ("sm", [128, 64], F32), ("Y32", [128, 512], F32), ("hT_", [128, 128], F32)]
            for t in range(8):
                lst.append(("T%d" % t, [128, 512], F32))
            for p in range(4):
                lst += [("XL%d" % p, [128, 2, 2, 128], BF16), ("LT%d" % p, [128, 2, 128], BF16),
                        ("ARB%d" % p, [128, 2, 128], BF16), ("AK%d" % p, [128, 2, 2, 128], BF16),
                        ("PT%d" % p, [128, 128], BF16), ("AKV%d" % p, [128, 128], BF16), ("Ub%d" % p, [128, 128], BF16)]
            if d == 1:
                lst += [("yfw", [128, 512], F32), ("ob", [128, 512], BF16), ("oT", [128, 4, 128], BF16)]
            for name, shape, dt in lst:
                dd[name] = SB("%s_%d" % (name, s_), shape, dt)
            sets.append(dd)

        order = list(range(NT)) if d == 0 else list(range(NT - 1, -1, -1))
        poolA = BankPool(k.banks[0:3])
        poolB = BankPool(k.banks[3:8])

        def tile_gen(it, i):
            pool = poolA
            D_ = sets[it % 2]
            t0 = i * 128
            zm = D_["zm"]
            T = [D_["T%d" % t] for t in range(8)]
            sm = D_["sm"]
            if d == 0:
                z, zp, zn, zt = zin
                P.dma("sp", out=z[:, :], in_=k.z_r[t0:t0 + 128, :])
                if i == 0:
                    P.G("memset", ap=zp[0:1, :], constant=0.0)
                    P.dma("sp", out=zp[1:128, :], in_=k.z_r[0:127, :])
                else:
                    P.dma("sp", out=zp[:, :], in_=k.z_r[t0 - 1:t0 + 127, :])
                if i == NT - 1:
                    P.G("memset", ap=zn[:, :], constant=0.0)
                    P.dma("sp", out=zn[0:127, :], in_=k.z_r[t0 + 1:t0 + 128, :])
                else:
                    P.dma("sp", out=zn[:, :], in_=k.z_r[t0 + 1:t0 + 129, :])
                P.G("tensor_tensor", out=zt[:, :], in0=zp[:, :], in1=zn[:, :], op=ALU.add)
                P.V("scalar_tensor_tensor", out=zt[:, :], in0=zt[:, :], scalar=0.5, in1=z[:, :], op0=ALU.mult,
                    op1=ALU.subtract)
                P.G("tensor_tensor", out=zt[:, :], in0=zt[:, :], in1=mub[:, :], op=ALU.mult)
                P.V("tensor_tensor", out=zm[:, :], in0=z[:, :], in1=zt[:, :], op=ALU.add)
                P.dma("pool", out=k.zmix[t0:t0 + 128, :], in_=zm[:, :])
            else:
                P.dma("sp", out=zm[:, :], in_=k.zmix[t0:t0 + 128, :])
                P.dma("sp", out=D_["yfw"][:, :], in_=k.y_fw[t0:t0 + 128, :])
            r_ = zm[:, 0:512]
            kx = zm[:, 512:1024]
            v_ = zm[:, 1024:1536]
            for _ in range(RW_DELAY):
                yield
            lor, lorT = D_["lor"], D_["lorT"]
            P.A("activation", out=lor[:, 0:128], in_=zm[:, 1536:1664], func=AF.Tanh)
            P.V("tensor_copy", out=lor[:, 128:256], in_=zm[:, 1664:1792])
            nl = 2
            if d == 1:
                P.A("activation", out=lor[:, 256:384], in_=zm[:, 1792:1920], func=AF.Sigmoid)
                nl = 3
            pb = pool.get()
            pv = bfv(pb)
            for c in range(nl):
                P.M("transpose", out=pv[:, c * 128:(c + 1) * 128], in_=lor[:, c * 128:(c + 1) * 128],
                    identity=k.ident[:, :], accum=(c > 0))
            P.A("activation", out=lorT[:, 0:nl, :].rearrange("p c t -> p (c t)"), in_=pv[:, 0:nl * 128], func=AF.Copy)
            ds = slice(d * 64, (d + 1) * 64)
            sg, asig = T[0], T[1]
            pb = pool.get()
            P.M("matmul", out=pb[:, 0:512], lhsT=lorT[ds, 0, :], rhs=w2s[ds, :], start=True, stop=True)
            P.V("tensor_tensor", out=sg[:, :], in0=pb[:, 0:512], in1=w0b[:, :], op=ALU.add)
            P.A("activation", out=sg[:, :], in_=sg[:, :], func=AF.Sigmoid)
            yield
            pb = pool.get()
            P.M("matmul", out=pb[:, 0:512], lhsT=lorT[ds, 1, :], rhs=a2s[ds, :], start=True, stop=True)
            P.V("tensor_tensor", out=asig[:, :], in0=pb[:, 0:512], in1=a0b[:, :], op=ALU.add)
            P.A("activation", out=asig[:, :], in_=asig[:, :], func=AF.Sigmoid)
            yield
            eP, eN, eX = T[2], T[3], T[4]
            pc = pool.get()
            P.M("matmul", out=pc[:, 0:512], lhsT=m2[:, 1, :], rhs=sg[:, :], start=True, stop=True)
            pcx = pool.get()
            P.M("matmul", out=pcx[:, 0:512], lhsT=m2[:, 0, :], rhs=sg[:, :], start=True, stop=True)
            P.A("activation", out=eP[:, :], in_=pc[:, 0:512], func=AF.Exp, scale=-C0)
            P.A("activation", out=eN[:, :], in_=pc[:, 0:512], func=AF.Exp, scale=C0)
            yield
            P.A("activation", out=eX[:, :], in_=pcx[:, 0:512], func=AF.Exp, scale=-C0)
            pg = pool.get()
            for p in range(4):
                P.M("matmul", out=pg[:, p:p + 1], lhsT=sg[:, p * 128:(p + 1) * 128], rhs=onec[:, 0:1], start=True,
                    stop=True, accum=(p > 0))
            P.A("activation", out=D_["gC"][:, :], in_=pg[:, 0:4], func=AF.Exp, scale=-C0)
            yield
            kk, kt, bb = T[5], T[6], T[7]
            tok4, vb = D_["tok4"], D_["vb"]
            P.G("tensor_tensor", out=kk[:, :], in0=kx, in1=kkb[:, :], op=ALU.mult)
            P.A("activation", out=kt[:, :], in_=kk[:, :], func=AF.Square)
            P.V("tensor_reduce", out=sm[:, 0:8], in_=kt[:, :].rearrange("p (h e) -> p h e", h=8), axis=AX.X, op=ALU.add)
            P.V("tensor_scalar", out=sm[:, 0:8], in0=sm[:, 0:8], scalar1=1e-24, scalar2=None, op0=ALU.max)
            yield
            P.A("activation", out=sm[:, 8:16], in_=sm[:, 0:8], func=AF.Ln)
            P.A("activation", out=sm[:, 16:24], in_=sm[:, 8:16], func=AF.Exp, scale=-0.5)
            kk3 = kk[:, :].rearrange("p (h e) -> p h e", h=8)
            P.V("tensor_tensor", out=kk3, in0=kk3, in1=bc(sm[:, 16:24], [128, 8, 64], 2), op=ALU.mult)
            yield
            P.V("scalar_tensor_tensor", out=kt[:, :], in0=asig[:, :], scalar=-1.0, in1=kab[:, :], op0=ALU.add, op1=ALU.mult)
            P.V("scalar_tensor_tensor", out=kt[:, :], in0=kt[:, :], scalar=1.0, in1=kx, op0=ALU.add, op1=ALU.mult)
            P.G("tensor_tensor", out=bb[:, :], in0=kk[:, :], in1=asig[:, :], op=ALU.mult)
            yield
            P.V("scalar_tensor_tensor", out=tok4[:, 0, :], in0=kk[:, :], scalar=-1.0, in1=eX[:, :], op0=ALU.mult, op1=ALU.mult)
            P.G("tensor_tensor", out=tok4[:, 1, :], in0=r_, in1=eP[:, :], op=ALU.mult)
            yield
            P.G("tensor_tensor", out=tok4[:, 2, :], in0=bb[:, :], in1=eN[:, :], op=ALU.mult)
            P.V("tensor_tensor", out=tok4[:, 3, :], in0=kt[:, :], in1=eN[:, :], op=ALU.mult)
            P.A("activation", out=vb[:, :], in_=v_, func=AF.Copy)
            pool = poolB
            yield "mid"
            TTs = D_["TTs"]
            for p0 in (0, 2):
                pb = pool.get()
                pv = bfv(pb)
                for p in (p0, p0 + 1):
                    for q in range(4):
                        o0 = (p - p0) * 512 + q * 128
                        P.M("transpose", out=pv[:, o0:o0 + 128], in_=tok4[:, q, p * 128:(p + 1) * 128],
                            identity=k.ident[:, :], accum=not (p == p0 and q == 0))
                P.A("activation", out=TTs[:, p0:p0 + 2, :, :].rearrange("p a q t -> p (a q t)"), in_=pv[:, 0:1024],
                    func=AF.Copy)
                yield
            yield
            XL = [D_["XL%d" % p] for p in range(4)]
            LT = [D_["LT%d" % p] for p in range(4)]
            ARB = [D_["ARB%d" % p] for p in range(4)]
            AK = [D_["AK%d" % p] for p in range(4)]
            PT = [D_["PT%d" % p] for p in range(4)]
            AKV = [D_["AKV%d" % p] for p in range(4)]
            Ub = [D_["Ub%d" % p] for p in range(4)]
            m2b = bc(m2[:, :, :].rearrange("p a t -> p (a t)"), [128, 2, 256], 1)
            for p in range(4):
                bB, bK, bL = pool.get(), pool.get(), pool.get()
                for e in range(2):
                    bs = slice(e * 64, (e + 1) * 64)
                    ar = TTs[bs, p, 0:2, :].rearrange("p q t -> p (q t)")
                    P.M("matmul", out=bB[:, e * 256:(e + 1) * 256], lhsT=TTs[bs, p, 2, :], rhs=ar, start=True, stop=True,
                        accum=(e > 0), drain=True)
                    P.M("matmul", out=bK[:, e * 256:(e + 1) * 256], lhsT=TTs[bs, p, 3, :], rhs=ar, start=True, stop=True,
                        accum=(e > 0))
                    P.M("matmul", out=bL[:, e * 128:(e + 1) * 128], lhsT=TTs[bs, p, 0, :], rhs=TTs[bs, p, 2, :], start=True,
                        stop=True, accum=(e > 0))
                bB4 = bB[:, 0:512].rearrange("p (e a t) -> p e a t", e=2, a=2)
                P.V("tensor_tensor", out=XL[p][:, :, 1, :], in0=bB4[:, :, 0, :], in1=bc(m2[:, 0, :], [128, 2, 128], 1),
                    op=ALU.mult)
                P.V("tensor_tensor", out=ARB[p][:, :, :], in0=bB4[:, :, 1, :], in1=bc(m2[:, 1, :], [128, 2, 128], 1),
                    op=ALU.mult)
                P.V("tensor_tensor", out=AK[p][:, :, :, :].rearrange("p e a t -> p e (a t)"),
                    in0=bK[:, 0:512].rearrange("p (e n) -> p e n", e=2), in1=m2b, op=ALU.mult)
                P.V("tensor_tensor", out=LT[p][:, :, :], in0=bL[:, 0:256].rearrange("p (e t) -> p e t", e=2),
                    in1=bc(mT[:, :], [128, 2, 128], 1), op=ALU.mult)
                P.G("tensor_copy", out=XL[p][:, :, 0, :], in_=bc(k.ident[:, :], [128, 2, 128], 1))
                yield
            yield
            for lev in range(7):
                last = (lev == 6)
                for p in range(4):
                    bb_ = pool.get()
                    for e in range(2):
                        if not last:
                            P.M("matmul", out=bb_[:, e * 256:(e + 1) * 256], lhsT=LT[p][:, e, :],
                                rhs=XL[p][:, e, :, :].rearrange("p a t -> p (a t)"), start=True, stop=True, accum=(e > 0))
                        else:
                            P.M("matmul", out=bb_[:, e * 256:e * 256 + 128], lhsT=LT[p][:, e, :],
                                rhs=XL[p][:, e, 0, :], start=True, stop=True, accum=(e > 0))
                    if not last:
                        ba_ = pool.get()
                        for e in range(2):
                            P.M("matmul", out=ba_[:, e * 128:(e + 1) * 128], lhsT=XL[p][:, e, 1, :], rhs=LT[p][:, e, :],
                                start=True, stop=True, accum=(e > 0))
                    b4 = bb_[:, 0:512].rearrange("p (e a t) -> p e a t", e=2, a=2)
                    P.V("tensor_tensor", out=XL[p][:, :, 0, :], in0=XL[p][:, :, 0, :], in1=b4[:, :, 0, :], op=ALU.add)
                    if not last:
                        P.A("activation", out=XL[p][:, :, 1, :], in_=b4[:, :, 1, :], func=AF.Copy)
                        P.A("activation", out=LT[p][:, :, :], in_=ba_[:, 0:256].rearrange("p (e t) -> p e t", e=2),
                            func=AF.Copy)
                    yield
            for p in range(4):
                pb = pool.get()
                P.M("matmul", out=pb[:, 0:256].rearrange("p (e t) -> p e t", e=2), lhsT=tok4[:, 0, p * 128:(p + 1) * 128],
                    rhs=XL[p][:, :, 0, :], start=True, stop=True)
                P.A("activation", out=PT[p][0:64, :], in_=pb[0:64, 0:128], func=AF.Copy)
                P.V("tensor_copy", out=PT[p][64:128, :], in_=pb[64:128, 128:256])
                pb2 = pool.get()
                for e in range(2):
                    h = 2 * p + e
                    P.M("matmul", out=pb2[:, e * 64:(e + 1) * 64], lhsT=AK[p][:, e, 0, :], rhs=vb[:, h * 64:(h + 1) * 64],
                        start=True, stop=True, accum=(e > 0))
                P.A("activation", out=AKV[p][:, :], in_=pb2[:, 0:128], func=AF.Copy)
                yield
            Y32 = D_["Y32"]
            for p in range(4):
                ps = slice(p * 128, (p + 1) * 128)
                bU = pool.get()
                for e in range(2):
                    P.M("matmul", out=bU[:, e * 64:(e + 1) * 64], lhsT=XL[p][:, e, 0, :], rhs=AKV[p][:, e * 64:(e + 1) * 64],
                        start=(e == 0), stop=False, accum=(e > 0))
                P.M("matmul", out=bU[:, 0:128], lhsT=PT[p][:, :], rhs=Hb[:, p, :], start=False, stop=True, accum=True)
                P.A("activation", out=Ub[p][:, :], in_=bU[:, 0:128], func=AF.Copy)
                bY = pool.get()
                for e in range(2):
                    h = 2 * p + e
                    P.M("matmul", out=bY[:, e * 64:(e + 1) * 64], lhsT=AK[p][:, e, 1, :], rhs=vb[:, h * 64:(h + 1) * 64],
                        start=(e == 0), stop=False, accum=(e > 0))
                P.M("matmul", out=bY[:, 0:128], lhsT=TTs[:, p, 1, :], rhs=Hb[:, p, :], start=False, stop=False, accum=True)
                for e in range(2):
                    P.M("matmul", out=bY[:, e * 64:(e + 1) * 64], lhsT=ARB[p][:, e, :], rhs=Ub[p][:, e * 64:(e + 1) * 64],
                        start=False, stop=(e == 1), accum=True)
                P.V("tensor_copy", out=Y32[:, ps], in_=bY[:, 0:128])
                bH = pool.get()
                P.M("matmul", out=bH[:, 0:128], lhsT=tok4[:, 2, ps], rhs=Ub[p][:, :], start=True, stop=False)
                P.M("matmul", out=bH[:, 0:128], lhsT=tok4[:, 3, ps], rhs=vb[:, ps], start=False, stop=True, accum=True)
                hT_ = D_["hT_"]
                P.V("tensor_tensor", out=hT_[:, :], in0=bH[:, 0:128], in1=H32[:, p, :], op=ALU.add)
                P.V("scalar_tensor_tensor", out=H32[:, p, :], in0=hT_[:, :], scalar=D_["gC"][:, p:p + 1], in1=bdm[:, :],
                    op0=ALU.mult, op1=ALU.mult)
                P.A("activation", out=Hb[:, p, :], in_=H32[:, p, :], func=AF.Copy)
                yield
            if d == 0:
                P.dma("pool", out=k.y_fw[t0:t0 + 128, :], in_=Y32[:, :])
            else:
                wkv, sq, bon = T[3], T[4], T[2]
                pb = pool.get()
                P.M("matmul", out=pb[:, 0:512], lhsT=lorT[0:64, 1, :], rhs=a2s[0:64, :], start=True, stop=True)
                P.V("tensor_tensor", out=T[0][:, :], in0=pb[:, 0:512], in1=a0o[:, :], op=ALU.add)
                P.A("activation", out=T[0][:, :], in_=T[0][:, :], func=AF.Sigmoid)
                P.V("scalar_tensor_tensor", out=T[0][:, :], in0=T[0][:, :], scalar=-1.0, in1=kab[:, :], op0=ALU.add,
                    op1=ALU.mult)
                P.V("scalar_tensor_tensor", out=T[0][:, :], in0=T[0][:, :], scalar=1.0, in1=kx, op0=ALU.add, op1=ALU.mult)
                P.G("tensor_tensor", out=T[0][:, :], in0=T[0][:, :], in1=kt[:, :], op=ALU.add)
                P.G("tensor_tensor", out=T[0][:, :], in0=T[0][:, :], in1=r_, op=ALU.mult)
                P.G("tensor_tensor", out=T[0][:, :], in0=T[0][:, :], in1=rkb[:, :], op=ALU.mult)
                P.V("tensor_reduce", out=sm[:, 24:32], in_=T[0][:, :].rearrange("p (h e) -> p h e", h=8), axis=AX.X,
                    op=ALU.add)
                P.V("tensor_tensor", out=bon[:, :].rearrange("p (h e) -> p h e", h=8),
                    in0=v_.rearrange("p (h e) -> p h e", h=8), in1=bc(sm[:, 24:32], [128, 8, 64], 2), op=ALU.mult)
                P.G("tensor_tensor", out=wkv[:, :], in0=Y32[:, :], in1=D_["yfw"][:, :], op=ALU.add)
                w3 = wkv[:, :].rearrange("p (h e) -> p h e", h=8)
                P.V("tensor_reduce", out=sm[:, 32:40], in_=w3, axis=AX.X, op=ALU.add)
                P.V("tensor_scalar", out=sm[:, 32:40], in0=sm[:, 32:40], scalar1=-1.0 / 64, scalar2=None, op0=ALU.mult)
                P.V("tensor_tensor", out=w3, in0=w3, in1=bc(sm[:, 32:40], [128, 8, 64], 2), op=ALU.add)
                P.A("activation", out=sq[:, :], in_=wkv[:, :], func=AF.Square)
                P.V("tensor_reduce", out=sm[:, 40:48], in_=sq[:, :].rearrange("p (h e) -> p h e", h=8), axis=AX.X, op=ALU.add)
                rstd_from_ss(k, sm[:, 48:56], sm[:, 40:48], sm[:, 56:64], 64.0, GN_EPS)
                P.V("tensor_tensor", out=w3, in0=w3, in1=bc(sm[:, 48:56], [128, 8, 64], 2), op=ALU.mult)
                P.G("tensor_tensor", out=wkv[:, :], in0=wkv[:, :], in1=lnw[:, :], op=ALU.mult)
                P.G("tensor_tensor", out=wkv[:, :], in0=wkv[:, :], in1=lnb[:, :], op=ALU.add)
                P.G("tensor_tensor", out=wkv[:, :], in0=wkv[:, :], in1=bon[:, :], op=ALU.add)
                pb = pool.get()
                P.M("matmul", out=pb[:, 0:512], lhsT=lorT[:, 2, :], rhs=g2s[:, :], start=True, stop=True)
                P.V("tensor_tensor", out=D_["ob"][:, :], in0=wkv[:, :], in1=pb[:, 0:512], op=ALU.mult)
                pb = pool.get()
                pv = bfv(pb)
                for c in range(4):
                    P.M("transpose", out=pv[:, c * 128:(c + 1) * 128], in_=D_["ob"][:, c * 128:(c + 1) * 128],
                        identity=k.ident[:, :], accum=(c > 0))
                P.A("activation", out=D_["oT"][:, :, :].rearrange("p c t -> p (c t)"), in_=pv[:, 0:512], func=AF.Copy)
                P.dma("pool", out=k.oT_rw[i], in_=D_["oT"][:, :, :])


        run_skewed([tile_gen(it, i) for it, i in enumerate(order)], ratio=RW_RATIO)


def phase_p3a(k, l, xin):
    nc, P, S, NT = k.nc, k.P, k.S, k.NT
    w = k.w
    with contextlib.ExitStack() as st:
        def SB(name, shape, dt):
            return st.enter_context(nc.sbuf_tensor(un("p3_" + name), list(shape), dt))
        alloc_banks(k, st, 6)
        pp2 = st.enter_context(nc.psum_tensor(un("p3_pp"), [128, 1024], F32))
        wg = SB("wg", [128, 8, 3072], BF16)
        wbr = SB("wbr", [128, 3, 4, 1024], BF16)
        wo = SB("wo", [128, 8, 1024], BF16)
        wq = SB("wq", [128, 8, 512], BF16)
        cwo = SB("cwo", [128, 4, 1024], BF16)
        bg = SB("bg", [1, 3072], BF16)
        ones = SB("ones", [1, 128], BF16)
        gcol = SB("gcol", [128, 16], F32)
        kmT = SB("kmT", [128, 4, 256], BF16)
        vm = SB("vm", [128, 2, 512], BF16)
        win = w["w_in"][l].rearrange("(c p) n -> p c n", p=128)
        for c in range(8):
            P.dma("pool", out=wg[:, c, :], in_=win[:, c, 3360:6432])
        for g in range(3):
            P.dma("pool", out=wbr[:, g, :, :], in_=w["w_branch"][l, g].rearrange("(c p) n -> p c n", p=128))
        P.dma("pool", out=wo[:, :, :], in_=w["w_out"][l].rearrange("(c p) n -> p c n", p=128))
        P.dma("pool", out=cwo[:, :, :], in_=w["cross_wo"][l].rearrange("(c p) n -> p c n", p=128))
        P.dma("pool", out=bg[0:1, :], in_=w["b_gate"][l:l + 1].rearrange("o g n -> o (g n)"))
        P.G("memset", ap=ones[0:1, :], constant=1.0)
        for c in range(8):
            load_col(k, "sp", gcol[:, c:c + 1], w["norm_cross"][l, c * 128:(c + 1) * 128], 128)
            load_col(k, "sp", gcol[:, 8 + c:9 + c], w["norm_mem"][l, c * 128:(c + 1) * 128], 128)
        with contextlib.ExitStack() as st2:
            stg = st2.enter_context(nc.sbuf_tensor(un("p3_stg"), [128, 8, 1024], F32))
            wkv = st2.enter_context(nc.sbuf_tensor(un("p3_wkv"), [128, 8, 1024], BF16))
            mx = st2.enter_context(nc.sbuf_tensor(un("p3_mx"), [128, 2, 1024], F32))
            mhb = st2.enter_context(nc.sbuf_tensor(un("p3_mhb"), [128, 2, 1024], BF16))
            mhT = st2.enter_context(nc.sbuf_tensor(un("p3_mhT"), [128, 8, 256], BF16))
            mjunk = st2.enter_context(nc.sbuf_tensor(un("p3_mjunk"), [128, 1024], BF16))
            mst = st2.enter_context(nc.sbuf_tensor(un("p3_mst"), [128, 8], F32))
            P.dma("sp", out=stg[:, :, 0:512], in_=w["cross_wq"][l].rearrange("(c p) n -> p c n", p=128))
            for c in range(8):
                P.V("tensor_scalar", out=wq[:, c, :], in0=stg[:, c, 0:512], scalar1=gcol[:, c:c + 1], scalar2=None,
                    op0=ALU.mult)
            P.dma("sp", out=stg[:, :, :], in_=w["cross_wkv"][l].rearrange("(c p) n -> p c n", p=128))
            for c in range(8):
                P.V("tensor_scalar", out=wkv[:, c, :], in0=stg[:, c, :], scalar1=gcol[:, 8 + c:9 + c], scalar2=None,
                    op0=ALU.mult)
            P.dma("sp", out=mx[:, :, :], in_=k.mem.rearrange("(j p) n -> p j n", p=128))
            for j in range(2):
                P.A("activation", out=mjunk[:, :], in_=mx[:, j, :], func=AF.Square, accum_out=mst[:, j:j + 1])
            rstd_from_ss(k, mst[:, 2:4], mst[:, 0:2], mst[:, 4:6], 1024.0, EPS)
            for j in range(2):
                P.V("tensor_scalar", out=mhb[:, j, :], in0=mx[:, j, :], scalar1=mst[:, 2 + j:3 + j], scalar2=None,
                    op0=ALU.mult)
                pb = bank(k)
                pv = bfv(pb)
                for c in range(8):
                    P.M("transpose", out=pv[:, c * 128:(c + 1) * 128], in_=mhb[:, j, c * 128:(c + 1) * 128],
                        identity=k.ident[:, :], accum=(c > 0))
                P.A("activation", out=mhT[:, :, j * 128:(j + 1) * 128],
                    in_=pv[:, 0:1024].rearrange("p (c t) -> p c t", c=8), func=AF.Copy)
            for h in range(4):
                pb = bank(k)
                for c in range(8):
                    P.M("matmul", out=pb[:, 0:256], lhsT=wkv[:, c, h * 128:(h + 1) * 128], rhs=mhT[:, c, :],
                        start=(c == 0), stop=(c == 7), accum=(c > 0))
                P.A("activation", out=kmT[:, h, :], in_=pb[:, 0:256], func=AF.Copy)
            for j in range(2):
                pb = bank(k)
                for c in range(8):
                    P.M("matmul", out=pb[:, 0:512], lhsT=mhT[:, c, j * 128:(j + 1) * 128], rhs=wkv[:, c, 512:1024],
                        start=(c == 0), stop=(c == 7), accum=(c > 0))
                P.A("activation", out=vm[:, j, :], in_=pb[:, 0:512], func=AF.Copy)
        P.barrier()

        sets = []
        for s_ in range(2):
            d = {}
            for name, shape, dt in [
                ("x", [128, 1024], F32), ("hT", [128, 8, 128], BF16), ("ob", [128, 2, 512], BF16),
                ("oT", [128, 3, 4, 128], BF16), ("gt", [128, 512], F32), ("tmp", [128, 512], F32),
                ("m", [128, 1024], F32), ("mb", [128, 1024], BF16), ("mT", [128, 8, 128], BF16),
                ("x1", [128, 1024], F32), ("junk", [128, 1024], BF16), ("st", [128, 16], F32),
                ("h2", [128, 1024], BF16), ("h2T", [128, 8, 128], BF16), ("qT", [128, 4, 128], BF16),
                ("p", [128, 4, 256], BF16), ("pn", [128, 4, 256], BF16), ("pT", [128, 8, 128], BF16),
                ("ocT", [128, 4, 128], BF16), ("x2", [128, 1024], F32),
            ]:
                d[name] = SB("%s%d" % (name, s_), shape, dt)
            sets.append(d)
        have_rw = "rb" in k.phases
        for i in range(NT):
            d = sets[i % 2]
            t0 = i * 128
            x, hT, oT, stt = d["x"], d["hT"], d["oT"], d["st"]
            P.dma("sp", out=x[:, :], in_=xin[t0:t0 + 128, :])
            P.dma("sp", out=hT[:, :, :], in_=k.hT_t[i])
            P.dma("sp", out=d["ob"][:, 0, :], in_=k.o_mla[t0:t0 + 128, :])
            P.dma("sp", out=d["ob"][:, 1, :], in_=k.o_gqa[t0:t0 + 128, :])
            if have_rw:
                P.dma("sp", out=oT[:, 2, :, :], in_=k.oT_rw[i])
            else:
                P.G("memset", ap=oT[:, 2, :, :], constant=0.0)
            pb = bank(k)
            pv = bfv(pb)
            obf = d["ob"][:, :, :].rearrange("p g n -> p (g n)")
            for c in range(8):
                P.M("transpose", out=pv[:, c * 128:(c + 1) * 128], in_=obf[:, c * 128:(c + 1) * 128],
                    identity=k.ident[:, :], accum=(c > 0))
            P.A("activation", out=oT[:, 0:2, :, :].rearrange("p g c t -> p (g c t)"), in_=pv[:, 0:1024], func=AF.Copy)
            for g in range(3):
                for n in range(2):
                    cs = slice(n * 512, (n + 1) * 512)
                    pz = bank(k)
                    for c in range(8):
                        P.M("matmul", out=pz[:, 0:512], lhsT=hT[:, c, :], rhs=wg[:, c, g * 1024 + n * 512:g * 1024 + (n + 1) * 512],
                            start=(c == 0), stop=False, accum=(c > 0))
                    P.M("matmul", out=pz[:, 0:512], lhsT=ones[0:1, :], rhs=bg[0:1, g * 1024 + n * 512:g * 1024 + (n + 1) * 512],
                        start=False, stop=True, accum=True)
                    P.A("activation", out=d["gt"][:, :], in_=pz[:, 0:512], func=AF.Sigmoid)
                    pbr = bank(k)
                    for c in range(4):
                        P.M("matmul", out=pbr[:, 0:512], lhsT=oT[:, g, c, :], rhs=wbr[:, g, c, cs],
                            start=(c == 0), stop=(c == 3), accum=(c > 0))
                    if g == 0:
                        P.V("tensor_tensor", out=d["m"][:, cs], in0=d["gt"][:, :], in1=pbr[:, 0:512], op=ALU.mult)
                    else:
                        P.V("tensor_tensor", out=d["tmp"][:, :], in0=d["gt"][:, :], in1=pbr[:, 0:512], op=ALU.mult)
                        if g == 1:
                            P.G("tensor_tensor", out=d["m"][:, cs], in0=d["m"][:, cs], in1=d["tmp"][:, :], op=ALU.add)
                        else:
                            P.G("tensor_tensor", out=d["mb"][:, cs], in0=d["m"][:, cs], in1=d["tmp"][:, :], op=ALU.add)
            pb = bank(k)
            pv = bfv(pb)
            for c in range(8):
                P.M("transpose", out=pv[:, c * 128:(c + 1) * 128], in_=d["mb"][:, c * 128:(c + 1) * 128],
                    identity=k.ident[:, :], accum=(c > 0))
            P.A("activation", out=d["mT"][:, :, :].rearrange("p c t -> p (c t)"), in_=pv[:, 0:1024], func=AF.Copy)
            for n in range(2):
                cs = slice(n * 512, (n + 1) * 512)
                pb = bank(k)
                for c in range(8):
                    P.M("matmul", out=pb[:, 0:512], lhsT=d["mT"][:, c, :], rhs=wo[:, c, cs],
                        start=(c == 0), stop=(c == 7), accum=(c > 0))
                P.V("tensor_tensor", out=d["x1"][:, cs], in0=x[:, cs], in1=pb[:, 0:512], op=ALU.add)
            x1 = d["x1"]
            P.A("activation", out=d["junk"][:, :], in_=x1[:, :], func=AF.Square, accum_out=stt[:, 0:1])
            rstd_from_ss(k, stt[:, 1:2], stt[:, 0:1], stt[:, 2:3], 1024.0, EPS)
            P.V("tensor_scalar", out=d["h2"][:, :], in0=x1[:, :], scalar1=stt[:, 1:2], scalar2=None, op0=ALU.mult)
            pb = bank(k)
            pv = bfv(pb)
            for c in range(8):
                P.M("transpose", out=pv[:, c * 128:(c + 1) * 128], in_=d["h2"][:, c * 128:(c + 1) * 128],
                    identity=k.ident[:, :], accum=(c > 0))
            P.A("activation", out=d["h2T"][:, :, :].rearrange("p c t -> p (c t)"), in_=pv[:, 0:1024], func=AF.Copy)
            pb = bank(k)
            for h in range(4):
                for c in range(8):
                    P.M("matmul", out=pb[:, h * 128:(h + 1) * 128], lhsT=wq[:, c, h * 128:(h + 1) * 128], rhs=d["h2T"][:, c, :],
                        start=(c == 0), stop=(c == 7), accum=(c > 0 or h > 0))
            P.A("activation", out=d["qT"][:, :, :].rearrange("p h t -> p (h t)"), in_=pb[:, 0:512], func=AF.Copy)
            for h in range(4):
                P.M("matmul", out=pp2[:, h * 256:(h + 1) * 256], lhsT=d["qT"][:, h, :], rhs=kmT[:, h, :],
                    start=True, stop=True, accum=(h > 0))
            for h in range(4):
                P.A("activation", out=d["p"][:, h, :], in_=pp2[:, h * 256:(h + 1) * 256], func=AF.Exp,
                    scale=128.0 ** -0.5, accum_out=stt[:, 4 + h:5 + h])
            P.V("reciprocal", out=stt[:, 8:12], in_=stt[:, 4:8])
            P.V("tensor_tensor", out=d["pn"][:, :, :], in0=d["p"][:, :, :], in1=bc(stt[:, 8:12], [128, 4, 256], 2),
                op=ALU.mult)
            pb = bank(k)
            pv = bfv(pb)
            pnf = d["pn"][:, :, :].rearrange("p h m -> p (h m)")
            for c in range(8):
                P.M("transpose", out=pv[:, c * 128:(c + 1) * 128], in_=pnf[:, c * 128:(c + 1) * 128],
                    identity=k.ident[:, :], accum=(c > 0))
            P.A("activation", out=d["pT"][:, :, :].rearrange("p c t -> p (c t)"), in_=pv[:, 0:1024], func=AF.Copy)
            pb = bank(k)
            for h in range(4):
                for mc in range(2):
                    P.M("matmul", out=pb[:, h * 128:(h + 1) * 128], lhsT=vm[:, mc, h * 128:(h + 1) * 128],
                        rhs=d["pT"][:, h * 2 + mc, :], start=(mc == 0), stop=(mc == 1), accum=(mc > 0 or h > 0))
            P.A("activation", out=d["ocT"][:, :, :].rearrange("p h t -> p (h t)"), in_=pb[:, 0:512], func=AF.Copy)
            for n in range(2):
                cs = slice(n * 512, (n + 1) * 512)
                pb = bank(k)
                for h in range(4):
                    P.M("matmul", out=pb[:, 0:512], lhsT=d["ocT"][:, h, :], rhs=cwo[:, h, cs],
                        start=(h == 0), stop=(h == 3), accum=(h > 0))
                P.V("tensor_tensor", out=d["x2"][:, cs], in0=x1[:, cs], in1=pb[:, 0:512], op=ALU.add)
            P.dma("pool", out=k.x2[t0:t0 + 128, :], in_=d["x2"][:, :])


def phase_p3b(k, l, xout, last):
    nc, P, S = k.nc, k.P, k.S
    w = k.w
    TT = 256
    NTT = S // TT
    with contextlib.ExitStack() as st:
        def SB(name, shape, dt):
            return st.enter_context(nc.sbuf_tensor(un("p4_" + name), list(shape), dt))
        alloc_banks(k, st)
        w1 = SB("w1", [128, 8, 4096], BF16)
        w2 = SB("w2", [128, 32, 1024], BF16)
        gcol = SB("gcol", [128, 8], F32)
        for c in range(8):
            load_col(k, "sp", gcol[:, c:c + 1], w["norm_mlp"][l, c * 128:(c + 1) * 128], 128)
        with contextlib.ExitStack() as st2:
            stgs = [st2.enter_context(nc.sbuf_tensor(un("p4_stg%d" % i), [128, 4096], F32)) for i in range(2)]
            w1v = w["mlp_w1"][l].rearrange("(c p) n -> p c n", p=128)
            for c in range(8):
                sg = stgs[c % 2]
                P.dma("sp", out=sg[:, :], in_=w1v[:, c, :])
                for hf in range(2):
                    P.V("tensor_scalar", out=w1[:, c, hf * 2048:(hf + 1) * 2048], in0=sg[:, hf * 2048:(hf + 1) * 2048],
                        scalar1=gcol[:, c:c + 1], scalar2=None, op0=ALU.mult)
        P.barrier()
        w2v = w["mlp_w2"][l].rearrange("(f p) n -> p f n", p=128)
        for f0 in range(0, 32, 4):
            P.dma("pool", out=w2[:, f0:f0 + 4, :], in_=w2v[:, f0:f0 + 4, :])
        if last:
            gf = SB("gf", [128, 1024], F32)
            P.dma("sp", out=gf[:, :], in_=w["norm_final"][0:1, :].broadcast_to([128, 1024]))
        uT = SB("uT", [128, 32, TT], BF16)
        rbuf = [SB("r%d" % i, [128, TT], BF16) for i in range(3)]
        junk = SB("junk", [128, 1024], BF16)
        hm = [SB("hm%d" % i, [128, 1024], BF16) for i in range(2)]
        sets = []
        for s_ in range(2):
            d = {}
            for name, shape, dt in [("x", [128, 2, 1024], F32), ("hmT", [128, 8, TT], BF16), ("st", [128, 16], F32),
                                    ("y", [128, 2, 1024], F32)]:
                if name == "y" and not last:
                    continue
                d[name] = SB("%s%d" % (name, s_), shape, dt)
            sets.append(d)
        nr = 0
        for i in range(NTT):
            d = sets[i % 2]
            t0 = i * TT
            x, hmT, stt = d["x"], d["hmT"], d["st"]
            P.dma("sp", out=x[:, :, :], in_=k.x2[t0:t0 + TT, :].rearrange("(j p) n -> p j n", p=128))
            for j in range(2):
                P.A("activation", out=junk[:, :], in_=x[:, j, :], func=AF.Square, accum_out=stt[:, j:j + 1])
            rstd_from_ss(k, stt[:, 2:4], stt[:, 0:2], stt[:, 4:6], 1024.0, EPS)
            for j in range(2):
                P.V("tensor_scalar", out=hm[j][:, :], in0=x[:, j, :], scalar1=stt[:, 2 + j:3 + j], scalar2=None,
                    op0=ALU.mult)
                pb = bank(k)
                pv = bfv(pb)
                for c in range(8):
                    P.M("transpose", out=pv[:, c * 128:(c + 1) * 128], in_=hm[j][:, c * 128:(c + 1) * 128],
                        identity=k.ident[:, :], accum=(c > 0))
                P.A("activation", out=hmT[:, :, j * 128:(j + 1) * 128],
                    in_=pv[:, 0:1024].rearrange("p (c t) -> p c t", c=8), func=AF.Copy)
            for f in range(32):
                pb = bank(k)
                for c in range(8):
                    P.M("matmul", out=pb[:, 0:TT], lhsT=w1[:, c, f * 128:(f + 1) * 128], rhs=hmT[:, c, :],
                        start=(c == 0), stop=(c == 7), accum=(c > 0))
                r = rbuf[nr % 3]
                nr += 1
                P.A("activation", out=r[:, :], in_=pb[:, 0:TT], func=AF.Relu)
                P.G("tensor_tensor", out=uT[:, f, :], in0=r[:, :], in1=r[:, :], op=ALU.mult)
            for j in range(2):
                for n in range(2):
                    cs = slice(n * 512, (n + 1) * 512)
                    pb = bank(k)
                    for f in range(32):
                        P.M("matmul", out=pb[:, 0:512], lhsT=uT[:, f, j * 128:(j + 1) * 128], rhs=w2[:, f, cs],
                            start=(f == 0), stop=(f == 31), accum=(f > 0))
                    P.V("tensor_tensor", out=x[:, j, cs], in0=x[:, j, cs], in1=pb[:, 0:512], op=ALU.add)
            if not last:
                P.dma("pool", out=xout[t0:t0 + TT, :].rearrange("(j p) n -> p j n", p=128), in_=x[:, :, :])
            else:
                for j in range(2):
                    P.A("activation", out=junk[:, :], in_=x[:, j, :], func=AF.Square, accum_out=stt[:, 8 + j:9 + j])
                rstd_from_ss(k, stt[:, 10:12], stt[:, 8:10], stt[:, 12:14], 1024.0, EPS)
                for j in range(2):
                    P.V("scalar_tensor_tensor", out=d["y"][:, j, :], in0=x[:, j, :], scalar=stt[:, 10 + j:11 + j],
                        in1=gf[:, :], op0=ALU.mult, op1=ALU.mult)
                P.dma("pool", out=k.y[t0:t0 + TT, :].rearrange("(j p) n -> p j n", p=128), in_=d["y"][:, :, :])
                if "xs0" in k.dbg or "xs1" in k.dbg:
                    P.dma("pool", out=xout[t0:t0 + TT, :].rearrange("(j p) n -> p j n", p=128), in_=x[:, :, :])


def rope_tables(pos, dim):
    inv = (10000.0 ** (-np.arange(0, dim, 2, dtype=np.float32) / dim)).astype(np.float32)
    ang = pos.astype(np.float32)[:, None] * inv[None, :]
    return np.cos(ang).astype(np.float32), np.sin(ang).astype(np.float32)


def const_inputs(S):
    pos = np.arange(S)
    c1, s1 = rope_tables(pos, 32)
    cr, sr = rope_tables(pos // 64, 32)
    cc, sc = rope_tables(pos % 64, 32)
    tab1 = np.stack([c1, s1], 1).astype(np.float32)
    tab2 = np.stack([np.stack([cr, cc], 1), np.stack([sr, sc], 1)], 1).astype(np.float32)
    s_idx = np.arange(128)[:, None]
    t_idx = np.arange(128)[None, :]
    msk = np.zeros((2, 2, 128, 128), np.float32)
    msk[0, 0] = s_idx < t_idx
    msk[0, 1] = s_idx <= t_idx
    msk[1, 0] = s_idx > t_idx
    msk[1, 1] = s_idx >= t_idx
    bdm = np.zeros((128, 128), np.float32)
    bdm[:64, :64] = 1
    bdm[64:, 64:] = 1
    return dict(tab1=tab1, tab2=tab2, ident=np.eye(128, dtype=np.float32), msk=msk, bdm=bdm)


_NC_CACHE = {}


def kernel(**inputs):
    S, depth = 8192, 2
    key = (S, depth)
    if key not in _NC_CACHE:
        _NC_CACHE[key] = build(S, depth)
    nc = _NC_CACHE[key]
    xp, xs = np.asarray(inputs["x_prompt"]), np.asarray(inputs["x_sample"])
    mp, ms = np.asarray(inputs["mem_prompt"]), np.asarray(inputs["mem_sample"])
    seqs = [(xp[b], mp[b]) for b in range(xp.shape[0])] + [(xs[b], ms[b]) for b in range(xs.shape[0])]
    n_seq = len(seqs)
    consts = const_inputs(S)
    wts = {name: np.ascontiguousarray(np.asarray(inputs[name], np.float32)) for name, _ in W_SPECS}
    wts["norm_final"] = np.ascontiguousarray(np.asarray(inputs["norm_final"], np.float32).reshape(1, D))
    in_maps = []
    for c in range(8):
        x, mem = seqs[c % n_seq]
        m = dict(x=np.ascontiguousarray(x, np.float32), mem=np.ascontiguousarray(mem, np.float32))
        m.update(wts)
        m.update(consts)
        in_maps.append(m)
    res = run_bass_kernel_spmd(nc, in_maps, core_ids=list(range(8)))
    ys = [np.asarray(res.results[c]["y"], np.float32) for c in range(n_seq)]
    y_prompt = np.stack(ys[:xp.shape[0]], 0)
    y_sample = np.stack(ys[xp.shape[0]:], 0)
    return (y_prompt, y_sample)
```

```python
import contextlib
import math
import numpy as np
import ml_dtypes
import concourse.bass as bass
import concourse.mybir as mybir
from concourse.bass_utils import run_bass_kernel_spmd

F32 = mybir.dt.float32
BF16 = mybir.dt.bfloat16
ALU = mybir.AluOpType
AF = mybir.ActivationFunctionType
AX = mybir.AxisListType

D = 1024
NMEM = 256
IN_COLS = 6432
EPS = 1e-6
GN_EPS = 64e-5
DECAY_C = math.exp(-0.5)

import os
MAXOPS = int(os.environ.get("MAXOPS", "100000000"))
RW_RATIO = int(os.environ.get("RW_RATIO", "2"))
RW_DELAY = int(os.environ.get("RW_DELAY", "2"))
ENGS = ("pe", "act", "dve", "pool", "sp")
N_DMA_SEMS = 16
READ_KEYS = ("in_", "in0", "in1", "lhsT", "rhs", "scalar", "scalar1", "scalar2", "bias", "scale",
             "identity", "data0", "data1", "initial")
WRITE_KEYS = ("out", "accum_out", "ap")


class Buf:
    __slots__ = ("name", "w", "r")

    def __init__(self, name):
        self.name = name
        self.w = None
        self.r = {}


class Prog:
    def __init__(self, nc):
        self.nc = nc
        self.es = contextlib.ExitStack()
        self.eobj = {"pe": nc.tensor, "act": nc.scalar, "dve": nc.vector, "pool": nc.gpsimd, "sp": nc.sync}
        self.sem = {}
        self.cnt = {}
        for e in ENGS:
            self.sem[e] = self.es.enter_context(nc.semaphore("s_" + e))
            self.cnt[e] = 0
        self.dq = {}
        for q in ("sp", "act", "pool"):
            sems = []
            for i in range(N_DMA_SEMS):
                k = "d_%s_%d" % (q, i)
                self.sem[k] = self.es.enter_context(nc.semaphore(k))
                self.cnt[k] = 0
                sems.append(k)
            self.dq[q] = [sems, 0]
        self.seen = {e: {} for e in ENGS}
        self.ops = {e: [] for e in ENGS}
        self.bufs = {}
        self.nops = 0

    def buf_of(self, ap):
        n = ap.name
        b = self.bufs.get(n)
        if b is None:
            b = self.bufs[n] = Buf(n)
        return b

    def _need(self, e, key, val, waits):
        if self.seen[e].get(key, 0) >= val:
            return
        self.seen[e][key] = val
        waits.append((key, val))

    def _collect(self, kw, extra_r, extra_w):
        reads, writes = [], []
        for k in READ_KEYS:
            v = kw.get(k)
            if v is not None and hasattr(v, "name") and hasattr(v, "ap"):
                reads.append(self.buf_of(v))
        for k in WRITE_KEYS:
            v = kw.get(k)
            if v is not None and hasattr(v, "name") and hasattr(v, "ap"):
                writes.append(self.buf_of(v))
        for v in extra_r:
            reads.append(v if isinstance(v, Buf) else self.buf_of(v))
        for v in extra_w:
            writes.append(v if isinstance(v, Buf) else self.buf_of(v))
        return reads, writes

    def _issue(self, e, inckey, incv, name, kw, reads, writes, accum):
        self.nissued = getattr(self, "nissued", 0) + 1
        if self.nissued > MAXOPS:
            return
        if self.nissued == MAXOPS:
            print("LAST OP:", e, name, {a: (str(b.name) + str(b.shape) if hasattr(b, "ap") else b) for a, b in kw.items()})
        need = {}
        for b in reads:
            if b.w is not None and need.get(b.w[0], 0) < b.w[1]:
                need[b.w[0]] = b.w[1]
        for b in writes:
            if b.w is not None and not (e == "pe" and b.w[0] == "pe") and need.get(b.w[0], 0) < b.w[1]:
                need[b.w[0]] = b.w[1]
            for k, v in b.r.items():
                if need.get(k, 0) < v:
                    need[k] = v
        waits = []
        for k, v in need.items():
            self._need(e, k, v, waits)
        self.cnt[inckey] += incv
        v = self.cnt[inckey]
        self.ops[e].append((waits, name, kw, inckey, incv))
        self.nops += 1 + len(waits)
        for b in reads:
            if b.r.get(inckey, 0) < v:
                b.r[inckey] = v
        for b in writes:
            b.w = (inckey, v)
            b.r = {}

    def op(self, e, name, R=(), W=(), accum=False, drain=False, **kw):
        reads, writes = self._collect(kw, R, W)
        if drain and self.cnt[e] > 0:
            d_ = Buf("drain")
            d_.w = (e, self.cnt[e])
            reads = list(reads) + [d_]
        self._issue(e, e, 1, name, kw, reads, writes, accum)

    def V(self, name, **kw):
        self.op("dve", name, **kw)

    def A(self, name, **kw):
        self.op("act", name, **kw)

    def G(self, name, **kw):
        self.op("pool", name, **kw)

    def M(self, name="matmul", **kw):
        self.op("pe", name, **kw)

    def dma(self, q, out, in_, R=(), W=()):
        kw = dict(out=out, in_=in_)
        reads, writes = self._collect(kw, R, W)
        sems, idx = self.dq[q]
        key = sems[idx % len(sems)]
        self.dq[q][1] = idx + 1
        if self.cnt[key] > 0:
            w = []
            self._need(q, key, self.cnt[key], w)
            pre = w
        else:
            pre = []
        n0 = len(self.ops[q])
        self._issue(q, key, 16, "dma_start", kw, reads, writes, False)
        if pre and len(self.ops[q]) > n0:
            waits, name, kw2, ik, iv = self.ops[q][n0]
            self.ops[q][n0] = (pre + waits, name, kw2, ik, iv)

    def barrier(self):
        for e in ENGS:
            waits = []
            for key, v in self.cnt.items():
                if v > 0:
                    self._need(e, key, v, waits)
            if waits:
                self.ops[e].append((waits, None, None, None, None))
                self.nops += len(waits)

    def wait_bufs(self, e, aps):
        waits = []
        for a in aps:
            b = a if isinstance(a, Buf) else self.buf_of(a)
            if b.w is not None:
                self._need(e, b.w[0], b.w[1], waits)
        self.ops[e].append((waits, None, None, None, None))

    def emit(self):
        nc = self.nc
        with nc.Block() as block:
            def run(e):
                def body(engine):
                    for waits, name, kw, ik, iv in self.ops[e]:
                        for k, v in waits:
                            engine.wait_ge(self.sem[k], v)
                        if name is None:
                            continue
                        inst = getattr(engine, name)(**kw)
                        inst.then_inc(self.sem[ik], iv)
                return body
            block.sync(run("sp"))
            block.scalar(run("act"))
            block.vector(run("dve"))
            block.gpsimd(run("pool"))
            block.tensor(run("pe"))

    def close(self):
        self.es.close()


W_SPECS = [
    ("norm_mix", (D,)), ("w_in", (D, IN_COLS)), ("mla_q_norm", (384,)), ("mla_w_uq", (384, 768)),
    ("mla_kv_norm", (256,)), ("mla_w_ukv", (256, 1024)), ("gqa_q_norm", (64,)), ("gqa_k_norm", (64,)),
    ("rwkv_mu", (1920,)), ("rwkv_w0", (2, 512)), ("rwkv_w2", (2, 64, 512)), ("rwkv_a0", (2, 512)),
    ("rwkv_a2", (2, 64, 512)), ("rwkv_g2", (128, 512)), ("rwkv_k_k", (512,)), ("rwkv_k_a", (512,)),
    ("rwkv_r_k", (8, 64)), ("rwkv_ln_w", (512,)), ("rwkv_ln_b", (512,)), ("w_branch", (3, 512, D)),
    ("b_gate", (3, D)), ("w_out", (D, D)), ("norm_cross", (D,)), ("norm_mem", (D,)),
    ("cross_wq", (D, 512)), ("cross_wkv", (D, 1024)), ("cross_wo", (512, D)), ("norm_mlp", (D,)),
    ("mlp_w1", (D, 4096)), ("mlp_w2", (4096, D)),
]


class K:
    pass


def build(S, depth, dbg=(), phases=("p1", "p2", "rf", "rb", "p3a", "p3b")):
    NT = S // 128
    nc = bass.Bass("TRN2", target_bir_lowering=False)
    P = Prog(nc)
    k = K()
    k.nc, k.P, k.S, k.NT, k.depth, k.dbg = nc, P, S, NT, depth, dbg
    k.phases = phases

    def din(name, shape):
        return nc.dram_tensor(name, list(shape), F32, kind="ExternalInput").ap()

    def dscr(name, shape, dt):
        kind = "ExternalOutput" if name in dbg else "Internal"
        return nc.dram_tensor(name, list(shape), dt, kind=kind).ap()

    k.x = din("x", (S, D))
    k.mem = din("mem", (NMEM, D))
    k.w = {}
    for name, shp in W_SPECS:
        k.w[name] = din(name, (depth,) + shp)
    k.w["norm_final"] = din("norm_final", (1, D))
    k.tab1 = din("tab1", (S, 2, 16))
    k.tab2 = din("tab2", (S, 2, 2, 16))
    k.ident_d = din("ident", (128, 128))
    k.msk_d = din("msk", (2, 2, 128, 128))
    k.bdm_d = din("bdm", (128, 128))
    k.y = nc.dram_tensor("y", [S, D], F32, kind="ExternalOutput").ap()

    k.h_tm = dscr("h_tm", (S, D), BF16)
    k.hT_t = dscr("hT_t", (NT, 128, 8, 128), BF16)
    k.qT_mla = dscr("qT_mla", (8, 96, S), BF16)
    k.kT_mla = dscr("kT_mla", (8, 96, S), BF16)
    k.v_mla = dscr("v_mla", (S, 512), BF16)
    k.qT_gqa = dscr("qT_gqa", (512, S), BF16)
    k.kT_gqa = dscr("kT_gqa", (128, S), BF16)
    k.v_gqa = dscr("v_gqa", (S, 128), BF16)
    k.o_mla = dscr("o_mla", (S, 512), BF16)
    k.o_gqa = dscr("o_gqa", (S, 512), BF16)
    k.y_fw = dscr("y_fw", (S, 512), F32)
    k.z_r = dscr("z_r", (S, 1920), F32)
    k.zmix = dscr("zmix", (S, 1920), F32)
    k.oT_rw = dscr("oT_rw", (NT, 128, 4, 128), BF16)
    k.x2 = dscr("x2", (S, D), F32)
    k.xs = [dscr("xs0", (S, D), F32), dscr("xs1", (S, D), F32)]

    with contextlib.ExitStack() as gst:
        k.gst = gst
        k.rot = [0]
        k.ident = gst.enter_context(nc.sbuf_tensor("ident_b", [128, 128], BF16))
        P.dma("pool", out=k.ident[:, :], in_=k.ident_d)
        k.identf = gst.enter_context(nc.sbuf_tensor("ident_f", [128, 128], F32))
        P.dma("sp", out=k.identf[:, :], in_=k.ident_d)
        for l in range(depth):
            xin = k.x if l == 0 else k.xs[(l - 1) % 2]
            xout = k.xs[l % 2]
            if "p1" in phases:
                phase_p1(k, l, xin)
                P.barrier()
            if "p2" in phases:
                phase_attn(k, l)
                P.barrier()
            if "rf" in phases:
                phase_rwkv(k, l, 0)
                P.barrier()
            if "rb" in phases:
                phase_rwkv(k, l, 1)
                P.barrier()
            if "p3a" in phases:
                phase_p3a(k, l, xin)
                P.barrier()
            if "p3b" in phases:
                phase_p3b(k, l, xout, last=(l == depth - 1))
                P.barrier()
        outs = [k.y]
        for name in dbg:
            outs.append(P.bufs[name]) if name in P.bufs else None
        P.wait_bufs("sp", outs)
        P.emit()
    P.close()
    return nc


_UID = [0]


def un(name):
    _UID[0] += 1
    return "%s_u%d" % (name, _UID[0])


def alloc_banks(k, st, n=8):
    k.banks = [st.enter_context(k.nc.psum_tensor(un("bank%d" % i), [128, 512], F32)) for i in range(n)]


class BankPool:
    def __init__(self, banks):
        self.banks = banks
        self.i = 0

    def get(self):
        b = self.banks[self.i % len(self.banks)]
        self.i += 1
        return b


def run_skewed(gens, ratio=1):
    old = None
    for new in gens:
        new_mid = False
        while True:
            for _ in range(ratio):
                if old is not None:
                    try:
                        next(old)
                    except StopIteration:
                        old = None
            if not new_mid:
                try:
                    if next(new) == "mid":
                        new_mid = True
                except StopIteration:
                    new_mid = True
                    new = None
            if old is None and new_mid:
                break
        old = new
    while old is not None:
        try:
            next(old)
        except StopIteration:
            old = None


def bank(k, lo=0, hi=None):
    if hi is None:
        hi = len(k.banks)
    n = hi - lo
    i = k.rot[0] % n
    k.rot[0] += 1
    return k.banks[lo + i]


def bfv(b, ncols=1024):
    return b[:, :].bitcast(BF16)


def rstd_from_ss(k, out, ss, t, n, eps):
    P = k.P
    if eps is not None and eps != 0.0:
        P.V("tensor_scalar", out=t, in0=ss, scalar1=1.0 / n, scalar2=float(eps), op0=ALU.mult, op1=ALU.add)
        P.A("activation", out=t, in_=t, func=AF.Ln)
    else:
        P.A("activation", out=t, in_=ss, func=AF.Ln, scale=1.0 / n)
    P.A("activation", out=out, in_=t, func=AF.Exp, scale=-0.5)


def load_col(k, q, dst, src1d, n):
    k.P.dma(q, out=dst, in_=src1d.rearrange("(p o) -> p o", o=1))


def bc(ap, shape, axis):
    return ap.unsqueeze(axis).broadcast_to(list(shape))


def phase_p1(k, l, xin):
    nc, P, S, NT = k.nc, k.P, k.S, k.NT
    w = k.w
    with contextlib.ExitStack() as st:
        def SB(name, shape, dt):
            return st.enter_context(nc.sbuf_tensor(un("p1_" + name), list(shape), dt))
        alloc_banks(k, st)
        w_att = SB("watt", [128, 8, 1440], BF16)
        w_r = SB("wr", [128, 8, 1920], BF16)
        w_uq = SB("wuq", [128, 3, 768], BF16)
        w_ukv = SB("wukv", [128, 2, 1024], BF16)
        stg = SB("stg", [128, 2304], F32)
        g_bc = SB("gbc", [128, 1024], F32)
        gq_bc = SB("gqbc", [128, 64], F32)
        gk_bc = SB("gkbc", [128, 64], F32)
        gcol = SB("gcol", [128, 5], F32)
        win = w["w_in"][l].rearrange("(c p) n -> p c n", p=128)
        for c in range(8):
            P.dma("pool", out=w_att[:, c, :], in_=win[:, c, 0:1440])
        for c in range(8):
            P.dma("pool", out=w_r[:, c, :], in_=win[:, c, 1440:3360])
        P.dma("sp", out=g_bc[:, :], in_=w["norm_mix"][l:l + 1, :].broadcast_to([128, 1024]))
        P.dma("sp", out=gq_bc[:, :], in_=w["gqa_q_norm"][l:l + 1, :].broadcast_to([128, 64]))
        P.dma("sp", out=gk_bc[:, :], in_=w["gqa_k_norm"][l:l + 1, :].broadcast_to([128, 64]))
        for c in range(3):
            load_col(k, "sp", gcol[:, c:c + 1], w["mla_q_norm"][l, c * 128:(c + 1) * 128], 128)
        for c in range(2):
            load_col(k, "sp", gcol[:, 3 + c:4 + c], w["mla_kv_norm"][l, c * 128:(c + 1) * 128], 128)
        P.dma("sp", out=stg[:, 0:2304].rearrange("p (c n) -> p c n", c=3),
              in_=w["mla_w_uq"][l].rearrange("(c p) n -> p c n", p=128))
        for c in range(3):
            P.V("tensor_scalar", out=w_uq[:, c, :], in0=stg[:, c * 768:(c + 1) * 768],
                scalar1=gcol[:, c:c + 1], scalar2=None, op0=ALU.mult)
        P.dma("sp", out=stg[:, 0:2048].rearrange("p (c n) -> p c n", c=2),
              in_=w["mla_w_ukv"][l].rearrange("(c p) n -> p c n", p=128))
        for c in range(2):
            P.V("tensor_scalar", out=w_ukv[:, c, :], in0=stg[:, c * 1024:(c + 1) * 1024],
                scalar1=gcol[:, 3 + c:4 + c], scalar2=None, op0=ALU.mult)

        sets = []
        for s in range(2):
            d = {}
            for name, shape, dt in [
                ("x", [128, 1024], F32), ("junk", [128, 1024], BF16), ("hb", [128, 1024], BF16),
                ("hT", [128, 8, 128], BF16), ("tb1", [128, 2, 16], F32), ("tb2", [128, 2, 2, 16], F32),
                ("st", [128, 16], F32), ("cqb", [128, 384], BF16), ("cqT", [128, 3, 128], BF16),
                ("q32", [128, 8, 96], F32), ("qrot", [128, 8, 96], BF16), ("qT", [96, 8, 128], BF16),
                ("ta", [128, 8, 2, 16], F32), ("tb", [128, 8, 2, 16], F32),
                ("ckvb", [128, 256], BF16), ("ckvT", [128, 2, 128], BF16), ("kt", [128, 8, 96], BF16),
                ("vt", [128, 8, 64], BF16), ("kr", [128, 32], F32), ("kT", [96, 8, 128], BF16),
                ("sq", [128, 512], F32), ("gst", [128, 24], F32), ("qn", [128, 8, 64], F32),
                ("gqr", [128, 8, 64], BF16), ("gqT", [128, 4, 128], BF16),
                ("kn", [128, 2, 64], F32), ("gkr", [128, 2, 64], BF16), ("gkT", [128, 128], BF16),
                ("gv", [128, 128], BF16), ("zr", [128, 1920], F32),
            ]:
                d[name] = SB("%s%d" % (name, s), shape, dt)
            sets.append(d)

        ident = k.ident
        poolA = BankPool(k.banks[0:3])
        poolB = BankPool(k.banks[3:8])

        def tile_gen(i):
            pool = poolA
            d = sets[i % 2]
            t0 = i * 128
            x, hb, hT, stt = d["x"], d["hb"], d["hT"], d["st"]
            P.dma("sp", out=x[:, :], in_=xin[t0:t0 + 128, :])
            P.dma("sp", out=d["tb1"][:, :, :], in_=k.tab1[t0:t0 + 128])
            P.dma("sp", out=d["tb2"][:, :, :, :], in_=k.tab2[t0:t0 + 128])
            P.A("activation", out=d["junk"][:, :], in_=x[:, :], func=AF.Square, accum_out=stt[:, 0:1])
            rstd_from_ss(k, stt[:, 1:2], stt[:, 0:1], stt[:, 2:3], 1024.0, EPS)
            P.V("scalar_tensor_tensor", out=hb[:, :], in0=x[:, :], scalar=stt[:, 1:2], in1=g_bc[:, :],
                op0=ALU.mult, op1=ALU.mult)
            P.dma("pool", out=k.h_tm[t0:t0 + 128, :], in_=hb[:, :])
            pb = pool.get()
            pv = bfv(pb)
            for c in range(8):
                P.M("transpose", out=pv[:, c * 128:(c + 1) * 128], in_=hb[:, c * 128:(c + 1) * 128],
                    identity=ident[:, :], accum=(c > 0))
            P.A("activation", out=hT[:, :, :].rearrange("p c t -> p (c t)"), in_=pv[:, 0:1024], func=AF.Copy)
            P.dma("pool", out=k.hT_t[i], in_=hT[:, :, :])
            yield

            def proj(c0, n):
                nonlocal pool
                b = pool.get()
                for c in range(8):
                    P.M("matmul", out=b[:, 0:n], lhsT=hT[:, c, :], rhs=w_att[:, c, c0:c0 + n],
                        start=(c == 0), stop=(c == 7), accum=(c > 0))
                return b

            for ci, (c0, n) in enumerate(((0, 512), (512, 512), (1024, 512), (1536, 384))):
                b = pool.get()
                for c in range(8):
                    P.M("matmul", out=b[:, 0:n], lhsT=hT[:, c, :], rhs=w_r[:, c, c0:c0 + n],
                        start=(c == 0), stop=(c == 7), accum=(c > 0))
                if ci % 2 == 0:
                    P.A("activation", out=d["zr"][:, c0:c0 + n], in_=b[:, 0:n], func=AF.Copy)
                else:
                    P.V("tensor_copy", out=d["zr"][:, c0:c0 + n], in_=b[:, 0:n])
                yield
            P.dma("pool", out=k.z_r[t0:t0 + 128, :], in_=d["zr"][:, :])
            pool = poolB
            yield "mid"
            cos1 = bc(d["tb1"][:, 0, :], [128, 8, 16], 1)
            sin1 = bc(d["tb1"][:, 1, :], [128, 8, 16], 1)
            pq = proj(0, 384)
            P.A("activation", out=d["junk"][:, 0:384], in_=pq[:, 0:384], func=AF.Square, accum_out=stt[:, 3:4])
            rstd_from_ss(k, stt[:, 4:5], stt[:, 3:4], stt[:, 5:6], 384.0, EPS)
            P.V("tensor_copy", out=d["cqb"][:, :], in_=pq[:, 0:384])
            yield
            pb = pool.get()
            pv = bfv(pb)
            for c in range(3):
                P.M("transpose", out=pv[:, c * 128:(c + 1) * 128], in_=d["cqb"][:, c * 128:(c + 1) * 128],
                    identity=ident[:, :], accum=(c > 0))
            P.A("activation", out=d["cqT"][:, :, :].rearrange("p c t -> p (c t)"), in_=pv[:, 0:384], func=AF.Copy)
            q32 = d["q32"]
            q32f = q32[:, :, :].rearrange("p h e -> p (h e)")
            for (c0, n) in ((0, 512), (512, 256)):
                b = pool.get()
                for c in range(3):
                    P.M("matmul", out=b[:, 0:n], lhsT=d["cqT"][:, c, :], rhs=w_uq[:, c, c0:c0 + n],
                        start=(c == 0), stop=(c == 2), accum=(c > 0))
                P.V("tensor_scalar", out=q32f[:, c0:c0 + n], in0=b[:, 0:n], scalar1=stt[:, 4:5], scalar2=None,
                    op0=ALU.mult)
            yield
            qrot = d["qrot"]
            ta = d["ta"][:, :, 0, :]
            tb = d["tb"][:, :, 0, :]
            P.G("tensor_copy", out=qrot[:, :, 0:64], in_=q32[:, :, 0:64])
            P.V("tensor_tensor", out=ta, in0=q32[:, :, 64:80], in1=cos1, op=ALU.mult)
            P.V("tensor_tensor", out=tb, in0=q32[:, :, 80:96], in1=sin1, op=ALU.mult)
            P.V("tensor_tensor", out=qrot[:, :, 64:80], in0=ta, in1=tb, op=ALU.subtract)
            P.V("tensor_tensor", out=ta, in0=q32[:, :, 64:80], in1=sin1, op=ALU.mult)
            P.V("tensor_tensor", out=tb, in0=q32[:, :, 80:96], in1=cos1, op=ALU.mult)
            P.V("tensor_tensor", out=qrot[:, :, 80:96], in0=ta, in1=tb, op=ALU.add)
            pb = pool.get()
            pv = bfv(pb)
            for h in range(8):
                P.M("transpose", out=pv[0:96, h * 128:(h + 1) * 128], in_=qrot[:, h, :], identity=ident[:, :],
                    accum=(h > 0))
            P.A("activation", out=d["qT"][:, :, :].rearrange("p h t -> p (h t)"), in_=pv[0:96, 0:1024], func=AF.Copy)
            P.dma("pool", out=k.qT_mla[:, :, t0:t0 + 128].rearrange("h e s -> e h s"), in_=d["qT"][:, :, :])
            yield
            pkv = proj(384, 288)
            P.A("activation", out=d["junk"][:, 0:256], in_=pkv[:, 0:256], func=AF.Square, accum_out=stt[:, 6:7])
            rstd_from_ss(k, stt[:, 7:8], stt[:, 6:7], stt[:, 8:9], 256.0, EPS)
            P.V("tensor_copy", out=d["ckvb"][:, :], in_=pkv[:, 0:256])
            kr = d["kr"]
            c1 = d["tb1"][:, 0, :]
            s1 = d["tb1"][:, 1, :]
            t2a = d["ta"][:, 0, 1, :]
            t2b = d["tb"][:, 0, 1, :]
            P.V("tensor_tensor", out=t2a, in0=pkv[:, 256:272], in1=c1, op=ALU.mult)
            P.V("tensor_tensor", out=t2b, in0=pkv[:, 272:288], in1=s1, op=ALU.mult)
            P.V("tensor_tensor", out=kr[:, 0:16], in0=t2a, in1=t2b, op=ALU.subtract)
            P.V("tensor_tensor", out=t2a, in0=pkv[:, 256:272], in1=s1, op=ALU.mult)
            P.V("tensor_tensor", out=t2b, in0=pkv[:, 272:288], in1=c1, op=ALU.mult)
            P.V("tensor_tensor", out=kr[:, 16:32], in0=t2a, in1=t2b, op=ALU.add)
            pb = pool.get()
            pv = bfv(pb)
            for c in range(2):
                P.M("transpose", out=pv[:, c * 128:(c + 1) * 128], in_=d["ckvb"][:, c * 128:(c + 1) * 128],
                    identity=ident[:, :], accum=(c > 0))
            P.A("activation", out=d["ckvT"][:, :, :].rearrange("p c t -> p (c t)"), in_=pv[:, 0:256], func=AF.Copy)
            yield
            kt, vt = d["kt"], d["vt"]
            for half in range(2):
                b = pool.get()
                for c in range(2):
                    P.M("matmul", out=b[:, 0:512], lhsT=d["ckvT"][:, c, :], rhs=w_ukv[:, c, half * 512:(half + 1) * 512],
                        start=(c == 0), stop=(c == 1), accum=(c > 0))
                b3 = b[:, 0:512].rearrange("p (h e) -> p h e", h=4)
                P.V("tensor_scalar", out=kt[:, half * 4:(half + 1) * 4, 0:64], in0=b3[:, :, 0:64], scalar1=stt[:, 7:8],
                    scalar2=None, op0=ALU.mult)
                P.V("tensor_scalar", out=vt[:, half * 4:(half + 1) * 4, :], in0=b3[:, :, 64:128], scalar1=stt[:, 7:8],
                    scalar2=None, op0=ALU.mult)
            P.G("tensor_copy", out=kt[:, :, 64:96], in_=bc(kr[:, :], [128, 8, 32], 1))
            pb = pool.get()
            pv = bfv(pb)
            for h in range(8):
                P.M("transpose", out=pv[0:96, h * 128:(h + 1) * 128], in_=kt[:, h, :], identity=ident[:, :],
                    accum=(h > 0))
            P.A("activation", out=d["kT"][:, :, :].rearrange("p h t -> p (h t)"), in_=pv[0:96, 0:1024], func=AF.Copy)
            P.dma("pool", out=k.kT_mla[:, :, t0:t0 + 128].rearrange("h e s -> e h s"), in_=d["kT"][:, :, :])
            P.dma("pool", out=k.v_mla[t0:t0 + 128, :], in_=vt[:, :, :].rearrange("p h e -> p (h e)"))

            yield
            def qknorm_rope(pb_ap, nh, gbc, n32, rot, soff):
                gs = d["gst"]
                P.A("activation", out=d["sq"][:, 0:nh * 64], in_=pb_ap, func=AF.Square)
                P.V("tensor_reduce", out=gs[:, soff:soff + nh],
                    in_=d["sq"][:, 0:nh * 64].rearrange("p (h e) -> p h e", h=nh), axis=AX.X, op=ALU.add)
                rstd_from_ss(k, gs[:, soff + 8:soff + 8 + nh], gs[:, soff:soff + nh], gs[:, soff + 16:soff + 16 + nh],
                             64.0, EPS)
                P.V("tensor_tensor", out=n32[:, :, :], in0=pb_ap.rearrange("p (h e) -> p h e", h=nh),
                    in1=bc(gs[:, soff + 8:soff + 8 + nh], [128, nh, 64], 2), op=ALU.mult)
                P.G("tensor_tensor", out=n32[:, :, :], in0=n32[:, :, :], in1=bc(gbc[:, :], [128, nh, 64], 1),
                    op=ALU.mult)
                v5 = n32[:, :, :].rearrange("p h (a b e) -> p h a b e", a=2, b=2)
                r5 = rot[:, :, :].rearrange("p h (a b e) -> p h a b e", a=2, b=2)
                x1, x2 = v5[:, :, :, 0, :], v5[:, :, :, 1, :]
                cos2 = bc(d["tb2"][:, 0, :, :], [128, nh, 2, 16], 1)
                sin2 = bc(d["tb2"][:, 1, :, :], [128, nh, 2, 16], 1)
                ta4 = d["ta"][:, 0:nh, :, :]
                tb4 = d["tb"][:, 0:nh, :, :]
                P.V("tensor_tensor", out=ta4, in0=x1, in1=cos2, op=ALU.mult)
                P.V("tensor_tensor", out=tb4, in0=x2, in1=sin2, op=ALU.mult)
                P.V("tensor_tensor", out=r5[:, :, :, 0, :], in0=ta4, in1=tb4, op=ALU.subtract)
                P.V("tensor_tensor", out=ta4, in0=x1, in1=sin2, op=ALU.mult)
                P.V("tensor_tensor", out=tb4, in0=x2, in1=cos2, op=ALU.mult)
                P.V("tensor_tensor", out=r5[:, :, :, 1, :], in0=ta4, in1=tb4, op=ALU.add)

            pgq = proj(672, 512)
            yield
            qknorm_rope(pgq[:, 0:512], 8, gq_bc, d["qn"], d["gqr"], 0)
            pb = pool.get()
            pv = bfv(pb)
            gqf = d["gqr"][:, :, :].rearrange("p h e -> p (h e)")
            for c in range(4):
                P.M("transpose", out=pv[:, c * 128:(c + 1) * 128], in_=gqf[:, c * 128:(c + 1) * 128],
                    identity=ident[:, :], accum=(c > 0))
            P.A("activation", out=d["gqT"][:, :, :].rearrange("p c t -> p (c t)"), in_=pv[:, 0:512], func=AF.Copy)
            P.dma("pool", out=k.qT_gqa[:, t0:t0 + 128].rearrange("(c p) s -> p c s", p=128), in_=d["gqT"][:, :, :])
            yield
            pgk = proj(1184, 256)
            P.A("activation", out=d["gv"][:, :], in_=pgk[:, 128:256], func=AF.Copy)
            P.dma("pool", out=k.v_gqa[t0:t0 + 128, :], in_=d["gv"][:, :])
            qknorm_rope(pgk[:, 0:128], 2, gk_bc, d["kn"], d["gkr"], 2)
            pb = pool.get()
            pv = bfv(pb)
            P.M("transpose", out=pv[:, 0:128], in_=d["gkr"][:, :, :].rearrange("p h e -> p (h e)"),
                identity=ident[:, :])
            P.A("activation", out=d["gkT"][:, :], in_=pv[:, 0:128], func=AF.Copy)
            P.dma("pool", out=k.kT_gqa[:, t0:t0 + 128], in_=d["gkT"][:, :])

        run_skewed([tile_gen(i) for i in range(NT)], ratio=2)


def phase_attn(k, l):
    nc, P, S, NT = k.nc, k.P, k.S, k.NT
    QB = 512
    NQB = S // QB
    NG = NT // 2
    with contextlib.ExitStack() as st:
        def SB(name, shape, dt):
            return st.enter_context(nc.sbuf_tensor(un("p2_" + name), list(shape), dt))
        pps = [st.enter_context(nc.psum_tensor(un("pp%d" % i), [128, 1024], F32)) for i in range(2)]
        obs = [st.enter_context(nc.psum_tensor(un("ob%d" % i), [128, 512], F32)) for i in range(4)]
        kts = [SB("kt%d" % i, [128, S], BF16) for i in range(2)]
        qts = [SB("qt%d" % i, [128, S], BF16) for i in range(2)]
        vxs = [SB("vx%d" % i, [128, NT, 65], BF16) for i in range(2)]
        pts = [SB("pt%d" % i, [128, 2 * QB], BF16) for i in range(3)]
        o32 = [SB("o32_%d" % i, [65, QB], F32) for i in range(4)]
        osb = [SB("o%d" % i, [128, 4, 64], BF16) for i in range(4)]
        rsb = [SB("rs%d" % i, [128, 4], F32) for i in range(4)]
        for vx in vxs:
            P.G("memset", ap=vx[:, :, 64:65], constant=1.0, W=[vx[:, :, :]])
        hd = []
        kvi = -1
        for ui in range(12):
            qt = qts[ui % 2]
            loads = []
            if ui < 8:
                h = ui
                kvi += 1
                kt, vx = kts[kvi % 2], vxs[kvi % 2]
                loads.append((kt[0:96, :], k.kT_mla[h]))
                vsrc = k.v_mla[:, h * 64:(h + 1) * 64].rearrange("(c p) e -> p c e", p=128)
                for c0 in range(0, NT, 8):
                    c1 = min(NT, c0 + 8)
                    loads.append((vx[:, c0:c1, 0:64], vsrc[:, c0:c1, :]))
                loads.append((qt[0:96, :], k.qT_mla[h]))
                hd.append(dict(kind="mla", kt=kt, vx=vx, qt=qt, dq=96, scale=96.0 ** -0.5, heads=[h], oscr=k.o_mla,
                               loads=loads, ngroups=NG))
            else:
                h0 = (ui - 8) * 2
                kvh = h0 // 4
                if h0 % 4 == 0:
                    kvi += 1
                    kt, vx = kts[kvi % 2], vxs[kvi % 2]
                    loads.append((kt[0:64, :], k.kT_gqa[kvh * 64:(kvh + 1) * 64, :]))
                    loads.append((kt[64:128, :], k.kT_gqa[kvh * 64:(kvh + 1) * 64, :]))
                    vsrc = k.v_gqa[:, kvh * 64:(kvh + 1) * 64].rearrange("(c p) e -> p c e", p=128)
                    for c0 in range(0, NT, 8):
                        c1 = min(NT, c0 + 8)
                        loads.append((vx[:, c0:c1, 0:64], vsrc[:, c0:c1, :]))
                loads.append((qt[0:128, :], k.qT_gqa[h0 * 64:(h0 + 2) * 64, :]))
                hd.append(dict(kind="gqa", kt=kt, vx=vx, qt=qt, dq=64, scale=64.0 ** -0.5, heads=[h0, h0 + 1],
                               oscr=k.o_gqa, loads=loads, ngroups=NT))
        items = []
        for ui in range(len(hd)):
            for qb in range(NQB):
                for g in range(hd[ui]["ngroups"]):
                    items.append((ui, qb, g))
        state = {"npp": 0, "npt": 0, "nqb": 0}

        def do_loads(ui):
            for (o_, i_) in hd[ui]["loads"]:
                P.dma("sp", out=o_, in_=i_)

        def emit_qk(ui, qb, g):
            H_ = hd[ui]
            pp = pps[state["npp"] % 2]
            pt = pts[state["npt"] % 3]
            state["npp"] += 1
            state["npt"] += 1
            dq = H_["dq"]
            for u in range(2):
                if H_["kind"] == "mla":
                    kc = 2 * g + u
                    P.M("matmul", out=pp[:, u * QB:(u + 1) * QB], lhsT=H_["kt"][0:dq, kc * 128:(kc + 1) * 128],
                        rhs=H_["qt"][0:dq, qb * QB:(qb + 1) * QB], start=True, stop=True, accum=(u > 0))
                else:
                    rs = slice(u * 64, (u + 1) * 64)
                    P.M("matmul", out=pp[:, u * QB:(u + 1) * QB], lhsT=H_["kt"][rs, g * 128:(g + 1) * 128],
                        rhs=H_["qt"][rs, qb * QB:(qb + 1) * QB], start=True, stop=True, accum=(u > 0))
            P.A("activation", out=pt[:, :], in_=pp[:, :], func=AF.Exp, scale=H_["scale"])
            return pt

        def emit_pv(ui, qb, g, pt):
            H_ = hd[ui]
            base = (state["nqb"] % 2) * 2
            lastg = (g == H_["ngroups"] - 1)
            for u in range(2):
                if H_["kind"] == "mla":
                    kc = 2 * g + u
                    ob = obs[base]
                else:
                    kc = g
                    ob = obs[base + u]
                P.M("matmul", out=ob[0:65, 0:QB], lhsT=H_["vx"][:, kc, 0:65], rhs=pt[:, u * QB:(u + 1) * QB],
                    start=(kc == 0), stop=(kc == NT - 1), accum=(kc > 0))
            if not lastg:
                return None
            state["nqb"] += 1
            eps = []
            for u, h in enumerate(H_["heads"]):
                ob = obs[base + u]
                o3, o_s, r_s = o32[base + u], osb[base + u], rsb[base + u]
                P.A("activation", out=o3[:, :], in_=ob[0:65, 0:QB], func=AF.Copy)

                def epilogue(ob=ob, o3=o3, o_s=o_s, r_s=r_s, h=h):
                    for j in range(4):
                        P.M("transpose", out=ob[:, j * 128:j * 128 + 65], in_=o3[0:65, j * 128:(j + 1) * 128],
                            identity=k.identf[0:65, 0:65], accum=(j > 0))
                    tb3 = ob[:, 0:512].rearrange("p (j e) -> p j e", j=4)
                    P.V("reciprocal", out=r_s[:, :], in_=tb3[:, :, 64])
                    P.V("tensor_tensor", out=o_s[:, :, :], in0=tb3[:, :, 0:64], in1=bc(r_s[:, :], [128, 4, 64], 2),
                        op=ALU.mult)
                    P.dma("pool", out=H_["oscr"][qb * QB:(qb + 1) * QB, h * 64:(h + 1) * 64].rearrange("(j p) e -> p j e", p=128),
                          in_=o_s[:, :, :])
                eps.append(epilogue)

            def run_eps():
                for f in eps:
                    f()
            return run_eps

        n = len(items)
        starts = {}
        for idx, (ui, qb, g) in enumerate(items):
            starts.setdefault(ui, idx)
        do_loads(0)
        prev_pt = None
        cur_pt = None
        pending = None
        for idx in range(n + 1):
            if idx < n:
                ui, qb, g = items[idx]
                if idx == starts[ui] + 2 and ui + 1 < len(hd):
                    do_loads(ui + 1)
                cur_pt = emit_qk(ui, qb, g)
            if idx >= 1:
                ui0, qb0, g0 = items[idx - 1]
                ep = emit_pv(ui0, qb0, g0, prev_pt)
                if pending is not None:
                    pending()
                pending = ep
            prev_pt = cur_pt
        if pending is not None:
            pending()


def phase_rwkv(k, l, d):
    nc, P, S, NT = k.nc, k.P, k.S, k.NT
    w = k.w
    C0 = DECAY_C
    with contextlib.ExitStack() as st:
        def SB(name, shape, dt):
            return st.enter_context(nc.sbuf_tensor(un("rw_" + name), list(shape), dt))
        alloc_banks(k, st)
        bcn = {}

        def load_bc(name, src_row, n=512):
            t = SB(name, [128, n], F32)
            P.dma("sp", out=t[:, :], in_=src_row.broadcast_to([128, n]))
            bcn[name] = t
            return t
        w0b = load_bc("w0", w["rwkv_w0"][l, d:d + 1, :])
        a0b = load_bc("a0", w["rwkv_a0"][l, d:d + 1, :])
        kkb = load_bc("kk", w["rwkv_k_k"][l:l + 1, :])
        kab = load_bc("ka", w["rwkv_k_a"][l:l + 1, :])
        if d == 0:
            mub = load_bc("mu", w["rwkv_mu"][l:l + 1, :], 1920)
        else:
            a0o = load_bc("a0o", w["rwkv_a0"][l, 0:1, :])
            rkb = load_bc("rk", w["rwkv_r_k"][l:l + 1].rearrange("o h n -> o (h n)"))
            lnw = load_bc("lnw", w["rwkv_ln_w"][l:l + 1, :])
            lnb = load_bc("lnb", w["rwkv_ln_b"][l:l + 1, :])
            g2s = SB("g2", [128, 512], BF16)
            P.dma("pool", out=g2s[:, :], in_=w["rwkv_g2"][l])
        w2s = SB("w2", [128, 512], BF16)
        a2s = SB("a2", [128, 512], BF16)
        P.dma("pool", out=w2s[:, :], in_=w["rwkv_w2"][l].rearrange("d r c -> (d r) c"))
        P.dma("pool", out=a2s[:, :], in_=w["rwkv_a2"][l].rearrange("d r c -> (d r) c"))
        m2 = SB("m2", [128, 2, 128], F32)
        mT = SB("mT", [128, 128], F32)
        bdm = SB("bdm", [128, 128], F32)
        onec = SB("onec", [128, 1], F32)
        P.dma("sp", out=m2[:, 0, :], in_=k.msk_d[d, 0])
        P.dma("sp", out=m2[:, 1, :], in_=k.msk_d[d, 1])
        P.dma("sp", out=mT[:, :], in_=k.msk_d[1 - d, 0])
        P.dma("sp", out=bdm[:, :], in_=k.bdm_d)
        P.G("memset", ap=onec[:, :], constant=1.0)
        H32 = SB("H32", [128, 4, 128], F32)
        Hb = SB("Hb", [128, 4, 128], BF16)
        P.G("memset", ap=H32[:, :, :], constant=0.0)
        P.G("memset", ap=Hb[:, :, :], constant=0.0)
        if d == 0:
            zin = [SB("z", [128, 1920], F32), SB("zp", [128, 1920], F32), SB("zn", [128, 1920], F32),
                   SB("zt", [128, 1920], F32)]
        sets = []
        for s_ in range(2):
            dd = {}
            lst = [("zm", [128, 1920], F32), ("lor", [128, 384], BF16), ("lorT", [128, 3, 128], BF16),
                   ("tok4", [128, 4, 512], BF16), ("vb", [128, 512], BF16), ("TTs", [128, 4, 4, 128], BF16),
                   ("gC", [128, 4], F32), ("sm", [128, 64], F32), ("Y32", [128, 512], F32), ("hT_", [128, 128], F32)]
            for t in range(8):
                lst.append(("T%d" % t, [128, 512], F32))
            for p in range(4):
                lst += [("XL%d" % p, [128, 2, 2, 128], BF16), ("LT%d" % p, [128, 2, 128], BF16),
                        ("ARB%d" % p, [128, 2, 128], BF16), ("AK%d" % p, [128, 2, 2, 128], BF16),
                        ("PT%d" % p, [128, 128], BF16), ("AKV%d" % p, [128, 128], BF16), ("Ub%d" % p, [128, 128], BF16)]
            if d == 1:
                lst += [("yfw", [128, 512], F32), ("ob", [128, 512], BF16), ("oT", [128, 4, 128], BF16)]
            for name, shape, dt in lst:
                dd[name] = SB("%s_%d" % (name, s_), shape, dt)
            sets.append(dd)

        order = list(range(NT)) if d == 0 else list(range(NT - 1, -1, -1))
        poolA = BankPool(k.banks[0:3])
        poolB = BankPool(k.banks[3:8])

        def tile_gen(it, i):
            pool = poolA
            D_ = sets[it % 2]
            t0 = i * 128
            zm = D_["zm"]
            T = [D_["T%d" % t] for t in range(8)]
            sm = D_["sm"]
            if d == 0:
                z, zp, zn, zt = zin
                P.dma("sp", out=z[:, :], in_=k.z_r[t0:t0 + 128, :])
                if i == 0:
                    P.G("memset", ap=zp[0:1, :], constant=0.0)
                    P.dma("sp", out=zp[1:128, :], in_=k.z_r[0:127, :])
                else:
                    P.dma("sp", out=zp[:, :], in_=k.z_r[t0 - 1:t0 + 127, :])
                if i == NT - 1:
                    P.G("memset", ap=zn[:, :], constant=0.0)
                    P.dma("sp", out=zn[0:127, :], in_=k.z_r[t0 + 1:t0 + 128, :])
                else:
                    P.dma("sp", out=zn[:, :], in_=k.z_r[t0 + 1:t0 + 129, :])
                P.G("tensor_tensor", out=zt[:, :], in0=zp[:, :], in1=zn[:, :], op=ALU.add)
                P.V("scalar_tensor_tensor", out=zt[:, :], in0=zt[:, :], scalar=0.5, in1=z[:, :], op0=ALU.mult,
                    op1=ALU.subtract)
                P.G("tensor_tensor", out=zt[:, :], in0=zt[:, :], in1=mub[:, :], op=ALU.mult)
                P.V("tensor_tensor", out=zm[:, :], in0=z[:, :], in1=zt[:, :], op=ALU.add)
                P.dma("pool", out=k.zmix[t0:t0 + 128, :], in_=zm[:, :])
            else:
                P.dma("sp", out=zm[:, :], in_=k.zmix[t0:t0 + 128, :])
                P.dma("sp", out=D_["yfw"][:, :], in_=k.y_fw[t0:t0 + 128, :])
            r_ = zm[:, 0:512]
            kx = zm[:, 512:1024]
            v_ = zm[:, 1024:1536]
            for _ in range(RW_DELAY):
                yield
            lor, lorT = D_["lor"], D_["lorT"]
            P.A("activation", out=lor[:, 0:128], in_=zm[:, 1536:1664], func=AF.Tanh)
            P.V("tensor_copy", out=lor[:, 128:256], in_=zm[:, 1664:1792])
            nl = 2
            if d == 1:
                P.A("activation", out=lor[:, 256:384], in_=zm[:, 1792:1920], func=AF.Sigmoid)
                nl = 3
            pb = pool.get()
            pv = bfv(pb)
            for c in range(nl):
                P.M("transpose", out=pv[:, c * 128:(c + 1) * 128], in_=lor[:, c * 128:(c + 1) * 128],
                    identity=k.ident[:, :], accum=(c > 0))
            P.A("activation", out=lorT[:, 0:nl, :].rearrange("p c t -> p (c t)"), in_=pv[:, 0:nl * 128], func=AF.Copy)
            ds = slice(d * 64, (d + 1) * 64)
            sg, asig = T[0], T[1]
            pb = pool.get()
            P.M("matmul", out=pb[:, 0:512], lhsT=lorT[ds, 0, :], rhs=w2s[ds, :], start=True, stop=True)
            P.V("tensor_tensor", out=sg[:, :], in0=pb[:, 0:512], in1=w0b[:, :], op=ALU.add)
            P.A("activation", out=sg[:, :], in_=sg[:, :], func=AF.Sigmoid)
            yield
            pb = pool.get()
            P.M("matmul", out=pb[:, 0:512], lhsT=lorT[ds, 1, :], rhs=a2s[ds, :], start=True, stop=True)
            P.V("tensor_tensor", out=asig[:, :], in0=pb[:, 0:512], in1=a0b[:, :], op=ALU.add)
            P.A("activation", out=asig[:, :], in_=asig[:, :], func=AF.Sigmoid)
            yield
            eP, eN, eX = T[2], T[3], T[4]
            pc = pool.get()
            P.M("matmul", out=pc[:, 0:512], lhsT=m2[:, 1, :], rhs=sg[:, :], start=True, stop=True)
            pcx = pool.get()
            P.M("matmul", out=pcx[:, 0:512], lhsT=m2[:, 0, :], rhs=sg[:, :], start=True, stop=True)
            P.A("activation", out=eP[:, :], in_=pc[:, 0:512], func=AF.Exp, scale=-C0)
            P.A("activation", out=eN[:, :], in_=pc[:, 0:512], func=AF.Exp, scale=C0)
            yield
            P.A("activation", out=eX[:, :], in_=pcx[:, 0:512], func=AF.Exp, scale=-C0)
            pg = pool.get()
            for p in range(4):
                P.M("matmul", out=pg[:, p:p + 1], lhsT=sg[:, p * 128:(p + 1) * 128], rhs=onec[:, 0:1], start=True,
                    stop=True, accum=(p > 0))
            P.A("activation", out=D_["gC"][:, :], in_=pg[:, 0:4], func=AF.Exp, scale=-C0)
            yield
            kk, kt, bb = T[5], T[6], T[7]
            tok4, vb = D_["tok4"], D_["vb"]
            P.G("tensor_tensor", out=kk[:, :], in0=kx, in1=kkb[:, :], op=ALU.mult)
            P.A("activation", out=kt[:, :], in_=kk[:, :], func=AF.Square)
            P.V("tensor_reduce", out=sm[:, 0:8], in_=kt[:, :].rearrange("p (h e) -> p h e", h=8), axis=AX.X, op=ALU.add)
            P.V("tensor_scalar", out=sm[:, 0:8], in0=sm[:, 0:8], scalar1=1e-24, scalar2=None, op0=ALU.max)
            yield
            P.A("activation", out=sm[:, 8:16], in_=sm[:, 0:8], func=AF.Ln)
            P.A("activation", out=sm[:, 16:24], in_=sm[:, 8:16], func=AF.Exp, scale=-0.5)
            kk3 = kk[:, :].rearrange("p (h e) -> p h e", h=8)
            P.V("tensor_tensor", out=kk3, in0=kk3, in1=bc(sm[:, 16:24], [128, 8, 64], 2), op=ALU.mult)
            yield
            P.V("scalar_tensor_tensor", out=kt[:, :], in0=asig[:, :], scalar=-1.0, in1=kab[:, :], op0=ALU.add, op1=ALU.mult)
            P.V("scalar_tensor_tensor", out=kt[:, :], in0=kt[:, :], scalar=1.0, in1=kx, op0=ALU.add, op1=ALU.mult)
            P.G("tensor_tensor", out=bb[:, :], in0=kk[:, :], in1=asig[:, :], op=ALU.mult)
            yield
            P.V("scalar_tensor_tensor", out=tok4[:, 0, :], in0=kk[:, :], scalar=-1.0, in1=eX[:, :], op0=ALU.mult, op1=ALU.mult)
            P.G("tensor_tensor", out=tok4[:, 1, :], in0=r_, in1=eP[:, :], op=ALU.mult)
            yield
            P.G("tensor_tensor", out=tok4[:, 2, :], in0=bb[:, :], in1=eN[:, :], op=ALU.mult)
            P.V("tensor_tensor", out=tok4[:, 3, :], in0=kt[:, :], in1=eN[:, :], op=ALU.mult)
            P.A("activation", out=vb[:, :], in_=v_, func=AF.Copy)
            pool = poolB
            yield "mid"
            TTs = D_["TTs"]
            for p0 in (0, 2):
                pb = pool.get()
                pv = bfv(pb)
                for p in (p0, p0 + 1):
                    for q in range(4):
                        o0 = (p - p0) * 512 + q * 128
                        P.M("transpose", out=pv[:, o0:o0 + 128], in_=tok4[:, q, p * 128:(p + 1) * 128],
                            identity=k.ident[:, :], accum=not (p == p0 and q == 0))
                P.A("activation", out=TTs[:, p0:p0 + 2, :, :].rearrange("p a q t -> p (a q t)"), in_=pv[:, 0:1024],
                    func=AF.Copy)
                yield
            yield
            XL = [D_["XL%d" % p] for p in range(4)]
            LT = [D_["LT%d" % p] for p in range(4)]
            ARB = [D_["ARB%d" % p] for p in range(4)]
            AK = [D_["AK%d" % p] for p in range(4)]
            PT = [D_["PT%d" % p] for p in range(4)]
            AKV = [D_["AKV%d" % p] for p in range(4)]
            Ub = [D_["Ub%d" % p] for p in range(4)]
            m2b = bc(m2[:, :, :].rearrange("p a t -> p (a t)"), [128, 2, 256], 1)
            for p in range(4):
                bB, bK, bL = pool.get(), pool.get(), pool.get()
                for e in range(2):
                    bs = slice(e * 64, (e + 1) * 64)
                    ar = TTs[bs, p, 0:2, :].rearrange("p q t -> p (q t)")
                    P.M("matmul", out=bB[:, e * 256:(e + 1) * 256], lhsT=TTs[bs, p, 2, :], rhs=ar, start=True, stop=True,
                        accum=(e > 0), drain=True)
                    P.M("matmul", out=bK[:, e * 256:(e + 1) * 256], lhsT=TTs[bs, p, 3, :], rhs=ar, start=True, stop=True,
                        accum=(e > 0))
                    P.M("matmul", out=bL[:, e * 128:(e + 1) * 128], lhsT=TTs[bs, p, 0, :], rhs=TTs[bs, p, 2, :], start=True,
                        stop=True, accum=(e > 0))
                bB4 = bB[:, 0:512].rearrange("p (e a t) -> p e a t", e=2, a=2)
                P.V("tensor_tensor", out=XL[p][:, :, 1, :], in0=bB4[:, :, 0, :], in1=bc(m2[:, 0, :], [128, 2, 128], 1),
                    op=ALU.mult)
                P.V("tensor_tensor", out=ARB[p][:, :, :], in0=bB4[:, :, 1, :], in1=bc(m2[:, 1, :], [128, 2, 128], 1),
                    op=ALU.mult)
                P.V("tensor_tensor", out=AK[p][:, :, :, :].rearrange("p e a t -> p e (a t)"),
                    in0=bK[:, 0:512].rearrange("p (e n) -> p e n", e=2), in1=m2b, op=ALU.mult)
                P.V("tensor_tensor", out=LT[p][:, :, :], in0=bL[:, 0:256].rearrange("p (e t) -> p e t", e=2),
                    in1=bc(mT[:, :], [128, 2, 128], 1), op=ALU.mult)
                P.G("tensor_copy", out=XL[p][:, :, 0, :], in_=bc(k.ident[:, :], [128, 2, 128], 1))
                yield
            yield
            for lev in range(7):
                last = (lev == 6)
                for p in range(4):
                    bb_ = pool.get()
                    for e in range(2):
                        if not last:
                            P.M("matmul", out=bb_[:, e * 256:(e + 1) * 256], lhsT=LT[p][:, e, :],
                                rhs=XL[p][:, e, :, :].rearrange("p a t -> p (a t)"), start=True, stop=True, accum=(e > 0))
                        else:
                            P.M("matmul", out=bb_[:, e * 256:e * 256 + 128], lhsT=LT[p][:, e, :],
                                rhs=XL[p][:, e, 0, :], start=True, stop=True, accum=(e > 0))
                    if not last:
                        ba_ = pool.get()
                        for e in range(2):
                            P.M("matmul", out=ba_[:, e * 128:(e + 1) * 128], lhsT=XL[p][:, e, 1, :], rhs=LT[p][:, e, :],
                                start=True, stop=True, accum=(e > 0))
                    b4 = bb_[:, 0:512].rearrange("p (e a t) -> p e a t", e=2, a=2)
                    P.V("tensor_tensor", out=XL[p][:, :, 0, :], in0=XL[p][:, :, 0, :], in1=b4[:, :, 0, :], op=ALU.add)
                    if not last:
                        P.A("activation", out=XL[p][:, :, 1, :], in_=b4[:, :, 1, :], func=AF.Copy)
                        P.A("activation", out=LT[p][:, :, :], in_=ba_[:, 0:256].rearrange("p (e t) -> p e t", e=2),
                            func=AF.Copy)
                    yield
            for p in range(4):
                pb = pool.get()
                P.M("matmul", out=pb[:, 0:256].rearrange("p (e t) -> p e t", e=2), lhsT=tok4[:, 0, p * 128:(p + 1) * 128],
                    rhs=XL[p][:, :, 0, :], start=True, stop=True)
                P.A("activation", out=PT[p][0:64, :], in_=pb[0:64, 0:128], func=AF.Copy)
                P.V("tensor_copy", out=PT[p][64:128, :], in_=pb[64:128, 128:256])
                pb2 = pool.get()
                for e in range(2):
                    h = 2 * p + e
                    P.M("matmul", out=pb2[:, e * 64:(e + 1) * 64], lhsT=AK[p][:, e, 0, :], rhs=vb[:, h * 64:(h + 1) * 64],
                        start=True, stop=True, accum=(e > 0))
                P.A("activation", out=AKV[p][:, :], in_=pb2[:, 0:128], func=AF.Copy)
                yield
            Y32 = D_["Y32"]
            for p in range(4):
                ps = slice(p * 128, (p + 1) * 128)
                bU = pool.get()
                for e in range(2):
                    P.M("matmul", out=bU[:, e * 64:(e + 1) * 64], lhsT=XL[p][:, e, 0, :], rhs=AKV[p][:, e * 64:(e + 1) * 64],
                        start=(e == 0), stop=False, accum=(e > 0))
                P.M("matmul", out=bU[:, 0:128], lhsT=PT[p][:, :], rhs=Hb[:, p, :], start=False, stop=True, accum=True)
                P.A("activation", out=Ub[p][:, :], in_=bU[:, 0:128], func=AF.Copy)
                bY = pool.get()
                for e in range(2):
                    h = 2 * p + e
                    P.M("matmul", out=bY[:, e * 64:(e + 1) * 64], lhsT=AK[p][:, e, 1, :], rhs=vb[:, h * 64:(h + 1) * 64],
                        start=(e == 0), stop=False, accum=(e > 0))
                P.M("matmul", out=bY[:, 0:128], lhsT=TTs[:, p, 1, :], rhs=Hb[:, p, :], start=False, stop=False, accum=True)
                for e in range(2):
                    P.M("matmul", out=bY[:, e * 64:(e + 1) * 64], lhsT=ARB[p][:, e, :], rhs=Ub[p][:, e * 64:(e + 1) * 64],
                        start=False, stop=(e == 1), accum=True)
                P.V("tensor_copy", out=Y32[:, ps], in_=bY[:, 0:128])
                bH = pool.get()
                P.M("matmul", out=bH[:, 0:128], lhsT=tok4[:, 2, ps], rhs=Ub[p][:, :], start=True, stop=False)
                P.M("matmul", out=bH[:, 0:128], lhsT=tok4[:, 3, ps], rhs=vb[:, ps], start=False, stop=True, accum=True)
                hT_ = D_["hT_"]
                P.V("tensor_tensor", out=hT_[:, :], in0=bH[:, 0:128], in1=H32[:, p, :], op=ALU.add)
                P.V("scalar_tensor_tensor", out=H32[:, p, :], in0=hT_[:, :], scalar=D_["gC"][:, p:p + 1], in1=bdm[:, :],
                    op0=ALU.mult, op1=ALU.mult)
                P.A("activation", out=Hb[:, p, :], in_=H32[:, p, :], func=AF.Copy)
                yield
            if d == 0:
                P.dma("pool", out=k.y_fw[t0:t0 + 128, :], in_=Y32[:, :])
            else:
                wkv, sq, bon = T[3], T[4], T[2]
                pb = pool.get()
                P.M("matmul", out=pb[:, 0:512], lhsT=lorT[0:64, 1, :], rhs=a2s[0:64, :], start=True, stop=True)
                P.V("tensor_tensor", out=T[0][:, :], in0=pb[:, 0:512], in1=a0o[:, :], op=ALU.add)
                P.A("activation", out=T[0][:, :], in_=T[0][:, :], func=AF.Sigmoid)
                P.V("scalar_tensor_tensor", out=T[0][:, :], in0=T[0][:, :], scalar=-1.0, in1=kab[:, :], op0=ALU.add,
                    op1=ALU.mult)
                P.V("scalar_tensor_tensor", out=T[0][:, :], in0=T[0][:, :], scalar=1.0, in1=kx, op0=ALU.add, op1=ALU.mult)
                P.G("tensor_tensor", out=T[0][:, :], in0=T[0][:, :], in1=kt[:, :], op=ALU.add)
                P.G("tensor_tensor", out=T[0][:, :], in0=T[0][:, :], in1=r_, op=ALU.mult)
                P.G("tensor_tensor", out=T[0][:, :], in0=T[0][:, :], in1=rkb[:, :], op=ALU.mult)
                P.V("tensor_reduce", out=sm[:, 24:32], in_=T[0][:, :].rearrange("p (h e) -> p h e", h=8), axis=AX.X,
                    op=ALU.add)
                P.V("tensor_tensor", out=bon[:, :].rearrange("p (h e) -> p h e", h=8),
                    in0=v_.rearrange("p (h e) -> p h e", h=8), in1=bc(sm[:, 24:32], [128, 8, 64], 2), op=ALU.mult)
                P.G("tensor_tensor", out=wkv[:, :], in0=Y32[:, :], in1=D_["yfw"][:, :], op=ALU.add)
                w3 = wkv[:, :].rearrange("p (h e) -> p h e", h=8)
                P.V("tensor_reduce", out=sm[:, 32:40], in_=w3, axis=AX.X, op=ALU.add)
                P.V("tensor_scalar", out=sm[:, 32:40], in0=sm[:, 32:40], scalar1=-1.0 / 64, scalar2=None, op0=ALU.mult)
                P.V("tensor_tensor", out=w3, in0=w3, in1=bc(sm[:, 32:40], [128, 8, 64], 2), op=ALU.add)
                P.A("activation", out=sq[:, :], in_=wkv[:, :], func=AF.Square)
                P.V("tensor_reduce", out=sm[:, 40:48], in_=sq[:, :].rearrange("p (h e) -> p h e", h=8), axis=AX.X, op=ALU.add)
                rstd_from_ss(k, sm[:, 48:56], sm[:, 40:48], sm[:, 56:64], 64.0, GN_EPS)
                P.V("tensor_tensor", out=w3, in0=w3, in1=bc(sm[:, 48:56], [128, 8, 64], 2), op=ALU.mult)
                P.G("tensor_tensor", out=wkv[:, :], in0=wkv[:, :], in1=lnw[:, :], op=ALU.mult)
                P.G("tensor_tensor", out=wkv[:, :], in0=wkv[:, :], in1=lnb[:, :], op=ALU.add)
                P.G("tensor_tensor", out=wkv[:, :], in0=wkv[:, :], in1=bon[:, :], op=ALU.add)
                pb = pool.get()
                P.M("matmul", out=pb[:, 0:512], lhsT=lorT[:, 2, :], rhs=g2s[:, :], start=True, stop=True)
                P.V("tensor_tensor", out=D_["ob"][:, :], in0=wkv[:, :], in1=pb[:, 0:512], op=ALU.mult)
                pb = pool.get()
                pv = bfv(pb)
                for c in range(4):
                    P.M("transpose", out=pv[:, c * 128:(c + 1) * 128], in_=D_["ob"][:, c * 128:(c + 1) * 128],
                        identity=k.ident[:, :], accum=(c > 0))
                P.A("activation", out=D_["oT"][:, :, :].rearrange("p c t -> p (c t)"), in_=pv[:, 0:512], func=AF.Copy)
                P.dma("pool", out=k.oT_rw[i], in_=D_["oT"][:, :, :])


        run_skewed([tile_gen(it, i) for it, i in enumerate(order)], ratio=RW_RATIO)


def phase_p3a(k, l, xin):
    nc, P, S, NT = k.nc, k.P, k.S, k.NT
    w = k.w
    with contextlib.ExitStack() as st:
        def SB(name, shape, dt):
            return st.enter_context(nc.sbuf_tensor(un("p3_" + name), list(shape), dt))
        alloc_banks(k, st, 6)
        pp2 = st.enter_context(nc.psum_tensor(un("p3_pp"), [128, 1024], F32))
        wg = SB("wg", [128, 8, 3072], BF16)
        wbr = SB("wbr", [128, 3, 4, 1024], BF16)
        wo = SB("wo", [128, 8, 1024], BF16)
        wq = SB("wq", [128, 8, 512], BF16)
        cwo = SB("cwo", [128, 4, 1024], BF16)
        bg = SB("bg", [1, 3072], BF16)
        ones = SB("ones", [1, 128], BF16)
        gcol = SB("gcol", [128, 16], F32)
        kmT = SB("kmT", [128, 4, 256], BF16)
        vm = SB("vm", [128, 2, 512], BF16)
        win = w["w_in"][l].rearrange("(c p) n -> p c n", p=128)
        for c in range(8):
            P.dma("pool", out=wg[:, c, :], in_=win[:, c, 3360:6432])
        for g in range(3):
            P.dma("pool", out=wbr[:, g, :, :], in_=w["w_branch"][l, g].rearrange("(c p) n -> p c n", p=128))
        P.dma("pool", out=wo[:, :, :], in_=w["w_out"][l].rearrange("(c p) n -> p c n", p=128))
        P.dma("pool", out=cwo[:, :, :], in_=w["cross_wo"][l].rearrange("(c p) n -> p c n", p=128))
        P.dma("pool", out=bg[0:1, :], in_=w["b_gate"][l:l + 1].rearrange("o g n -> o (g n)"))
        P.G("memset", ap=ones[0:1, :], constant=1.0)
        for c in range(8):
            load_col(k, "sp", gcol[:, c:c + 1], w["norm_cross"][l, c * 128:(c + 1) * 128], 128)
            load_col(k, "sp", gcol[:, 8 + c:9 + c], w["norm_mem"][l, c * 128:(c + 1) * 128], 128)
        with contextlib.ExitStack() as st2:
            stg = st2.enter_context(nc.sbuf_tensor(un("p3_stg"), [128, 8, 1024], F32))
            wkv = st2.enter_context(nc.sbuf_tensor(un("p3_wkv"), [128, 8, 1024], BF16))
            mx = st2.enter_context(nc.sbuf_tensor(un("p3_mx"), [128, 2, 1024], F32))
            mhb = st2.enter_context(nc.sbuf_tensor(un("p3_mhb"), [128, 2, 1024], BF16))
            mhT = st2.enter_context(nc.sbuf_tensor(un("p3_mhT"), [128, 8, 256], BF16))
            mjunk = st2.enter_context(nc.sbuf_tensor(un("p3_mjunk"), [128, 1024], BF16))
            mst = st2.enter_context(nc.sbuf_tensor(un("p3_mst"), [128, 8], F32))
            P.dma("sp", out=stg[:, :, 0:512], in_=w["cross_wq"][l].rearrange("(c p) n -> p c n", p=128))
            for c in range(8):
                P.V("tensor_scalar", out=wq[:, c, :], in0=stg[:, c, 0:512], scalar1=gcol[:, c:c + 1], scalar2=None,
                    op0=ALU.mult)
            P.dma("sp", out=stg[:, :, :], in_=w["cross_wkv"][l].rearrange("(c p) n -> p c n", p=128))
            for c in range(8):
                P.V("tensor_scalar", out=wkv[:, c, :], in0=stg[:, c, :], scalar1=gcol[:, 8 + c:9 + c], scalar2=None,
                    op0=ALU.mult)
            P.dma("sp", out=mx[:, :, :], in_=k.mem.rearrange("(j p) n -> p j n", p=128))
            for j in range(2):
                P.A("activation", out=mjunk[:, :], in_=mx[:, j, :], func=AF.Square, accum_out=mst[:, j:j + 1])
            rstd_from_ss(k, mst[:, 2:4], mst[:, 0:2], mst[:, 4:6], 1024.0, EPS)
            for j in range(2):
                P.V("tensor_scalar", out=mhb[:, j, :], in0=mx[:, j, :], scalar1=mst[:, 2 + j:3 + j], scalar2=None,
                    op0=ALU.mult)
                pb = bank(k)
                pv = bfv(pb)
                for c in range(8):
                    P.M("transpose", out=pv[:, c * 128:(c + 1) * 128], in_=mhb[:, j, c * 128:(c + 1) * 128],
                        identity=k.ident[:, :], accum=(c > 0))
                P.A("activation", out=mhT[:, :, j * 128:(j + 1) * 128],
                    in_=pv[:, 0:1024].rearrange("p (c t) -> p c t", c=8), func=AF.Copy)
            for h in range(4):
                pb = bank(k)
                for c in range(8):
                    P.M("matmul", out=pb[:, 0:256], lhsT=wkv[:, c, h * 128:(h + 1) * 128], rhs=mhT[:, c, :],
                        start=(c == 0), stop=(c == 7), accum=(c > 0))
                P.A("activation", out=kmT[:, h, :], in_=pb[:, 0:256], func=AF.Copy)
            for j in range(2):
                pb = bank(k)
                for c in range(8):
                    P.M("matmul", out=pb[:, 0:512], lhsT=mhT[:, c, j * 128:(j + 1) * 128], rhs=wkv[:, c, 512:1024],
                        start=(c == 0), stop=(c == 7), accum=(c > 0))
                P.A("activation", out=vm[:, j, :], in_=pb[:, 0:512], func=AF.Copy)
        P.barrier()

        sets = []
        for s_ in range(2):
            d = {}
            for name, shape, dt in [
                ("x", [128, 1024], F32), ("hT", [128, 8, 128], BF16), ("ob", [128, 2, 512], BF16),
                ("oT", [128, 3, 4, 128], BF16), ("gt", [128, 512], F32), ("tmp", [128, 512], F32),
                ("m", [128, 1024], F32), ("mb", [128, 1024], BF16), ("mT", [128, 8, 128], BF16),
                ("x1", [128, 1024], F32), ("junk", [128, 1024], BF16), ("st", [128, 16], F32),
                ("h2", [128, 1024], BF16), ("h2T", [128, 8, 128], BF16), ("qT", [128, 4, 128], BF16),
                ("p", [128, 4, 256], BF16), ("pn", [128, 4, 256], BF16), ("pT", [128, 8, 128], BF16),
                ("ocT", [128, 4, 128], BF16), ("x2", [128, 1024], F32),
            ]:
                d[name] = SB("%s%d" % (name, s_), shape, dt)
            sets.append(d)
        have_rw = "rb" in k.phases
        for i in range(NT):
            d = sets[i % 2]
            t0 = i * 128
            x, hT, oT, stt = d["x"], d["hT"], d["oT"], d["st"]
            P.dma("sp", out=x[:, :], in_=xin[t0:t0 + 128, :])
            P.dma("sp", out=hT[:, :, :], in_=k.hT_t[i])
            P.dma("sp", out=d["ob"][:, 0, :], in_=k.o_mla[t0:t0 + 128, :])
            P.dma("sp", out=d["ob"][:, 1, :], in_=k.o_gqa[t0:t0 + 128, :])
            if have_rw:
                P.dma("sp", out=oT[:, 2, :, :], in_=k.oT_rw[i])
            else:
                P.G("memset", ap=oT[:, 2, :, :], constant=0.0)
            pb = bank(k)
            pv = bfv(pb)
            obf = d["ob"][:, :, :].rearrange("p g n -> p (g n)")
            for c in range(8):
                P.M("transpose", out=pv[:, c * 128:(c + 1) * 128], in_=obf[:, c * 128:(c + 1) * 128],
                    identity=k.ident[:, :], accum=(c > 0))
            P.A("activation", out=oT[:, 0:2, :, :].rearrange("p g c t -> p (g c t)"), in_=pv[:, 0:1024], func=AF.Copy)
            for g in range(3):
                for n in range(2):
                    cs = slice(n * 512, (n + 1) * 512)
                    pz = bank(k)
                    for c in range(8):
                        P.M("matmul", out=pz[:, 0:512], lhsT=hT[:, c, :], rhs=wg[:, c, g * 1024 + n * 512:g * 1024 + (n + 1) * 512],
                            start=(c == 0), stop=False, accum=(c > 0))
                    P.M("matmul", out=pz[:, 0:512], lhsT=ones[0:1, :], rhs=bg[0:1, g * 1024 + n * 512:g * 1024 + (n + 1) * 512],
                        start=False, stop=True, accum=True)
                    P.A("activation", out=d["gt"][:, :], in_=pz[:, 0:512], func=AF.Sigmoid)
                    pbr = bank(k)
                    for c in range(4):
                        P.M("matmul", out=pbr[:, 0:512], lhsT=oT[:, g, c, :], rhs=wbr[:, g, c, cs],
                            start=(c == 0), stop=(c == 3), accum=(c > 0))
                    if g == 0:
                        P.V("tensor_tensor", out=d["m"][:, cs], in0=d["gt"][:, :], in1=pbr[:, 0:512], op=ALU.mult)
                    else:
                        P.V("tensor_tensor", out=d["tmp"][:, :], in0=d["gt"][:, :], in1=pbr[:, 0:512], op=ALU.mult)
                        if g == 1:
                            P.G("tensor_tensor", out=d["m"][:, cs], in0=d["m"][:, cs], in1=d["tmp"][:, :], op=ALU.add)
                        else:
                            P.G("tensor_tensor", out=d["mb"][:, cs], in0=d["m"][:, cs], in1=d["tmp"][:, :], op=ALU.add)
            pb = bank(k)
            pv = bfv(pb)
            for c in range(8):
                P.M("transpose", out=pv[:, c * 128:(c + 1) * 128], in_=d["mb"][:, c * 128:(c + 1) * 128],
                    identity=k.ident[:, :], accum=(c > 0))
            P.A("activation", out=d["mT"][:, :, :].rearrange("p c t -> p (c t)"), in_=pv[:, 0:1024], func=AF.Copy)
            for n in range(2):
                cs = slice(n * 512, (n + 1) * 512)
                pb = bank(k)
                for c in range(8):
                    P.M("matmul", out=pb[:, 0:512], lhsT=d["mT"][:, c, :], rhs=wo[:, c, cs],
                        start=(c == 0), stop=(c == 7), accum=(c > 0))
                P.V("tensor_tensor", out=d["x1"][:, cs], in0=x[:, cs], in1=pb[:, 0:512], op=ALU.add)
            x1 = d["x1"]
            P.A("activation", out=d["junk"][:, :], in_=x1[:, :], func=AF.Square, accum_out=stt[:, 0:1])
            rstd_from_ss(k, stt[:, 1:2], stt[:, 0:1], stt[:, 2:3], 1024.0, EPS)
            P.V("tensor_scalar", out=d["h2"][:, :], in0=x1[:, :], scalar1=stt[:, 1:2], scalar2=None, op0=ALU.mult)
            pb = bank(k)
            pv = bfv(pb)
            for c in range(8):
                P.M("transpose", out=pv[:, c * 128:(c + 1) * 128], in_=d["h2"][:, c * 128:(c + 1) * 128],
                    identity=k.ident[:, :], accum=(c > 0))
            P.A("activation", out=d["h2T"][:, :, :].rearrange("p c t -> p (c t)"), in_=pv[:, 0:1024], func=AF.Copy)
            pb = bank(k)
            for h in range(4):
                for c in range(8):
                    P.M("matmul", out=pb[:, h * 128:(h + 1) * 128], lhsT=wq[:, c, h * 128:(h + 1) * 128], rhs=d["h2T"][:, c, :],
                        start=(c == 0), stop=(c == 7), accum=(c > 0 or h > 0))
            P.A("activation", out=d["qT"][:, :, :].rearrange("p h t -> p (h t)"), in_=pb[:, 0:512], func=AF.Copy)
            for h in range(4):
                P.M("matmul", out=pp2[:, h * 256:(h + 1) * 256], lhsT=d["qT"][:, h, :], rhs=kmT[:, h, :],
                    start=True, stop=True, accum=(h > 0))
            for h in range(4):
                P.A("activation", out=d["p"][:, h, :], in_=pp2[:, h * 256:(h + 1) * 256], func=AF.Exp,
                    scale=128.0 ** -0.5, accum_out=stt[:, 4 + h:5 + h])
            P.V("reciprocal", out=stt[:, 8:12], in_=stt[:, 4:8])
            P.V("tensor_tensor", out=d["pn"][:, :, :], in0=d["p"][:, :, :], in1=bc(stt[:, 8:12], [128, 4, 256], 2),
                op=ALU.mult)
            pb = bank(k)
            pv = bfv(pb)
            pnf = d["pn"][:, :, :].rearrange("p h m -> p (h m)")
            for c in range(8):
                P.M("transpose", out=pv[:, c * 128:(c + 1) * 128], in_=pnf[:, c * 128:(c + 1) * 128],
                    identity=k.ident[:, :], accum=(c > 0))
            P.A("activation", out=d["pT"][:, :, :].rearrange("p c t -> p (c t)"), in_=pv[:, 0:1024], func=AF.Copy)
            pb = bank(k)
            for h in range(4):
                for mc in range(2):
                    P.M("matmul", out=pb[:, h * 128:(h + 1) * 128], lhsT=vm[:, mc, h * 128:(h + 1) * 128],
                        rhs=d["pT"][:, h * 2 + mc, :], start=(mc == 0), stop=(mc == 1), accum=(mc > 0 or h > 0))
            P.A("activation", out=d["ocT"][:, :, :].rearrange("p h t -> p (h t)"), in_=pb[:, 0:512], func=AF.Copy)
            for n in range(2):
                cs = slice(n * 512, (n + 1) * 512)
                pb = bank(k)
                for h in range(4):
                    P.M("matmul", out=pb[:, 0:512], lhsT=d["ocT"][:, h, :], rhs=cwo[:, h, cs],
                        start=(h == 0), stop=(h == 3), accum=(h > 0))
                P.V("tensor_tensor", out=d["x2"][:, cs], in0=x1[:, cs], in1=pb[:, 0:512], op=ALU.add)
            P.dma("pool", out=k.x2[t0:t0 + 128, :], in_=d["x2"][:, :])


def phase_p3b(k, l, xout, last):
    nc, P, S = k.nc, k.P, k.S
    w = k.w
    TT = 256
    NTT = S // TT
    with contextlib.ExitStack() as st:
        def SB(name, shape, dt):
            return st.enter_context(nc.sbuf_tensor(un("p4_" + name), list(shape), dt))
        alloc_banks(k, st)
        w1 = SB("w1", [128, 8, 4096], BF16)
        w2 = SB("w2", [128, 32, 1024], BF16)
        gcol = SB("gcol", [128, 8], F32)
        for c in range(8):
            load_col(k, "sp", gcol[:, c:c + 1], w["norm_mlp"][l, c * 128:(c + 1) * 128], 128)
        with contextlib.ExitStack() as st2:
            stgs = [st2.enter_context(nc.sbuf_tensor(un("p4_stg%d" % i), [128, 4096], F32)) for i in range(2)]
            w1v = w["mlp_w1"][l].rearrange("(c p) n -> p c n", p=128)
            for c in range(8):
                sg = stgs[c % 2]
                P.dma("sp", out=sg[:, :], in_=w1v[:, c, :])
                for hf in range(2):
                    P.V("tensor_scalar", out=w1[:, c, hf * 2048:(hf + 1) * 2048], in0=sg[:, hf * 2048:(hf + 1) * 2048],
                        scalar1=gcol[:, c:c + 1], scalar2=None, op0=ALU.mult)
        P.barrier()
        w2v = w["mlp_w2"][l].rearrange("(f p) n -> p f n", p=128)
        for f0 in range(0, 32, 4):
            P.dma("pool", out=w2[:, f0:f0 + 4, :], in_=w2v[:, f0:f0 + 4, :])
        if last:
            gf = SB("gf", [128, 1024], F32)
            P.dma("sp", out=gf[:, :], in_=w["norm_final"][0:1, :].broadcast_to([128, 1024]))
        uT = SB("uT", [128, 32, TT], BF16)
        rbuf = [SB("r%d" % i, [128, TT], BF16) for i in range(3)]
        junk = SB("junk", [128, 1024], BF16)
        hm = [SB("hm%d" % i, [128, 1024], BF16) for i in range(2)]
        sets = []
        for s_ in range(2):
            d = {}
            for name, shape, dt in [("x", [128, 2, 1024], F32), ("hmT", [128, 8, TT], BF16), ("st", [128, 16], F32),
                                    ("y", [128, 2, 1024], F32)]:
                if name == "y" and not last:
                    continue
                d[name] = SB("%s%d" % (name, s_), shape, dt)
            sets.append(d)
        nr = 0
        for i in range(NTT):
            d = sets[i % 2]
            t0 = i * TT
            x, hmT, stt = d["x"], d["hmT"], d["st"]
            P.dma("sp", out=x[:, :, :], in_=k.x2[t0:t0 + TT, :].rearrange("(j p) n -> p j n", p=128))
            for j in range(2):
                P.A("activation", out=junk[:, :], in_=x[:, j, :], func=AF.Square, accum_out=stt[:, j:j + 1])
            rstd_from_ss(k, stt[:, 2:4], stt[:, 0:2], stt[:, 4:6], 1024.0, EPS)
            for j in range(2):
                P.V("tensor_scalar", out=hm[j][:, :], in0=x[:, j, :], scalar1=stt[:, 2 + j:3 + j], scalar2=None,
                    op0=ALU.mult)
                pb = bank(k)
                pv = bfv(pb)
                for c in range(8):
                    P.M("transpose", out=pv[:, c * 128:(c + 1) * 128], in_=hm[j][:, c * 128:(c + 1) * 128],
                        identity=k.ident[:, :], accum=(c > 0))
                P.A("activation", out=hmT[:, :, j * 128:(j + 1) * 128],
                    in_=pv[:, 0:1024].rearrange("p (c t) -> p c t", c=8), func=AF.Copy)
            for f in range(32):
                pb = bank(k)
                for c in range(8):
                    P.M("matmul", out=pb[:, 0:TT], lhsT=w1[:, c, f * 128:(f + 1) * 128], rhs=hmT[:, c, :],
                        start=(c == 0), stop=(c == 7), accum=(c > 0))
                r = rbuf[nr % 3]
                nr += 1
                P.A("activation", out=r[:, :], in_=pb[:, 0:TT], func=AF.Relu)
                P.G("tensor_tensor", out=uT[:, f, :], in0=r[:, :], in1=r[:, :], op=ALU.mult)
            for j in range(2):
                for n in range(2):
                    cs = slice(n * 512, (n + 1) * 512)
                    pb = bank(k)
                    for f in range(32):
                        P.M("matmul", out=pb[:, 0:512], lhsT=uT[:, f, j * 128:(j + 1) * 128], rhs=w2[:, f, cs],
                            start=(f == 0), stop=(f == 31), accum=(f > 0))
                    P.V("tensor_tensor", out=x[:, j, cs], in0=x[:, j, cs], in1=pb[:, 0:512], op=ALU.add)
            if not last:
                P.dma("pool", out=xout[t0:t0 + TT, :].rearrange("(j p) n -> p j n", p=128), in_=x[:, :, :])
            else:
                for j in range(2):
                    P.A("activation", out=junk[:, :], in_=x[:, j, :], func=AF.Square, accum_out=stt[:, 8 + j:9 + j])
                rstd_from_ss(k, stt[:, 10:12], stt[:, 8:10], stt[:, 12:14], 1024.0, EPS)
                for j in range(2):
                    P.V("scalar_tensor_tensor", out=d["y"][:, j, :], in0=x[:, j, :], scalar=stt[:, 10 + j:11 + j],
                        in1=gf[:, :], op0=ALU.mult, op1=ALU.mult)
                P.dma("pool", out=k.y[t0:t0 + TT, :].rearrange("(j p) n -> p j n", p=128), in_=d["y"][:, :, :])
                if "xs0" in k.dbg or "xs1" in k.dbg:
                    P.dma("pool", out=xout[t0:t0 + TT, :].rearrange("(j p) n -> p j n", p=128), in_=x[:, :, :])


def rope_tables(pos, dim):
    inv = (10000.0 ** (-np.arange(0, dim, 2, dtype=np.float32) / dim)).astype(np.float32)
    ang = pos.astype(np.float32)[:, None] * inv[None, :]
    return np.cos(ang).astype(np.float32), np.sin(ang).astype(np.float32)


def const_inputs(S):
    pos = np.arange(S)
    c1, s1 = rope_tables(pos, 32)
    cr, sr = rope_tables(pos // 64, 32)
    cc, sc = rope_tables(pos % 64, 32)
    tab1 = np.stack([c1, s1], 1).astype(np.float32)
    tab2 = np.stack([np.stack([cr, cc], 1), np.stack([sr, sc], 1)], 1).astype(np.float32)
    s_idx = np.arange(128)[:, None]
    t_idx = np.arange(128)[None, :]
    msk = np.zeros((2, 2, 128, 128), np.float32)
    msk[0, 0] = s_idx < t_idx
    msk[0, 1] = s_idx <= t_idx
    msk[1, 0] = s_idx > t_idx
    msk[1, 1] = s_idx >= t_idx
    bdm = np.zeros((128, 128), np.float32)
    bdm[:64, :64] = 1
    bdm[64:, 64:] = 1
    return dict(tab1=tab1, tab2=tab2, ident=np.eye(128, dtype=np.float32), msk=msk, bdm=bdm)


_NC_CACHE = {}


def kernel(**inputs):
    S, depth = 8192, 2
    key = (S, depth)
    if key not in _NC_CACHE:
        _NC_CACHE[key] = build(S, depth)
    nc = _NC_CACHE[key]
    xp, xs = np.asarray(inputs["x_prompt"]), np.asarray(inputs["x_sample"])
    mp, ms = np.asarray(inputs["mem_prompt"]), np.asarray(inputs["mem_sample"])
    seqs = [(xp[b], mp[b]) for b in range(xp.shape[0])] + [(xs[b], ms[b]) for b in range(xs.shape[0])]
    n_seq = len(seqs)
    consts = const_inputs(S)
    wts = {name: np.ascontiguousarray(np.asarray(inputs[name], np.float32)) for name, _ in W_SPECS}
    wts["norm_final"] = np.ascontiguousarray(np.asarray(inputs["norm_final"], np.float32).reshape(1, D))
    in_maps = []
    for c in range(8):
        x, mem = seqs[c % n_seq]
        m = dict(x=np.ascontiguousarray(x, np.float32), mem=np.ascontiguousarray(mem, np.float32))
        m.update(wts)
        m.update(consts)
        in_maps.append(m)
    res = run_bass_kernel_spmd(nc, in_maps, core_ids=list(range(8)))
    ys = [np.asarray(res.results[c]["y"], np.float32) for c in range(n_seq)]
    y_prompt = np.stack(ys[:xp.shape[0]], 0)
    y_sample = np.stack(ys[xp.shape[0]:], 0)
    return (y_prompt, y_sample)
```

```python
import contextlib
import math
import numpy as np
import ml_dtypes
import concourse.bass as bass
import concourse.mybir as mybir
from concourse.bass_utils import run_bass_kernel_spmd

F32 = mybir.dt.float32
BF16 = mybir.dt.bfloat16
ALU = mybir.AluOpType
AF = mybir.ActivationFunctionType
AX = mybir.AxisListType

D = 1024
NMEM = 256
IN_COLS = 6432
EPS = 1e-6
GN_EPS = 64e-5
DECAY_C = math.exp(-0.5)

import os
MAXOPS = int(os.environ.get("MAXOPS", "100000000"))
RW_RATIO = int(os.environ.get("RW_RATIO", "2"))
RW_DELAY = int(os.environ.get("RW_DELAY", "2"))
ENGS = ("pe", "act", "dve", "pool", "sp")
N_DMA_SEMS = 16
READ_KEYS = ("in_", "in0", "in1", "lhsT", "rhs", "scalar", "scalar1", "scalar2", "bias", "scale",
             "identity", "data0", "data1", "initial")
WRITE_KEYS = ("out", "accum_out", "ap")


class Buf:
    __slots__ = ("name", "w", "r")

    def __init__(self, name):
        self.name = name
        self.w = None
        self.r = {}


class Prog:
    def __init__(self, nc):
        self.nc = nc
        self.es = contextlib.ExitStack()
        self.eobj = {"pe": nc.tensor, "act": nc.scalar, "dve": nc.vector, "pool": nc.gpsimd, "sp": nc.sync}
        self.sem = {}
        self.cnt = {}
        for e in ENGS:
            self.sem[e] = self.es.enter_context(nc.semaphore("s_" + e))
            self.cnt[e] = 0
        self.dq = {}
        for q in ("sp", "act", "pool"):
            sems = []
            for i in range(N_DMA_SEMS):
                k = "d_%s_%d" % (q, i)
                self.sem[k] = self.es.enter_context(nc.semaphore(k))
                self.cnt[k] = 0
                sems.append(k)
            self.dq[q] = [sems, 0]
        self.seen = {e: {} for e in ENGS}
        self.ops = {e: [] for e in ENGS}
        self.bufs = {}
        self.nops = 0

    def buf_of(self, ap):
        n = ap.name
        b = self.bufs.get(n)
        if b is None:
            b = self.bufs[n] = Buf(n)
        return b

    def _need(self, e, key, val, waits):
        if self.seen[e].get(key, 0) >= val:
            return
        self.seen[e][key] = val
        waits.append((key, val))

    def _collect(self, kw, extra_r, extra_w):
        reads, writes = [], []
        for k in READ_KEYS:
            v = kw.get(k)
            if v is not None and hasattr(v, "name") and hasattr(v, "ap"):
                reads.append(self.buf_of(v))
        for k in WRITE_KEYS:
            v = kw.get(k)
            if v is not None and hasattr(v, "name") and hasattr(v, "ap"):
                writes.append(self.buf_of(v))
        for v in extra_r:
            reads.append(v if isinstance(v, Buf) else self.buf_of(v))
        for v in extra_w:
            writes.append(v if isinstance(v, Buf) else self.buf_of(v))
        return reads, writes

    def _issue(self, e, inckey, incv, name, kw, reads, writes, accum):
        self.nissued = getattr(self, "nissued", 0) + 1
        if self.nissued > MAXOPS:
            return
        if self.nissued == MAXOPS:
            print("LAST OP:", e, name, {a: (str(b.name) + str(b.shape) if hasattr(b, "ap") else b) for a, b in kw.items()})
        need = {}
        for b in reads:
            if b.w is not None and need.get(b.w[0], 0) < b.w[1]:
                need[b.w[0]] = b.w[1]
        for b in writes:
            if b.w is not None and not (e == "pe" and b.w[0] == "pe") and need.get(b.w[0], 0) < b.w[1]:
                need[b.w[0]] = b.w[1]
            for k, v in b.r.items():
                if need.get(k, 0) < v:
                    need[k] = v
        waits = []
        for k, v in need.items():
            self._need(e, k, v, waits)
        self.cnt[inckey] += incv
        v = self.cnt[inckey]
        self.ops[e].append((waits, name, kw, inckey, incv))
        self.nops += 1 + len(waits)
        for b in reads:
            if b.r.get(inckey, 0) < v:
                b.r[inckey] = v
        for b in writes:
            b.w = (inckey, v)
            b.r = {}

    def op(self, e, name, R=(), W=(), accum=False, drain=False, **kw):
        reads, writes = self._collect(kw, R, W)
        if drain and self.cnt[e] > 0:
            d_ = Buf("drain")
            d_.w = (e, self.cnt[e])
            reads = list(reads) + [d_]
        self._issue(e, e, 1, name, kw, reads, writes, accum)

    def V(self, name, **kw):
        self.op("dve", name, **kw)

    def A(self, name, **kw):
        self.op("act", name, **kw)

    def G(self, name, **kw):
        self.op("pool", name, **kw)

    def M(self, name="matmul", **kw):
        self.op("pe", name, **kw)

    def dma(self, q, out, in_, R=(), W=()):
        kw = dict(out=out, in_=in_)
        reads, writes = self._collect(kw, R, W)
        sems, idx = self.dq[q]
        key = sems[idx % len(sems)]
        self.dq[q][1] = idx + 1
        if self.cnt[key] > 0:
            w = []
            self._need(q, key, self.cnt[key], w)
            pre = w
        else:
            pre = []
        n0 = len(self.ops[q])
        self._issue(q, key, 16, "dma_start", kw, reads, writes, False)
        if pre and len(self.ops[q]) > n0:
            waits, name, kw2, ik, iv = self.ops[q][n0]
            self.ops[q][n0] = (pre + waits, name, kw2, ik, iv)

    def barrier(self):
        for e in ENGS:
            waits = []
            for key, v in self.cnt.items():
                if v > 0:
                    self._need(e, key, v, waits)
            if waits:
                self.ops[e].append((waits, None, None, None, None))
                self.nops += len(waits)

    def wait_bufs(self, e, aps):
        waits = []
        for a in aps:
            b = a if isinstance(a, Buf) else self.buf_of(a)
            if b.w is not None:
                self._need(e, b.w[0], b.w[1], waits)
        self.ops[e].append((waits, None, None, None, None))

    def emit(self):
        nc = self.nc
        with nc.Block() as block:
            def run(e):
                def body(engine):
                    for waits, name, kw, ik, iv in self.ops[e]:
                        for k, v in waits:
                            engine.wait_ge(self.sem[k], v)
                        if name is None:
                            continue
                        inst = getattr(engine, name)(**kw)
                        inst.then_inc(self.sem[ik], iv)
                return body
            block.sync(run("sp"))
            block.scalar(run("act"))
            block.vector(run("dve"))
            block.gpsimd(run("pool"))
            block.tensor(run("pe"))

    def close(self):
        self.es.close()


W_SPECS = [
    ("norm_mix", (D,)), ("w_in", (D, IN_COLS)), ("mla_q_norm", (384,)), ("mla_w_uq", (384, 768)),
    ("mla_kv_norm", (256,)), ("mla_w_ukv", (256, 1024)), ("gqa_q_norm", (64,)), ("gqa_k_norm", (64,)),
    ("rwkv_mu", (1920,)), ("rwkv_w0", (2, 512)), ("rwkv_w2", (2, 64, 512)), ("rwkv_a0", (2, 512)),
    ("rwkv_a2", (2, 64, 512)), ("rwkv_g2", (128, 512)), ("rwkv_k_k", (512,)), ("rwkv_k_a", (512,)),
    ("rwkv_r_k", (8, 64)), ("rwkv_ln_w", (512,)), ("rwkv_ln_b", (512,)), ("w_branch", (3, 512, D)),
    ("b_gate", (3, D)), ("w_out", (D, D)), ("norm_cross", (D,)), ("norm_mem", (D,)),
    ("cross_wq", (D, 512)), ("cross_wkv", (D, 1024)), ("cross_wo", (512, D)), ("norm_mlp", (D,)),
    ("mlp_w1", (D, 4096)), ("mlp_w2", (4096, D)),
]


class K:
    pass


def build(S, depth, dbg=(), phases=("p1", "p2", "rf", "rb", "p3a", "p3b")):
    NT = S // 128
    nc = bass.Bass("TRN2", target_bir_lowering=False)
    P = Prog(nc)
    k = K()
    k.nc, k.P, k.S, k.NT, k.depth, k.dbg = nc, P, S, NT, depth, dbg
    k.phases = phases
    k.y_done = set()

    def din(name, shape):
        return nc.dram_tensor(name, list(shape), F32, kind="ExternalInput").ap()

    def dscr(name, shape, dt):
        kind = "ExternalOutput" if name in dbg else "Internal"
        return nc.dram_tensor(name, list(shape), dt, kind=kind).ap()

    k.x = din("x", (S, D))
    k.mem = din("mem", (NMEM, D))
    k.w = {}
    for name, shp in W_SPECS:
        k.w[name] = din(name, (depth,) + shp)
    k.w["norm_final"] = din("norm_final", (1, D))
    k.tab1 = din("tab1", (S, 2, 16))
    k.tab2 = din("tab2", (S, 2, 2, 16))
    k.ident_d = din("ident", (128, 128))
    k.msk_d = din("msk", (2, 2, 128, 128))
    k.bdm_d = din("bdm", (128, 128))
    k.y = nc.dram_tensor("y", [S, D], F32, kind="ExternalOutput").ap()

    k.h_tm = dscr("h_tm", (S, D), BF16)
    k.hT_t = dscr("hT_t", (NT, 128, 8, 128), BF16)
    k.qT_mla = dscr("qT_mla", (8, 96, S), BF16)
    k.kT_mla = dscr("kT_mla", (8, 96, S), BF16)
    k.v_mla = dscr("v_mla", (S, 512), BF16)
    k.qT_gqa = dscr("qT_gqa", (512, S), BF16)
    k.kT_gqa = dscr("kT_gqa", (128, S), BF16)
    k.v_gqa = dscr("v_gqa", (S, 128), BF16)
    k.o_mla = dscr("o_mla", (S, 512), BF16)
    k.o_gqa = dscr("o_gqa", (S, 512), BF16)
    k.y_fw = dscr("y_fw", (S, 512), F32)
    k.z_r = dscr("z_r", (S, 1920), F32)
    k.zmix = dscr("zmix", (S, 1920), F32)
    k.oT_rw = dscr("oT_rw", (NT, 128, 4, 128), BF16)
    k.x2 = dscr("x2", (S, D), F32)
    k.xs = [dscr("xs0", (S, D), F32), dscr("xs1", (S, D), F32)]

    with contextlib.ExitStack() as gst:
        k.gst = gst
        k.rot = [0]
        k.ident = gst.enter_context(nc.sbuf_tensor("ident_b", [128, 128], BF16))
        P.dma("pool", out=k.ident[:, :], in_=k.ident_d)
        k.identf = gst.enter_context(nc.sbuf_tensor("ident_f", [128, 128], F32))
        P.dma("sp", out=k.identf[:, :], in_=k.ident_d)
        for l in range(depth):
            xin = k.x if l == 0 else k.xs[(l - 1) % 2]
            xout = k.xs[l % 2]
            if "p1" in phases:
                phase_p1(k, l, xin)
                P.barrier()
            if "p2" in phases:
                phase_attn(k, l)
                P.barrier()
            if "rw2" in phases:
                phase_rwkv2(k, l)
                P.barrier()
            if "rf" in phases:
                phase_rwkv(k, l, 0)
                P.barrier()
            if "rb" in phases:
                phase_rwkv(k, l, 1)
                P.barrier()
            if "p3a" in phases:
                phase_p3a(k, l, xin)
                P.barrier()
            if "p3b" in phases:
                phase_p3b(k, l, xout, last=(l == depth - 1))
                P.barrier()
        outs = [k.y]
        for name in dbg:
            outs.append(P.bufs[name]) if name in P.bufs else None
        P.wait_bufs("sp", outs)
        P.emit()
    P.close()
    return nc


_UID = [0]


def un(name):
    _UID[0] += 1
    return "%s_u%d" % (name, _UID[0])


def alloc_banks(k, st, n=8):
    k.banks = [st.enter_context(k.nc.psum_tensor(un("bank%d" % i), [128, 512], F32)) for i in range(n)]


class BankPool:
    def __init__(self, banks):
        self.banks = banks
        self.i = 0

    def get(self):
        b = self.banks[self.i % len(self.banks)]
        self.i += 1
        return b


def run_skewed(gens, ratio=1):
    old = None
    for new in gens:
        new_mid = False
        while True:
            for _ in range(ratio):
                if old is not None:
                    try:
                        next(old)
                    except StopIteration:
                        old = None
            if not new_mid:
                try:
                    if next(new) == "mid":
                        new_mid = True
                except StopIteration:
                    new_mid = True
                    new = None
            if old is None and new_mid:
                break
        old = new
    while old is not None:
        try:
            next(old)
        except StopIteration:
            old = None


def bank(k, lo=0, hi=None):
    if hi is None:
        hi = len(k.banks)
    n = hi - lo
    i = k.rot[0] % n
    k.rot[0] += 1
    return k.banks[lo + i]


def bfv(b, ncols=1024):
    return b[:, :].bitcast(BF16)


def rstd_from_ss(k, out, ss, t, n, eps):
    P = k.P
    if eps is not None and eps != 0.0:
        P.V("tensor_scalar", out=t, in0=ss, scalar1=1.0 / n, scalar2=float(eps), op0=ALU.mult, op1=ALU.add)
        P.A("activation", out=t, in_=t, func=AF.Ln)
    else:
        P.A("activation", out=t, in_=ss, func=AF.Ln, scale=1.0 / n)
    P.A("activation", out=out, in_=t, func=AF.Exp, scale=-0.5)


def load_col(k, q, dst, src1d, n):
    k.P.dma(q, out=dst, in_=src1d.rearrange("(p o) -> p o", o=1))


def bc(ap, shape, axis):
    return ap.unsqueeze(axis).broadcast_to(list(shape))


def phase_p1(k, l, xin):
    nc, P, S, NT = k.nc, k.P, k.S, k.NT
    w = k.w
    with contextlib.ExitStack() as st:
        def SB(name, shape, dt):
            return st.enter_context(nc.sbuf_tensor(un("p1_" + name), list(shape), dt))
        alloc_banks(k, st)
        w_att = SB("watt", [128, 8, 1440], BF16)
        w_r = SB("wr", [128, 8, 1920], BF16)
        w_uq = SB("wuq", [128, 3, 768], BF16)
        w_ukv = SB("wukv", [128, 2, 1024], BF16)
        stg = SB("stg", [128, 2304], F32)
        g_bc = SB("gbc", [128, 1024], F32)
        gq_bc = SB("gqbc", [128, 64], F32)
        gk_bc = SB("gkbc", [128, 64], F32)
        gcol = SB("gcol", [128, 5], F32)
        win = w["w_in"][l].rearrange("(c p) n -> p c n", p=128)
        for c in range(8):
            P.dma("pool", out=w_att[:, c, :], in_=win[:, c, 0:1440])
        for c in range(8):
            P.dma("pool", out=w_r[:, c, :], in_=win[:, c, 1440:3360])
        P.dma("sp", out=g_bc[:, :], in_=w["norm_mix"][l:l + 1, :].broadcast_to([128, 1024]))
        P.dma("sp", out=gq_bc[:, :], in_=w["gqa_q_norm"][l:l + 1, :].broadcast_to([128, 64]))
        P.dma("sp", out=gk_bc[:, :], in_=w["gqa_k_norm"][l:l + 1, :].broadcast_to([128, 64]))
        for c in range(3):
            load_col(k, "sp", gcol[:, c:c + 1], w["mla_q_norm"][l, c * 128:(c + 1) * 128], 128)
        for c in range(2):
            load_col(k, "sp", gcol[:, 3 + c:4 + c], w["mla_kv_norm"][l, c * 128:(c + 1) * 128], 128)
        P.dma("sp", out=stg[:, 0:2304].rearrange("p (c n) -> p c n", c=3),
              in_=w["mla_w_uq"][l].rearrange("(c p) n -> p c n", p=128))
        for c in range(3):
            P.V("tensor_scalar", out=w_uq[:, c, :], in0=stg[:, c * 768:(c + 1) * 768],
                scalar1=gcol[:, c:c + 1], scalar2=None, op0=ALU.mult)
        P.dma("sp", out=stg[:, 0:2048].rearrange("p (c n) -> p c n", c=2),
              in_=w["mla_w_ukv"][l].rearrange("(c p) n -> p c n", p=128))
        for c in range(2):
            P.V("tensor_scalar", out=w_ukv[:, c, :], in0=stg[:, c * 1024:(c + 1) * 1024],
                scalar1=gcol[:, 3 + c:4 + c], scalar2=None, op0=ALU.mult)

        sets = []
        for s in range(2):
            d = {}
            for name, shape, dt in [
                ("x", [128, 1024], F32), ("junk", [128, 1024], BF16), ("hb", [128, 1024], BF16),
                ("hT", [128, 8, 128], BF16), ("tb1", [128, 2, 16], F32), ("tb2", [128, 2, 2, 16], F32),
                ("st", [128, 16], F32), ("cqb", [128, 384], BF16), ("cqT", [128, 3, 128], BF16),
                ("q32", [128, 8, 96], F32), ("qrot", [128, 8, 96], BF16), ("qT", [96, 8, 128], BF16),
                ("ta", [128, 8, 2, 16], F32), ("tb", [128, 8, 2, 16], F32),
                ("ckvb", [128, 256], BF16), ("ckvT", [128, 2, 128], BF16), ("kt", [128, 8, 96], BF16),
                ("vt", [128, 8, 64], BF16), ("kr", [128, 32], F32), ("kT", [96, 8, 128], BF16),
                ("sq", [128, 512], F32), ("gst", [128, 24], F32), ("qn", [128, 8, 64], F32),
                ("gqr", [128, 8, 64], BF16), ("gqT", [128, 4, 128], BF16),
                ("kn", [128, 2, 64], F32), ("gkr", [128, 2, 64], BF16), ("gkT", [128, 128], BF16),
                ("gv", [128, 128], BF16), ("zr", [128, 1920], F32),
            ]:
                d[name] = SB("%s%d" % (name, s), shape, dt)
            sets.append(d)

        ident = k.ident
        poolA = BankPool(k.banks[0:3])
        poolB = BankPool(k.banks[3:8])

        def tile_gen(i):
            pool = poolA
            d = sets[i % 2]
            t0 = i * 128
            x, hb, hT, stt = d["x"], d["hb"], d["hT"], d["st"]
            P.dma("sp", out=x[:, :], in_=xin[t0:t0 + 128, :])
            P.dma("sp", out=d["tb1"][:, :, :], in_=k.tab1[t0:t0 + 128])
            P.dma("sp", out=d["tb2"][:, :, :, :], in_=k.tab2[t0:t0 + 128])
            P.A("activation", out=d["junk"][:, :], in_=x[:, :], func=AF.Square, accum_out=stt[:, 0:1])
            rstd_from_ss(k, stt[:, 1:2], stt[:, 0:1], stt[:, 2:3], 1024.0, EPS)
            P.V("scalar_tensor_tensor", out=hb[:, :], in0=x[:, :], scalar=stt[:, 1:2], in1=g_bc[:, :],
                op0=ALU.mult, op1=ALU.mult)
            P.dma("pool", out=k.h_tm[t0:t0 + 128, :], in_=hb[:, :])
            pb = pool.get()
            pv = bfv(pb)
            for c in range(8):
                P.M("transpose", out=pv[:, c * 128:(c + 1) * 128], in_=hb[:, c * 128:(c + 1) * 128],
                    identity=ident[:, :], accum=(c > 0))
            P.A("activation", out=hT[:, :, :].rearrange("p c t -> p (c t)"), in_=pv[:, 0:1024], func=AF.Copy)
            P.dma("pool", out=k.hT_t[i], in_=hT[:, :, :])
            yield

            def proj(c0, n):
                nonlocal pool
                b = pool.get()
                for c in range(8):
                    P.M("matmul", out=b[:, 0:n], lhsT=hT[:, c, :], rhs=w_att[:, c, c0:c0 + n],
                        start=(c == 0), stop=(c == 7), accum=(c > 0))
                return b

            for ci, (c0, n) in enumerate(((0, 512), (512, 512), (1024, 512), (1536, 384))):
                b = pool.get()
                for c in range(8):
                    P.M("matmul", out=b[:, 0:n], lhsT=hT[:, c, :], rhs=w_r[:, c, c0:c0 + n],
                        start=(c == 0), stop=(c == 7), accum=(c > 0))
                if ci % 2 == 0:
                    P.A("activation", out=d["zr"][:, c0:c0 + n], in_=b[:, 0:n], func=AF.Copy)
                else:
                    P.V("tensor_copy", out=d["zr"][:, c0:c0 + n], in_=b[:, 0:n])
                yield
            P.dma("pool", out=k.z_r[t0:t0 + 128, :], in_=d["zr"][:, :])
            pool = poolB
            yield "mid"
            cos1 = bc(d["tb1"][:, 0, :], [128, 8, 16], 1)
            sin1 = bc(d["tb1"][:, 1, :], [128, 8, 16], 1)
            pq = proj(0, 384)
            P.A("activation", out=d["junk"][:, 0:384], in_=pq[:, 0:384], func=AF.Square, accum_out=stt[:, 3:4])
            rstd_from_ss(k, stt[:, 4:5], stt[:, 3:4], stt[:, 5:6], 384.0, EPS)
            P.V("tensor_copy", out=d["cqb"][:, :], in_=pq[:, 0:384])
            yield
            pb = pool.get()
            pv = bfv(pb)
            for c in range(3):
                P.M("transpose", out=pv[:, c * 128:(c + 1) * 128], in_=d["cqb"][:, c * 128:(c + 1) * 128],
                    identity=ident[:, :], accum=(c > 0))
            P.A("activation", out=d["cqT"][:, :, :].rearrange("p c t -> p (c t)"), in_=pv[:, 0:384], func=AF.Copy)
            q32 = d["q32"]
            q32f = q32[:, :, :].rearrange("p h e -> p (h e)")
            for (c0, n) in ((0, 512), (512, 256)):
                b = pool.get()
                for c in range(3):
                    P.M("matmul", out=b[:, 0:n], lhsT=d["cqT"][:, c, :], rhs=w_uq[:, c, c0:c0 + n],
                        start=(c == 0), stop=(c == 2), accum=(c > 0))
                P.V("tensor_scalar", out=q32f[:, c0:c0 + n], in0=b[:, 0:n], scalar1=stt[:, 4:5], scalar2=None,
                    op0=ALU.mult)
            yield
            qrot = d["qrot"]
            ta = d["ta"][:, :, 0, :]
            tb = d["tb"][:, :, 0, :]
            P.G("tensor_copy", out=qrot[:, :, 0:64], in_=q32[:, :, 0:64])
            P.V("tensor_tensor", out=ta, in0=q32[:, :, 64:80], in1=cos1, op=ALU.mult)
            P.V("tensor_tensor", out=tb, in0=q32[:, :, 80:96], in1=sin1, op=ALU.mult)
            P.V("tensor_tensor", out=qrot[:, :, 64:80], in0=ta, in1=tb, op=ALU.subtract)
            P.V("tensor_tensor", out=ta, in0=q32[:, :, 64:80], in1=sin1, op=ALU.mult)
            P.V("tensor_tensor", out=tb, in0=q32[:, :, 80:96], in1=cos1, op=ALU.mult)
            P.V("tensor_tensor", out=qrot[:, :, 80:96], in0=ta, in1=tb, op=ALU.add)
            pb = pool.get()
            pv = bfv(pb)
            for h in range(8):
                P.M("transpose", out=pv[0:96, h * 128:(h + 1) * 128], in_=qrot[:, h, :], identity=ident[:, :],
                    accum=(h > 0))
            P.A("activation", out=d["qT"][:, :, :].rearrange("p h t -> p (h t)"), in_=pv[0:96, 0:1024], func=AF.Copy)
            P.dma("pool", out=k.qT_mla[:, :, t0:t0 + 128].rearrange("h e s -> e h s"), in_=d["qT"][:, :, :])
            yield
            pkv = proj(384, 288)
            P.A("activation", out=d["junk"][:, 0:256], in_=pkv[:, 0:256], func=AF.Square, accum_out=stt[:, 6:7])
            rstd_from_ss(k, stt[:, 7:8], stt[:, 6:7], stt[:, 8:9], 256.0, EPS)
            P.V("tensor_copy", out=d["ckvb"][:, :], in_=pkv[:, 0:256])
            kr = d["kr"]
            c1 = d["tb1"][:, 0, :]
            s1 = d["tb1"][:, 1, :]
            t2a = d["ta"][:, 0, 1, :]
            t2b = d["tb"][:, 0, 1, :]
            P.V("tensor_tensor", out=t2a, in0=pkv[:, 256:272], in1=c1, op=ALU.mult)
            P.V("tensor_tensor", out=t2b, in0=pkv[:, 272:288], in1=s1, op=ALU.mult)
            P.V("tensor_tensor", out=kr[:, 0:16], in0=t2a, in1=t2b, op=ALU.subtract)
            P.V("tensor_tensor", out=t2a, in0=pkv[:, 256:272], in1=s1, op=ALU.mult)
            P.V("tensor_tensor", out=t2b, in0=pkv[:, 272:288], in1=c1, op=ALU.mult)
            P.V("tensor_tensor", out=kr[:, 16:32], in0=t2a, in1=t2b, op=ALU.add)
            pb = pool.get()
            pv = bfv(pb)
            for c in range(2):
                P.M("transpose", out=pv[:, c * 128:(c + 1) * 128], in_=d["ckvb"][:, c * 128:(c + 1) * 128],
                    identity=ident[:, :], accum=(c > 0))
            P.A("activation", out=d["ckvT"][:, :, :].rearrange("p c t -> p (c t)"), in_=pv[:, 0:256], func=AF.Copy)
            yield
            kt, vt = d["kt"], d["vt"]
            for half in range(2):
                b = pool.get()
                for c in range(2):
                    P.M("matmul", out=b[:, 0:512], lhsT=d["ckvT"][:, c, :], rhs=w_ukv[:, c, half * 512:(half + 1) * 512],
                        start=(c == 0), stop=(c == 1), accum=(c > 0))
                b3 = b[:, 0:512].rearrange("p (h e) -> p h e", h=4)
                P.V("tensor_scalar", out=kt[:, half * 4:(half + 1) * 4, 0:64], in0=b3[:, :, 0:64], scalar1=stt[:, 7:8],
                    scalar2=None, op0=ALU.mult)
                P.V("tensor_scalar", out=vt[:, half * 4:(half + 1) * 4, :], in0=b3[:, :, 64:128], scalar1=stt[:, 7:8],
                    scalar2=None, op0=ALU.mult)
            P.G("tensor_copy", out=kt[:, :, 64:96], in_=bc(kr[:, :], [128, 8, 32], 1))
            pb = pool.get()
            pv = bfv(pb)
            for h in range(8):
                P.M("transpose", out=pv[0:96, h * 128:(h + 1) * 128], in_=kt[:, h, :], identity=ident[:, :],
                    accum=(h > 0))
            P.A("activation", out=d["kT"][:, :, :].rearrange("p h t -> p (h t)"), in_=pv[0:96, 0:1024], func=AF.Copy)
            P.dma("pool", out=k.kT_mla[:, :, t0:t0 + 128].rearrange("h e s -> e h s"), in_=d["kT"][:, :, :])
            P.dma("pool", out=k.v_mla[t0:t0 + 128, :], in_=vt[:, :, :].rearrange("p h e -> p (h e)"))

            yield
            def qknorm_rope(pb_ap, nh, gbc, n32, rot, soff):
                gs = d["gst"]
                P.A("activation", out=d["sq"][:, 0:nh * 64], in_=pb_ap, func=AF.Square)
                P.V("tensor_reduce", out=gs[:, soff:soff + nh],
                    in_=d["sq"][:, 0:nh * 64].rearrange("p (h e) -> p h e", h=nh), axis=AX.X, op=ALU.add)
                rstd_from_ss(k, gs[:, soff + 8:soff + 8 + nh], gs[:, soff:soff + nh], gs[:, soff + 16:soff + 16 + nh],
                             64.0, EPS)
                P.V("tensor_tensor", out=n32[:, :, :], in0=pb_ap.rearrange("p (h e) -> p h e", h=nh),
                    in1=bc(gs[:, soff + 8:soff + 8 + nh], [128, nh, 64], 2), op=ALU.mult)
                P.G("tensor_tensor", out=n32[:, :, :], in0=n32[:, :, :], in1=bc(gbc[:, :], [128, nh, 64], 1),
                    op=ALU.mult)
                v5 = n32[:, :, :].rearrange("p h (a b e) -> p h a b e", a=2, b=2)
                r5 = rot[:, :, :].rearrange("p h (a b e) -> p h a b e", a=2, b=2)
                x1, x2 = v5[:, :, :, 0, :], v5[:, :, :, 1, :]
                cos2 = bc(d["tb2"][:, 0, :, :], [128, nh, 2, 16], 1)
                sin2 = bc(d["tb2"][:, 1, :, :], [128, nh, 2, 16], 1)
                ta4 = d["ta"][:, 0:nh, :, :]
                tb4 = d["tb"][:, 0:nh, :, :]
                P.V("tensor_tensor", out=ta4, in0=x1, in1=cos2, op=ALU.mult)
                P.V("tensor_tensor", out=tb4, in0=x2, in1=sin2, op=ALU.mult)
                P.V("tensor_tensor", out=r5[:, :, :, 0, :], in0=ta4, in1=tb4, op=ALU.subtract)
                P.V("tensor_tensor", out=ta4, in0=x1, in1=sin2, op=ALU.mult)
                P.V("tensor_tensor", out=tb4, in0=x2, in1=cos2, op=ALU.mult)
                P.V("tensor_tensor", out=r5[:, :, :, 1, :], in0=ta4, in1=tb4, op=ALU.add)

            pgq = proj(672, 512)
            yield
            qknorm_rope(pgq[:, 0:512], 8, gq_bc, d["qn"], d["gqr"], 0)
            pb = pool.get()
            pv = bfv(pb)
            gqf = d["gqr"][:, :, :].rearrange("p h e -> p (h e)")
            for c in range(4):
                P.M("transpose", out=pv[:, c * 128:(c + 1) * 128], in_=gqf[:, c * 128:(c + 1) * 128],
                    identity=ident[:, :], accum=(c > 0))
            P.A("activation", out=d["gqT"][:, :, :].rearrange("p c t -> p (c t)"), in_=pv[:, 0:512], func=AF.Copy)
            P.dma("pool", out=k.qT_gqa[:, t0:t0 + 128].rearrange("(c p) s -> p c s", p=128), in_=d["gqT"][:, :, :])
            yield
            pgk = proj(1184, 256)
            P.A("activation", out=d["gv"][:, :], in_=pgk[:, 128:256], func=AF.Copy)
            P.dma("pool", out=k.v_gqa[t0:t0 + 128, :], in_=d["gv"][:, :])
            qknorm_rope(pgk[:, 0:128], 2, gk_bc, d["kn"], d["gkr"], 2)
            pb = pool.get()
            pv = bfv(pb)
            P.M("transpose", out=pv[:, 0:128], in_=d["gkr"][:, :, :].rearrange("p h e -> p (h e)"),
                identity=ident[:, :])
            P.A("activation", out=d["gkT"][:, :], in_=pv[:, 0:128], func=AF.Copy)
            P.dma("pool", out=k.kT_gqa[:, t0:t0 + 128], in_=d["gkT"][:, :])

        run_skewed([tile_gen(i) for i in range(NT)], ratio=2)


def phase_attn(k, l):
    nc, P, S, NT = k.nc, k.P, k.S, k.NT
    QB = 512
    NQB = S // QB
    NG = NT // 2
    with contextlib.ExitStack() as st:
        def SB(name, shape, dt):
            return st.enter_context(nc.sbuf_tensor(un("p2_" + name), list(shape), dt))
        pps = [st.enter_context(nc.psum_tensor(un("pp%d" % i), [128, 1024], F32)) for i in range(2)]
        obs = [st.enter_context(nc.psum_tensor(un("ob%d" % i), [128, 512], F32)) for i in range(4)]
        kts = [SB("kt%d" % i, [128, S], BF16) for i in range(2)]
        qts = [SB("qt%d" % i, [128, S], BF16) for i in range(2)]
        vxs = [SB("vx%d" % i, [128, NT, 65], BF16) for i in range(2)]
        pts = [SB("pt%d" % i, [128, 2 * QB], BF16) for i in range(3)]
        o32 = [SB("o32_%d" % i, [65, QB], F32) for i in range(4)]
        osb = [SB("o%d" % i, [128, 4, 64], BF16) for i in range(4)]
        rsb = [SB("rs%d" % i, [128, 4], F32) for i in range(4)]
        for vx in vxs:
            P.G("memset", ap=vx[:, :, 64:65], constant=1.0, W=[vx[:, :, :]])
        mub = SB("mub", [128, 1920], F32)
        P.dma("sp", out=mub[:, :], in_=k.w["rwkv_mu"][l:l + 1, :].broadcast_to([128, 1920]))
        mixb = [[SB("mz%d_%d" % (a, b), [128, 1920], F32) for a in range(3)] for b in range(2)]

        def mix_tile(j):
            z, zp, zn = mixb[j % 2]
            t0 = j * 128
            P.dma("sp", out=z[:, :], in_=k.z_r[t0:t0 + 128, :])
            if j == 0:
                P.G("memset", ap=zp[0:1, :], constant=0.0)
                P.dma("sp", out=zp[1:128, :], in_=k.z_r[0:127, :])
            else:
                P.dma("sp", out=zp[:, :], in_=k.z_r[t0 - 1:t0 + 127, :])
            if j == NT - 1:
                P.G("memset", ap=zn[:, :], constant=0.0)
                P.dma("sp", out=zn[0:127, :], in_=k.z_r[t0 + 1:t0 + 128, :])
            else:
                P.dma("sp", out=zn[:, :], in_=k.z_r[t0 + 1:t0 + 129, :])
            P.G("tensor_tensor", out=zp[:, :], in0=zp[:, :], in1=zn[:, :], op=ALU.add)
            P.V("scalar_tensor_tensor", out=zp[:, :], in0=zp[:, :], scalar=0.5, in1=z[:, :], op0=ALU.mult,
                op1=ALU.subtract)
            P.G("tensor_tensor", out=zp[:, :], in0=zp[:, :], in1=mub[:, :], op=ALU.mult)
            P.V("tensor_tensor", out=zn[:, :], in0=z[:, :], in1=zp[:, :], op=ALU.add)
            P.dma("pool", out=k.zmix[t0:t0 + 128, :], in_=zn[:, :])
        hd = []
        kvi = -1
        for ui in range(12):
            qt = qts[ui % 2]
            loads = []
            if ui < 8:
                h = ui
                kvi += 1
                kt, vx = kts[kvi % 2], vxs[kvi % 2]
                loads.append((kt[0:96, :], k.kT_mla[h]))
                vsrc = k.v_mla[:, h * 64:(h + 1) * 64].rearrange("(c p) e -> p c e", p=128)
                for c0 in range(0, NT, 8):
                    c1 = min(NT, c0 + 8)
                    loads.append((vx[:, c0:c1, 0:64], vsrc[:, c0:c1, :]))
                loads.append((qt[0:96, :], k.qT_mla[h]))
                hd.append(dict(kind="mla", kt=kt, vx=vx, qt=qt, dq=96, scale=96.0 ** -0.5, heads=[h], oscr=k.o_mla,
                               loads=loads, ngroups=NG))
            else:
                h0 = (ui - 8) * 2
                kvh = h0 // 4
                if h0 % 4 == 0:
                    kvi += 1
                    kt, vx = kts[kvi % 2], vxs[kvi % 2]
                    loads.append((kt[0:64, :], k.kT_gqa[kvh * 64:(kvh + 1) * 64, :]))
                    loads.append((kt[64:128, :], k.kT_gqa[kvh * 64:(kvh + 1) * 64, :]))
                    vsrc = k.v_gqa[:, kvh * 64:(kvh + 1) * 64].rearrange("(c p) e -> p c e", p=128)
                    for c0 in range(0, NT, 8):
                        c1 = min(NT, c0 + 8)
                        loads.append((vx[:, c0:c1, 0:64], vsrc[:, c0:c1, :]))
                loads.append((qt[0:128, :], k.qT_gqa[h0 * 64:(h0 + 2) * 64, :]))
                hd.append(dict(kind="gqa", kt=kt, vx=vx, qt=qt, dq=64, scale=64.0 ** -0.5, heads=[h0, h0 + 1],
                               oscr=k.o_gqa, loads=loads, ngroups=NT))
        items = []
        for ui in range(len(hd)):
            for qb in range(NQB):
                for g in range(hd[ui]["ngroups"]):
                    items.append((ui, qb, g))
        state = {"npp": 0, "npt": 0, "nqb": 0}

        def do_loads(ui):
            for (o_, i_) in hd[ui]["loads"]:
                P.dma("sp", out=o_, in_=i_)

        def emit_qk(ui, qb, g):
            H_ = hd[ui]
            pp = pps[state["npp"] % 2]
            pt = pts[state["npt"] % 3]
            state["npp"] += 1
            state["npt"] += 1
            dq = H_["dq"]
            for u in range(2):
                if H_["kind"] == "mla":
                    kc = 2 * g + u
                    P.M("matmul", out=pp[:, u * QB:(u + 1) * QB], lhsT=H_["kt"][0:dq, kc * 128:(kc + 1) * 128],
                        rhs=H_["qt"][0:dq, qb * QB:(qb + 1) * QB], start=True, stop=True, accum=(u > 0))
                else:
                    rs = slice(u * 64, (u + 1) * 64)
                    P.M("matmul", out=pp[:, u * QB:(u + 1) * QB], lhsT=H_["kt"][rs, g * 128:(g + 1) * 128],
                        rhs=H_["qt"][rs, qb * QB:(qb + 1) * QB], start=True, stop=True, accum=(u > 0))
            P.A("activation", out=pt[:, :], in_=pp[:, :], func=AF.Exp, scale=H_["scale"])
            return pt

        def emit_pv(ui, qb, g, pt):
            H_ = hd[ui]
            base = (state["nqb"] % 2) * 2
            lastg = (g == H_["ngroups"] - 1)
            for u in range(2):
                if H_["kind"] == "mla":
                    kc = 2 * g + u
                    ob = obs[base]
                else:
                    kc = g
                    ob = obs[base + u]
                P.M("matmul", out=ob[0:65, 0:QB], lhsT=H_["vx"][:, kc, 0:65], rhs=pt[:, u * QB:(u + 1) * QB],
                    start=(kc == 0), stop=(kc == NT - 1), accum=(kc > 0))
            if not lastg:
                return None
            state["nqb"] += 1
            eps = []
            for u, h in enumerate(H_["heads"]):
                ob = obs[base + u]
                o3, o_s, r_s = o32[base + u], osb[base + u], rsb[base + u]
                P.A("activation", out=o3[:, :], in_=ob[0:65, 0:QB], func=AF.Copy)

                def epilogue(ob=ob, o3=o3, o_s=o_s, r_s=r_s, h=h):
                    for j in range(4):
                        P.M("transpose", out=ob[:, j * 128:j * 128 + 65], in_=o3[0:65, j * 128:(j + 1) * 128],
                            identity=k.identf[0:65, 0:65], accum=(j > 0))
                    tb3 = ob[:, 0:512].rearrange("p (j e) -> p j e", j=4)
                    P.V("reciprocal", out=r_s[:, :], in_=tb3[:, :, 64])
                    P.V("tensor_tensor", out=o_s[:, :, :], in0=tb3[:, :, 0:64], in1=bc(r_s[:, :], [128, 4, 64], 2),
                        op=ALU.mult)
                    P.dma("pool", out=H_["oscr"][qb * QB:(qb + 1) * QB, h * 64:(h + 1) * 64].rearrange("(j p) e -> p j e", p=128),
                          in_=o_s[:, :, :])
                eps.append(epilogue)

            def run_eps():
                for f in eps:
                    f()
            return run_eps

        n = len(items)
        starts = {}
        for idx, (ui, qb, g) in enumerate(items):
            starts.setdefault(ui, idx)
        do_loads(0)
        prev_pt = None
        cur_pt = None
        pending = None
        mix_every = max(1, n // NT)
        nmix = 0
        for idx in range(n + 1):
            if idx % mix_every == 1 and nmix < NT:
                mix_tile(nmix)
                nmix += 1
            if idx < n:
                ui, qb, g = items[idx]
                if idx == starts[ui] + min(2, NQB * hd[ui]["ngroups"] - 1) and ui + 1 < len(hd):
                    do_loads(ui + 1)
                cur_pt = emit_qk(ui, qb, g)
            if idx >= 1:
                ui0, qb0, g0 = items[idx - 1]
                ep = emit_pv(ui0, qb0, g0, prev_pt)
                if pending is not None:
                    pending()
                pending = ep
            prev_pt = cur_pt
        if pending is not None:
            pending()
        while nmix < NT:
            mix_tile(nmix)
            nmix += 1


def rwkv_dir(k, l, d, st, poolA, poolB, nsets, both=False):
    nc, P, S, NT = k.nc, k.P, k.S, k.NT
    w = k.w
    C0 = DECAY_C
    def SB(name, shape, dt):
        return st.enter_context(nc.sbuf_tensor(un("rw_" + name), list(shape), dt))
    bcn = {}

    def load_bc(name, src_row, n=512):
        t = SB(name, [128, n], F32)
        P.dma("sp", out=t[:, :], in_=src_row.broadcast_to([128, n]))
        bcn[name] = t
        return t
    w0b = load_bc("w0", w["rwkv_w0"][l, d:d + 1, :])
    a0b = load_bc("a0", w["rwkv_a0"][l, d:d + 1, :])
    kkb = load_bc("kk", w["rwkv_k_k"][l:l + 1, :])
    kab = load_bc("ka", w["rwkv_k_a"][l:l + 1, :])
    od = 1 - d
    osl = slice(od * 64, (od + 1) * 64)
    if d == 1 or both:
        a0o = load_bc("a0o", w["rwkv_a0"][l, od:od + 1, :])
        rkb = load_bc("rk", w["rwkv_r_k"][l:l + 1].rearrange("o h n -> o (h n)"))
        lnw = load_bc("lnw", w["rwkv_ln_w"][l:l + 1, :])
        lnb = load_bc("lnb", w["rwkv_ln_b"][l:l + 1, :])
        g2s = SB("g2", [128, 512], BF16)
        P.dma("pool", out=g2s[:, :], in_=w["rwkv_g2"][l])
    w2s = SB("w2", [128, 512], BF16)
    a2s = SB("a2", [128, 512], BF16)
    P.dma("pool", out=w2s[:, :], in_=w["rwkv_w2"][l].rearrange("d r c -> (d r) c"))
    P.dma("pool", out=a2s[:, :], in_=w["rwkv_a2"][l].rearrange("d r c -> (d r) c"))
    m2 = SB("m2", [128, 2, 128], F32)
    mT = SB("mT", [128, 128], F32)
    bdm = SB("bdm", [128, 128], F32)
    onec = SB("onec", [128, 1], F32)
    P.dma("sp", out=m2[:, 0, :], in_=k.msk_d[d, 0])
    P.dma("sp", out=m2[:, 1, :], in_=k.msk_d[d, 1])
    P.dma("sp", out=mT[:, :], in_=k.msk_d[1 - d, 0])
    P.dma("sp", out=bdm[:, :], in_=k.bdm_d)
    P.G("memset", ap=onec[:, :], constant=1.0)
    H32 = SB("H32", [128, 4, 128], F32)
    Hb = SB("Hb", [128, 4, 128], BF16)
    P.G("memset", ap=H32[:, :, :], constant=0.0)
    P.G("memset", ap=Hb[:, :, :], constant=0.0)
    sets = []
    for s_ in range(nsets):
        dd = {}
        lst = [("zm", [128, 1920], F32), ("lor", [128, 384], BF16), ("lorT", [128, 3, 128], BF16),
               ("tok4", [128, 4, 512], BF16), ("vb", [128, 512], BF16), ("TTs", [128, 4, 4, 128], BF16),
               ("gC", [128, 4], F32), ("sm", [128, 64], F32), ("Y32", [128, 512], F32), ("hT_", [128, 128], F32)]
        for t in range(8):
            lst.append(("T%d" % t, [128, 512], F32))
        for p in range(4):
            lst += [("XL%d" % p, [128, 2, 2, 128], BF16), ("LT%d" % p, [128, 2, 128], BF16),
                    ("ARB%d" % p, [128, 2, 128], BF16), ("AK%d" % p, [128, 2, 2, 128], BF16),
                    ("PT%d" % p, [128, 128], BF16), ("AKV%d" % p, [128, 128], BF16), ("Ub%d" % p, [128, 128], BF16)]
        if d == 1 or both:
            lst += [("yfw", [128, 512], F32), ("ob", [128, 512], BF16), ("oT", [128, 4, 128], BF16)]
        for name, shape, dt in lst:
            dd[name] = SB("%s_%d" % (name, s_), shape, dt)
        sets.append(dd)

    order = list(range(NT)) if d == 0 else list(range(NT - 1, -1, -1))

    def tile_gen(it, i):
        pool = poolA
        D_ = sets[it % nsets]
        final = (it >= NT // 2) if both else (d == 1)
        t0 = i * 128
        zm = D_["zm"]
        T = [D_["T%d" % t] for t in range(8)]
        sm = D_["sm"]
        P.dma("sp", out=zm[:, :], in_=k.zmix[t0:t0 + 128, :])
        r_ = zm[:, 0:512]
        kx = zm[:, 512:1024]
        v_ = zm[:, 1024:1536]
        for _ in range(RW_DELAY):
            yield
        lor, lorT = D_["lor"], D_["lorT"]
        P.A("activation", out=lor[:, 0:128], in_=zm[:, 1536:1664], func=AF.Tanh)
        P.V("tensor_copy", out=lor[:, 128:256], in_=zm[:, 1664:1792])
        nl = 2
        if final:
            P.A("activation", out=lor[:, 256:384], in_=zm[:, 1792:1920], func=AF.Sigmoid)
            nl = 3
        pb = pool.get()
        pv = bfv(pb)
        for c in range(nl):
            P.M("transpose", out=pv[:, c * 128:(c + 1) * 128], in_=lor[:, c * 128:(c + 1) * 128],
                identity=k.ident[:, :], accum=(c > 0))
        P.A("activation", out=lorT[:, 0:nl, :].rearrange("p c t -> p (c t)"), in_=pv[:, 0:nl * 128], func=AF.Copy)
        ds = slice(d * 64, (d + 1) * 64)
        sg, asig = T[0], T[1]
        pb = pool.get()
        P.M("matmul", out=pb[:, 0:512], lhsT=lorT[ds, 0, :], rhs=w2s[ds, :], start=True, stop=True)
        P.V("tensor_tensor", out=sg[:, :], in0=pb[:, 0:512], in1=w0b[:, :], op=ALU.add)
        P.A("activation", out=sg[:, :], in_=sg[:, :], func=AF.Sigmoid)
        yield
        pb = pool.get()
        P.M("matmul", out=pb[:, 0:512], lhsT=lorT[ds, 1, :], rhs=a2s[ds, :], start=True, stop=True)
        P.V("tensor_tensor", out=asig[:, :], in0=pb[:, 0:512], in1=a0b[:, :], op=ALU.add)
        P.A("activation", out=asig[:, :], in_=asig[:, :], func=AF.Sigmoid)
        yield
        eP, eN, eX = T[2], T[3], T[4]
        pc = pool.get()
        P.M("matmul", out=pc[:, 0:512], lhsT=m2[:, 1, :], rhs=sg[:, :], start=True, stop=True)
        pcx = pool.get()
        P.M("matmul", out=pcx[:, 0:512], lhsT=m2[:, 0, :], rhs=sg[:, :], start=True, stop=True)
        P.A("activation", out=eP[:, :], in_=pc[:, 0:512], func=AF.Exp, scale=-C0)
        P.A("activation", out=eN[:, :], in_=pc[:, 0:512], func=AF.Exp, scale=C0)
        yield
        P.A("activation", out=eX[:, :], in_=pcx[:, 0:512], func=AF.Exp, scale=-C0)
        pg = pool.get()
        for p in range(4):
            P.M("matmul", out=pg[:, p:p + 1], lhsT=sg[:, p * 128:(p + 1) * 128], rhs=onec[:, 0:1], start=True,
                stop=True, accum=(p > 0))
        P.A("activation", out=D_["gC"][:, :], in_=pg[:, 0:4], func=AF.Exp, scale=-C0)
        yield
        kk, kt, bb = T[5], T[6], T[7]
        tok4, vb = D_["tok4"], D_["vb"]
        P.G("tensor_tensor", out=kk[:, :], in0=kx, in1=kkb[:, :], op=ALU.mult)
        P.A("activation", out=kt[:, :], in_=kk[:, :], func=AF.Square)
        P.V("tensor_reduce", out=sm[:, 0:8], in_=kt[:, :].rearrange("p (h e) -> p h e", h=8), axis=AX.X, op=ALU.add)
        P.V("tensor_scalar", out=sm[:, 0:8], in0=sm[:, 0:8], scalar1=1e-24, scalar2=None, op0=ALU.max)
        yield
        P.A("activation", out=sm[:, 8:16], in_=sm[:, 0:8], func=AF.Ln)
        P.A("activation", out=sm[:, 16:24], in_=sm[:, 8:16], func=AF.Exp, scale=-0.5)
        kk3 = kk[:, :].rearrange("p (h e) -> p h e", h=8)
        P.V("tensor_tensor", out=kk3, in0=kk3, in1=bc(sm[:, 16:24], [128, 8, 64], 2), op=ALU.mult)
        yield
        P.V("scalar_tensor_tensor", out=kt[:, :], in0=asig[:, :], scalar=-1.0, in1=kab[:, :], op0=ALU.add, op1=ALU.mult)
        P.V("scalar_tensor_tensor", out=kt[:, :], in0=kt[:, :], scalar=1.0, in1=kx, op0=ALU.add, op1=ALU.mult)
        P.G("tensor_tensor", out=bb[:, :], in0=kk[:, :], in1=asig[:, :], op=ALU.mult)
        yield
        P.V("scalar_tensor_tensor", out=tok4[:, 0, :], in0=kk[:, :], scalar=-1.0, in1=eX[:, :], op0=ALU.mult, op1=ALU.mult)
        P.G("tensor_tensor", out=tok4[:, 1, :], in0=r_, in1=eP[:, :], op=ALU.mult)
        yield
        P.G("tensor_tensor", out=tok4[:, 2, :], in0=bb[:, :], in1=eN[:, :], op=ALU.mult)
        P.V("tensor_tensor", out=tok4[:, 3, :], in0=kt[:, :], in1=eN[:, :], op=ALU.mult)
        P.A("activation", out=vb[:, :], in_=v_, func=AF.Copy)
        pool = poolB
        yield "mid"
        TTs = D_["TTs"]
        for p0 in (0, 2):
            pb = pool.get()
            pv = bfv(pb)
            for p in (p0, p0 + 1):
                for q in range(4):
                    o0 = (p - p0) * 512 + q * 128
                    P.M("transpose", out=pv[:, o0:o0 + 128], in_=tok4[:, q, p * 128:(p + 1) * 128],
                        identity=k.ident[:, :], accum=not (p == p0 and q == 0))
            P.A("activation", out=TTs[:, p0:p0 + 2, :, :].rearrange("p a q t -> p (a q t)"), in_=pv[:, 0:1024],
                func=AF.Copy)
            yield
        yield
        XL = [D_["XL%d" % p] for p in range(4)]
        LT = [D_["LT%d" % p] for p in range(4)]
        ARB = [D_["ARB%d" % p] for p in range(4)]
        AK = [D_["AK%d" % p] for p in range(4)]
        PT = [D_["PT%d" % p] for p in range(4)]
        AKV = [D_["AKV%d" % p] for p in range(4)]
        Ub = [D_["Ub%d" % p] for p in range(4)]
        m2b = bc(m2[:, :, :].rearrange("p a t -> p (a t)"), [128, 2, 256], 1)
        for p in range(4):
            bB, bK, bL = pool.get(), pool.get(), pool.get()
            for e in range(2):
                bs = slice(e * 64, (e + 1) * 64)
                ar = TTs[bs, p, 0:2, :].rearrange("p q t -> p (q t)")
                P.M("matmul", out=bB[:, e * 256:(e + 1) * 256], lhsT=TTs[bs, p, 2, :], rhs=ar, start=True, stop=True,
                    accum=(e > 0), drain=True)
                P.M("matmul", out=bK[:, e * 256:(e + 1) * 256], lhsT=TTs[bs, p, 3, :], rhs=ar, start=True, stop=True,
                    accum=(e > 0))
                P.M("matmul", out=bL[:, e * 128:(e + 1) * 128], lhsT=TTs[bs, p, 0, :], rhs=TTs[bs, p, 2, :], start=True,
                    stop=True, accum=(e > 0))
            bB4 = bB[:, 0:512].rearrange("p (e a t) -> p e a t", e=2, a=2)
            P.V("tensor_tensor", out=XL[p][:, :, 1, :], in0=bB4[:, :, 0, :], in1=bc(m2[:, 0, :], [128, 2, 128], 1),
                op=ALU.mult)
            P.V("tensor_tensor", out=ARB[p][:, :, :], in0=bB4[:, :, 1, :], in1=bc(m2[:, 1, :], [128, 2, 128], 1),
                op=ALU.mult)
            P.V("tensor_tensor", out=AK[p][:, :, :, :].rearrange("p e a t -> p e (a t)"),
                in0=bK[:, 0:512].rearrange("p (e n) -> p e n", e=2), in1=m2b, op=ALU.mult)
            P.V("tensor_tensor", out=LT[p][:, :, :], in0=bL[:, 0:256].rearrange("p (e t) -> p e t", e=2),
                in1=bc(mT[:, :], [128, 2, 128], 1), op=ALU.mult)
            P.G("tensor_copy", out=XL[p][:, :, 0, :], in_=bc(k.ident[:, :], [128, 2, 128], 1))
            yield
        yield
        for lev in range(7):
            last = (lev == 6)
            for p in range(4):
                bb_ = pool.get()
                for e in range(2):
                    if not last:
                        P.M("matmul", out=bb_[:, e * 256:(e + 1) * 256], lhsT=LT[p][:, e, :],
                            rhs=XL[p][:, e, :, :].rearrange("p a t -> p (a t)"), start=True, stop=True, accum=(e > 0))
                    else:
                        P.M("matmul", out=bb_[:, e * 256:e * 256 + 128], lhsT=LT[p][:, e, :],
                            rhs=XL[p][:, e, 0, :], start=True, stop=True, accum=(e > 0))
                if not last:
                    ba_ = pool.get()
                    for e in range(2):
                        P.M("matmul", out=ba_[:, e * 128:(e + 1) * 128], lhsT=XL[p][:, e, 1, :], rhs=LT[p][:, e, :],
                            start=True, stop=True, accum=(e > 0))
                b4 = bb_[:, 0:512].rearrange("p (e a t) -> p e a t", e=2, a=2)
                P.V("tensor_tensor", out=XL[p][:, :, 0, :], in0=XL[p][:, :, 0, :], in1=b4[:, :, 0, :], op=ALU.add)
                if not last:
                    P.A("activation", out=XL[p][:, :, 1, :], in_=b4[:, :, 1, :], func=AF.Copy)
                    P.A("activation", out=LT[p][:, :, :], in_=ba_[:, 0:256].rearrange("p (e t) -> p e t", e=2),
                        func=AF.Copy)
                yield
        for p in range(4):
            pb = pool.get()
            P.M("matmul", out=pb[:, 0:256].rearrange("p (e t) -> p e t", e=2), lhsT=tok4[:, 0, p * 128:(p + 1) * 128],
                rhs=XL[p][:, :, 0, :], start=True, stop=True)
            P.A("activation", out=PT[p][0:64, :], in_=pb[0:64, 0:128], func=AF.Copy)
            P.V("tensor_copy", out=PT[p][64:128, :], in_=pb[64:128, 128:256])
            pb2 = pool.get()
            for e in range(2):
                h = 2 * p + e
                P.M("matmul", out=pb2[:, e * 64:(e + 1) * 64], lhsT=AK[p][:, e, 0, :], rhs=vb[:, h * 64:(h + 1) * 64],
                    start=True, stop=True, accum=(e > 0))
            P.A("activation", out=AKV[p][:, :], in_=pb2[:, 0:128], func=AF.Copy)
            yield
        Y32 = D_["Y32"]
        for p in range(4):
            ps = slice(p * 128, (p + 1) * 128)
            bU = pool.get()
            for e in range(2):
                P.M("matmul", out=bU[:, e * 64:(e + 1) * 64], lhsT=XL[p][:, e, 0, :], rhs=AKV[p][:, e * 64:(e + 1) * 64],
                    start=(e == 0), stop=False, accum=(e > 0))
            P.M("matmul", out=bU[:, 0:128], lhsT=PT[p][:, :], rhs=Hb[:, p, :], start=False, stop=True, accum=True)
            P.A("activation", out=Ub[p][:, :], in_=bU[:, 0:128], func=AF.Copy)
            bY = pool.get()
            for e in range(2):
                h = 2 * p + e
                P.M("matmul", out=bY[:, e * 64:(e + 1) * 64], lhsT=AK[p][:, e, 1, :], rhs=vb[:, h * 64:(h + 1) * 64],
                    start=(e == 0), stop=False, accum=(e > 0))
            P.M("matmul", out=bY[:, 0:128], lhsT=TTs[:, p, 1, :], rhs=Hb[:, p, :], start=False, stop=False, accum=True)
            for e in range(2):
                P.M("matmul", out=bY[:, e * 64:(e + 1) * 64], lhsT=ARB[p][:, e, :], rhs=Ub[p][:, e * 64:(e + 1) * 64],
                    start=False, stop=(e == 1), accum=True)
            P.V("tensor_copy", out=Y32[:, ps], in_=bY[:, 0:128])
            bH = pool.get()
            P.M("matmul", out=bH[:, 0:128], lhsT=tok4[:, 2, ps], rhs=Ub[p][:, :], start=True, stop=False)
            P.M("matmul", out=bH[:, 0:128], lhsT=tok4[:, 3, ps], rhs=vb[:, ps], start=False, stop=True, accum=True)
            hT_ = D_["hT_"]
            P.V("tensor_tensor", out=hT_[:, :], in0=bH[:, 0:128], in1=H32[:, p, :], op=ALU.add)
            P.V("scalar_tensor_tensor", out=H32[:, p, :], in0=hT_[:, :], scalar=D_["gC"][:, p:p + 1], in1=bdm[:, :],
                op0=ALU.mult, op1=ALU.mult)
            P.A("activation", out=Hb[:, p, :], in_=H32[:, p, :], func=AF.Copy)
            yield
        if not final:
            P.dma("pool", out=k.y_fw[t0:t0 + 128, :], in_=Y32[:, :])
            k.y_done.add(i)
        else:
            while both and i not in k.y_done:
                yield "wait"
            P.dma("sp", out=D_["yfw"][:, :], in_=k.y_fw[t0:t0 + 128, :])
            wkv, sq, bon = T[3], T[4], T[2]
            pb = pool.get()
            P.M("matmul", out=pb[:, 0:512], lhsT=lorT[osl, 1, :], rhs=a2s[osl, :], start=True, stop=True)
            P.V("tensor_tensor", out=T[0][:, :], in0=pb[:, 0:512], in1=a0o[:, :], op=ALU.add)
            P.A("activation", out=T[0][:, :], in_=T[0][:, :], func=AF.Sigmoid)
            P.V("scalar_tensor_tensor", out=T[0][:, :], in0=T[0][:, :], scalar=-1.0, in1=kab[:, :], op0=ALU.add,
                op1=ALU.mult)
            P.V("scalar_tensor_tensor", out=T[0][:, :], in0=T[0][:, :], scalar=1.0, in1=kx, op0=ALU.add, op1=ALU.mult)
            P.G("tensor_tensor", out=T[0][:, :], in0=T[0][:, :], in1=kt[:, :], op=ALU.add)
            P.G("tensor_tensor", out=T[0][:, :], in0=T[0][:, :], in1=r_, op=ALU.mult)
            P.G("tensor_tensor", out=T[0][:, :], in0=T[0][:, :], in1=rkb[:, :], op=ALU.mult)
            P.V("tensor_reduce", out=sm[:, 24:32], in_=T[0][:, :].rearrange("p (h e) -> p h e", h=8), axis=AX.X,
                op=ALU.add)
            P.V("tensor_tensor", out=bon[:, :].rearrange("p (h e) -> p h e", h=8),
                in0=v_.rearrange("p (h e) -> p h e", h=8), in1=bc(sm[:, 24:32], [128, 8, 64], 2), op=ALU.mult)
            P.G("tensor_tensor", out=wkv[:, :], in0=Y32[:, :], in1=D_["yfw"][:, :], op=ALU.add)
            w3 = wkv[:, :].rearrange("p (h e) -> p h e", h=8)
            P.V("tensor_reduce", out=sm[:, 32:40], in_=w3, axis=AX.X, op=ALU.add)
            P.V("tensor_scalar", out=sm[:, 32:40], in0=sm[:, 32:40], scalar1=-1.0 / 64, scalar2=None, op0=ALU.mult)
            P.V("tensor_tensor", out=w3, in0=w3, in1=bc(sm[:, 32:40], [128, 8, 64], 2), op=ALU.add)
            P.A("activation", out=sq[:, :], in_=wkv[:, :], func=AF.Square)
            P.V("tensor_reduce", out=sm[:, 40:48], in_=sq[:, :].rearrange("p (h e) -> p h e", h=8), axis=AX.X, op=ALU.add)
            rstd_from_ss(k, sm[:, 48:56], sm[:, 40:48], sm[:, 56:64], 64.0, GN_EPS)
            P.V("tensor_tensor", out=w3, in0=w3, in1=bc(sm[:, 48:56], [128, 8, 64], 2), op=ALU.mult)
            P.G("tensor_tensor", out=wkv[:, :], in0=wkv[:, :], in1=lnw[:, :], op=ALU.mult)
            P.G("tensor_tensor", out=wkv[:, :], in0=wkv[:, :], in1=lnb[:, :], op=ALU.add)
            P.G("tensor_tensor", out=wkv[:, :], in0=wkv[:, :], in1=bon[:, :], op=ALU.add)
            pb = pool.get()
            P.M("matmul", out=pb[:, 0:512], lhsT=lorT[:, 2, :], rhs=g2s[:, :], start=True, stop=True)
            P.V("tensor_tensor", out=D_["ob"][:, :], in0=wkv[:, :], in1=pb[:, 0:512], op=ALU.mult)
            pb = pool.get()
            pv = bfv(pb)
            for c in range(4):
                P.M("transpose", out=pv[:, c * 128:(c + 1) * 128], in_=D_["ob"][:, c * 128:(c + 1) * 128],
                    identity=k.ident[:, :], accum=(c > 0))
            P.A("activation", out=D_["oT"][:, :, :].rearrange("p c t -> p (c t)"), in_=pv[:, 0:512], func=AF.Copy)
            P.dma("pool", out=k.oT_rw[i], in_=D_["oT"][:, :, :])


    return [tile_gen(it, i) for it, i in enumerate(order)]


def phase_rwkv(k, l, d):
    with contextlib.ExitStack() as st:
        alloc_banks(k, st)
        gens = rwkv_dir(k, l, d, st, BankPool(k.banks[0:3]), BankPool(k.banks[3:8]), 2)
        run_skewed(gens, ratio=RW_RATIO)


def phase_rwkv2(k, l):
    with contextlib.ExitStack() as st:
        alloc_banks(k, st)
        pf = BankPool(k.banks[0:4])
        pb = BankPool(k.banks[4:8])
        k.y_done = set()
        gf = rwkv_dir(k, l, 0, st, pf, pf, 1, both=True)
        gb = rwkv_dir(k, l, 1, st, pb, pb, 1, both=True)
        streams = [iter(gf), iter(gb)]
        cur = [next(streams[0], None), None]
        started_b = False
        while cur[0] is not None or cur[1] is not None or not started_b:
            for j in range(2):
                if j == 1 and not started_b:
                    continue
                if cur[j] is None:
                    continue
                try:
                    r = next(cur[j])
                    if j == 0 and r == "mid" and not started_b:
                        started_b = True
                        cur[1] = next(streams[1], None)
                except StopIteration:
                    cur[j] = next(streams[j], None)
            if cur[0] is None and not started_b:
                started_b = True
                cur[1] = next(streams[1], None)


def phase_p3a(k, l, xin):
    nc, P, S, NT = k.nc, k.P, k.S, k.NT
    w = k.w
    with contextlib.ExitStack() as st:
        def SB(name, shape, dt):
            return st.enter_context(nc.sbuf_tensor(un("p3_" + name), list(shape), dt))
        alloc_banks(k, st, 6)
        pp2 = st.enter_context(nc.psum_tensor(un("p3_pp"), [128, 1024], F32))
        wg = SB("wg", [128, 8, 3072], BF16)
        wbr = SB("wbr", [128, 3, 4, 1024], BF16)
        wo = SB("wo", [128, 8, 1024], BF16)
        wq = SB("wq", [128, 8, 512], BF16)
        cwo = SB("cwo", [128, 4, 1024], BF16)
        bg = SB("bg", [1, 3072], BF16)
        ones = SB("ones", [1, 128], BF16)
        gcol = SB("gcol", [128, 16], F32)
        kmT = SB("kmT", [128, 4, 256], BF16)
        vm = SB("vm", [128, 2, 512], BF16)
        win = w["w_in"][l].rearrange("(c p) n -> p c n", p=128)
        for c in range(8):
            P.dma("pool", out=wg[:, c, :], in_=win[:, c, 3360:6432])
        for g in range(3):
            P.dma("pool", out=wbr[:, g, :, :], in_=w["w_branch"][l, g].rearrange("(c p) n -> p c n", p=128))
        P.dma("pool", out=wo[:, :, :], in_=w["w_out"][l].rearrange("(c p) n -> p c n", p=128))
        P.dma("pool", out=cwo[:, :, :], in_=w["cross_wo"][l].rearrange("(c p) n -> p c n", p=128))
        P.dma("pool", out=bg[0:1, :], in_=w["b_gate"][l:l + 1].rearrange("o g n -> o (g n)"))
        P.G("memset", ap=ones[0:1, :], constant=1.0)
        for c in range(8):
            load_col(k, "sp", gcol[:, c:c + 1], w["norm_cross"][l, c * 128:(c + 1) * 128], 128)
            load_col(k, "sp", gcol[:, 8 + c:9 + c], w["norm_mem"][l, c * 128:(c + 1) * 128], 128)
        with contextlib.ExitStack() as st2:
            stg = st2.enter_context(nc.sbuf_tensor(un("p3_stg"), [128, 8, 1024], F32))
            wkv = st2.enter_context(nc.sbuf_tensor(un("p3_wkv"), [128, 8, 1024], BF16))
            mx = st2.enter_context(nc.sbuf_tensor(un("p3_mx"), [128, 2, 1024], F32))
            mhb = st2.enter_context(nc.sbuf_tensor(un("p3_mhb"), [128, 2, 1024], BF16))
            mhT = st2.enter_context(nc.sbuf_tensor(un("p3_mhT"), [128, 8, 256], BF16))
            mjunk = st2.enter_context(nc.sbuf_tensor(un("p3_mjunk"), [128, 1024], BF16))
            mst = st2.enter_context(nc.sbuf_tensor(un("p3_mst"), [128, 8], F32))
            P.dma("sp", out=stg[:, :, 0:512], in_=w["cross_wq"][l].rearrange("(c p) n -> p c n", p=128))
            for c in range(8):
                P.V("tensor_scalar", out=wq[:, c, :], in0=stg[:, c, 0:512], scalar1=gcol[:, c:c + 1], scalar2=None,
                    op0=ALU.mult)
            P.dma("sp", out=stg[:, :, :], in_=w["cross_wkv"][l].rearrange("(c p) n -> p c n", p=128))
            for c in range(8):
                P.V("tensor_scalar", out=wkv[:, c, :], in0=stg[:, c, :], scalar1=gcol[:, 8 + c:9 + c], scalar2=None,
                    op0=ALU.mult)
            P.dma("sp", out=mx[:, :, :], in_=k.mem.rearrange("(j p) n -> p j n", p=128))
            for j in range(2):
                P.A("activation", out=mjunk[:, :], in_=mx[:, j, :], func=AF.Square, accum_out=mst[:, j:j + 1])
            rstd_from_ss(k, mst[:, 2:4], mst[:, 0:2], mst[:, 4:6], 1024.0, EPS)
            for j in range(2):
                P.V("tensor_scalar", out=mhb[:, j, :], in0=mx[:, j, :], scalar1=mst[:, 2 + j:3 + j], scalar2=None,
                    op0=ALU.mult)
                pb = bank(k)
                pv = bfv(pb)
                for c in range(8):
                    P.M("transpose", out=pv[:, c * 128:(c + 1) * 128], in_=mhb[:, j, c * 128:(c + 1) * 128],
                        identity=k.ident[:, :], accum=(c > 0))
                P.A("activation", out=mhT[:, :, j * 128:(j + 1) * 128],
                    in_=pv[:, 0:1024].rearrange("p (c t) -> p c t", c=8), func=AF.Copy)
            for h in range(4):
                pb = bank(k)
                for c in range(8):
                    P.M("matmul", out=pb[:, 0:256], lhsT=wkv[:, c, h * 128:(h + 1) * 128], rhs=mhT[:, c, :],
                        start=(c == 0), stop=(c == 7), accum=(c > 0))
                P.A("activation", out=kmT[:, h, :], in_=pb[:, 0:256], func=AF.Copy)
            for j in range(2):
                pb = bank(k)
                for c in range(8):
                    P.M("matmul", out=pb[:, 0:512], lhsT=mhT[:, c, j * 128:(j + 1) * 128], rhs=wkv[:, c, 512:1024],
                        start=(c == 0), stop=(c == 7), accum=(c > 0))
                P.A("activation", out=vm[:, j, :], in_=pb[:, 0:512], func=AF.Copy)
        P.barrier()

        sets = []
        for s_ in range(2):
            d = {}
            for name, shape, dt in [
                ("x", [128, 1024], F32), ("hT", [128, 8, 128], BF16), ("ob", [128, 2, 512], BF16),
                ("oT", [128, 3, 4, 128], BF16), ("gt", [128, 512], F32), ("tmp", [128, 512], F32),
                ("m", [128, 1024], F32), ("mb", [128, 1024], BF16), ("mT", [128, 8, 128], BF16),
                ("x1", [128, 1024], F32), ("junk", [128, 1024], BF16), ("st", [128, 16], F32),
                ("h2", [128, 1024], BF16), ("h2T", [128, 8, 128], BF16), ("qT", [128, 4, 128], BF16),
                ("p", [128, 4, 256], BF16), ("pn", [128, 4, 256], BF16), ("pT", [128, 8, 128], BF16),
                ("ocT", [128, 4, 128], BF16), ("x2", [128, 1024], F32),
            ]:
                d[name] = SB("%s%d" % (name, s_), shape, dt)
            sets.append(d)
        have_rw = ("rb" in k.phases) or ("rw2" in k.phases)
        for i in range(NT):
            d = sets[i % 2]
            t0 = i * 128
            x, hT, oT, stt = d["x"], d["hT"], d["oT"], d["st"]
            P.dma("sp", out=x[:, :], in_=xin[t0:t0 + 128, :])
            P.dma("sp", out=hT[:, :, :], in_=k.hT_t[i])
            P.dma("sp", out=d["ob"][:, 0, :], in_=k.o_mla[t0:t0 + 128, :])
            P.dma("sp", out=d["ob"][:, 1, :], in_=k.o_gqa[t0:t0 + 128, :])
            if have_rw:
                P.dma("sp", out=oT[:, 2, :, :], in_=k.oT_rw[i])
            else:
                P.G("memset", ap=oT[:, 2, :, :], constant=0.0)
            pb = bank(k)
            pv = bfv(pb)
            obf = d["ob"][:, :, :].rearrange("p g n -> p (g n)")
            for c in range(8):
                P.M("transpose", out=pv[:, c * 128:(c + 1) * 128], in_=obf[:, c * 128:(c + 1) * 128],
                    identity=k.ident[:, :], accum=(c > 0))
            P.A("activation", out=oT[:, 0:2, :, :].rearrange("p g c t -> p (g c t)"), in_=pv[:, 0:1024], func=AF.Copy)
            for g in range(3):
                for n in range(2):
                    cs = slice(n * 512, (n + 1) * 512)
                    pz = bank(k)
                    for c in range(8):
                        P.M("matmul", out=pz[:, 0:512], lhsT=hT[:, c, :], rhs=wg[:, c, g * 1024 + n * 512:g * 1024 + (n + 1) * 512],
                            start=(c == 0), stop=False, accum=(c > 0))
                    P.M("matmul", out=pz[:, 0:512], lhsT=ones[0:1, :], rhs=bg[0:1, g * 1024 + n * 512:g * 1024 + (n + 1) * 512],
                        start=False, stop=True, accum=True)
                    P.A("activation", out=d["gt"][:, :], in_=pz[:, 0:512], func=AF.Sigmoid)
                    pbr = bank(k)
                    for c in range(4):
                        P.M("matmul", out=pbr[:, 0:512], lhsT=oT[:, g, c, :], rhs=wbr[:, g, c, cs],
                            start=(c == 0), stop=(c == 3), accum=(c > 0))
                    if g == 0:
                        P.V("tensor_tensor", out=d["m"][:, cs], in0=d["gt"][:, :], in1=pbr[:, 0:512], op=ALU.mult)
                    else:
                        P.V("tensor_tensor", out=d["tmp"][:, :], in0=d["gt"][:, :], in1=pbr[:, 0:512], op=ALU.mult)
                        if g == 1:
                            P.G("tensor_tensor", out=d["m"][:, cs], in0=d["m"][:, cs], in1=d["tmp"][:, :], op=ALU.add)
                        else:
                            P.G("tensor_tensor", out=d["mb"][:, cs], in0=d["m"][:, cs], in1=d["tmp"][:, :], op=ALU.add)
            pb = bank(k)
            pv = bfv(pb)
            for c in range(8):
                P.M("transpose", out=pv[:, c * 128:(c + 1) * 128], in_=d["mb"][:, c * 128:(c + 1) * 128],
                    identity=k.ident[:, :], accum=(c > 0))
            P.A("activation", out=d["mT"][:, :, :].rearrange("p c t -> p (c t)"), in_=pv[:, 0:1024], func=AF.Copy)
            for n in range(2):
                cs = slice(n * 512, (n + 1) * 512)
                pb = bank(k)
                for c in range(8):
                    P.M("matmul", out=pb[:, 0:512], lhsT=d["mT"][:, c, :], rhs=wo[:, c, cs],
                        start=(c == 0), stop=(c == 7), accum=(c > 0))
                P.V("tensor_tensor", out=d["x1"][:, cs], in0=x[:, cs], in1=pb[:, 0:512], op=ALU.add)
            x1 = d["x1"]
            P.A("activation", out=d["junk"][:, :], in_=x1[:, :], func=AF.Square, accum_out=stt[:, 0:1])
            rstd_from_ss(k, stt[:, 1:2], stt[:, 0:1], stt[:, 2:3], 1024.0, EPS)
            P.V("tensor_scalar", out=d["h2"][:, :], in0=x1[:, :], scalar1=stt[:, 1:2], scalar2=None, op0=ALU.mult)
            pb = bank(k)
            pv = bfv(pb)
            for c in range(8):
                P.M("transpose", out=pv[:, c * 128:(c + 1) * 128], in_=d["h2"][:, c * 128:(c + 1) * 128],
                    identity=k.ident[:, :], accum=(c > 0))
            P.A("activation", out=d["h2T"][:, :, :].rearrange("p c t -> p (c t)"), in_=pv[:, 0:1024], func=AF.Copy)
            pb = bank(k)
            for h in range(4):
                for c in range(8):
                    P.M("matmul", out=pb[:, h * 128:(h + 1) * 128], lhsT=wq[:, c, h * 128:(h + 1) * 128], rhs=d["h2T"][:, c, :],
                        start=(c == 0), stop=(c == 7), accum=(c > 0 or h > 0))
            P.A("activation", out=d["qT"][:, :, :].rearrange("p h t -> p (h t)"), in_=pb[:, 0:512], func=AF.Copy)
            for h in range(4):
                P.M("matmul", out=pp2[:, h * 256:(h + 1) * 256], lhsT=d["qT"][:, h, :], rhs=kmT[:, h, :],
                    start=True, stop=True, accum=(h > 0))
            for h in range(4):
                P.A("activation", out=d["p"][:, h, :], in_=pp2[:, h * 256:(h + 1) * 256], func=AF.Exp,
                    scale=128.0 ** -0.5, accum_out=stt[:, 4 + h:5 + h])
            P.V("reciprocal", out=stt[:, 8:12], in_=stt[:, 4:8])
            P.V("tensor_tensor", out=d["pn"][:, :, :], in0=d["p"][:, :, :], in1=bc(stt[:, 8:12], [128, 4, 256], 2),
                op=ALU.mult)
            pb = bank(k)
            pv = bfv(pb)
            pnf = d["pn"][:, :, :].rearrange("p h m -> p (h m)")
            for c in range(8):
                P.M("transpose", out=pv[:, c * 128:(c + 1) * 128], in_=pnf[:, c * 128:(c + 1) * 128],
                    identity=k.ident[:, :], accum=(c > 0))
            P.A("activation", out=d["pT"][:, :, :].rearrange("p c t -> p (c t)"), in_=pv[:, 0:1024], func=AF.Copy)
            pb = bank(k)
            for h in range(4):
                for mc in range(2):
                    P.M("matmul", out=pb[:, h * 128:(h + 1) * 128], lhsT=vm[:, mc, h * 128:(h + 1) * 128],
                        rhs=d["pT"][:, h * 2 + mc, :], start=(mc == 0), stop=(mc == 1), accum=(mc > 0 or h > 0))
            P.A("activation", out=d["ocT"][:, :, :].rearrange("p h t -> p (h t)"), in_=pb[:, 0:512], func=AF.Copy)
            for n in range(2):
                cs = slice(n * 512, (n + 1) * 512)
                pb = bank(k)
                for h in range(4):
                    P.M("matmul", out=pb[:, 0:512], lhsT=d["ocT"][:, h, :], rhs=cwo[:, h, cs],
                        start=(h == 0), stop=(h == 3), accum=(h > 0))
                P.V("tensor_tensor", out=d["x2"][:, cs], in0=x1[:, cs], in1=pb[:, 0:512], op=ALU.add)
            P.dma("pool", out=k.x2[t0:t0 + 128, :], in_=d["x2"][:, :])


def phase_p3b(k, l, xout, last):
    nc, P, S = k.nc, k.P, k.S
    w = k.w
    TT = 256
    NTT = S // TT
    with contextlib.ExitStack() as st:
        def SB(name, shape, dt):
            return st.enter_context(nc.sbuf_tensor(un("p4_" + name), list(shape), dt))
        alloc_banks(k, st)
        w1 = SB("w1", [128, 8, 4096], BF16)
        w2 = SB("w2", [128, 32, 1024], BF16)
        gcol = SB("gcol", [128, 8], F32)
        for c in range(8):
            load_col(k, "sp", gcol[:, c:c + 1], w["norm_mlp"][l, c * 128:(c + 1) * 128], 128)
        with contextlib.ExitStack() as st2:
            stgs = [st2.enter_context(nc.sbuf_tensor(un("p4_stg%d" % i), [128, 4096], F32)) for i in range(2)]
            w1v = w["mlp_w1"][l].rearrange("(c p) n -> p c n", p=128)
            for c in range(8):
                sg = stgs[c % 2]
                P.dma("sp", out=sg[:, :], in_=w1v[:, c, :])
                for hf in range(2):
                    P.V("tensor_scalar", out=w1[:, c, hf * 2048:(hf + 1) * 2048], in0=sg[:, hf * 2048:(hf + 1) * 2048],
                        scalar1=gcol[:, c:c + 1], scalar2=None, op0=ALU.mult)
        P.barrier()
        w2v = w["mlp_w2"][l].rearrange("(f p) n -> p f n", p=128)
        for f0 in range(0, 32, 4):
            P.dma("pool", out=w2[:, f0:f0 + 4, :], in_=w2v[:, f0:f0 + 4, :])
        if last:
            gf = SB("gf", [128, 1024], F32)
            P.dma("sp", out=gf[:, :], in_=w["norm_final"][0:1, :].broadcast_to([128, 1024]))
        uT = SB("uT", [128, 32, TT], BF16)
        rbuf = [SB("r%d" % i, [128, TT], BF16) for i in range(3)]
        junk = SB("junk", [128, 1024], BF16)
        hm = [SB("hm%d" % i, [128, 1024], BF16) for i in range(2)]
        sets = []
        for s_ in range(2):
            d = {}
            for name, shape, dt in [("x", [128, 2, 1024], F32), ("hmT", [128, 8, TT], BF16), ("st", [128, 16], F32),
                                    ("y", [128, 2, 1024], F32)]:
                if name == "y" and not last:
                    continue
                d[name] = SB("%s%d" % (name, s_), shape, dt)
            sets.append(d)
        nr = 0
        for i in range(NTT):
            d = sets[i % 2]
            t0 = i * TT
            x, hmT, stt = d["x"], d["hmT"], d["st"]
            P.dma("sp", out=x[:, :, :], in_=k.x2[t0:t0 + TT, :].rearrange("(j p) n -> p j n", p=128))
            for j in range(2):
                P.A("activation", out=junk[:, :], in_=x[:, j, :], func=AF.Square, accum_out=stt[:, j:j + 1])
            rstd_from_ss(k, stt[:, 2:4], stt[:, 0:2], stt[:, 4:6], 1024.0, EPS)
            for j in range(2):
                P.V("tensor_scalar", out=hm[j][:, :], in0=x[:, j, :], scalar1=stt[:, 2 + j:3 + j], scalar2=None,
                    op0=ALU.mult)
                pb = bank(k)
                pv = bfv(pb)
                for c in range(8):
                    P.M("transpose", out=pv[:, c * 128:(c + 1) * 128], in_=hm[j][:, c * 128:(c + 1) * 128],
                        identity=k.ident[:, :], accum=(c > 0))
                P.A("activation", out=hmT[:, :, j * 128:(j + 1) * 128],
                    in_=pv[:, 0:1024].rearrange("p (c t) -> p c t", c=8), func=AF.Copy)
            for f in range(32):
                pb = bank(k)
                for c in range(8):
                    P.M("matmul", out=pb[:, 0:TT], lhsT=w1[:, c, f * 128:(f + 1) * 128], rhs=hmT[:, c, :],
                        start=(c == 0), stop=(c == 7), accum=(c > 0))
                r = rbuf[nr % 3]
                nr += 1
                P.A("activation", out=r[:, :], in_=pb[:, 0:TT], func=AF.Relu)
                P.G("tensor_tensor", out=uT[:, f, :], in0=r[:, :], in1=r[:, :], op=ALU.mult)
            for j in range(2):
                for n in range(2):
                    cs = slice(n * 512, (n + 1) * 512)
                    pb = bank(k)
                    for f in range(32):
                        P.M("matmul", out=pb[:, 0:512], lhsT=uT[:, f, j * 128:(j + 1) * 128], rhs=w2[:, f, cs],
                            start=(f == 0), stop=(f == 31), accum=(f > 0))
                    P.V("tensor_tensor", out=x[:, j, cs], in0=x[:, j, cs], in1=pb[:, 0:512], op=ALU.add)
            if not last:
                P.dma("pool", out=xout[t0:t0 + TT, :].rearrange("(j p) n -> p j n", p=128), in_=x[:, :, :])
            else:
                for j in range(2):
                    P.A("activation", out=junk[:, :], in_=x[:, j, :], func=AF.Square, accum_out=stt[:, 8 + j:9 + j])
                rstd_from_ss(k, stt[:, 10:12], stt[:, 8:10], stt[:, 12:14], 1024.0, EPS)
                for j in range(2):
                    P.V("scalar_tensor_tensor", out=d["y"][:, j, :], in0=x[:, j, :], scalar=stt[:, 10 + j:11 + j],
                        in1=gf[:, :], op0=ALU.mult, op1=ALU.mult)
                P.dma("pool", out=k.y[t0:t0 + TT, :].rearrange("(j p) n -> p j n", p=128), in_=d["y"][:, :, :])
                if "xs0" in k.dbg or "xs1" in k.dbg:
                    P.dma("pool", out=xout[t0:t0 + TT, :].rearrange("(j p) n -> p j n", p=128), in_=x[:, :, :])


def rope_tables(pos, dim):
    inv = (10000.0 ** (-np.arange(0, dim, 2, dtype=np.float32) / dim)).astype(np.float32)
    ang = pos.astype(np.float32)[:, None] * inv[None, :]
    return np.cos(ang).astype(np.float32), np.sin(ang).astype(np.float32)


def const_inputs(S):
    pos = np.arange(S)
    c1, s1 = rope_tables(pos, 32)
    cr, sr = rope_tables(pos // 64, 32)
    cc, sc = rope_tables(pos % 64, 32)
    tab1 = np.stack([c1, s1], 1).astype(np.float32)
    tab2 = np.stack([np.stack([cr, cc], 1), np.stack([sr, sc], 1)], 1).astype(np.float32)
    s_idx = np.arange(128)[:, None]
    t_idx = np.arange(128)[None, :]
    msk = np.zeros((2, 2, 128, 128), np.float32)
    msk[0, 0] = s_idx < t_idx
    msk[0, 1] = s_idx <= t_idx
    msk[1, 0] = s_idx > t_idx
    msk[1, 1] = s_idx >= t_idx
    bdm = np.zeros((128, 128), np.float32)
    bdm[:64, :64] = 1
    bdm[64:, 64:] = 1
    return dict(tab1=tab1, tab2=tab2, ident=np.eye(128, dtype=np.float32), msk=msk, bdm=bdm)


_NC_CACHE = {}


def kernel(**inputs):
    S, depth = 8192, 2
    key = (S, depth)
    if key not in _NC_CACHE:
        _NC_CACHE[key] = build(S, depth)
    nc = _NC_CACHE[key]
    xp, xs = np.asarray(inputs["x_prompt"]), np.asarray(inputs["x_sample"])
    mp, ms = np.asarray(inputs["mem_prompt"]), np.asarray(inputs["mem_sample"])
    seqs = [(xp[b], mp[b]) for b in range(xp.shape[0])] + [(xs[b], ms[b]) for b in range(xs.shape[0])]
    n_seq = len(seqs)
    consts = const_inputs(S)
    wts = {name: np.ascontiguousarray(np.asarray(inputs[name], np.float32)) for name, _ in W_SPECS}
    wts["norm_final"] = np.ascontiguousarray(np.asarray(inputs["norm_final"], np.float32).reshape(1, D))
    in_maps = []
    for c in range(8):
        x, mem = seqs[c % n_seq]
        m = dict(x=np.ascontiguousarray(x, np.float32), mem=np.ascontiguousarray(mem, np.float32))
        m.update(wts)
        m.update(consts)
        in_maps.append(m)
    res = run_bass_kernel_spmd(nc, in_maps, core_ids=list(range(8)))
    ys = [np.asarray(res.results[c]["y"], np.float32) for c in range(n_seq)]
    y_prompt = np.stack(ys[:xp.shape[0]], 0)
    y_sample = np.stack(ys[xp.shape[0]:], 0)
    return (y_prompt, y_sample)
```

```python
import contextlib
import math
import numpy as np
import ml_dtypes
import concourse.bass as bass
import concourse.mybir as mybir
from concourse.bass_utils import run_bass_kernel_spmd

F32 = mybir.dt.float32
BF16 = mybir.dt.bfloat16
ALU = mybir.AluOpType
AF = mybir.ActivationFunctionType
AX = mybir.AxisListType

D = 1024
NMEM = 256
IN_COLS = 6432
EPS = 1e-6
GN_EPS = 64e-5
DECAY_C = math.exp(-0.5)

import os
MAXOPS = int(os.environ.get("MAXOPS", "100000000"))
RW_RATIO = int(os.environ.get("RW_RATIO", "2"))
RW_DELAY = int(os.environ.get("RW_DELAY", "2"))
P3_RATIO = int(os.environ.get("P3_RATIO", "1"))
P1_RATIO = int(os.environ.get("P1_RATIO", "1"))
ENGS = ("pe", "act", "dve", "pool", "sp")
N_DMA_SEMS = 16
READ_KEYS = ("in_", "in0", "in1", "lhsT", "rhs", "scalar", "scalar1", "scalar2", "bias", "scale",
             "identity", "data0", "data1", "initial")
WRITE_KEYS = ("out", "accum_out", "ap")


class Buf:
    __slots__ = ("name", "w", "r")

    def __init__(self, name):
        self.name = name
        self.w = None
        self.r = {}


class Prog:
    def __init__(self, nc):
        self.nc = nc
        self.es = contextlib.ExitStack()
        self.eobj = {"pe": nc.tensor, "act": nc.scalar, "dve": nc.vector, "pool": nc.gpsimd, "sp": nc.sync}
        self.sem = {}
        self.cnt = {}
        for e in ENGS:
            self.sem[e] = self.es.enter_context(nc.semaphore("s_" + e))
            self.cnt[e] = 0
        self.dq = {}
        for q in ("sp", "act", "pool"):
            sems = []
            for i in range(N_DMA_SEMS):
                k = "d_%s_%d" % (q, i)
                self.sem[k] = self.es.enter_context(nc.semaphore(k))
                self.cnt[k] = 0
                sems.append(k)
            self.dq[q] = [sems, 0]
        self.seen = {e: {} for e in ENGS}
        self.ops = {e: [] for e in ENGS}
        self.bufs = {}
        self.nops = 0

    def buf_of(self, ap):
        n = ap.name
        b = self.bufs.get(n)
        if b is None:
            b = self.bufs[n] = Buf(n)
        return b

    def _need(self, e, key, val, waits):
        if self.seen[e].get(key, 0) >= val:
            return
        self.seen[e][key] = val
        waits.append((key, val))

    def _collect(self, kw, extra_r, extra_w):
        reads, writes = [], []
        for k in READ_KEYS:
            v = kw.get(k)
            if v is not None and hasattr(v, "name") and hasattr(v, "ap"):
                reads.append(self.buf_of(v))
        for k in WRITE_KEYS:
            v = kw.get(k)
            if v is not None and hasattr(v, "name") and hasattr(v, "ap"):
                writes.append(self.buf_of(v))
        for v in extra_r:
            reads.append(v if isinstance(v, Buf) else self.buf_of(v))
        for v in extra_w:
            writes.append(v if isinstance(v, Buf) else self.buf_of(v))
        return reads, writes

    def _issue(self, e, inckey, incv, name, kw, reads, writes, accum):
        self.nissued = getattr(self, "nissued", 0) + 1
        if self.nissued > MAXOPS:
            return
        if self.nissued == MAXOPS:
            print("LAST OP:", e, name, {a: (str(b.name) + str(b.shape) if hasattr(b, "ap") else b) for a, b in kw.items()})
        need = {}
        for b in reads:
            if b.w is not None and need.get(b.w[0], 0) < b.w[1]:
                need[b.w[0]] = b.w[1]
        for b in writes:
            if b.w is not None and not (e == "pe" and b.w[0] == "pe") and need.get(b.w[0], 0) < b.w[1]:
                need[b.w[0]] = b.w[1]
            for k, v in b.r.items():
                if need.get(k, 0) < v:
                    need[k] = v
        waits = []
        for k, v in need.items():
            self._need(e, k, v, waits)
        self.cnt[inckey] += incv
        v = self.cnt[inckey]
        self.ops[e].append((waits, name, kw, inckey, incv))
        self.nops += 1 + len(waits)
        for b in reads:
            if b.r.get(inckey, 0) < v:
                b.r[inckey] = v
        for b in writes:
            b.w = (inckey, v)
            b.r = {}

    def op(self, e, name, R=(), W=(), accum=False, drain=False, **kw):
        reads, writes = self._collect(kw, R, W)
        if drain and self.cnt[e] > 0:
            d_ = Buf("drain")
            d_.w = (e, self.cnt[e])
            reads = list(reads) + [d_]
        self._issue(e, e, 1, name, kw, reads, writes, accum)

    def V(self, name, **kw):
        self.op("dve", name, **kw)

    def A(self, name, **kw):
        self.op("act", name, **kw)

    def G(self, name, **kw):
        self.op("pool", name, **kw)

    def M(self, name="matmul", **kw):
        self.op("pe", name, **kw)

    def dma(self, q, out, in_, R=(), W=()):
        kw = dict(out=out, in_=in_)
        reads, writes = self._collect(kw, R, W)
        sems, idx = self.dq[q]
        key = sems[idx % len(sems)]
        self.dq[q][1] = idx + 1
        if self.cnt[key] > 0:
            w = []
            self._need(q, key, self.cnt[key], w)
            pre = w
        else:
            pre = []
        n0 = len(self.ops[q])
        self._issue(q, key, 16, "dma_start", kw, reads, writes, False)
        if pre and len(self.ops[q]) > n0:
            waits, name, kw2, ik, iv = self.ops[q][n0]
            self.ops[q][n0] = (pre + waits, name, kw2, ik, iv)

    def barrier(self):
        for e in ENGS:
            waits = []
            for key, v in self.cnt.items():
                if v > 0:
                    self._need(e, key, v, waits)
            if waits:
                self.ops[e].append((waits, None, None, None, None))
                self.nops += len(waits)

    def wait_bufs(self, e, aps):
        waits = []
        for a in aps:
            b = a if isinstance(a, Buf) else self.buf_of(a)
            if b.w is not None:
                self._need(e, b.w[0], b.w[1], waits)
        self.ops[e].append((waits, None, None, None, None))

    def emit(self):
        nc = self.nc
        with nc.Block() as block:
            def run(e):
                def body(engine):
                    for waits, name, kw, ik, iv in self.ops[e]:
                        for k, v in waits:
                            engine.wait_ge(self.sem[k], v)
                        if name is None:
                            continue
                        inst = getattr(engine, name)(**kw)
                        inst.then_inc(self.sem[ik], iv)
                return body
            block.sync(run("sp"))
            block.scalar(run("act"))
            block.vector(run("dve"))
            block.gpsimd(run("pool"))
            block.tensor(run("pe"))

    def close(self):
        self.es.close()


W_SPECS = [
    ("norm_mix", (D,)), ("w_in", (D, IN_COLS)), ("mla_q_norm", (384,)), ("mla_w_uq", (384, 768)),
    ("mla_kv_norm", (256,)), ("mla_w_ukv", (256, 1024)), ("gqa_q_norm", (64,)), ("gqa_k_norm", (64,)),
    ("rwkv_mu", (1920,)), ("rwkv_w0", (2, 512)), ("rwkv_w2", (2, 64, 512)), ("rwkv_a0", (2, 512)),
    ("rwkv_a2", (2, 64, 512)), ("rwkv_g2", (128, 512)), ("rwkv_k_k", (512,)), ("rwkv_k_a", (512,)),
    ("rwkv_r_k", (8, 64)), ("rwkv_ln_w", (512,)), ("rwkv_ln_b", (512,)), ("w_branch", (3, 512, D)),
    ("b_gate", (3, D)), ("w_out", (D, D)), ("norm_cross", (D,)), ("norm_mem", (D,)),
    ("cross_wq", (D, 512)), ("cross_wkv", (D, 1024)), ("cross_wo", (512, D)), ("norm_mlp", (D,)),
    ("mlp_w1", (D, 4096)), ("mlp_w2", (4096, D)),
]


class K:
    pass


def build(S, depth, dbg=(), phases=("p1", "p2", "rf", "rb", "p3a", "p3b")):
    NT = S // 128
    nc = bass.Bass("TRN2", target_bir_lowering=False)
    P = Prog(nc)
    k = K()
    k.nc, k.P, k.S, k.NT, k.depth, k.dbg = nc, P, S, NT, depth, dbg
    k.phases = phases
    k.y_done = set()

    def din(name, shape):
        return nc.dram_tensor(name, list(shape), F32, kind="ExternalInput").ap()

    def dscr(name, shape, dt):
        kind = "ExternalOutput" if name in dbg else "Internal"
        return nc.dram_tensor(name, list(shape), dt, kind=kind).ap()

    k.x = din("x", (S, D))
    k.mem = din("mem", (NMEM, D))
    k.w = {}
    for name, shp in W_SPECS:
        k.w[name] = din(name, (depth,) + shp)
    k.w["norm_final"] = din("norm_final", (1, D))
    k.tab1 = din("tab1", (S, 2, 16))
    k.tab2 = din("tab2", (S, 2, 2, 16))
    k.ident_d = din("ident", (128, 128))
    k.msk_d = din("msk", (2, 2, 128, 128))
    k.bdm_d = din("bdm", (128, 128))
    k.y = nc.dram_tensor("y", [S, D], F32, kind="ExternalOutput").ap()

    k.h_tm = dscr("h_tm", (S, D), BF16)
    k.hT_t = dscr("hT_t", (NT, 128, 8, 128), BF16)
    k.qT_mla = dscr("qT_mla", (8, 96, S), BF16)
    k.kT_mla = dscr("kT_mla", (8, 96, S), BF16)
    k.v_mla = dscr("v_mla", (S, 512), BF16)
    k.qT_gqa = dscr("qT_gqa", (512, S), BF16)
    k.kT_gqa = dscr("kT_gqa", (128, S), BF16)
    k.v_gqa = dscr("v_gqa", (S, 128), BF16)
    k.o_mla = dscr("o_mla", (S, 512), BF16)
    k.o_gqa = dscr("o_gqa", (S, 512), BF16)
    k.y_fw = dscr("y_fw", (S, 512), F32)
    k.z_r = dscr("z_r", (S, 1920), F32)
    k.zmix = dscr("zmix", (S, 1920), F32)
    k.oT_rw = dscr("oT_rw", (NT, 128, 4, 128), BF16)
    k.x2 = dscr("x2", (S, D), F32)
    k.xs = [dscr("xs0", (S, D), F32), dscr("xs1", (S, D), F32)]

    with contextlib.ExitStack() as gst:
        k.gst = gst
        k.rot = [0]
        k.ident = gst.enter_context(nc.sbuf_tensor("ident_b", [128, 128], BF16))
        P.dma("pool", out=k.ident[:, :], in_=k.ident_d)
        k.identf = gst.enter_context(nc.sbuf_tensor("ident_f", [128, 128], F32))
        P.dma("sp", out=k.identf[:, :], in_=k.ident_d)
        for l in range(depth):
            xin = k.x if l == 0 else k.xs[(l - 1) % 2]
            xout = k.xs[l % 2]
            if "p1" in phases:
                phase_p1(k, l, xin)
                P.barrier()
            if "p2" in phases:
                phase_attn(k, l)
                P.barrier()
            if "rw2" in phases:
                phase_rwkv2(k, l)
                P.barrier()
            if "rf" in phases:
                phase_rwkv(k, l, 0)
                P.barrier()
            if "rb" in phases:
                phase_rwkv(k, l, 1)
                P.barrier()
            if "p3a" in phases:
                phase_p3a(k, l, xin)
                P.barrier()
            if "p3b" in phases:
                phase_p3b(k, l, xout, last=(l == depth - 1))
                P.barrier()
        outs = [k.y]
        for name in dbg:
            outs.append(P.bufs[name]) if name in P.bufs else None
        P.wait_bufs("sp", outs)
        P.emit()
    P.close()
    return nc


_UID = [0]


def un(name):
    _UID[0] += 1
    return "%s_u%d" % (name, _UID[0])


def alloc_banks(k, st, n=8):
    k.banks = [st.enter_context(k.nc.psum_tensor(un("bank%d" % i), [128, 512], F32)) for i in range(n)]


class BankPool:
    def __init__(self, banks):
        self.banks = banks
        self.i = 0

    def get(self):
        b = self.banks[self.i % len(self.banks)]
        self.i += 1
        return b


def run_skewed(gens, ratio=1):
    old = None
    for new in gens:
        new_mid = False
        while True:
            for _ in range(ratio):
                if old is not None:
                    try:
                        next(old)
                    except StopIteration:
                        old = None
            if not new_mid:
                try:
                    if next(new) == "mid":
                        new_mid = True
                except StopIteration:
                    new_mid = True
                    new = None
            if old is None and new_mid:
                break
        old = new
    while old is not None:
        try:
            next(old)
        except StopIteration:
            old = None


def bank(k, lo=0, hi=None):
    if hi is None:
        hi = len(k.banks)
    n = hi - lo
    i = k.rot[0] % n
    k.rot[0] += 1
    return k.banks[lo + i]


def bfv(b, ncols=1024):
    return b[:, :].bitcast(BF16)


def rstd_from_ss(k, out, ss, t, n, eps):
    P = k.P
    if eps is not None and eps != 0.0:
        P.V("tensor_scalar", out=t, in0=ss, scalar1=1.0 / n, scalar2=float(eps), op0=ALU.mult, op1=ALU.add)
        P.A("activation", out=t, in_=t, func=AF.Ln)
    else:
        P.A("activation", out=t, in_=ss, func=AF.Ln, scale=1.0 / n)
    P.A("activation", out=out, in_=t, func=AF.Exp, scale=-0.5)


def load_col(k, q, dst, src1d, n):
    k.P.dma(q, out=dst, in_=src1d.rearrange("(p o) -> p o", o=1))


def bc(ap, shape, axis):
    return ap.unsqueeze(axis).broadcast_to(list(shape))


def phase_p1(k, l, xin):
    nc, P, S, NT = k.nc, k.P, k.S, k.NT
    w = k.w
    with contextlib.ExitStack() as st:
        def SB(name, shape, dt):
            return st.enter_context(nc.sbuf_tensor(un("p1_" + name), list(shape), dt))
        alloc_banks(k, st)
        w_att = SB("watt", [128, 8, 1440], BF16)
        w_r = SB("wr", [128, 8, 1920], BF16)
        w_uq = SB("wuq", [128, 3, 768], BF16)
        w_ukv = SB("wukv", [128, 2, 1024], BF16)
        stg = SB("stg", [128, 2304], F32)
        g_bc = SB("gbc", [128, 1024], F32)
        gq_bc = SB("gqbc", [128, 64], F32)
        gk_bc = SB("gkbc", [128, 64], F32)
        gcol = SB("gcol", [128, 5], F32)
        win = w["w_in"][l].rearrange("(c p) n -> p c n", p=128)
        for c in range(8):
            P.dma("pool", out=w_att[:, c, :], in_=win[:, c, 0:1440])
        for c in range(8):
            P.dma("pool", out=w_r[:, c, :], in_=win[:, c, 1440:3360])
        P.dma("sp", out=g_bc[:, :], in_=w["norm_mix"][l:l + 1, :].broadcast_to([128, 1024]))
        P.dma("sp", out=gq_bc[:, :], in_=w["gqa_q_norm"][l:l + 1, :].broadcast_to([128, 64]))
        P.dma("sp", out=gk_bc[:, :], in_=w["gqa_k_norm"][l:l + 1, :].broadcast_to([128, 64]))
        for c in range(3):
            load_col(k, "sp", gcol[:, c:c + 1], w["mla_q_norm"][l, c * 128:(c + 1) * 128], 128)
        for c in range(2):
            load_col(k, "sp", gcol[:, 3 + c:4 + c], w["mla_kv_norm"][l, c * 128:(c + 1) * 128], 128)
        P.dma("sp", out=stg[:, 0:2304].rearrange("p (c n) -> p c n", c=3),
              in_=w["mla_w_uq"][l].rearrange("(c p) n -> p c n", p=128))
        for c in range(3):
            P.V("tensor_scalar", out=w_uq[:, c, :], in0=stg[:, c * 768:(c + 1) * 768],
                scalar1=gcol[:, c:c + 1], scalar2=None, op0=ALU.mult)
        P.dma("sp", out=stg[:, 0:2048].rearrange("p (c n) -> p c n", c=2),
              in_=w["mla_w_ukv"][l].rearrange("(c p) n -> p c n", p=128))
        for c in range(2):
            P.V("tensor_scalar", out=w_ukv[:, c, :], in0=stg[:, c * 1024:(c + 1) * 1024],
                scalar1=gcol[:, 3 + c:4 + c], scalar2=None, op0=ALU.mult)

        sets = []
        for s in range(2):
            d = {}
            for name, shape, dt in [
                ("x", [128, 1024], F32), ("junk", [128, 1024], BF16), ("hb", [128, 1024], BF16),
                ("hT", [128, 8, 128], BF16), ("tb1", [128, 2, 16], F32), ("tb2", [128, 2, 2, 16], F32),
                ("st", [128, 16], F32), ("cqb", [128, 384], BF16), ("cqT", [128, 3, 128], BF16),
                ("q32", [128, 8, 96], F32), ("qrot", [128, 8, 96], BF16), ("qT", [96, 8, 128], BF16),
                ("ta", [128, 8, 2, 16], F32), ("tb", [128, 8, 2, 16], F32),
                ("ckvb", [128, 256], BF16), ("ckvT", [128, 2, 128], BF16), ("kt", [128, 8, 96], BF16),
                ("vt", [128, 8, 64], BF16), ("kr", [128, 32], F32), ("kT", [96, 8, 128], BF16),
                ("sq", [128, 512], F32), ("gst", [128, 24], F32), ("qn", [128, 8, 64], F32),
                ("gqr", [128, 8, 64], BF16), ("gqT", [128, 4, 128], BF16),
                ("kn", [128, 2, 64], F32), ("gkr", [128, 2, 64], BF16), ("gkT", [128, 128], BF16),
                ("gv", [128, 128], BF16), ("zr", [128, 1920], F32), ("gk32", [128, 128], F32),
            ]:
                d[name] = SB("%s%d" % (name, s), shape, dt)
            sets.append(d)

        ident = k.ident
        poolA = BankPool(k.banks[0:3])
        poolB = BankPool(k.banks[3:8])

        def tile_gen(i):
            pool = poolA
            d = sets[i % 2]
            t0 = i * 128
            x, hb, hT, stt = d["x"], d["hb"], d["hT"], d["st"]
            P.dma("sp", out=x[:, :], in_=xin[t0:t0 + 128, :])
            P.dma("sp", out=d["tb1"][:, :, :], in_=k.tab1[t0:t0 + 128])
            P.dma("sp", out=d["tb2"][:, :, :, :], in_=k.tab2[t0:t0 + 128])
            P.A("activation", out=d["junk"][:, :], in_=x[:, :], func=AF.Square, accum_out=stt[:, 0:1])
            rstd_from_ss(k, stt[:, 1:2], stt[:, 0:1], stt[:, 2:3], 1024.0, EPS)
            P.V("scalar_tensor_tensor", out=hb[:, :], in0=x[:, :], scalar=stt[:, 1:2], in1=g_bc[:, :],
                op0=ALU.mult, op1=ALU.mult)
            P.dma("pool", out=k.h_tm[t0:t0 + 128, :], in_=hb[:, :])
            pb = pool.get()
            pv = bfv(pb)
            for c in range(8):
                P.M("transpose", out=pv[:, c * 128:(c + 1) * 128], in_=hb[:, c * 128:(c + 1) * 128],
                    identity=ident[:, :], accum=(c > 0))
            P.A("activation", out=hT[:, :, :].rearrange("p c t -> p (c t)"), in_=pv[:, 0:1024], func=AF.Copy)
            P.dma("pool", out=k.hT_t[i], in_=hT[:, :, :])
            yield

            def proj(c0, n):
                nonlocal pool
                b = pool.get()
                for c in range(8):
                    P.M("matmul", out=b[:, 0:n], lhsT=hT[:, c, :], rhs=w_att[:, c, c0:c0 + n],
                        start=(c == 0), stop=(c == 7), accum=(c > 0))
                return b

            for ci, (c0, n) in enumerate(((0, 512), (512, 512), (1024, 512), (1536, 384))):
                b = pool.get()
                for c in range(8):
                    P.M("matmul", out=b[:, 0:n], lhsT=hT[:, c, :], rhs=w_r[:, c, c0:c0 + n],
                        start=(c == 0), stop=(c == 7), accum=(c > 0))
                if ci % 2 == 0:
                    P.A("activation", out=d["zr"][:, c0:c0 + n], in_=b[:, 0:n], func=AF.Copy)
                else:
                    P.V("tensor_copy", out=d["zr"][:, c0:c0 + n], in_=b[:, 0:n])
                yield
            P.dma("pool", out=k.z_r[t0:t0 + 128, :], in_=d["zr"][:, :])
            pool = poolB
            yield "mid"
            cos1 = bc(d["tb1"][:, 0, :], [128, 8, 16], 1)
            sin1 = bc(d["tb1"][:, 1, :], [128, 8, 16], 1)
            pq = proj(0, 384)
            P.A("activation", out=d["junk"][:, 0:384], in_=pq[:, 0:384], func=AF.Square, accum_out=stt[:, 3:4])
            rstd_from_ss(k, stt[:, 4:5], stt[:, 3:4], stt[:, 5:6], 384.0, EPS)
            P.V("tensor_copy", out=d["cqb"][:, :], in_=pq[:, 0:384])
            yield
            pb = pool.get()
            pv = bfv(pb)
            for c in range(3):
                P.M("transpose", out=pv[:, c * 128:(c + 1) * 128], in_=d["cqb"][:, c * 128:(c + 1) * 128],
                    identity=ident[:, :], accum=(c > 0))
            P.A("activation", out=d["cqT"][:, :, :].rearrange("p c t -> p (c t)"), in_=pv[:, 0:384], func=AF.Copy)
            q32 = d["q32"]
            q32f = q32[:, :, :].rearrange("p h e -> p (h e)")
            for (c0, n) in ((0, 512), (512, 256)):
                b = pool.get()
                for c in range(3):
                    P.M("matmul", out=b[:, 0:n], lhsT=d["cqT"][:, c, :], rhs=w_uq[:, c, c0:c0 + n],
                        start=(c == 0), stop=(c == 2), accum=(c > 0))
                P.V("tensor_scalar", out=q32f[:, c0:c0 + n], in0=b[:, 0:n], scalar1=stt[:, 4:5], scalar2=None,
                    op0=ALU.mult)
            yield
            qrot = d["qrot"]
            ta = d["ta"][:, :, 0, :]
            tb = d["tb"][:, :, 0, :]
            P.G("tensor_copy", out=qrot[:, :, 0:64], in_=q32[:, :, 0:64])
            P.V("tensor_tensor", out=ta, in0=q32[:, :, 64:80], in1=cos1, op=ALU.mult)
            P.V("tensor_tensor", out=tb, in0=q32[:, :, 80:96], in1=sin1, op=ALU.mult)
            P.V("tensor_tensor", out=qrot[:, :, 64:80], in0=ta, in1=tb, op=ALU.subtract)
            P.V("tensor_tensor", out=ta, in0=q32[:, :, 64:80], in1=sin1, op=ALU.mult)
            P.V("tensor_tensor", out=tb, in0=q32[:, :, 80:96], in1=cos1, op=ALU.mult)
            P.V("tensor_tensor", out=qrot[:, :, 80:96], in0=ta, in1=tb, op=ALU.add)
            pb = pool.get()
            pv = bfv(pb)
            for h in range(8):
                P.M("transpose", out=pv[0:96, h * 128:(h + 1) * 128], in_=qrot[:, h, :], identity=ident[:, :],
                    accum=(h > 0))
            P.A("activation", out=d["qT"][:, :, :].rearrange("p h t -> p (h t)"), in_=pv[0:96, 0:1024], func=AF.Copy)
            P.dma("pool", out=k.qT_mla[:, :, t0:t0 + 128].rearrange("h e s -> e h s"), in_=d["qT"][:, :, :])
            yield
            pkv = proj(384, 288)
            P.A("activation", out=d["junk"][:, 0:256], in_=pkv[:, 0:256], func=AF.Square, accum_out=stt[:, 6:7])
            rstd_from_ss(k, stt[:, 7:8], stt[:, 6:7], stt[:, 8:9], 256.0, EPS)
            P.V("tensor_copy", out=d["ckvb"][:, :], in_=pkv[:, 0:256])
            kr = d["kr"]
            c1 = d["tb1"][:, 0, :]
            s1 = d["tb1"][:, 1, :]
            t2a = d["ta"][:, 0, 1, :]
            t2b = d["tb"][:, 0, 1, :]
            P.V("tensor_tensor", out=t2a, in0=pkv[:, 256:272], in1=c1, op=ALU.mult)
            P.V("tensor_tensor", out=t2b, in0=pkv[:, 272:288], in1=s1, op=ALU.mult)
            P.V("tensor_tensor", out=kr[:, 0:16], in0=t2a, in1=t2b, op=ALU.subtract)
            P.V("tensor_tensor", out=t2a, in0=pkv[:, 256:272], in1=s1, op=ALU.mult)
            P.V("tensor_tensor", out=t2b, in0=pkv[:, 272:288], in1=c1, op=ALU.mult)
            P.V("tensor_tensor", out=kr[:, 16:32], in0=t2a, in1=t2b, op=ALU.add)
            pb = pool.get()
            pv = bfv(pb)
            for c in range(2):
                P.M("transpose", out=pv[:, c * 128:(c + 1) * 128], in_=d["ckvb"][:, c * 128:(c + 1) * 128],
                    identity=ident[:, :], accum=(c > 0))
            P.A("activation", out=d["ckvT"][:, :, :].rearrange("p c t -> p (c t)"), in_=pv[:, 0:256], func=AF.Copy)
            yield
            kt, vt = d["kt"], d["vt"]
            for half in range(2):
                b = pool.get()
                for c in range(2):
                    P.M("matmul", out=b[:, 0:512], lhsT=d["ckvT"][:, c, :], rhs=w_ukv[:, c, half * 512:(half + 1) * 512],
                        start=(c == 0), stop=(c == 1), accum=(c > 0))
                b3 = b[:, 0:512].rearrange("p (h e) -> p h e", h=4)
                P.V("tensor_scalar", out=kt[:, half * 4:(half + 1) * 4, 0:64], in0=b3[:, :, 0:64], scalar1=stt[:, 7:8],
                    scalar2=None, op0=ALU.mult)
                P.V("tensor_scalar", out=vt[:, half * 4:(half + 1) * 4, :], in0=b3[:, :, 64:128], scalar1=stt[:, 7:8],
                    scalar2=None, op0=ALU.mult)
            P.G("tensor_copy", out=kt[:, :, 64:96], in_=bc(kr[:, :], [128, 8, 32], 1))
            pb = pool.get()
            pv = bfv(pb)
            for h in range(8):
                P.M("transpose", out=pv[0:96, h * 128:(h + 1) * 128], in_=kt[:, h, :], identity=ident[:, :],
                    accum=(h > 0))
            P.A("activation", out=d["kT"][:, :, :].rearrange("p h t -> p (h t)"), in_=pv[0:96, 0:1024], func=AF.Copy)
            P.dma("pool", out=k.kT_mla[:, :, t0:t0 + 128].rearrange("h e s -> e h s"), in_=d["kT"][:, :, :])
            P.dma("pool", out=k.v_mla[t0:t0 + 128, :], in_=vt[:, :, :].rearrange("p h e -> p (h e)"))

            yield
            def qknorm_rope(pb_ap, nh, gbc, n32, rot, soff):
                gs = d["gst"]
                P.A("activation", out=d["sq"][:, 0:nh * 64], in_=pb_ap, func=AF.Square)
                P.V("tensor_reduce", out=gs[:, soff:soff + nh],
                    in_=d["sq"][:, 0:nh * 64].rearrange("p (h e) -> p h e", h=nh), axis=AX.X, op=ALU.add)
                rstd_from_ss(k, gs[:, soff + 8:soff + 8 + nh], gs[:, soff:soff + nh], gs[:, soff + 16:soff + 16 + nh],
                             64.0, EPS)
                P.V("tensor_tensor", out=n32[:, :, :], in0=pb_ap.rearrange("p (h e) -> p h e", h=nh),
                    in1=bc(gs[:, soff + 8:soff + 8 + nh], [128, nh, 64], 2), op=ALU.mult)
                P.G("tensor_tensor", out=n32[:, :, :], in0=n32[:, :, :], in1=bc(gbc[:, :], [128, nh, 64], 1),
                    op=ALU.mult)
                v5 = n32[:, :, :].rearrange("p h (a b e) -> p h a b e", a=2, b=2)
                r5 = rot[:, :, :].rearrange("p h (a b e) -> p h a b e", a=2, b=2)
                x1, x2 = v5[:, :, :, 0, :], v5[:, :, :, 1, :]
                cos2 = bc(d["tb2"][:, 0, :, :], [128, nh, 2, 16], 1)
                sin2 = bc(d["tb2"][:, 1, :, :], [128, nh, 2, 16], 1)
                ta4 = d["ta"][:, 0:nh, :, :]
                tb4 = d["tb"][:, 0:nh, :, :]
                P.V("tensor_tensor", out=ta4, in0=x1, in1=cos2, op=ALU.mult)
                P.V("tensor_tensor", out=tb4, in0=x2, in1=sin2, op=ALU.mult)
                P.V("tensor_tensor", out=r5[:, :, :, 0, :], in0=ta4, in1=tb4, op=ALU.subtract)
                P.V("tensor_tensor", out=ta4, in0=x1, in1=sin2, op=ALU.mult)
                P.V("tensor_tensor", out=tb4, in0=x2, in1=cos2, op=ALU.mult)
                P.V("tensor_tensor", out=r5[:, :, :, 1, :], in0=ta4, in1=tb4, op=ALU.add)

            pgq = proj(672, 512)
            yield
            qknorm_rope(pgq[:, 0:512], 8, gq_bc, d["qn"], d["gqr"], 0)
            pb = pool.get()
            pv = bfv(pb)
            gqf = d["gqr"][:, :, :].rearrange("p h e -> p (h e)")
            for c in range(4):
                P.M("transpose", out=pv[:, c * 128:(c + 1) * 128], in_=gqf[:, c * 128:(c + 1) * 128],
                    identity=ident[:, :], accum=(c > 0))
            P.A("activation", out=d["gqT"][:, :, :].rearrange("p c t -> p (c t)"), in_=pv[:, 0:512], func=AF.Copy)
            P.dma("pool", out=k.qT_gqa[:, t0:t0 + 128].rearrange("(c p) s -> p c s", p=128), in_=d["gqT"][:, :, :])
            yield
            pgk = proj(1184, 256)
            P.A("activation", out=d["gv"][:, :], in_=pgk[:, 128:256], func=AF.Copy)
            P.dma("pool", out=k.v_gqa[t0:t0 + 128, :], in_=d["gv"][:, :])
            qknorm_rope(pgk[:, 0:128], 2, gk_bc, d["kn"], d["gkr"], 2)
            pb = pool.get()
            pv = bfv(pb)
            P.M("transpose", out=pv[:, 0:128], in_=d["gkr"][:, :, :].rearrange("p h e -> p (h e)"),
                identity=ident[:, :])
            P.A("activation", out=d["gkT"][:, :], in_=pv[:, 0:128], func=AF.Copy)
            P.dma("pool", out=k.kT_gqa[:, t0:t0 + 128], in_=d["gkT"][:, :])

        run_skewed([tile_gen(i) for i in range(NT)], ratio=P1_RATIO)


def phase_attn(k, l):
    nc, P, S, NT = k.nc, k.P, k.S, k.NT
    QB = 512
    NQB = S // QB
    NG = NT // 2
    with contextlib.ExitStack() as st:
        def SB(name, shape, dt):
            return st.enter_context(nc.sbuf_tensor(un("p2_" + name), list(shape), dt))
        pps = [st.enter_context(nc.psum_tensor(un("pp%d" % i), [128, 1024], F32)) for i in range(2)]
        obs = [st.enter_context(nc.psum_tensor(un("ob%d" % i), [128, 512], F32)) for i in range(4)]
        kts = [SB("kt%d" % i, [128, S], BF16) for i in range(2)]
        qts = [SB("qt%d" % i, [128, S], BF16) for i in range(2)]
        vxs = [SB("vx%d" % i, [128, NT, 65], BF16) for i in range(2)]
        pts = [SB("pt%d" % i, [128, 2 * QB], BF16) for i in range(3)]
        o32 = [SB("o32_%d" % i, [65, QB], F32) for i in range(4)]
        osb = [SB("o%d" % i, [128, 4, 64], BF16) for i in range(4)]
        rsb = [SB("rs%d" % i, [128, 4], F32) for i in range(4)]
        for vx in vxs:
            P.G("memset", ap=vx[:, :, 64:65], constant=1.0, W=[vx[:, :, :]])
        mub = SB("mub", [128, 1920], F32)
        P.dma("sp", out=mub[:, :], in_=k.w["rwkv_mu"][l:l + 1, :].broadcast_to([128, 1920]))
        mixb = [[SB("mz%d_%d" % (a, b), [128, 1920], F32) for a in range(3)] for b in range(2)]

        def mix_tile(j):
            z, zp, zn = mixb[j % 2]
            t0 = j * 128
            P.dma("sp", out=z[:, :], in_=k.z_r[t0:t0 + 128, :])
            if j == 0:
                P.G("memset", ap=zp[0:1, :], constant=0.0)
                P.dma("sp", out=zp[1:128, :], in_=k.z_r[0:127, :])
            else:
                P.dma("sp", out=zp[:, :], in_=k.z_r[t0 - 1:t0 + 127, :])
            if j == NT - 1:
                P.G("memset", ap=zn[:, :], constant=0.0)
                P.dma("sp", out=zn[0:127, :], in_=k.z_r[t0 + 1:t0 + 128, :])
            else:
                P.dma("sp", out=zn[:, :], in_=k.z_r[t0 + 1:t0 + 129, :])
            P.G("tensor_tensor", out=zp[:, :], in0=zp[:, :], in1=zn[:, :], op=ALU.add)
            P.V("scalar_tensor_tensor", out=zp[:, :], in0=zp[:, :], scalar=0.5, in1=z[:, :], op0=ALU.mult,
                op1=ALU.subtract)
            P.G("tensor_tensor", out=zp[:, :], in0=zp[:, :], in1=mub[:, :], op=ALU.mult)
            P.V("tensor_tensor", out=zn[:, :], in0=z[:, :], in1=zp[:, :], op=ALU.add)
            P.dma("pool", out=k.zmix[t0:t0 + 128, :], in_=zn[:, :])
        hd = []
        kvi = -1
        for ui in range(12):
            qt = qts[ui % 2]
            loads = []
            if ui < 8:
                h = ui
                kvi += 1
                kt, vx = kts[kvi % 2], vxs[kvi % 2]
                loads.append((kt[0:96, :], k.kT_mla[h]))
                vsrc = k.v_mla[:, h * 64:(h + 1) * 64].rearrange("(c p) e -> p c e", p=128)
                for c0 in range(0, NT, 8):
                    c1 = min(NT, c0 + 8)
                    loads.append((vx[:, c0:c1, 0:64], vsrc[:, c0:c1, :]))
                loads.append((qt[0:96, :], k.qT_mla[h]))
                hd.append(dict(kind="mla", kt=kt, vx=vx, qt=qt, dq=96, scale=96.0 ** -0.5, heads=[h], oscr=k.o_mla,
                               loads=loads, ngroups=NG))
            else:
                h0 = (ui - 8) * 2
                kvh = h0 // 4
                if h0 % 4 == 0:
                    kvi += 1
                    kt, vx = kts[kvi % 2], vxs[kvi % 2]
                    loads.append((kt[0:64, :], k.kT_gqa[kvh * 64:(kvh + 1) * 64, :]))
                    loads.append((kt[64:128, :], k.kT_gqa[kvh * 64:(kvh + 1) * 64, :]))
                    vsrc = k.v_gqa[:, kvh * 64:(kvh + 1) * 64].rearrange("(c p) e -> p c e", p=128)
                    for c0 in range(0, NT, 8):
                        c1 = min(NT, c0 + 8)
                        loads.append((vx[:, c0:c1, 0:64], vsrc[:, c0:c1, :]))
                loads.append((qt[0:128, :], k.qT_gqa[h0 * 64:(h0 + 2) * 64, :]))
                hd.append(dict(kind="gqa", kt=kt, vx=vx, qt=qt, dq=64, scale=64.0 ** -0.5, heads=[h0, h0 + 1],
                               oscr=k.o_gqa, loads=loads, ngroups=NT))
        items = []
        for ui in range(len(hd)):
            for qb in range(NQB):
                for g in range(hd[ui]["ngroups"]):
                    items.append((ui, qb, g))
        state = {"npp": 0, "npt": 0, "nqb": 0}

        def do_loads(ui):
            for (o_, i_) in hd[ui]["loads"]:
                P.dma("sp", out=o_, in_=i_)

        def emit_qk(ui, qb, g):
            H_ = hd[ui]
            pp = pps[state["npp"] % 2]
            pt = pts[state["npt"] % 3]
            state["npp"] += 1
            state["npt"] += 1
            dq = H_["dq"]
            for u in range(2):
                if H_["kind"] == "mla":
                    kc = 2 * g + u
                    P.M("matmul", out=pp[:, u * QB:(u + 1) * QB], lhsT=H_["kt"][0:dq, kc * 128:(kc + 1) * 128],
                        rhs=H_["qt"][0:dq, qb * QB:(qb + 1) * QB], start=True, stop=True, accum=(u > 0))
                else:
                    rs = slice(u * 64, (u + 1) * 64)
                    P.M("matmul", out=pp[:, u * QB:(u + 1) * QB], lhsT=H_["kt"][rs, g * 128:(g + 1) * 128],
                        rhs=H_["qt"][rs, qb * QB:(qb + 1) * QB], start=True, stop=True, accum=(u > 0))
            P.A("activation", out=pt[:, :], in_=pp[:, :], func=AF.Exp, scale=H_["scale"])
            return pt

        def emit_pv(ui, qb, g, pt):
            H_ = hd[ui]
            base = (state["nqb"] % 2) * 2
            lastg = (g == H_["ngroups"] - 1)
            for u in range(2):
                if H_["kind"] == "mla":
                    kc = 2 * g + u
                    ob = obs[base]
                else:
                    kc = g
                    ob = obs[base + u]
                P.M("matmul", out=ob[0:65, 0:QB], lhsT=H_["vx"][:, kc, 0:65], rhs=pt[:, u * QB:(u + 1) * QB],
                    start=(kc == 0), stop=(kc == NT - 1), accum=(kc > 0))
            if not lastg:
                return None
            state["nqb"] += 1
            eps = []
            for u, h in enumerate(H_["heads"]):
                ob = obs[base + u]
                o3, o_s, r_s = o32[base + u], osb[base + u], rsb[base + u]
                P.A("activation", out=o3[:, :], in_=ob[0:65, 0:QB], func=AF.Copy)

                def epilogue(ob=ob, o3=o3, o_s=o_s, r_s=r_s, h=h):
                    for j in range(4):
                        P.M("transpose", out=ob[:, j * 128:j * 128 + 65], in_=o3[0:65, j * 128:(j + 1) * 128],
                            identity=k.identf[0:65, 0:65], accum=(j > 0))
                    tb3 = ob[:, 0:512].rearrange("p (j e) -> p j e", j=4)
                    P.V("reciprocal", out=r_s[:, :], in_=tb3[:, :, 64])
                    P.V("tensor_tensor", out=o_s[:, :, :], in0=tb3[:, :, 0:64], in1=bc(r_s[:, :], [128, 4, 64], 2),
                        op=ALU.mult)
                    P.dma("pool", out=H_["oscr"][qb * QB:(qb + 1) * QB, h * 64:(h + 1) * 64].rearrange("(j p) e -> p j e", p=128),
                          in_=o_s[:, :, :])
                eps.append(epilogue)

            def run_eps():
                for f in eps:
                    f()
            return run_eps

        n = len(items)
        starts = {}
        for idx, (ui, qb, g) in enumerate(items):
            starts.setdefault(ui, idx)
        do_loads(0)
        prev_pt = None
        cur_pt = None
        pending = None
        mix_every = max(1, n // NT)
        nmix = 0
        for idx in range(n + 1):
            if idx % mix_every == 1 and nmix < NT:
                mix_tile(nmix)
                nmix += 1
            if idx < n:
                ui, qb, g = items[idx]
                if idx == starts[ui] + min(2, NQB * hd[ui]["ngroups"] - 1) and ui + 1 < len(hd):
                    do_loads(ui + 1)
                cur_pt = emit_qk(ui, qb, g)
            if idx >= 1:
                ui0, qb0, g0 = items[idx - 1]
                ep = emit_pv(ui0, qb0, g0, prev_pt)
                if pending is not None:
                    pending()
                pending = ep
            prev_pt = cur_pt
        if pending is not None:
            pending()
        while nmix < NT:
            mix_tile(nmix)
            nmix += 1


def rwkv_dir(k, l, d, st, poolA, poolB, nsets, both=False):
    nc, P, S, NT = k.nc, k.P, k.S, k.NT
    w = k.w
    C0 = DECAY_C
    def SB(name, shape, dt):
        return st.enter_context(nc.sbuf_tensor(un("rw_" + name), list(shape), dt))
    bcn = {}

    def load_bc(name, src_row, n=512):
        t = SB(name, [128, n], F32)
        P.dma("sp", out=t[:, :], in_=src_row.broadcast_to([128, n]))
        bcn[name] = t
        return t
    w0b = load_bc("w0", w["rwkv_w0"][l, d:d + 1, :])
    a0b = load_bc("a0", w["rwkv_a0"][l, d:d + 1, :])
    kkb = load_bc("kk", w["rwkv_k_k"][l:l + 1, :])
    kab = load_bc("ka", w["rwkv_k_a"][l:l + 1, :])
    od = 1 - d
    osl = slice(od * 64, (od + 1) * 64)
    if d == 1 or both:
        a0o = load_bc("a0o", w["rwkv_a0"][l, od:od + 1, :])
        rkb = load_bc("rk", w["rwkv_r_k"][l:l + 1].rearrange("o h n -> o (h n)"))
        lnw = load_bc("lnw", w["rwkv_ln_w"][l:l + 1, :])
        lnb = load_bc("lnb", w["rwkv_ln_b"][l:l + 1, :])
        g2s = SB("g2", [128, 512], BF16)
        P.dma("pool", out=g2s[:, :], in_=w["rwkv_g2"][l])
    w2s = SB("w2", [128, 512], BF16)
    a2s = SB("a2", [128, 512], BF16)
    P.dma("pool", out=w2s[:, :], in_=w["rwkv_w2"][l].rearrange("d r c -> (d r) c"))
    P.dma("pool", out=a2s[:, :], in_=w["rwkv_a2"][l].rearrange("d r c -> (d r) c"))
    m2 = SB("m2", [128, 2, 128], F32)
    mT = SB("mT", [128, 128], F32)
    bdm = SB("bdm", [128, 128], F32)
    onec = SB("onec", [128, 1], F32)
    P.dma("sp", out=m2[:, 0, :], in_=k.msk_d[d, 0])
    P.dma("sp", out=m2[:, 1, :], in_=k.msk_d[d, 1])
    P.dma("sp", out=mT[:, :], in_=k.msk_d[1 - d, 0])
    P.dma("sp", out=bdm[:, :], in_=k.bdm_d)
    P.G("memset", ap=onec[:, :], constant=1.0)
    H32 = SB("H32", [128, 4, 128], F32)
    Hb = SB("Hb", [128, 4, 128], BF16)
    P.G("memset", ap=H32[:, :, :], constant=0.0)
    P.G("memset", ap=Hb[:, :, :], constant=0.0)
    sets = []
    for s_ in range(nsets):
        dd = {}
        lst = [("zm", [128, 1920], F32), ("lor", [128, 384], BF16), ("lorT", [128, 3, 128], BF16),
               ("tok4", [128, 4, 512], BF16), ("vb", [128, 512], BF16), ("TTs", [128, 4, 4, 128], BF16),
               ("gC", [128, 4], F32), ("sm", [128, 64], F32), ("Y32", [128, 512], F32), ("hT_", [128, 128], F32)]
        for t in range(8):
            lst.append(("T%d" % t, [128, 512], F32))
        for p in range(4):
            lst += [("XL%d" % p, [128, 2, 2, 128], BF16), ("LT%d" % p, [128, 2, 128], BF16),
                    ("ARB%d" % p, [128, 2, 128], BF16), ("AK%d" % p, [128, 2, 2, 128], BF16),
                    ("PT%d" % p, [128, 128], BF16), ("AKV%d" % p, [128, 128], BF16), ("Ub%d" % p, [128, 128], BF16)]
        if d == 1 or both:
            lst += [("yfw", [128, 512], F32), ("ob", [128, 512], BF16), ("oT", [128, 4, 128], BF16)]
        for name, shape, dt in lst:
            dd[name] = SB("%s_%d" % (name, s_), shape, dt)
        sets.append(dd)

    order = list(range(NT)) if d == 0 else list(range(NT - 1, -1, -1))

    def tile_gen(it, i):
        pool = poolA
        D_ = sets[it % nsets]
        final = (it >= NT // 2) if both else (d == 1)
        t0 = i * 128
        zm = D_["zm"]
        T = [D_["T%d" % t] for t in range(8)]
        sm = D_["sm"]
        P.dma("sp", out=zm[:, :], in_=k.zmix[t0:t0 + 128, :])
        r_ = zm[:, 0:512]
        kx = zm[:, 512:1024]
        v_ = zm[:, 1024:1536]
        for _ in range(RW_DELAY):
            yield
        lor, lorT = D_["lor"], D_["lorT"]
        P.A("activation", out=lor[:, 0:128], in_=zm[:, 1536:1664], func=AF.Tanh)
        P.V("tensor_copy", out=lor[:, 128:256], in_=zm[:, 1664:1792])
        nl = 2
        if final:
            P.A("activation", out=lor[:, 256:384], in_=zm[:, 1792:1920], func=AF.Sigmoid)
            nl = 3
        pb = pool.get()
        pv = bfv(pb)
        for c in range(nl):
            P.M("transpose", out=pv[:, c * 128:(c + 1) * 128], in_=lor[:, c * 128:(c + 1) * 128],
                identity=k.ident[:, :], accum=(c > 0))
        P.A("activation", out=lorT[:, 0:nl, :].rearrange("p c t -> p (c t)"), in_=pv[:, 0:nl * 128], func=AF.Copy)
        ds = slice(d * 64, (d + 1) * 64)
        sg, asig = T[0], T[1]
        pb = pool.get()
        P.M("matmul", out=pb[:, 0:512], lhsT=lorT[ds, 0, :], rhs=w2s[ds, :], start=True, stop=True)
        P.V("tensor_tensor", out=sg[:, :], in0=pb[:, 0:512], in1=w0b[:, :], op=ALU.add)
        P.A("activation", out=sg[:, :], in_=sg[:, :], func=AF.Sigmoid)
        yield
        pb = pool.get()
        P.M("matmul", out=pb[:, 0:512], lhsT=lorT[ds, 1, :], rhs=a2s[ds, :], start=True, stop=True)
        P.V("tensor_tensor", out=asig[:, :], in0=pb[:, 0:512], in1=a0b[:, :], op=ALU.add)
        P.A("activation", out=asig[:, :], in_=asig[:, :], func=AF.Sigmoid)
        yield
        eP, eN, eX = T[2], T[3], T[4]
        pc = pool.get()
        P.M("matmul", out=pc[:, 0:512], lhsT=m2[:, 1, :], rhs=sg[:, :], start=True, stop=True)
        pcx = pool.get()
        P.M("matmul", out=pcx[:, 0:512], lhsT=m2[:, 0, :], rhs=sg[:, :], start=True, stop=True)
        P.A("activation", out=eP[:, :], in_=pc[:, 0:512], func=AF.Exp, scale=-C0)
        P.A("activation", out=eN[:, :], in_=pc[:, 0:512], func=AF.Exp, scale=C0)
        yield
        P.A("activation", out=eX[:, :], in_=pcx[:, 0:512], func=AF.Exp, scale=-C0)
        pg = pool.get()
        for p in range(4):
            P.M("matmul", out=pg[:, p:p + 1], lhsT=sg[:, p * 128:(p + 1) * 128], rhs=onec[:, 0:1], start=True,
                stop=True, accum=(p > 0))
        P.A("activation", out=D_["gC"][:, :], in_=pg[:, 0:4], func=AF.Exp, scale=-C0)
        yield
        kk, kt, bb = T[5], T[6], T[7]
        tok4, vb = D_["tok4"], D_["vb"]
        P.V("tensor_tensor", out=kk[:, :], in0=kx, in1=kkb[:, :], op=ALU.mult)
        P.G("tensor_tensor", out=bb[:, :], in0=asig[:, :], in1=eN[:, :], op=ALU.mult)
        P.G("tensor_tensor", out=tok4[:, 1, :], in0=r_, in1=eP[:, :], op=ALU.mult)
        P.A("activation", out=kt[:, :], in_=kk[:, :], func=AF.Square)
        P.V("tensor_reduce", out=sm[:, 0:8], in_=kt[:, :].rearrange("p (h e) -> p h e", h=8), axis=AX.X, op=ALU.add)
        P.V("tensor_scalar", out=sm[:, 0:8], in0=sm[:, 0:8], scalar1=1e-24, scalar2=None, op0=ALU.max)
        yield
        P.A("activation", out=sm[:, 8:16], in_=sm[:, 0:8], func=AF.Ln)
        P.A("activation", out=sm[:, 16:24], in_=sm[:, 8:16], func=AF.Exp, scale=-0.5)
        kk3 = kk[:, :].rearrange("p (h e) -> p h e", h=8)
        P.V("tensor_tensor", out=kk3, in0=kk3, in1=bc(sm[:, 16:24], [128, 8, 64], 2), op=ALU.mult)
        yield
        P.V("scalar_tensor_tensor", out=kt[:, :], in0=asig[:, :], scalar=-1.0, in1=kab[:, :], op0=ALU.add, op1=ALU.mult)
        P.V("scalar_tensor_tensor", out=kt[:, :], in0=kt[:, :], scalar=1.0, in1=kx, op0=ALU.add, op1=ALU.mult)
        yield
        P.V("scalar_tensor_tensor", out=tok4[:, 0, :], in0=kk[:, :], scalar=-1.0, in1=eX[:, :], op0=ALU.mult, op1=ALU.mult)
        yield
        P.V("tensor_tensor", out=tok4[:, 2, :], in0=kk[:, :], in1=bb[:, :], op=ALU.mult)
        P.V("tensor_tensor", out=tok4[:, 3, :], in0=kt[:, :], in1=eN[:, :], op=ALU.mult)
        P.A("activation", out=vb[:, :], in_=v_, func=AF.Copy)
        pool = poolB
        yield "mid"
        TTs = D_["TTs"]
        for p0 in (0, 2):
            pb = pool.get()
            pv = bfv(pb)
            for p in (p0, p0 + 1):
                for q in range(4):
                    o0 = (p - p0) * 512 + q * 128
                    P.M("transpose", out=pv[:, o0:o0 + 128], in_=tok4[:, q, p * 128:(p + 1) * 128],
                        identity=k.ident[:, :], accum=not (p == p0 and q == 0))
            P.A("activation", out=TTs[:, p0:p0 + 2, :, :].rearrange("p a q t -> p (a q t)"), in_=pv[:, 0:1024],
                func=AF.Copy)
            yield
        yield
        XL = [D_["XL%d" % p] for p in range(4)]
        LT = [D_["LT%d" % p] for p in range(4)]
        ARB = [D_["ARB%d" % p] for p in range(4)]
        AK = [D_["AK%d" % p] for p in range(4)]
        PT = [D_["PT%d" % p] for p in range(4)]
        AKV = [D_["AKV%d" % p] for p in range(4)]
        Ub = [D_["Ub%d" % p] for p in range(4)]
        m2b = bc(m2[:, :, :].rearrange("p a t -> p (a t)"), [128, 2, 256], 1)
        for p in range(4):
            bB, bK, bL = pool.get(), pool.get(), pool.get()
            for e in range(2):
                bs = slice(e * 64, (e + 1) * 64)
                ar = TTs[bs, p, 0:2, :].rearrange("p q t -> p (q t)")
                P.M("matmul", out=bB[:, e * 256:(e + 1) * 256], lhsT=TTs[bs, p, 2, :], rhs=ar, start=True, stop=True,
                    accum=(e > 0), drain=True)
                P.M("matmul", out=bK[:, e * 256:(e + 1) * 256], lhsT=TTs[bs, p, 3, :], rhs=ar, start=True, stop=True,
                    accum=(e > 0))
                P.M("matmul", out=bL[:, e * 128:(e + 1) * 128], lhsT=TTs[bs, p, 0, :], rhs=TTs[bs, p, 2, :], start=True,
                    stop=True, accum=(e > 0))
            bB4 = bB[:, 0:512].rearrange("p (e a t) -> p e a t", e=2, a=2)
            P.V("tensor_tensor", out=XL[p][:, :, 1, :], in0=bB4[:, :, 0, :], in1=bc(m2[:, 0, :], [128, 2, 128], 1),
                op=ALU.mult)
            P.V("tensor_tensor", out=ARB[p][:, :, :], in0=bB4[:, :, 1, :], in1=bc(m2[:, 1, :], [128, 2, 128], 1),
                op=ALU.mult)
            P.V("tensor_tensor", out=AK[p][:, :, :, :].rearrange("p e a t -> p e (a t)"),
                in0=bK[:, 0:512].rearrange("p (e n) -> p e n", e=2), in1=m2b, op=ALU.mult)
            P.V("tensor_tensor", out=LT[p][:, :, :], in0=bL[:, 0:256].rearrange("p (e t) -> p e t", e=2),
                in1=bc(mT[:, :], [128, 2, 128], 1), op=ALU.mult)
            P.G("tensor_copy", out=XL[p][:, :, 0, :], in_=bc(k.ident[:, :], [128, 2, 128], 1))
            yield
        yield
        for lev in range(7):
            last = (lev == 6)
            for p in range(4):
                bb_ = pool.get()
                for e in range(2):
                    if not last:
                        P.M("matmul", out=bb_[:, e * 256:(e + 1) * 256], lhsT=LT[p][:, e, :],
                            rhs=XL[p][:, e, :, :].rearrange("p a t -> p (a t)"), start=True, stop=True, accum=(e > 0))
                    else:
                        P.M("matmul", out=bb_[:, e * 256:e * 256 + 128], lhsT=LT[p][:, e, :],
                            rhs=XL[p][:, e, 0, :], start=True, stop=True, accum=(e > 0))
                if not last:
                    ba_ = pool.get()
                    for e in range(2):
                        P.M("matmul", out=ba_[:, e * 128:(e + 1) * 128], lhsT=XL[p][:, e, 1, :], rhs=LT[p][:, e, :],
                            start=True, stop=True, accum=(e > 0))
                b4 = bb_[:, 0:512].rearrange("p (e a t) -> p e a t", e=2, a=2)
                P.V("tensor_tensor", out=XL[p][:, :, 0, :], in0=XL[p][:, :, 0, :], in1=b4[:, :, 0, :], op=ALU.add)
                if not last:
                    P.A("activation", out=XL[p][:, :, 1, :], in_=b4[:, :, 1, :], func=AF.Copy)
                    P.A("activation", out=LT[p][:, :, :], in_=ba_[:, 0:256].rearrange("p (e t) -> p e t", e=2),
                        func=AF.Copy)
                yield
        for p in range(4):
            pb = pool.get()
            P.M("matmul", out=pb[:, 0:256].rearrange("p (e t) -> p e t", e=2), lhsT=tok4[:, 0, p * 128:(p + 1) * 128],
                rhs=XL[p][:, :, 0, :], start=True, stop=True)
            P.A("activation", out=PT[p][0:64, :], in_=pb[0:64, 0:128], func=AF.Copy)
            P.V("tensor_copy", out=PT[p][64:128, :], in_=pb[64:128, 128:256])
            pb2 = pool.get()
            for e in range(2):
                h = 2 * p + e
                P.M("matmul", out=pb2[:, e * 64:(e + 1) * 64], lhsT=AK[p][:, e, 0, :], rhs=vb[:, h * 64:(h + 1) * 64],
                    start=True, stop=True, accum=(e > 0))
            P.A("activation", out=AKV[p][:, :], in_=pb2[:, 0:128], func=AF.Copy)
            yield
        Y32 = D_["Y32"]
        for p in range(4):
            ps = slice(p * 128, (p + 1) * 128)
            bU = pool.get()
            for e in range(2):
                P.M("matmul", out=bU[:, e * 64:(e + 1) * 64], lhsT=XL[p][:, e, 0, :], rhs=AKV[p][:, e * 64:(e + 1) * 64],
                    start=(e == 0), stop=False, accum=(e > 0))
            P.M("matmul", out=bU[:, 0:128], lhsT=PT[p][:, :], rhs=Hb[:, p, :], start=False, stop=True, accum=True)
            P.A("activation", out=Ub[p][:, :], in_=bU[:, 0:128], func=AF.Copy)
            bY = pool.get()
            for e in range(2):
                h = 2 * p + e
                P.M("matmul", out=bY[:, e * 64:(e + 1) * 64], lhsT=AK[p][:, e, 1, :], rhs=vb[:, h * 64:(h + 1) * 64],
                    start=(e == 0), stop=False, accum=(e > 0))
            P.M("matmul", out=bY[:, 0:128], lhsT=TTs[:, p, 1, :], rhs=Hb[:, p, :], start=False, stop=False, accum=True)
            for e in range(2):
                P.M("matmul", out=bY[:, e * 64:(e + 1) * 64], lhsT=ARB[p][:, e, :], rhs=Ub[p][:, e * 64:(e + 1) * 64],
                    start=False, stop=(e == 1), accum=True)
            P.V("tensor_copy", out=Y32[:, ps], in_=bY[:, 0:128])
            bH = pool.get()
            P.M("matmul", out=bH[:, 0:128], lhsT=tok4[:, 2, ps], rhs=Ub[p][:, :], start=True, stop=False)
            P.M("matmul", out=bH[:, 0:128], lhsT=tok4[:, 3, ps], rhs=vb[:, ps], start=False, stop=True, accum=True)
            hT_ = D_["hT_"]
            P.V("tensor_tensor", out=hT_[:, :], in0=bH[:, 0:128], in1=H32[:, p, :], op=ALU.add)
            P.V("scalar_tensor_tensor", out=H32[:, p, :], in0=hT_[:, :], scalar=D_["gC"][:, p:p + 1], in1=bdm[:, :],
                op0=ALU.mult, op1=ALU.mult)
            P.A("activation", out=Hb[:, p, :], in_=H32[:, p, :], func=AF.Copy)
            yield
        if not final:
            P.dma("pool", out=k.y_fw[t0:t0 + 128, :], in_=Y32[:, :])
            k.y_done.add(i)
        else:
            while both and i not in k.y_done:
                yield "wait"
            P.dma("sp", out=D_["yfw"][:, :], in_=k.y_fw[t0:t0 + 128, :])
            wkv, sq, bon = T[3], T[4], T[2]
            pb = pool.get()
            P.M("matmul", out=pb[:, 0:512], lhsT=lorT[osl, 1, :], rhs=a2s[osl, :], start=True, stop=True)
            P.V("tensor_tensor", out=T[0][:, :], in0=pb[:, 0:512], in1=a0o[:, :], op=ALU.add)
            P.A("activation", out=T[0][:, :], in_=T[0][:, :], func=AF.Sigmoid)
            P.V("scalar_tensor_tensor", out=T[0][:, :], in0=T[0][:, :], scalar=-1.0, in1=kab[:, :], op0=ALU.add,
                op1=ALU.mult)
            P.V("scalar_tensor_tensor", out=T[0][:, :], in0=T[0][:, :], scalar=1.0, in1=kx, op0=ALU.add, op1=ALU.mult)
            P.V("tensor_tensor", out=T[0][:, :], in0=T[0][:, :], in1=kt[:, :], op=ALU.add)
            P.V("tensor_tensor", out=T[0][:, :], in0=T[0][:, :], in1=r_, op=ALU.mult)
            P.V("tensor_tensor", out=T[0][:, :], in0=T[0][:, :], in1=rkb[:, :], op=ALU.mult)
            P.V("tensor_reduce", out=sm[:, 24:32], in_=T[0][:, :].rearrange("p (h e) -> p h e", h=8), axis=AX.X,
                op=ALU.add)
            P.V("tensor_tensor", out=bon[:, :].rearrange("p (h e) -> p h e", h=8),
                in0=v_.rearrange("p (h e) -> p h e", h=8), in1=bc(sm[:, 24:32], [128, 8, 64], 2), op=ALU.mult)
            P.V("tensor_tensor", out=wkv[:, :], in0=Y32[:, :], in1=D_["yfw"][:, :], op=ALU.add)
            w3 = wkv[:, :].rearrange("p (h e) -> p h e", h=8)
            P.V("tensor_reduce", out=sm[:, 32:40], in_=w3, axis=AX.X, op=ALU.add)
            P.V("tensor_scalar", out=sm[:, 32:40], in0=sm[:, 32:40], scalar1=-1.0 / 64, scalar2=None, op0=ALU.mult)
            P.V("tensor_tensor", out=w3, in0=w3, in1=bc(sm[:, 32:40], [128, 8, 64], 2), op=ALU.add)
            P.A("activation", out=sq[:, :], in_=wkv[:, :], func=AF.Square)
            P.V("tensor_reduce", out=sm[:, 40:48], in_=sq[:, :].rearrange("p (h e) -> p h e", h=8), axis=AX.X, op=ALU.add)
            rstd_from_ss(k, sm[:, 48:56], sm[:, 40:48], sm[:, 56:64], 64.0, GN_EPS)
            P.V("tensor_tensor", out=w3, in0=w3, in1=bc(sm[:, 48:56], [128, 8, 64], 2), op=ALU.mult)
            P.V("tensor_tensor", out=wkv[:, :], in0=wkv[:, :], in1=lnw[:, :], op=ALU.mult)
            P.V("tensor_tensor", out=wkv[:, :], in0=wkv[:, :], in1=lnb[:, :], op=ALU.add)
            P.V("tensor_tensor", out=wkv[:, :], in0=wkv[:, :], in1=bon[:, :], op=ALU.add)
            pb = pool.get()
            P.M("matmul", out=pb[:, 0:512], lhsT=lorT[:, 2, :], rhs=g2s[:, :], start=True, stop=True)
            P.V("tensor_tensor", out=D_["ob"][:, :], in0=wkv[:, :], in1=pb[:, 0:512], op=ALU.mult)
            pb = pool.get()
            pv = bfv(pb)
            for c in range(4):
                P.M("transpose", out=pv[:, c * 128:(c + 1) * 128], in_=D_["ob"][:, c * 128:(c + 1) * 128],
                    identity=k.ident[:, :], accum=(c > 0))
            P.A("activation", out=D_["oT"][:, :, :].rearrange("p c t -> p (c t)"), in_=pv[:, 0:512], func=AF.Copy)
            P.dma("pool", out=k.oT_rw[i], in_=D_["oT"][:, :, :])


    return [tile_gen(it, i) for it, i in enumerate(order)]


def phase_rwkv(k, l, d):
    with contextlib.ExitStack() as st:
        alloc_banks(k, st)
        gens = rwkv_dir(k, l, d, st, BankPool(k.banks[0:3]), BankPool(k.banks[3:8]), 2)
        run_skewed(gens, ratio=RW_RATIO)


def phase_rwkv2(k, l):
    with contextlib.ExitStack() as st:
        alloc_banks(k, st)
        pf = BankPool(k.banks[0:4])
        pb = BankPool(k.banks[4:8])
        k.y_done = set()
        gf = rwkv_dir(k, l, 0, st, pf, pf, 1, both=True)
        gb = rwkv_dir(k, l, 1, st, pb, pb, 1, both=True)
        streams = [iter(gf), iter(gb)]
        cur = [next(streams[0], None), None]
        started_b = False
        while cur[0] is not None or cur[1] is not None or not started_b:
            for j in range(2):
                if j == 1 and not started_b:
                    continue
                if cur[j] is None:
                    continue
                try:
                    r = next(cur[j])
                    if j == 0 and r == "mid" and not started_b:
                        started_b = True
                        cur[1] = next(streams[1], None)
                except StopIteration:
                    cur[j] = next(streams[j], None)
            if cur[0] is None and not started_b:
                started_b = True
                cur[1] = next(streams[1], None)


def phase_p3a(k, l, xin):
    nc, P, S, NT = k.nc, k.P, k.S, k.NT
    w = k.w
    with contextlib.ExitStack() as st:
        def SB(name, shape, dt):
            return st.enter_context(nc.sbuf_tensor(un("p3_" + name), list(shape), dt))
        alloc_banks(k, st, 6)
        pp2 = st.enter_context(nc.psum_tensor(un("p3_pp"), [128, 1024], F32))
        wg = SB("wg", [128, 8, 3072], BF16)
        wbr = SB("wbr", [128, 3, 4, 1024], BF16)
        wo = SB("wo", [128, 8, 1024], BF16)
        wq = SB("wq", [128, 8, 512], BF16)
        cwo = SB("cwo", [128, 4, 1024], BF16)
        bg = SB("bg", [1, 3072], BF16)
        ones = SB("ones", [1, 128], BF16)
        gcol = SB("gcol", [128, 16], F32)
        kmT = SB("kmT", [128, 4, 256], BF16)
        vm = SB("vm", [128, 2, 512], BF16)
        win = w["w_in"][l].rearrange("(c p) n -> p c n", p=128)
        for c in range(8):
            P.dma("pool", out=wg[:, c, :], in_=win[:, c, 3360:6432])
        for g in range(3):
            P.dma("pool", out=wbr[:, g, :, :], in_=w["w_branch"][l, g].rearrange("(c p) n -> p c n", p=128))
        P.dma("pool", out=wo[:, :, :], in_=w["w_out"][l].rearrange("(c p) n -> p c n", p=128))
        P.dma("pool", out=cwo[:, :, :], in_=w["cross_wo"][l].rearrange("(c p) n -> p c n", p=128))
        P.dma("pool", out=bg[0:1, :], in_=w["b_gate"][l:l + 1].rearrange("o g n -> o (g n)"))
        P.G("memset", ap=ones[0:1, :], constant=1.0)
        for c in range(8):
            load_col(k, "sp", gcol[:, c:c + 1], w["norm_cross"][l, c * 128:(c + 1) * 128], 128)
            load_col(k, "sp", gcol[:, 8 + c:9 + c], w["norm_mem"][l, c * 128:(c + 1) * 128], 128)
        with contextlib.ExitStack() as st2:
            stg = st2.enter_context(nc.sbuf_tensor(un("p3_stg"), [128, 8, 1024], F32))
            wkv = st2.enter_context(nc.sbuf_tensor(un("p3_wkv"), [128, 8, 1024], BF16))
            mx = st2.enter_context(nc.sbuf_tensor(un("p3_mx"), [128, 2, 1024], F32))
            mhb = st2.enter_context(nc.sbuf_tensor(un("p3_mhb"), [128, 2, 1024], BF16))
            mhT = st2.enter_context(nc.sbuf_tensor(un("p3_mhT"), [128, 8, 256], BF16))
            mjunk = st2.enter_context(nc.sbuf_tensor(un("p3_mjunk"), [128, 1024], BF16))
            mst = st2.enter_context(nc.sbuf_tensor(un("p3_mst"), [128, 8], F32))
            P.dma("sp", out=stg[:, :, 0:512], in_=w["cross_wq"][l].rearrange("(c p) n -> p c n", p=128))
            for c in range(8):
                P.V("tensor_scalar", out=wq[:, c, :], in0=stg[:, c, 0:512], scalar1=gcol[:, c:c + 1], scalar2=None,
                    op0=ALU.mult)
            P.dma("sp", out=stg[:, :, :], in_=w["cross_wkv"][l].rearrange("(c p) n -> p c n", p=128))
            for c in range(8):
                P.V("tensor_scalar", out=wkv[:, c, :], in0=stg[:, c, :], scalar1=gcol[:, 8 + c:9 + c], scalar2=None,
                    op0=ALU.mult)
            P.dma("sp", out=mx[:, :, :], in_=k.mem.rearrange("(j p) n -> p j n", p=128))
            for j in range(2):
                P.A("activation", out=mjunk[:, :], in_=mx[:, j, :], func=AF.Square, accum_out=mst[:, j:j + 1])
            rstd_from_ss(k, mst[:, 2:4], mst[:, 0:2], mst[:, 4:6], 1024.0, EPS)
            for j in range(2):
                P.V("tensor_scalar", out=mhb[:, j, :], in0=mx[:, j, :], scalar1=mst[:, 2 + j:3 + j], scalar2=None,
                    op0=ALU.mult)
                pb = bank(k)
                pv = bfv(pb)
                for c in range(8):
                    P.M("transpose", out=pv[:, c * 128:(c + 1) * 128], in_=mhb[:, j, c * 128:(c + 1) * 128],
                        identity=k.ident[:, :], accum=(c > 0))
                P.A("activation", out=mhT[:, :, j * 128:(j + 1) * 128],
                    in_=pv[:, 0:1024].rearrange("p (c t) -> p c t", c=8), func=AF.Copy)
            for h in range(4):
                pb = bank(k)
                for c in range(8):
                    P.M("matmul", out=pb[:, 0:256], lhsT=wkv[:, c, h * 128:(h + 1) * 128], rhs=mhT[:, c, :],
                        start=(c == 0), stop=(c == 7), accum=(c > 0))
                P.A("activation", out=kmT[:, h, :], in_=pb[:, 0:256], func=AF.Copy)
            for j in range(2):
                pb = bank(k)
                for c in range(8):
                    P.M("matmul", out=pb[:, 0:512], lhsT=mhT[:, c, j * 128:(j + 1) * 128], rhs=wkv[:, c, 512:1024],
                        start=(c == 0), stop=(c == 7), accum=(c > 0))
                P.A("activation", out=vm[:, j, :], in_=pb[:, 0:512], func=AF.Copy)
        P.barrier()

        sets = []
        for s_ in range(2):
            d = {}
            for name, shape, dt in [
                ("x", [128, 1024], F32), ("hT", [128, 8, 128], BF16), ("ob", [128, 2, 512], BF16),
                ("oT", [128, 3, 4, 128], BF16), ("gt", [128, 512], F32), ("tmp", [128, 512], F32),
                ("m", [128, 1024], F32), ("mb", [128, 1024], BF16), ("mT", [128, 8, 128], BF16),
                ("x1", [128, 1024], F32), ("junk", [128, 1024], BF16), ("st", [128, 16], F32),
                ("h2", [128, 1024], BF16), ("h2T", [128, 8, 128], BF16), ("qT", [128, 4, 128], BF16),
                ("p", [128, 4, 256], BF16), ("pn", [128, 4, 256], BF16), ("pT", [128, 8, 128], BF16),
                ("ocT", [128, 4, 128], BF16), ("x2", [128, 1024], F32),
            ]:
                d[name] = SB("%s%d" % (name, s_), shape, dt)
            sets.append(d)
        have_rw = ("rb" in k.phases) or ("rw2" in k.phases)
        poolA = BankPool(k.banks[0:3])
        poolB = BankPool(k.banks[3:6])

        def tile_gen(i):
            pool = poolA
            d = sets[i % 2]
            t0 = i * 128
            x, hT, oT, stt = d["x"], d["hT"], d["oT"], d["st"]
            P.dma("sp", out=x[:, :], in_=xin[t0:t0 + 128, :])
            P.dma("sp", out=hT[:, :, :], in_=k.hT_t[i])
            P.dma("sp", out=d["ob"][:, 0, :], in_=k.o_mla[t0:t0 + 128, :])
            P.dma("sp", out=d["ob"][:, 1, :], in_=k.o_gqa[t0:t0 + 128, :])
            if have_rw:
                P.dma("sp", out=oT[:, 2, :, :], in_=k.oT_rw[i])
            else:
                P.G("memset", ap=oT[:, 2, :, :], constant=0.0)
            pb = pool.get()
            pv = bfv(pb)
            obf = d["ob"][:, :, :].rearrange("p g n -> p (g n)")
            for c in range(8):
                P.M("transpose", out=pv[:, c * 128:(c + 1) * 128], in_=obf[:, c * 128:(c + 1) * 128],
                    identity=k.ident[:, :], accum=(c > 0))
            P.A("activation", out=oT[:, 0:2, :, :].rearrange("p g c t -> p (g c t)"), in_=pv[:, 0:1024], func=AF.Copy)
            yield
            for g in range(3):
                for n in range(2):
                    cs = slice(n * 512, (n + 1) * 512)
                    pz = pool.get()
                    for c in range(8):
                        P.M("matmul", out=pz[:, 0:512], lhsT=hT[:, c, :], rhs=wg[:, c, g * 1024 + n * 512:g * 1024 + (n + 1) * 512],
                            start=(c == 0), stop=False, accum=(c > 0))
                    P.M("matmul", out=pz[:, 0:512], lhsT=ones[0:1, :], rhs=bg[0:1, g * 1024 + n * 512:g * 1024 + (n + 1) * 512],
                        start=False, stop=True, accum=True)
                    P.A("activation", out=d["gt"][:, :], in_=pz[:, 0:512], func=AF.Sigmoid)
                    pbr = pool.get()
                    for c in range(4):
                        P.M("matmul", out=pbr[:, 0:512], lhsT=oT[:, g, c, :], rhs=wbr[:, g, c, cs],
                            start=(c == 0), stop=(c == 3), accum=(c > 0))
                    if g == 0:
                        P.V("tensor_tensor", out=d["m"][:, cs], in0=d["gt"][:, :], in1=pbr[:, 0:512], op=ALU.mult)
                    else:
                        P.V("tensor_tensor", out=d["tmp"][:, :], in0=d["gt"][:, :], in1=pbr[:, 0:512], op=ALU.mult)
                        if g == 1:
                            P.G("tensor_tensor", out=d["m"][:, cs], in0=d["m"][:, cs], in1=d["tmp"][:, :], op=ALU.add)
                        else:
                            P.G("tensor_tensor", out=d["mb"][:, cs], in0=d["m"][:, cs], in1=d["tmp"][:, :], op=ALU.add)
                    yield
            pool = poolB
            yield "mid"
            pb = pool.get()
            pv = bfv(pb)
            for c in range(8):
                P.M("transpose", out=pv[:, c * 128:(c + 1) * 128], in_=d["mb"][:, c * 128:(c + 1) * 128],
                    identity=k.ident[:, :], accum=(c > 0))
            P.A("activation", out=d["mT"][:, :, :].rearrange("p c t -> p (c t)"), in_=pv[:, 0:1024], func=AF.Copy)
            yield
            for n in range(2):
                cs = slice(n * 512, (n + 1) * 512)
                pb = pool.get()
                for c in range(8):
                    P.M("matmul", out=pb[:, 0:512], lhsT=d["mT"][:, c, :], rhs=wo[:, c, cs],
                        start=(c == 0), stop=(c == 7), accum=(c > 0))
                P.V("tensor_tensor", out=d["x1"][:, cs], in0=x[:, cs], in1=pb[:, 0:512], op=ALU.add)
            yield
            x1 = d["x1"]
            P.A("activation", out=d["junk"][:, :], in_=x1[:, :], func=AF.Square, accum_out=stt[:, 0:1])
            rstd_from_ss(k, stt[:, 1:2], stt[:, 0:1], stt[:, 2:3], 1024.0, EPS)
            P.V("tensor_scalar", out=d["h2"][:, :], in0=x1[:, :], scalar1=stt[:, 1:2], scalar2=None, op0=ALU.mult)
            pb = pool.get()
            pv = bfv(pb)
            for c in range(8):
                P.M("transpose", out=pv[:, c * 128:(c + 1) * 128], in_=d["h2"][:, c * 128:(c + 1) * 128],
                    identity=k.ident[:, :], accum=(c > 0))
            P.A("activation", out=d["h2T"][:, :, :].rearrange("p c t -> p (c t)"), in_=pv[:, 0:1024], func=AF.Copy)
            yield
            pb = pool.get()
            for h in range(4):
                for c in range(8):
                    P.M("matmul", out=pb[:, h * 128:(h + 1) * 128], lhsT=wq[:, c, h * 128:(h + 1) * 128], rhs=d["h2T"][:, c, :],
                        start=(c == 0), stop=(c == 7), accum=(c > 0 or h > 0))
            P.A("activation", out=d["qT"][:, :, :].rearrange("p h t -> p (h t)"), in_=pb[:, 0:512], func=AF.Copy)
            yield
            for h in range(4):
                P.M("matmul", out=pp2[:, h * 256:(h + 1) * 256], lhsT=d["qT"][:, h, :], rhs=kmT[:, h, :],
                    start=True, stop=True, accum=(h > 0))
            for h in range(4):
                P.A("activation", out=d["p"][:, h, :], in_=pp2[:, h * 256:(h + 1) * 256], func=AF.Exp,
                    scale=128.0 ** -0.5, accum_out=stt[:, 4 + h:5 + h])
            P.V("reciprocal", out=stt[:, 8:12], in_=stt[:, 4:8])
            yield
            P.V("tensor_tensor", out=d["pn"][:, :, :], in0=d["p"][:, :, :], in1=bc(stt[:, 8:12], [128, 4, 256], 2),
                op=ALU.mult)
            pb = pool.get()
            pv = bfv(pb)
            pnf = d["pn"][:, :, :].rearrange("p h m -> p (h m)")
            for c in range(8):
                P.M("transpose", out=pv[:, c * 128:(c + 1) * 128], in_=pnf[:, c * 128:(c + 1) * 128],
                    identity=k.ident[:, :], accum=(c > 0))
            P.A("activation", out=d["pT"][:, :, :].rearrange("p c t -> p (c t)"), in_=pv[:, 0:1024], func=AF.Copy)
            yield
            pb = pool.get()
            for h in range(4):
                for mc in range(2):
                    P.M("matmul", out=pb[:, h * 128:(h + 1) * 128], lhsT=vm[:, mc, h * 128:(h + 1) * 128],
                        rhs=d["pT"][:, h * 2 + mc, :], start=(mc == 0), stop=(mc == 1), accum=(mc > 0 or h > 0))
            P.A("activation", out=d["ocT"][:, :, :].rearrange("p h t -> p (h t)"), in_=pb[:, 0:512], func=AF.Copy)
            yield
            for n in range(2):
                cs = slice(n * 512, (n + 1) * 512)
                pb = pool.get()
                for h in range(4):
                    P.M("matmul", out=pb[:, 0:512], lhsT=d["ocT"][:, h, :], rhs=cwo[:, h, cs],
                        start=(h == 0), stop=(h == 3), accum=(h > 0))
                P.V("tensor_tensor", out=d["x2"][:, cs], in0=x1[:, cs], in1=pb[:, 0:512], op=ALU.add)
            P.dma("pool", out=k.x2[t0:t0 + 128, :], in_=d["x2"][:, :])

        run_skewed([tile_gen(i) for i in range(NT)], ratio=P3_RATIO)


def phase_p3b(k, l, xout, last):
    nc, P, S = k.nc, k.P, k.S
    w = k.w
    TT = 256
    NTT = S // TT
    with contextlib.ExitStack() as st:
        def SB(name, shape, dt):
            return st.enter_context(nc.sbuf_tensor(un("p4_" + name), list(shape), dt))
        alloc_banks(k, st)
        w1 = SB("w1", [128, 8, 4096], BF16)
        w2 = SB("w2", [128, 32, 1024], BF16)
        gcol = SB("gcol", [128, 8], F32)
        for c in range(8):
            load_col(k, "sp", gcol[:, c:c + 1], w["norm_mlp"][l, c * 128:(c + 1) * 128], 128)
        with contextlib.ExitStack() as st2:
            stgs = [st2.enter_context(nc.sbuf_tensor(un("p4_stg%d" % i), [128, 4096], F32)) for i in range(2)]
            w1v = w["mlp_w1"][l].rearrange("(c p) n -> p c n", p=128)
            for c in range(8):
                sg = stgs[c % 2]
                P.dma("sp", out=sg[:, :], in_=w1v[:, c, :])
                for hf in range(2):
                    P.V("tensor_scalar", out=w1[:, c, hf * 2048:(hf + 1) * 2048], in0=sg[:, hf * 2048:(hf + 1) * 2048],
                        scalar1=gcol[:, c:c + 1], scalar2=None, op0=ALU.mult)
        P.barrier()
        w2v = w["mlp_w2"][l].rearrange("(f p) n -> p f n", p=128)
        for f0 in range(0, 32, 4):
            P.dma("pool", out=w2[:, f0:f0 + 4, :], in_=w2v[:, f0:f0 + 4, :])
        if last:
            gf = SB("gf", [128, 1024], F32)
            P.dma("sp", out=gf[:, :], in_=w["norm_final"][0:1, :].broadcast_to([128, 1024]))
        uT = SB("uT", [128, 32, TT], BF16)
        rbuf = [SB("r%d" % i, [128, TT], BF16) for i in range(3)]
        junk = SB("junk", [128, 1024], BF16)
        hm = [SB("hm%d" % i, [128, 1024], BF16) for i in range(2)]
        sets = []
        for s_ in range(2):
            d = {}
            for name, shape, dt in [("x", [128, 2, 1024], F32), ("hmT", [128, 8, TT], BF16), ("st", [128, 16], F32),
                                    ("y", [128, 2, 1024], F32)]:
                if name == "y" and not last:
                    continue
                d[name] = SB("%s%d" % (name, s_), shape, dt)
            sets.append(d)
        nr = 0
        for i in range(NTT):
            d = sets[i % 2]
            t0 = i * TT
            x, hmT, stt = d["x"], d["hmT"], d["st"]
            P.dma("sp", out=x[:, :, :], in_=k.x2[t0:t0 + TT, :].rearrange("(j p) n -> p j n", p=128))
            for j in range(2):
                P.A("activation", out=junk[:, :], in_=x[:, j, :], func=AF.Square, accum_out=stt[:, j:j + 1])
            rstd_from_ss(k, stt[:, 2:4], stt[:, 0:2], stt[:, 4:6], 1024.0, EPS)
            for j in range(2):
                P.V("tensor_scalar", out=hm[j][:, :], in0=x[:, j, :], scalar1=stt[:, 2 + j:3 + j], scalar2=None,
                    op0=ALU.mult)
                pb = bank(k)
                pv = bfv(pb)
                for c in range(8):
                    P.M("transpose", out=pv[:, c * 128:(c + 1) * 128], in_=hm[j][:, c * 128:(c + 1) * 128],
                        identity=k.ident[:, :], accum=(c > 0))
                P.A("activation", out=hmT[:, :, j * 128:(j + 1) * 128],
                    in_=pv[:, 0:1024].rearrange("p (c t) -> p c t", c=8), func=AF.Copy)
            for f in range(32):
                pb = bank(k)
                for c in range(8):
                    P.M("matmul", out=pb[:, 0:TT], lhsT=w1[:, c, f * 128:(f + 1) * 128], rhs=hmT[:, c, :],
                        start=(c == 0), stop=(c == 7), accum=(c > 0))
                r = rbuf[nr % 3]
                nr += 1
                P.A("activation", out=r[:, :], in_=pb[:, 0:TT], func=AF.Relu)
                P.G("tensor_tensor", out=uT[:, f, :], in0=r[:, :], in1=r[:, :], op=ALU.mult)
            for j in range(2):
                for n in range(2):
                    cs = slice(n * 512, (n + 1) * 512)
                    pb = bank(k)
                    for f in range(32):
                        P.M("matmul", out=pb[:, 0:512], lhsT=uT[:, f, j * 128:(j + 1) * 128], rhs=w2[:, f, cs],
                            start=(f == 0), stop=(f == 31), accum=(f > 0))
                    P.V("tensor_tensor", out=x[:, j, cs], in0=x[:, j, cs], in1=pb[:, 0:512], op=ALU.add)
            if not last:
                P.dma("pool", out=xout[t0:t0 + TT, :].rearrange("(j p) n -> p j n", p=128), in_=x[:, :, :])
            else:
                for j in range(2):
                    P.A("activation", out=junk[:, :], in_=x[:, j, :], func=AF.Square, accum_out=stt[:, 8 + j:9 + j])
                rstd_from_ss(k, stt[:, 10:12], stt[:, 8:10], stt[:, 12:14], 1024.0, EPS)
                for j in range(2):
                    P.V("scalar_tensor_tensor", out=d["y"][:, j, :], in0=x[:, j, :], scalar=stt[:, 10 + j:11 + j],
                        in1=gf[:, :], op0=ALU.mult, op1=ALU.mult)
                P.dma("pool", out=k.y[t0:t0 + TT, :].rearrange("(j p) n -> p j n", p=128), in_=d["y"][:, :, :])
                if "xs0" in k.dbg or "xs1" in k.dbg:
                    P.dma("pool", out=xout[t0:t0 + TT, :].rearrange("(j p) n -> p j n", p=128), in_=x[:, :, :])


def rope_tables(pos, dim):
    inv = (10000.0 ** (-np.arange(0, dim, 2, dtype=np.float32) / dim)).astype(np.float32)
    ang = pos.astype(np.float32)[:, None] * inv[None, :]
    return np.cos(ang).astype(np.float32), np.sin(ang).astype(np.float32)


def const_inputs(S):
    pos = np.arange(S)
    c1, s1 = rope_tables(pos, 32)
    cr, sr = rope_tables(pos // 64, 32)
    cc, sc = rope_tables(pos % 64, 32)
    tab1 = np.stack([c1, s1], 1).astype(np.float32)
    tab2 = np.stack([np.stack([cr, cc], 1), np.stack([sr, sc], 1)], 1).astype(np.float32)
    s_idx = np.arange(128)[:, None]
    t_idx = np.arange(128)[None, :]
    msk = np.zeros((2, 2, 128, 128), np.float32)
    msk[0, 0] = s_idx < t_idx
    msk[0, 1] = s_idx <= t_idx
    msk[1, 0] = s_idx > t_idx
    msk[1, 1] = s_idx >= t_idx
    bdm = np.zeros((128, 128), np.float32)
    bdm[:64, :64] = 1
    bdm[64:, 64:] = 1
    return dict(tab1=tab1, tab2=tab2, ident=np.eye(128, dtype=np.float32), msk=msk, bdm=bdm)


_NC_CACHE = {}


def kernel(**inputs):
    S, depth = 8192, 2
    key = (S, depth)
    if key not in _NC_CACHE:
        _NC_CACHE[key] = build(S, depth)
    nc = _NC_CACHE[key]
    xp, xs = np.asarray(inputs["x_prompt"]), np.asarray(inputs["x_sample"])
    mp, ms = np.asarray(inputs["mem_prompt"]), np.asarray(inputs["mem_sample"])
    seqs = [(xp[b], mp[b]) for b in range(xp.shape[0])] + [(xs[b], ms[b]) for b in range(xs.shape[0])]
    n_seq = len(seqs)
    consts = const_inputs(S)
    wts = {name: np.ascontiguousarray(np.asarray(inputs[name], np.float32)) for name, _ in W_SPECS}
    wts["norm_final"] = np.ascontiguousarray(np.asarray(inputs["norm_final"], np.float32).reshape(1, D))
    in_maps = []
    for c in range(8):
        x, mem = seqs[c % n_seq]
        m = dict(x=np.ascontiguousarray(x, np.float32), mem=np.ascontiguousarray(mem, np.float32))
        m.update(wts)
        m.update(consts)
        in_maps.append(m)
    res = run_bass_kernel_spmd(nc, in_maps, core_ids=list(range(8)))
    ys = [np.asarray(res.results[c]["y"], np.float32) for c in range(n_seq)]
    y_prompt = np.stack(ys[:xp.shape[0]], 0)
    y_sample = np.stack(ys[xp.shape[0]:], 0)
    return (y_prompt, y_sample)
```

```python
import contextlib
import math
import numpy as np
import ml_dtypes
import concourse.bass as bass
import concourse.mybir as mybir
from concourse.bass_utils import run_bass_kernel_spmd

F32 = mybir.dt.float32
BF16 = mybir.dt.bfloat16
ALU = mybir.AluOpType
AF = mybir.ActivationFunctionType
AX = mybir.AxisListType

D = 1024
NMEM = 256
IN_COLS = 6432
EPS = 1e-6
GN_EPS = 64e-5
DECAY_C = math.exp(-0.5)

import os
MAXOPS = int(os.environ.get("MAXOPS", "100000000"))
RW_RATIO = int(os.environ.get("RW_RATIO", "2"))
RW_DELAY = int(os.environ.get("RW_DELAY", "2"))
P3_RATIO = int(os.environ.get("P3_RATIO", "1"))
P1_RATIO = int(os.environ.get("P1_RATIO", "1"))
ENGS = ("pe", "act", "dve", "pool", "sp")
N_DMA_SEMS = 16
READ_KEYS = ("in_", "in0", "in1", "lhsT", "rhs", "scalar", "scalar1", "scalar2", "bias", "scale",
             "identity", "data0", "data1", "initial")
WRITE_KEYS = ("out", "accum_out", "ap")


class Buf:
    __slots__ = ("name", "w", "r")

    def __init__(self, name):
        self.name = name
        self.w = None
        self.r = {}


class Prog:
    def __init__(self, nc):
        self.nc = nc
        self.es = contextlib.ExitStack()
        self.eobj = {"pe": nc.tensor, "act": nc.scalar, "dve": nc.vector, "pool": nc.gpsimd, "sp": nc.sync}
        self.sem = {}
        self.cnt = {}
        for e in ENGS:
            self.sem[e] = self.es.enter_context(nc.semaphore("s_" + e))
            self.cnt[e] = 0
        self.dq = {}
        for q in ("sp", "act", "pool"):
            sems = []
            for i in range(N_DMA_SEMS):
                k = "d_%s_%d" % (q, i)
                self.sem[k] = self.es.enter_context(nc.semaphore(k))
                self.cnt[k] = 0
                sems.append(k)
            self.dq[q] = [sems, 0]
        self.seen = {e: {} for e in ENGS}
        self.ops = {e: [] for e in ENGS}
        self.bufs = {}
        self.nops = 0

    def buf_of(self, ap):
        n = ap.name
        b = self.bufs.get(n)
        if b is None:
            b = self.bufs[n] = Buf(n)
        return b

    def _need(self, e, key, val, waits):
        if self.seen[e].get(key, 0) >= val:
            return
        self.seen[e][key] = val
        waits.append((key, val))

    def _collect(self, kw, extra_r, extra_w):
        reads, writes = [], []
        for k in READ_KEYS:
            v = kw.get(k)
            if v is not None and hasattr(v, "name") and hasattr(v, "ap"):
                reads.append(self.buf_of(v))
        for k in WRITE_KEYS:
            v = kw.get(k)
            if v is not None and hasattr(v, "name") and hasattr(v, "ap"):
                writes.append(self.buf_of(v))
        for v in extra_r:
            reads.append(v if isinstance(v, Buf) else self.buf_of(v))
        for v in extra_w:
            writes.append(v if isinstance(v, Buf) else self.buf_of(v))
        return reads, writes

    def _issue(self, e, inckey, incv, name, kw, reads, writes, accum):
        self.nissued = getattr(self, "nissued", 0) + 1
        if self.nissued > MAXOPS:
            return
        if self.nissued == MAXOPS:
            print("LAST OP:", e, name, {a: (str(b.name) + str(b.shape) if hasattr(b, "ap") else b) for a, b in kw.items()})
        need = {}
        for b in reads:
            if b.w is not None and need.get(b.w[0], 0) < b.w[1]:
                need[b.w[0]] = b.w[1]
        for b in writes:
            if b.w is not None and not (e == "pe" and b.w[0] == "pe") and need.get(b.w[0], 0) < b.w[1]:
                need[b.w[0]] = b.w[1]
            for k, v in b.r.items():
                if need.get(k, 0) < v:
                    need[k] = v
        waits = []
        for k, v in need.items():
            self._need(e, k, v, waits)
        self.cnt[inckey] += incv
        v = self.cnt[inckey]
        self.ops[e].append((waits, name, kw, inckey, incv))
        self.nops += 1 + len(waits)
        for b in reads:
            if b.r.get(inckey, 0) < v:
                b.r[inckey] = v
        for b in writes:
            b.w = (inckey, v)
            b.r = {}

    def op(self, e, name, R=(), W=(), accum=False, drain=False, **kw):
        reads, writes = self._collect(kw, R, W)
        if drain and self.cnt[e] > 0:
            d_ = Buf("drain")
            d_.w = (e, self.cnt[e])
            reads = list(reads) + [d_]
        self._issue(e, e, 1, name, kw, reads, writes, accum)

    def V(self, name, **kw):
        self.op("dve", name, **kw)

    def A(self, name, **kw):
        self.op("act", name, **kw)

    def G(self, name, **kw):
        self.op("pool", name, **kw)

    def M(self, name="matmul", **kw):
        self.op("pe", name, **kw)

    def dma(self, q, out, in_, R=(), W=()):
        kw = dict(out=out, in_=in_)
        reads, writes = self._collect(kw, R, W)
        sems, idx = self.dq[q]
        key = sems[idx % len(sems)]
        self.dq[q][1] = idx + 1
        if self.cnt[key] > 0:
            w = []
            self._need(q, key, self.cnt[key], w)
            pre = w
        else:
            pre = []
        n0 = len(self.ops[q])
        self._issue(q, key, 16, "dma_start", kw, reads, writes, False)
        if pre and len(self.ops[q]) > n0:
            waits, name, kw2, ik, iv = self.ops[q][n0]
            self.ops[q][n0] = (pre + waits, name, kw2, ik, iv)

    def barrier(self):
        for e in ENGS:
            waits = []
            for key, v in self.cnt.items():
                if v > 0:
                    self._need(e, key, v, waits)
            if waits:
                self.ops[e].append((waits, None, None, None, None))
                self.nops += len(waits)

    def wait_bufs(self, e, aps):
        waits = []
        for a in aps:
            b = a if isinstance(a, Buf) else self.buf_of(a)
            if b.w is not None:
                self._need(e, b.w[0], b.w[1], waits)
        self.ops[e].append((waits, None, None, None, None))

    def emit(self):
        nc = self.nc
        with nc.Block() as block:
            def run(e):
                def body(engine):
                    for waits, name, kw, ik, iv in self.ops[e]:
                        for k, v in waits:
                            engine.wait_ge(self.sem[k], v)
                        if name is None:
                            continue
                        inst = getattr(engine, name)(**kw)
                        inst.then_inc(self.sem[ik], iv)
                return body
            block.sync(run("sp"))
            block.scalar(run("act"))
            block.vector(run("dve"))
            block.gpsimd(run("pool"))
            block.tensor(run("pe"))

    def close(self):
        self.es.close()


W_SPECS = [
    ("norm_mix", (D,)), ("w_in", (D, IN_COLS)), ("mla_q_norm", (384,)), ("mla_w_uq", (384, 768)),
    ("mla_kv_norm", (256,)), ("mla_w_ukv", (256, 1024)), ("gqa_q_norm", (64,)), ("gqa_k_norm", (64,)),
    ("rwkv_mu", (1920,)), ("rwkv_w0", (2, 512)), ("rwkv_w2", (2, 64, 512)), ("rwkv_a0", (2, 512)),
    ("rwkv_a2", (2, 64, 512)), ("rwkv_g2", (128, 512)), ("rwkv_k_k", (512,)), ("rwkv_k_a", (512,)),
    ("rwkv_r_k", (8, 64)), ("rwkv_ln_w", (512,)), ("rwkv_ln_b", (512,)), ("w_branch", (3, 512, D)),
    ("b_gate", (3, D)), ("w_out", (D, D)), ("norm_cross", (D,)), ("norm_mem", (D,)),
    ("cross_wq", (D, 512)), ("cross_wkv", (D, 1024)), ("cross_wo", (512, D)), ("norm_mlp", (D,)),
    ("mlp_w1", (D, 4096)), ("mlp_w2", (4096, D)),
]


class K:
    pass


def build(S, depth, dbg=(), phases=("p1", "p2", "rf", "rb", "p3a", "p3b")):
    NT = S // 128
    nc = bass.Bass("TRN2", target_bir_lowering=False)
    P = Prog(nc)
    k = K()
    k.nc, k.P, k.S, k.NT, k.depth, k.dbg = nc, P, S, NT, depth, dbg
    k.phases = phases
    k.y_done = set()

    def din(name, shape):
        return nc.dram_tensor(name, list(shape), F32, kind="ExternalInput").ap()

    def dscr(name, shape, dt):
        kind = "ExternalOutput" if name in dbg else "Internal"
        return nc.dram_tensor(name, list(shape), dt, kind=kind).ap()

    k.x = din("x", (S, D))
    k.mem = din("mem", (NMEM, D))
    k.w = {}
    for name, shp in W_SPECS:
        k.w[name] = din(name, (depth,) + shp)
    k.w["norm_final"] = din("norm_final", (1, D))
    k.tab1 = din("tab1", (S, 2, 16))
    k.tab2 = din("tab2", (S, 2, 2, 16))
    k.ident_d = din("ident", (128, 128))
    k.msk_d = din("msk", (2, 2, 128, 128))
    k.bdm_d = din("bdm", (128, 128))
    k.y = nc.dram_tensor("y", [S, D], F32, kind="ExternalOutput").ap()

    k.h_tm = dscr("h_tm", (S, D), BF16)
    k.hT_t = dscr("hT_t", (NT, 128, 8, 128), BF16)
    k.qT_mla = dscr("qT_mla", (8, 96, S), BF16)
    k.kT_mla = dscr("kT_mla", (8, 96, S), BF16)
    k.v_mla = dscr("v_mla", (S, 512), BF16)
    k.qT_gqa = dscr("qT_gqa", (512, S), BF16)
    k.kT_gqa = dscr("kT_gqa", (128, S), BF16)
    k.v_gqa = dscr("v_gqa", (S, 128), BF16)
    k.o_mla = dscr("o_mla", (S, 512), BF16)
    k.o_gqa = dscr("o_gqa", (S, 512), BF16)
    k.y_fw = dscr("y_fw", (S, 512), F32)
    k.z_r = dscr("z_r", (S, 1920), F32)
    k.zmix = dscr("zmix", (S, 1920), F32)
    k.oT_rw = dscr("oT_rw", (NT, 128, 4, 128), BF16)
    k.x2 = dscr("x2", (S, D), F32)
    k.xs = [dscr("xs0", (S, D), F32), dscr("xs1", (S, D), F32)]

    with contextlib.ExitStack() as gst:
        k.gst = gst
        k.rot = [0]
        k.ident = gst.enter_context(nc.sbuf_tensor("ident_b", [128, 128], BF16))
        P.dma("pool", out=k.ident[:, :], in_=k.ident_d)
        k.identf = gst.enter_context(nc.sbuf_tensor("ident_f", [128, 128], F32))
        P.dma("sp", out=k.identf[:, :], in_=k.ident_d)
        for l in range(depth):
            xin = k.x if l == 0 else k.xs[(l - 1) % 2]
            xout = k.xs[l % 2]
            if "p1" in phases:
                phase_p1(k, l, xin)
                P.barrier()
            if "p2" in phases:
                phase_attn(k, l)
                P.barrier()
            if "rw2" in phases:
                phase_rwkv2(k, l)
                P.barrier()
            if "rf" in phases:
                phase_rwkv(k, l, 0)
                P.barrier()
            if "rb" in phases:
                phase_rwkv(k, l, 1)
                P.barrier()
            if "p3a" in phases:
                phase_p3a(k, l, xin)
                P.barrier()
            if "p3b" in phases:
                phase_p3b(k, l, xout, last=(l == depth - 1))
                P.barrier()
        outs = [k.y]
        for name in dbg:
            outs.append(P.bufs[name]) if name in P.bufs else None
        P.wait_bufs("sp", outs)
        P.emit()
    P.close()
    return nc


_UID = [0]


def un(name):
    _UID[0] += 1
    return "%s_u%d" % (name, _UID[0])


def alloc_banks(k, st, n=8):
    k.banks = [st.enter_context(k.nc.psum_tensor(un("bank%d" % i), [128, 512], F32)) for i in range(n)]


class BankPool:
    def __init__(self, banks):
        self.banks = banks
        self.i = 0

    def get(self):
        b = self.banks[self.i % len(self.banks)]
        self.i += 1
        return b


def run_skewed(gens, ratio=1):
    old = None
    for new in gens:
        new_mid = False
        while True:
            for _ in range(ratio):
                if old is not None:
                    try:
                        next(old)
                    except StopIteration:
                        old = None
            if not new_mid:
                try:
                    if next(new) == "mid":
                        new_mid = True
                except StopIteration:
                    new_mid = True
                    new = None
            if old is None and new_mid:
                break
        old = new
    while old is not None:
        try:
            next(old)
        except StopIteration:
            old = None


def bank(k, lo=0, hi=None):
    if hi is None:
        hi = len(k.banks)
    n = hi - lo
    i = k.rot[0] % n
    k.rot[0] += 1
    return k.banks[lo + i]


def bfv(b, ncols=1024):
    return b[:, :].bitcast(BF16)


def rstd_from_ss(k, out, ss, t, n, eps):
    P = k.P
    if eps is not None and eps != 0.0:
        P.V("tensor_scalar", out=t, in0=ss, scalar1=1.0 / n, scalar2=float(eps), op0=ALU.mult, op1=ALU.add)
        P.A("activation", out=t, in_=t, func=AF.Ln)
    else:
        P.A("activation", out=t, in_=ss, func=AF.Ln, scale=1.0 / n)
    P.A("activation", out=out, in_=t, func=AF.Exp, scale=-0.5)


def load_col(k, q, dst, src1d, n):
    k.P.dma(q, out=dst, in_=src1d.rearrange("(p o) -> p o", o=1))


def bc(ap, shape, axis):
    return ap.unsqueeze(axis).broadcast_to(list(shape))


def phase_p1(k, l, xin):
    nc, P, S, NT = k.nc, k.P, k.S, k.NT
    w = k.w
    with contextlib.ExitStack() as st:
        def SB(name, shape, dt):
            return st.enter_context(nc.sbuf_tensor(un("p1_" + name), list(shape), dt))
        alloc_banks(k, st)
        w_att = SB("watt", [128, 8, 1440], BF16)
        w_r = SB("wr", [128, 8, 1920], BF16)
        w_uq = SB("wuq", [128, 3, 768], BF16)
        w_ukv = SB("wukv", [128, 2, 1024], BF16)
        stg = SB("stg", [128, 2304], F32)
        g_bc = SB("gbc", [128, 1024], F32)
        gq_bc = SB("gqbc", [128, 64], F32)
        gk_bc = SB("gkbc", [128, 64], F32)
        gcol = SB("gcol", [128, 5], F32)
        win = w["w_in"][l].rearrange("(c p) n -> p c n", p=128)
        for c in range(8):
            P.dma("pool", out=w_att[:, c, :], in_=win[:, c, 0:1440])
        for c in range(8):
            P.dma("pool", out=w_r[:, c, :], in_=win[:, c, 1440:3360])
        P.dma("sp", out=g_bc[:, :], in_=w["norm_mix"][l:l + 1, :].broadcast_to([128, 1024]))
        P.dma("sp", out=gq_bc[:, :], in_=w["gqa_q_norm"][l:l + 1, :].broadcast_to([128, 64]))
        P.dma("sp", out=gk_bc[:, :], in_=w["gqa_k_norm"][l:l + 1, :].broadcast_to([128, 64]))
        for c in range(3):
            load_col(k, "sp", gcol[:, c:c + 1], w["mla_q_norm"][l, c * 128:(c + 1) * 128], 128)
        for c in range(2):
            load_col(k, "sp", gcol[:, 3 + c:4 + c], w["mla_kv_norm"][l, c * 128:(c + 1) * 128], 128)
        P.dma("sp", out=stg[:, 0:2304].rearrange("p (c n) -> p c n", c=3),
              in_=w["mla_w_uq"][l].rearrange("(c p) n -> p c n", p=128))
        for c in range(3):
            P.V("tensor_scalar", out=w_uq[:, c, :], in0=stg[:, c * 768:(c + 1) * 768],
                scalar1=gcol[:, c:c + 1], scalar2=None, op0=ALU.mult)
        P.dma("sp", out=stg[:, 0:2048].rearrange("p (c n) -> p c n", c=2),
              in_=w["mla_w_ukv"][l].rearrange("(c p) n -> p c n", p=128))
        for c in range(2):
            P.V("tensor_scalar", out=w_ukv[:, c, :], in0=stg[:, c * 1024:(c + 1) * 1024],
                scalar1=gcol[:, 3 + c:4 + c], scalar2=None, op0=ALU.mult)

        sets = []
        for s in range(2):
            d = {}
            for name, shape, dt in [
                ("x", [128, 1024], F32), ("junk", [128, 1024], BF16), ("hb", [128, 1024], BF16),
                ("hT", [128, 8, 128], BF16), ("tb1", [128, 2, 16], F32), ("tb2", [128, 2, 2, 16], F32),
                ("st", [128, 16], F32), ("cqb", [128, 384], BF16), ("cqT", [128, 3, 128], BF16),
                ("q32", [128, 8, 96], F32), ("qrot", [128, 8, 96], BF16), ("qT", [96, 8, 128], BF16),
                ("ta", [128, 8, 2, 16], F32), ("tb", [128, 8, 2, 16], F32),
                ("ckvb", [128, 256], BF16), ("ckvT", [128, 2, 128], BF16), ("kt", [128, 8, 96], BF16),
                ("vt", [128, 8, 64], BF16), ("kr", [128, 32], F32), ("kT", [96, 8, 128], BF16),
                ("sq", [128, 512], F32), ("gst", [128, 24], F32), ("qn", [128, 8, 64], F32),
                ("gqr", [128, 8, 64], BF16), ("gqT", [128, 4, 128], BF16),
                ("kn", [128, 2, 64], F32), ("gkr", [128, 2, 64], BF16), ("gkT", [128, 128], BF16),
                ("gv", [128, 128], BF16), ("zr", [128, 1920], F32), ("gk32", [128, 128], F32),
            ]:
                d[name] = SB("%s%d" % (name, s), shape, dt)
            sets.append(d)

        ident = k.ident
        poolA = BankPool(k.banks[0:3])
        poolB = BankPool(k.banks[3:8])

        def tile_gen(i):
            pool = poolA
            d = sets[i % 2]
            t0 = i * 128
            x, hb, hT, stt = d["x"], d["hb"], d["hT"], d["st"]
            P.dma("sp", out=x[:, :], in_=xin[t0:t0 + 128, :])
            P.dma("sp", out=d["tb1"][:, :, :], in_=k.tab1[t0:t0 + 128])
            P.dma("sp", out=d["tb2"][:, :, :, :], in_=k.tab2[t0:t0 + 128])
            P.A("activation", out=d["junk"][:, :], in_=x[:, :], func=AF.Square, accum_out=stt[:, 0:1])
            rstd_from_ss(k, stt[:, 1:2], stt[:, 0:1], stt[:, 2:3], 1024.0, EPS)
            P.V("scalar_tensor_tensor", out=hb[:, :], in0=x[:, :], scalar=stt[:, 1:2], in1=g_bc[:, :],
                op0=ALU.mult, op1=ALU.mult)
            P.dma("pool", out=k.h_tm[t0:t0 + 128, :], in_=hb[:, :])
            pb = pool.get()
            pv = bfv(pb)
            for c in range(8):
                P.M("transpose", out=pv[:, c * 128:(c + 1) * 128], in_=hb[:, c * 128:(c + 1) * 128],
                    identity=ident[:, :], accum=(c > 0))
            P.A("activation", out=hT[:, :, :].rearrange("p c t -> p (c t)"), in_=pv[:, 0:1024], func=AF.Copy)
            P.dma("pool", out=k.hT_t[i], in_=hT[:, :, :])
            yield

            def proj(c0, n):
                nonlocal pool
                b = pool.get()
                for c in range(8):
                    P.M("matmul", out=b[:, 0:n], lhsT=hT[:, c, :], rhs=w_att[:, c, c0:c0 + n],
                        start=(c == 0), stop=(c == 7), accum=(c > 0))
                return b

            for ci, (c0, n) in enumerate(((0, 512), (512, 512), (1024, 512), (1536, 384))):
                b = pool.get()
                for c in range(8):
                    P.M("matmul", out=b[:, 0:n], lhsT=hT[:, c, :], rhs=w_r[:, c, c0:c0 + n],
                        start=(c == 0), stop=(c == 7), accum=(c > 0))
                if ci % 2 == 0:
                    P.A("activation", out=d["zr"][:, c0:c0 + n], in_=b[:, 0:n], func=AF.Copy)
                else:
                    P.V("tensor_copy", out=d["zr"][:, c0:c0 + n], in_=b[:, 0:n])
                yield
            P.dma("pool", out=k.z_r[t0:t0 + 128, :], in_=d["zr"][:, :])
            pool = poolB
            yield "mid"
            cos1 = bc(d["tb1"][:, 0, :], [128, 8, 16], 1)
            sin1 = bc(d["tb1"][:, 1, :], [128, 8, 16], 1)
            pq = proj(0, 384)
            P.A("activation", out=d["junk"][:, 0:384], in_=pq[:, 0:384], func=AF.Square, accum_out=stt[:, 3:4])
            rstd_from_ss(k, stt[:, 4:5], stt[:, 3:4], stt[:, 5:6], 384.0, EPS)
            P.V("tensor_copy", out=d["cqb"][:, :], in_=pq[:, 0:384])
            yield
            pb = pool.get()
            pv = bfv(pb)
            for c in range(3):
                P.M("transpose", out=pv[:, c * 128:(c + 1) * 128], in_=d["cqb"][:, c * 128:(c + 1) * 128],
                    identity=ident[:, :], accum=(c > 0))
            P.A("activation", out=d["cqT"][:, :, :].rearrange("p c t -> p (c t)"), in_=pv[:, 0:384], func=AF.Copy)
            q32 = d["q32"]
            q32f = q32[:, :, :].rearrange("p h e -> p (h e)")
            for (c0, n) in ((0, 512), (512, 256)):
                b = pool.get()
                for c in range(3):
                    P.M("matmul", out=b[:, 0:n], lhsT=d["cqT"][:, c, :], rhs=w_uq[:, c, c0:c0 + n],
                        start=(c == 0), stop=(c == 2), accum=(c > 0))
                P.V("tensor_scalar", out=q32f[:, c0:c0 + n], in0=b[:, 0:n], scalar1=stt[:, 4:5], scalar2=None,
                    op0=ALU.mult)
            yield
            qrot = d["qrot"]
            ta = d["ta"][:, :, 0, :]
            tb = d["tb"][:, :, 0, :]
            P.G("tensor_copy", out=qrot[:, :, 0:64], in_=q32[:, :, 0:64])
            P.V("tensor_tensor", out=ta, in0=q32[:, :, 64:80], in1=cos1, op=ALU.mult)
            P.V("tensor_tensor", out=tb, in0=q32[:, :, 80:96], in1=sin1, op=ALU.mult)
            P.V("tensor_tensor", out=qrot[:, :, 64:80], in0=ta, in1=tb, op=ALU.subtract)
            P.V("tensor_tensor", out=ta, in0=q32[:, :, 64:80], in1=sin1, op=ALU.mult)
            P.V("tensor_tensor", out=tb, in0=q32[:, :, 80:96], in1=cos1, op=ALU.mult)
            P.V("tensor_tensor", out=qrot[:, :, 80:96], in0=ta, in1=tb, op=ALU.add)
            pb = pool.get()
            pv = bfv(pb)
            for h in range(8):
                P.M("transpose", out=pv[0:96, h * 128:(h + 1) * 128], in_=qrot[:, h, :], identity=ident[:, :],
                    accum=(h > 0))
            P.A("activation", out=d["qT"][:, :, :].rearrange("p h t -> p (h t)"), in_=pv[0:96, 0:1024], func=AF.Copy)
            P.dma("pool", out=k.qT_mla[:, :, t0:t0 + 128].rearrange("h e s -> e h s"), in_=d["qT"][:, :, :])
            yield
            pkv = proj(384, 288)
            P.A("activation", out=d["junk"][:, 0:256], in_=pkv[:, 0:256], func=AF.Square, accum_out=stt[:, 6:7])
            rstd_from_ss(k, stt[:, 7:8], stt[:, 6:7], stt[:, 8:9], 256.0, EPS)
            P.V("tensor_copy", out=d["ckvb"][:, :], in_=pkv[:, 0:256])
            kr = d["kr"]
            c1 = d["tb1"][:, 0, :]
            s1 = d["tb1"][:, 1, :]
            t2a = d["ta"][:, 0, 1, :]
            t2b = d["tb"][:, 0, 1, :]
            P.V("tensor_tensor", out=t2a, in0=pkv[:, 256:272], in1=c1, op=ALU.mult)
            P.V("tensor_tensor", out=t2b, in0=pkv[:, 272:288], in1=s1, op=ALU.mult)
            P.V("tensor_tensor", out=kr[:, 0:16], in0=t2a, in1=t2b, op=ALU.subtract)
            P.V("tensor_tensor", out=t2a, in0=pkv[:, 256:272], in1=s1, op=ALU.mult)
            P.V("tensor_tensor", out=t2b, in0=pkv[:, 272:288], in1=c1, op=ALU.mult)
            P.V("tensor_tensor", out=kr[:, 16:32], in0=t2a, in1=t2b, op=ALU.add)
            pb = pool.get()
            pv = bfv(pb)
            for c in range(2):
                P.M("transpose", out=pv[:, c * 128:(c + 1) * 128], in_=d["ckvb"][:, c * 128:(c + 1) * 128],
                    identity=ident[:, :], accum=(c > 0))
            P.A("activation", out=d["ckvT"][:, :, :].rearrange("p c t -> p (c t)"), in_=pv[:, 0:256], func=AF.Copy)
            yield
            kt, vt = d["kt"], d["vt"]
            for half in range(2):
                b = pool.get()
                for c in range(2):
                    P.M("matmul", out=b[:, 0:512], lhsT=d["ckvT"][:, c, :], rhs=w_ukv[:, c, half * 512:(half + 1) * 512],
                        start=(c == 0), stop=(c == 1), accum=(c > 0))
                b3 = b[:, 0:512].rearrange("p (h e) -> p h e", h=4)
                P.V("tensor_scalar", out=kt[:, half * 4:(half + 1) * 4, 0:64], in0=b3[:, :, 0:64], scalar1=stt[:, 7:8],
                    scalar2=None, op0=ALU.mult)
                P.V("tensor_scalar", out=vt[:, half * 4:(half + 1) * 4, :], in0=b3[:, :, 64:128], scalar1=stt[:, 7:8],
                    scalar2=None, op0=ALU.mult)
            P.G("tensor_copy", out=kt[:, :, 64:96], in_=bc(kr[:, :], [128, 8, 32], 1))
            pb = pool.get()
            pv = bfv(pb)
            for h in range(8):
                P.M("transpose", out=pv[0:96, h * 128:(h + 1) * 128], in_=kt[:, h, :], identity=ident[:, :],
                    accum=(h > 0))
            P.A("activation", out=d["kT"][:, :, :].rearrange("p h t -> p (h t)"), in_=pv[0:96, 0:1024], func=AF.Copy)
            P.dma("pool", out=k.kT_mla[:, :, t0:t0 + 128].rearrange("h e s -> e h s"), in_=d["kT"][:, :, :])
            P.dma("pool", out=k.v_mla[t0:t0 + 128, :], in_=vt[:, :, :].rearrange("p h e -> p (h e)"))

            yield
            def qknorm_rope(pb_ap, nh, gbc, n32, rot, soff):
                gs = d["gst"]
                P.A("activation", out=d["sq"][:, 0:nh * 64], in_=pb_ap, func=AF.Square)
                P.V("tensor_reduce", out=gs[:, soff:soff + nh],
                    in_=d["sq"][:, 0:nh * 64].rearrange("p (h e) -> p h e", h=nh), axis=AX.X, op=ALU.add)
                rstd_from_ss(k, gs[:, soff + 8:soff + 8 + nh], gs[:, soff:soff + nh], gs[:, soff + 16:soff + 16 + nh],
                             64.0, EPS)
                P.V("tensor_tensor", out=n32[:, :, :], in0=pb_ap.rearrange("p (h e) -> p h e", h=nh),
                    in1=bc(gs[:, soff + 8:soff + 8 + nh], [128, nh, 64], 2), op=ALU.mult)
                P.G("tensor_tensor", out=n32[:, :, :], in0=n32[:, :, :], in1=bc(gbc[:, :], [128, nh, 64], 1),
                    op=ALU.mult)
                v5 = n32[:, :, :].rearrange("p h (a b e) -> p h a b e", a=2, b=2)
                r5 = rot[:, :, :].rearrange("p h (a b e) -> p h a b e", a=2, b=2)
                x1, x2 = v5[:, :, :, 0, :], v5[:, :, :, 1, :]
                cos2 = bc(d["tb2"][:, 0, :, :], [128, nh, 2, 16], 1)
                sin2 = bc(d["tb2"][:, 1, :, :], [128, nh, 2, 16], 1)
                ta4 = d["ta"][:, 0:nh, :, :]
                tb4 = d["tb"][:, 0:nh, :, :]
                P.V("tensor_tensor", out=ta4, in0=x1, in1=cos2, op=ALU.mult)
                P.V("tensor_tensor", out=tb4, in0=x2, in1=sin2, op=ALU.mult)
                P.V("tensor_tensor", out=r5[:, :, :, 0, :], in0=ta4, in1=tb4, op=ALU.subtract)
                P.V("tensor_tensor", out=ta4, in0=x1, in1=sin2, op=ALU.mult)
                P.V("tensor_tensor", out=tb4, in0=x2, in1=cos2, op=ALU.mult)
                P.V("tensor_tensor", out=r5[:, :, :, 1, :], in0=ta4, in1=tb4, op=ALU.add)

            pgq = proj(672, 512)
            yield
            qknorm_rope(pgq[:, 0:512], 8, gq_bc, d["qn"], d["gqr"], 0)
            pb = pool.get()
            pv = bfv(pb)
            gqf = d["gqr"][:, :, :].rearrange("p h e -> p (h e)")
            for c in range(4):
                P.M("transpose", out=pv[:, c * 128:(c + 1) * 128], in_=gqf[:, c * 128:(c + 1) * 128],
                    identity=ident[:, :], accum=(c > 0))
            P.A("activation", out=d["gqT"][:, :, :].rearrange("p c t -> p (c t)"), in_=pv[:, 0:512], func=AF.Copy)
            P.dma("pool", out=k.qT_gqa[:, t0:t0 + 128].rearrange("(c p) s -> p c s", p=128), in_=d["gqT"][:, :, :])
            yield
            pgk = proj(1184, 256)
            P.A("activation", out=d["gv"][:, :], in_=pgk[:, 128:256], func=AF.Copy)
            P.dma("pool", out=k.v_gqa[t0:t0 + 128, :], in_=d["gv"][:, :])
            qknorm_rope(pgk[:, 0:128], 2, gk_bc, d["kn"], d["gkr"], 2)
            pb = pool.get()
            pv = bfv(pb)
            P.M("transpose", out=pv[:, 0:128], in_=d["gkr"][:, :, :].rearrange("p h e -> p (h e)"),
                identity=ident[:, :])
            P.A("activation", out=d["gkT"][:, :], in_=pv[:, 0:128], func=AF.Copy)
            P.dma("pool", out=k.kT_gqa[:, t0:t0 + 128], in_=d["gkT"][:, :])

        run_skewed([tile_gen(i) for i in range(NT)], ratio=P1_RATIO)


def phase_attn(k, l):
    nc, P, S, NT = k.nc, k.P, k.S, k.NT
    QB = 512
    NQB = S // QB
    NG = NT // 2
    with contextlib.ExitStack() as st:
        def SB(name, shape, dt):
            return st.enter_context(nc.sbuf_tensor(un("p2_" + name), list(shape), dt))
        pps = [st.enter_context(nc.psum_tensor(un("pp%d" % i), [128, 1024], F32)) for i in range(2)]
        obs = [st.enter_context(nc.psum_tensor(un("ob%d" % i), [128, 512], F32)) for i in range(4)]
        kts = [SB("kt%d" % i, [128, S], BF16) for i in range(2)]
        qts = [SB("qt%d" % i, [128, S], BF16) for i in range(2)]
        vxs = [SB("vx%d" % i, [128, NT, 65], BF16) for i in range(2)]
        pts = [SB("pt%d" % i, [128, 2 * QB], BF16) for i in range(3)]
        o32 = [SB("o32_%d" % i, [65, QB], F32) for i in range(4)]
        osb = [SB("o%d" % i, [128, 4, 64], BF16) for i in range(4)]
        rsb = [SB("rs%d" % i, [128, 4], F32) for i in range(4)]
        for vx in vxs:
            P.G("memset", ap=vx[:, :, 64:65], constant=1.0, W=[vx[:, :, :]])
        mub = SB("mub", [128, 1920], F32)
        P.dma("sp", out=mub[:, :], in_=k.w["rwkv_mu"][l:l + 1, :].broadcast_to([128, 1920]))
        mixb = [[SB("mz%d_%d" % (a, b), [128, 1920], F32) for a in range(3)] for b in range(2)]

        def mix_tile(j):
            z, zp, zn = mixb[j % 2]
            t0 = j * 128
            P.dma("sp", out=z[:, :], in_=k.z_r[t0:t0 + 128, :])
            if j == 0:
                P.G("memset", ap=zp[0:1, :], constant=0.0)
                P.dma("sp", out=zp[1:128, :], in_=k.z_r[0:127, :])
            else:
                P.dma("sp", out=zp[:, :], in_=k.z_r[t0 - 1:t0 + 127, :])
            if j == NT - 1:
                P.G("memset", ap=zn[:, :], constant=0.0)
                P.dma("sp", out=zn[0:127, :], in_=k.z_r[t0 + 1:t0 + 128, :])
            else:
                P.dma("sp", out=zn[:, :], in_=k.z_r[t0 + 1:t0 + 129, :])
            P.G("tensor_tensor", out=zp[:, :], in0=zp[:, :], in1=zn[:, :], op=ALU.add)
            P.V("scalar_tensor_tensor", out=zp[:, :], in0=zp[:, :], scalar=0.5, in1=z[:, :], op0=ALU.mult,
                op1=ALU.subtract)
            P.G("tensor_tensor", out=zp[:, :], in0=zp[:, :], in1=mub[:, :], op=ALU.mult)
            P.V("tensor_tensor", out=zn[:, :], in0=z[:, :], in1=zp[:, :], op=ALU.add)
            P.dma("pool", out=k.zmix[t0:t0 + 128, :], in_=zn[:, :])
        hd = []
        kvi = -1
        for ui in range(12):
            qt = qts[ui % 2]
            loads = []
            if ui < 8:
                h = ui
                kvi += 1
                kt, vx = kts[kvi % 2], vxs[kvi % 2]
                loads.append((kt[0:96, :], k.kT_mla[h]))
                vsrc = k.v_mla[:, h * 64:(h + 1) * 64].rearrange("(c p) e -> p c e", p=128)
                for c0 in range(0, NT, 8):
                    c1 = min(NT, c0 + 8)
                    loads.append((vx[:, c0:c1, 0:64], vsrc[:, c0:c1, :]))
                loads.append((qt[0:96, :], k.qT_mla[h]))
                hd.append(dict(kind="mla", kt=kt, vx=vx, qt=qt, dq=96, scale=96.0 ** -0.5, heads=[h], oscr=k.o_mla,
                               loads=loads, ngroups=NG))
            else:
                h0 = (ui - 8) * 2
                kvh = h0 // 4
                if h0 % 4 == 0:
                    kvi += 1
                    kt, vx = kts[kvi % 2], vxs[kvi % 2]
                    loads.append((kt[0:64, :], k.kT_gqa[kvh * 64:(kvh + 1) * 64, :]))
                    loads.append((kt[64:128, :], k.kT_gqa[kvh * 64:(kvh + 1) * 64, :]))
                    vsrc = k.v_gqa[:, kvh * 64:(kvh + 1) * 64].rearrange("(c p) e -> p c e", p=128)
                    for c0 in range(0, NT, 8):
                        c1 = min(NT, c0 + 8)
                        loads.append((vx[:, c0:c1, 0:64], vsrc[:, c0:c1, :]))
                loads.append((qt[0:128, :], k.qT_gqa[h0 * 64:(h0 + 2) * 64, :]))
                hd.append(dict(kind="gqa", kt=kt, vx=vx, qt=qt, dq=64, scale=64.0 ** -0.5, heads=[h0, h0 + 1],
                               oscr=k.o_gqa, loads=loads, ngroups=NT))
        items = []
        for ui in range(len(hd)):
            for qb in range(NQB):
                for g in range(hd[ui]["ngroups"]):
                    items.append((ui, qb, g))
        state = {"npp": 0, "npt": 0, "nqb": 0}

        def do_loads(ui):
            for (o_, i_) in hd[ui]["loads"]:
                P.dma("sp", out=o_, in_=i_)

        def emit_qk(ui, qb, g):
            H_ = hd[ui]
            pp = pps[state["npp"] % 2]
            pt = pts[state["npt"] % 3]
            state["npp"] += 1
            state["npt"] += 1
            dq = H_["dq"]
            for u in range(2):
                if H_["kind"] == "mla":
                    kc = 2 * g + u
                    P.M("matmul", out=pp[:, u * QB:(u + 1) * QB], lhsT=H_["kt"][0:dq, kc * 128:(kc + 1) * 128],
                        rhs=H_["qt"][0:dq, qb * QB:(qb + 1) * QB], start=True, stop=True, accum=(u > 0))
                else:
                    rs = slice(u * 64, (u + 1) * 64)
                    P.M("matmul", out=pp[:, u * QB:(u + 1) * QB], lhsT=H_["kt"][rs, g * 128:(g + 1) * 128],
                        rhs=H_["qt"][rs, qb * QB:(qb + 1) * QB], start=True, stop=True, accum=(u > 0))
            P.A("activation", out=pt[:, :], in_=pp[:, :], func=AF.Exp, scale=H_["scale"])
            return pt

        def emit_pv(ui, qb, g, pt):
            H_ = hd[ui]
            base = (state["nqb"] % 2) * 2
            lastg = (g == H_["ngroups"] - 1)
            for u in range(2):
                if H_["kind"] == "mla":
                    kc = 2 * g + u
                    ob = obs[base]
                else:
                    kc = g
                    ob = obs[base + u]
                P.M("matmul", out=ob[0:65, 0:QB], lhsT=H_["vx"][:, kc, 0:65], rhs=pt[:, u * QB:(u + 1) * QB],
                    start=(kc == 0), stop=(kc == NT - 1), accum=(kc > 0))
            if not lastg:
                return None
            state["nqb"] += 1
            eps = []
            for u, h in enumerate(H_["heads"]):
                ob = obs[base + u]
                o3, o_s, r_s = o32[base + u], osb[base + u], rsb[base + u]
                P.A("activation", out=o3[:, :], in_=ob[0:65, 0:QB], func=AF.Copy)

                def epilogue(ob=ob, o3=o3, o_s=o_s, r_s=r_s, h=h):
                    for j in range(4):
                        P.M("transpose", out=ob[:, j * 128:j * 128 + 65], in_=o3[0:65, j * 128:(j + 1) * 128],
                            identity=k.identf[0:65, 0:65], accum=(j > 0))
                    tb3 = ob[:, 0:512].rearrange("p (j e) -> p j e", j=4)
                    P.V("reciprocal", out=r_s[:, :], in_=tb3[:, :, 64])
                    P.V("tensor_tensor", out=o_s[:, :, :], in0=tb3[:, :, 0:64], in1=bc(r_s[:, :], [128, 4, 64], 2),
                        op=ALU.mult)
                    P.dma("pool", out=H_["oscr"][qb * QB:(qb + 1) * QB, h * 64:(h + 1) * 64].rearrange("(j p) e -> p j e", p=128),
                          in_=o_s[:, :, :])
                eps.append(epilogue)

            def run_eps():
                for f in eps:
                    f()
            return run_eps

        n = len(items)
        starts = {}
        for idx, (ui, qb, g) in enumerate(items):
            starts.setdefault(ui, idx)
        do_loads(0)
        prev_pt = None
        cur_pt = None
        pending = None
        mix_every = max(1, n // NT)
        nmix = 0
        for idx in range(n + 1):
            if idx % mix_every == 1 and nmix < NT:
                mix_tile(nmix)
                nmix += 1
            if idx < n:
                ui, qb, g = items[idx]
                if idx == starts[ui] + min(2, NQB * hd[ui]["ngroups"] - 1) and ui + 1 < len(hd):
                    do_loads(ui + 1)
                cur_pt = emit_qk(ui, qb, g)
            if idx >= 1:
                ui0, qb0, g0 = items[idx - 1]
                ep = emit_pv(ui0, qb0, g0, prev_pt)
                if pending is not None:
                    pending()
                pending = ep
            prev_pt = cur_pt
        if pending is not None:
            pending()
        while nmix < NT:
            mix_tile(nmix)
            nmix += 1


def rwkv_dir(k, l, d, st, poolA, poolB, nsets, both=False):
    nc, P, S, NT = k.nc, k.P, k.S, k.NT
    w = k.w
    C0 = DECAY_C
    def SB(name, shape, dt):
        return st.enter_context(nc.sbuf_tensor(un("rw_" + name), list(shape), dt))
    bcn = {}

    def load_bc(name, src_row, n=512):
        t = SB(name, [128, n], F32)
        P.dma("sp", out=t[:, :], in_=src_row.broadcast_to([128, n]))
        bcn[name] = t
        return t
    w0b = load_bc("w0", w["rwkv_w0"][l, d:d + 1, :])
    a0b = load_bc("a0", w["rwkv_a0"][l, d:d + 1, :])
    kkb = load_bc("kk", w["rwkv_k_k"][l:l + 1, :])
    kab = load_bc("ka", w["rwkv_k_a"][l:l + 1, :])
    od = 1 - d
    osl = slice(od * 64, (od + 1) * 64)
    if d == 1 or both:
        a0o = load_bc("a0o", w["rwkv_a0"][l, od:od + 1, :])
        rkb = load_bc("rk", w["rwkv_r_k"][l:l + 1].rearrange("o h n -> o (h n)"))
        lnw = load_bc("lnw", w["rwkv_ln_w"][l:l + 1, :])
        lnb = load_bc("lnb", w["rwkv_ln_b"][l:l + 1, :])
        g2s = SB("g2", [128, 512], BF16)
        P.dma("pool", out=g2s[:, :], in_=w["rwkv_g2"][l])
    w2s = SB("w2", [128, 512], BF16)
    a2s = SB("a2", [128, 512], BF16)
    P.dma("pool", out=w2s[:, :], in_=w["rwkv_w2"][l].rearrange("d r c -> (d r) c"))
    P.dma("pool", out=a2s[:, :], in_=w["rwkv_a2"][l].rearrange("d r c -> (d r) c"))
    m2 = SB("m2", [128, 2, 128], F32)
    mT = SB("mT", [128, 128], F32)
    bdm = SB("bdm", [128, 128], F32)
    onec = SB("onec", [128, 1], F32)
    P.dma("sp", out=m2[:, 0, :], in_=k.msk_d[d, 0])
    P.dma("sp", out=m2[:, 1, :], in_=k.msk_d[d, 1])
    P.dma("sp", out=mT[:, :], in_=k.msk_d[1 - d, 0])
    P.dma("sp", out=bdm[:, :], in_=k.bdm_d)
    P.G("memset", ap=onec[:, :], constant=1.0)
    H32 = SB("H32", [128, 4, 128], F32)
    Hb = SB("Hb", [128, 4, 128], BF16)
    P.G("memset", ap=H32[:, :, :], constant=0.0)
    P.G("memset", ap=Hb[:, :, :], constant=0.0)
    sets = []
    for s_ in range(nsets):
        dd = {}
        lst = [("zm", [128, 1920], F32), ("lor", [128, 384], BF16), ("lorT", [128, 3, 128], BF16),
               ("tok4", [128, 4, 512], BF16), ("vb", [128, 512], BF16), ("TTs", [128, 4, 4, 128], BF16),
               ("gC", [128, 4], F32), ("sm", [128, 64], F32), ("Y32", [128, 512], F32), ("hT_", [128, 128], F32)]
        for t in range(8):
            lst.append(("T%d" % t, [128, 512], F32))
        for p in range(4):
            lst += [("XL%d" % p, [128, 2, 2, 128], BF16), ("LT%d" % p, [128, 2, 128], BF16),
                    ("ARB%d" % p, [128, 2, 128], BF16), ("AK%d" % p, [128, 2, 2, 128], BF16),
                    ("PT%d" % p, [128, 128], BF16), ("AKV%d" % p, [128, 128], BF16), ("Ub%d" % p, [128, 128], BF16)]
        if d == 1 or both:
            lst += [("yfw", [128, 512], F32), ("ob", [128, 512], BF16), ("oT", [128, 4, 128], BF16)]
        for name, shape, dt in lst:
            dd[name] = SB("%s_%d" % (name, s_), shape, dt)
        sets.append(dd)

    order = list(range(NT)) if d == 0 else list(range(NT - 1, -1, -1))

    def tile_gen(it, i):
        pool = poolA
        D_ = sets[it % nsets]
        final = (it >= NT // 2) if both else (d == 1)
        t0 = i * 128
        zm = D_["zm"]
        T = [D_["T%d" % t] for t in range(8)]
        sm = D_["sm"]
        P.dma("sp", out=zm[:, :], in_=k.zmix[t0:t0 + 128, :])
        r_ = zm[:, 0:512]
        kx = zm[:, 512:1024]
        v_ = zm[:, 1024:1536]
        for _ in range(RW_DELAY):
            yield
        lor, lorT = D_["lor"], D_["lorT"]
        P.A("activation", out=lor[:, 0:128], in_=zm[:, 1536:1664], func=AF.Tanh)
        P.V("tensor_copy", out=lor[:, 128:256], in_=zm[:, 1664:1792])
        nl = 2
        if final:
            P.A("activation", out=lor[:, 256:384], in_=zm[:, 1792:1920], func=AF.Sigmoid)
            nl = 3
        pb = pool.get()
        pv = bfv(pb)
        for c in range(nl):
            P.M("transpose", out=pv[:, c * 128:(c + 1) * 128], in_=lor[:, c * 128:(c + 1) * 128],
                identity=k.ident[:, :], accum=(c > 0))
        P.A("activation", out=lorT[:, 0:nl, :].rearrange("p c t -> p (c t)"), in_=pv[:, 0:nl * 128], func=AF.Copy)
        ds = slice(d * 64, (d + 1) * 64)
        sg, asig = T[0], T[1]
        pb = pool.get()
        P.M("matmul", out=pb[:, 0:512], lhsT=lorT[ds, 0, :], rhs=w2s[ds, :], start=True, stop=True)
        P.V("tensor_tensor", out=sg[:, :], in0=pb[:, 0:512], in1=w0b[:, :], op=ALU.add)
        P.A("activation", out=sg[:, :], in_=sg[:, :], func=AF.Sigmoid)
        yield
        pb = pool.get()
        P.M("matmul", out=pb[:, 0:512], lhsT=lorT[ds, 1, :], rhs=a2s[ds, :], start=True, stop=True)
        P.V("tensor_tensor", out=asig[:, :], in0=pb[:, 0:512], in1=a0b[:, :], op=ALU.add)
        P.A("activation", out=asig[:, :], in_=asig[:, :], func=AF.Sigmoid)
        yield
        eP, eN, eX = T[2], T[3], T[4]
        pc = pool.get()
        P.M("matmul", out=pc[:, 0:512], lhsT=m2[:, 1, :], rhs=sg[:, :], start=True, stop=True)
        pcx = pool.get()
        P.M("matmul", out=pcx[:, 0:512], lhsT=m2[:, 0, :], rhs=sg[:, :], start=True, stop=True)
        P.A("activation", out=eP[:, :], in_=pc[:, 0:512], func=AF.Exp, scale=-C0)
        P.A("activation", out=eN[:, :], in_=pc[:, 0:512], func=AF.Exp, scale=C0)
        yield
        P.A("activation", out=eX[:, :], in_=pcx[:, 0:512], func=AF.Exp, scale=-C0)
        pg = pool.get()
        for p in range(4):
            P.M("matmul", out=pg[:, p:p + 1], lhsT=sg[:, p * 128:(p + 1) * 128], rhs=onec[:, 0:1], start=True,
                stop=True, accum=(p > 0))
        P.A("activation", out=D_["gC"][:, :], in_=pg[:, 0:4], func=AF.Exp, scale=-C0)
        yield
        kk, kt, bb = T[5], T[6], T[7]
        tok4, vb = D_["tok4"], D_["vb"]
        P.V("tensor_tensor", out=kk[:, :], in0=kx, in1=kkb[:, :], op=ALU.mult)
        P.G("tensor_tensor", out=bb[:, :], in0=asig[:, :], in1=eN[:, :], op=ALU.mult)
        P.G("tensor_tensor", out=tok4[:, 1, :], in0=r_, in1=eP[:, :], op=ALU.mult)
        P.A("activation", out=kt[:, :], in_=kk[:, :], func=AF.Square)
        P.V("tensor_reduce", out=sm[:, 0:8], in_=kt[:, :].rearrange("p (h e) -> p h e", h=8), axis=AX.X, op=ALU.add)
        P.V("tensor_scalar", out=sm[:, 0:8], in0=sm[:, 0:8], scalar1=1e-24, scalar2=None, op0=ALU.max)
        yield
        P.A("activation", out=sm[:, 8:16], in_=sm[:, 0:8], func=AF.Ln)
        P.A("activation", out=sm[:, 16:24], in_=sm[:, 8:16], func=AF.Exp, scale=-0.5)
        kk3 = kk[:, :].rearrange("p (h e) -> p h e", h=8)
        P.V("tensor_tensor", out=kk3, in0=kk3, in1=bc(sm[:, 16:24], [128, 8, 64], 2), op=ALU.mult)
        yield
        P.V("scalar_tensor_tensor", out=kt[:, :], in0=asig[:, :], scalar=-1.0, in1=kab[:, :], op0=ALU.add, op1=ALU.mult)
        P.V("scalar_tensor_tensor", out=kt[:, :], in0=kt[:, :], scalar=1.0, in1=kx, op0=ALU.add, op1=ALU.mult)
        yield
        P.V("scalar_tensor_tensor", out=tok4[:, 0, :], in0=kk[:, :], scalar=-1.0, in1=eX[:, :], op0=ALU.mult, op1=ALU.mult)
        yield
        P.V("tensor_tensor", out=tok4[:, 2, :], in0=kk[:, :], in1=bb[:, :], op=ALU.mult)
        P.V("tensor_tensor", out=tok4[:, 3, :], in0=kt[:, :], in1=eN[:, :], op=ALU.mult)
        P.A("activation", out=vb[:, :], in_=v_, func=AF.Copy)
        pool = poolB
        yield "mid"
        TTs = D_["TTs"]
        for p0 in (0, 2):
            pb = pool.get()
            pv = bfv(pb)
            for p in (p0, p0 + 1):
                for q in range(4):
                    o0 = (p - p0) * 512 + q * 128
                    P.M("transpose", out=pv[:, o0:o0 + 128], in_=tok4[:, q, p * 128:(p + 1) * 128],
                        identity=k.ident[:, :], accum=not (p == p0 and q == 0))
            P.A("activation", out=TTs[:, p0:p0 + 2, :, :].rearrange("p a q t -> p (a q t)"), in_=pv[:, 0:1024],
                func=AF.Copy)
            yield
        yield
        XL = [D_["XL%d" % p] for p in range(4)]
        LT = [D_["LT%d" % p] for p in range(4)]
        ARB = [D_["ARB%d" % p] for p in range(4)]
        AK = [D_["AK%d" % p] for p in range(4)]
        PT = [D_["PT%d" % p] for p in range(4)]
        AKV = [D_["AKV%d" % p] for p in range(4)]
        Ub = [D_["Ub%d" % p] for p in range(4)]
        m2b = bc(m2[:, :, :].rearrange("p a t -> p (a t)"), [128, 2, 256], 1)
        for p in range(4):
            bB, bK, bL = pool.get(), pool.get(), pool.get()
            for e in range(2):
                bs = slice(e * 64, (e + 1) * 64)
                ar = TTs[bs, p, 0:2, :].rearrange("p q t -> p (q t)")
                P.M("matmul", out=bB[:, e * 256:(e + 1) * 256], lhsT=TTs[bs, p, 2, :], rhs=ar, start=True, stop=True,
                    accum=(e > 0), drain=True)
                P.M("matmul", out=bK[:, e * 256:(e + 1) * 256], lhsT=TTs[bs, p, 3, :], rhs=ar, start=True, stop=True,
                    accum=(e > 0))
                P.M("matmul", out=bL[:, e * 128:(e + 1) * 128], lhsT=TTs[bs, p, 0, :], rhs=TTs[bs, p, 2, :], start=True,
                    stop=True, accum=(e > 0))
            bB4 = bB[:, 0:512].rearrange("p (e a t) -> p e a t", e=2, a=2)
            P.V("tensor_tensor", out=XL[p][:, :, 1, :], in0=bB4[:, :, 0, :], in1=bc(m2[:, 0, :], [128, 2, 128], 1),
                op=ALU.mult)
            P.V("tensor_tensor", out=ARB[p][:, :, :], in0=bB4[:, :, 1, :], in1=bc(m2[:, 1, :], [128, 2, 128], 1),
                op=ALU.mult)
            P.V("tensor_tensor", out=AK[p][:, :, :, :].rearrange("p e a t -> p e (a t)"),
                in0=bK[:, 0:512].rearrange("p (e n) -> p e n", e=2), in1=m2b, op=ALU.mult)
            P.V("tensor_tensor", out=LT[p][:, :, :], in0=bL[:, 0:256].rearrange("p (e t) -> p e t", e=2),
                in1=bc(mT[:, :], [128, 2, 128], 1), op=ALU.mult)
            P.G("tensor_copy", out=XL[p][:, :, 0, :], in_=bc(k.ident[:, :], [128, 2, 128], 1))
            yield
        yield
        for lev in range(7):
            last = (lev == 6)
            for p in range(4):
                bb_ = pool.get()
                for e in range(2):
                    if not last:
                        P.M("matmul", out=bb_[:, e * 256:(e + 1) * 256], lhsT=LT[p][:, e, :],
                            rhs=XL[p][:, e, :, :].rearrange("p a t -> p (a t)"), start=True, stop=True, accum=(e > 0))
                    else:
                        P.M("matmul", out=bb_[:, e * 256:e * 256 + 128], lhsT=LT[p][:, e, :],
                            rhs=XL[p][:, e, 0, :], start=True, stop=True, accum=(e > 0))
                if not last:
                    ba_ = pool.get()
                    for e in range(2):
                        P.M("matmul", out=ba_[:, e * 128:(e + 1) * 128], lhsT=XL[p][:, e, 1, :], rhs=LT[p][:, e, :],
                            start=True, stop=True, accum=(e > 0))
                b4 = bb_[:, 0:512].rearrange("p (e a t) -> p e a t", e=2, a=2)
                P.V("tensor_tensor", out=XL[p][:, :, 0, :], in0=XL[p][:, :, 0, :], in1=b4[:, :, 0, :], op=ALU.add)
                if not last:
                    P.A("activation", out=XL[p][:, :, 1, :], in_=b4[:, :, 1, :], func=AF.Copy)
                    P.A("activation", out=LT[p][:, :, :], in_=ba_[:, 0:256].rearrange("p (e t) -> p e t", e=2),
                        func=AF.Copy)
                yield
        for p in range(4):
            pb = pool.get()
            P.M("matmul", out=pb[:, 0:256].rearrange("p (e t) -> p e t", e=2), lhsT=tok4[:, 0, p * 128:(p + 1) * 128],
                rhs=XL[p][:, :, 0, :], start=True, stop=True)
            P.A("activation", out=PT[p][0:64, :], in_=pb[0:64, 0:128], func=AF.Copy)
            P.V("tensor_copy", out=PT[p][64:128, :], in_=pb[64:128, 128:256])
            pb2 = pool.get()
            for e in range(2):
                h = 2 * p + e
                P.M("matmul", out=pb2[:, e * 64:(e + 1) * 64], lhsT=AK[p][:, e, 0, :], rhs=vb[:, h * 64:(h + 1) * 64],
                    start=True, stop=True, accum=(e > 0))
            P.A("activation", out=AKV[p][:, :], in_=pb2[:, 0:128], func=AF.Copy)
            yield
        Y32 = D_["Y32"]
        for p in range(4):
            ps = slice(p * 128, (p + 1) * 128)
            bU = pool.get()
            for e in range(2):
                P.M("matmul", out=bU[:, e * 64:(e + 1) * 64], lhsT=XL[p][:, e, 0, :], rhs=AKV[p][:, e * 64:(e + 1) * 64],
                    start=(e == 0), stop=False, accum=(e > 0))
            P.M("matmul", out=bU[:, 0:128], lhsT=PT[p][:, :], rhs=Hb[:, p, :], start=False, stop=True, accum=True)
            P.A("activation", out=Ub[p][:, :], in_=bU[:, 0:128], func=AF.Copy)
            bY = pool.get()
            for e in range(2):
                h = 2 * p + e
                P.M("matmul", out=bY[:, e * 64:(e + 1) * 64], lhsT=AK[p][:, e, 1, :], rhs=vb[:, h * 64:(h + 1) * 64],
                    start=(e == 0), stop=False, accum=(e > 0))
            P.M("matmul", out=bY[:, 0:128], lhsT=TTs[:, p, 1, :], rhs=Hb[:, p, :], start=False, stop=False, accum=True)
            for e in range(2):
                P.M("matmul", out=bY[:, e * 64:(e + 1) * 64], lhsT=ARB[p][:, e, :], rhs=Ub[p][:, e * 64:(e + 1) * 64],
                    start=False, stop=(e == 1), accum=True)
            P.V("tensor_copy", out=Y32[:, ps], in_=bY[:, 0:128])
            bH = pool.get()
            P.M("matmul", out=bH[:, 0:128], lhsT=tok4[:, 2, ps], rhs=Ub[p][:, :], start=True, stop=False)
            P.M("matmul", out=bH[:, 0:128], lhsT=tok4[:, 3, ps], rhs=vb[:, ps], start=False, stop=True, accum=True)
            hT_ = D_["hT_"]
            P.V("tensor_tensor", out=hT_[:, :], in0=bH[:, 0:128], in1=H32[:, p, :], op=ALU.add)
            P.V("scalar_tensor_tensor", out=H32[:, p, :], in0=hT_[:, :], scalar=D_["gC"][:, p:p + 1], in1=bdm[:, :],
                op0=ALU.mult, op1=ALU.mult)
            P.A("activation", out=Hb[:, p, :], in_=H32[:, p, :], func=AF.Copy)
            yield
        if not final:
            P.dma("pool", out=k.y_fw[t0:t0 + 128, :], in_=Y32[:, :])
            k.y_done.add(i)
        else:
            while both and i not in k.y_done:
                yield "wait"
            P.dma("sp", out=D_["yfw"][:, :], in_=k.y_fw[t0:t0 + 128, :])
            wkv, sq, bon = T[3], T[4], T[2]
            pb = pool.get()
            P.M("matmul", out=pb[:, 0:512], lhsT=lorT[osl, 1, :], rhs=a2s[osl, :], start=True, stop=True)
            P.V("tensor_tensor", out=T[0][:, :], in0=pb[:, 0:512], in1=a0o[:, :], op=ALU.add)
            P.A("activation", out=T[0][:, :], in_=T[0][:, :], func=AF.Sigmoid)
            P.V("scalar_tensor_tensor", out=T[0][:, :], in0=T[0][:, :], scalar=-1.0, in1=kab[:, :], op0=ALU.add,
                op1=ALU.mult)
            P.V("scalar_tensor_tensor", out=T[0][:, :], in0=T[0][:, :], scalar=1.0, in1=kx, op0=ALU.add, op1=ALU.mult)
            P.V("tensor_tensor", out=T[0][:, :], in0=T[0][:, :], in1=kt[:, :], op=ALU.add)
            P.V("tensor_tensor", out=T[0][:, :], in0=T[0][:, :], in1=r_, op=ALU.mult)
            P.V("tensor_tensor", out=T[0][:, :], in0=T[0][:, :], in1=rkb[:, :], op=ALU.mult)
            P.V("tensor_reduce", out=sm[:, 24:32], in_=T[0][:, :].rearrange("p (h e) -> p h e", h=8), axis=AX.X,
                op=ALU.add)
            P.V("tensor_tensor", out=bon[:, :].rearrange("p (h e) -> p h e", h=8),
                in0=v_.rearrange("p (h e) -> p h e", h=8), in1=bc(sm[:, 24:32], [128, 8, 64], 2), op=ALU.mult)
            P.V("tensor_tensor", out=wkv[:, :], in0=Y32[:, :], in1=D_["yfw"][:, :], op=ALU.add)
            w3 = wkv[:, :].rearrange("p (h e) -> p h e", h=8)
            P.V("tensor_reduce", out=sm[:, 32:40], in_=w3, axis=AX.X, op=ALU.add)
            P.V("tensor_scalar", out=sm[:, 32:40], in0=sm[:, 32:40], scalar1=-1.0 / 64, scalar2=None, op0=ALU.mult)
            P.V("tensor_tensor", out=w3, in0=w3, in1=bc(sm[:, 32:40], [128, 8, 64], 2), op=ALU.add)
            P.A("activation", out=sq[:, :], in_=wkv[:, :], func=AF.Square)
            P.V("tensor_reduce", out=sm[:, 40:48], in_=sq[:, :].rearrange("p (h e) -> p h e", h=8), axis=AX.X, op=ALU.add)
            rstd_from_ss(k, sm[:, 48:56], sm[:, 40:48], sm[:, 56:64], 64.0, GN_EPS)
            P.V("tensor_tensor", out=w3, in0=w3, in1=bc(sm[:, 48:56], [128, 8, 64], 2), op=ALU.mult)
            P.V("tensor_tensor", out=wkv[:, :], in0=wkv[:, :], in1=lnw[:, :], op=ALU.mult)
            P.V("tensor_tensor", out=wkv[:, :], in0=wkv[:, :], in1=lnb[:, :], op=ALU.add)
            P.V("tensor_tensor", out=wkv[:, :], in0=wkv[:, :], in1=bon[:, :], op=ALU.add)
            pb = pool.get()
            P.M("matmul", out=pb[:, 0:512], lhsT=lorT[:, 2, :], rhs=g2s[:, :], start=True, stop=True)
            P.V("tensor_tensor", out=D_["ob"][:, :], in0=wkv[:, :], in1=pb[:, 0:512], op=ALU.mult)
            pb = pool.get()
            pv = bfv(pb)
            for c in range(4):
                P.M("transpose", out=pv[:, c * 128:(c + 1) * 128], in_=D_["ob"][:, c * 128:(c + 1) * 128],
                    identity=k.ident[:, :], accum=(c > 0))
            P.A("activation", out=D_["oT"][:, :, :].rearrange("p c t -> p (c t)"), in_=pv[:, 0:512], func=AF.Copy)
            P.dma("pool", out=k.oT_rw[i], in_=D_["oT"][:, :, :])


    return [tile_gen(it, i) for it, i in enumerate(order)]


def phase_rwkv(k, l, d):
    with contextlib.ExitStack() as st:
        alloc_banks(k, st)
        gens = rwkv_dir(k, l, d, st, BankPool(k.banks[0:2]), BankPool(k.banks[2:8]), 2)
        run_skewed(gens, ratio=RW_RATIO)


def phase_rwkv2(k, l):
    with contextlib.ExitStack() as st:
        alloc_banks(k, st)
        pf = BankPool(k.banks[0:4])
        pb = BankPool(k.banks[4:8])
        k.y_done = set()
        gf = rwkv_dir(k, l, 0, st, pf, pf, 1, both=True)
        gb = rwkv_dir(k, l, 1, st, pb, pb, 1, both=True)
        streams = [iter(gf), iter(gb)]
        cur = [next(streams[0], None), None]
        started_b = False
        while cur[0] is not None or cur[1] is not None or not started_b:
            for j in range(2):
                if j == 1 and not started_b:
                    continue
                if cur[j] is None:
                    continue
                try:
                    r = next(cur[j])
                    if j == 0 and r == "mid" and not started_b:
                        started_b = True
                        cur[1] = next(streams[1], None)
                except StopIteration:
                    cur[j] = next(streams[j], None)
            if cur[0] is None and not started_b:
                started_b = True
                cur[1] = next(streams[1], None)


def phase_p3a(k, l, xin):
    nc, P, S, NT = k.nc, k.P, k.S, k.NT
    w = k.w
    with contextlib.ExitStack() as st:
        def SB(name, shape, dt):
            return st.enter_context(nc.sbuf_tensor(un("p3_" + name), list(shape), dt))
        alloc_banks(k, st, 6)
        pp2 = st.enter_context(nc.psum_tensor(un("p3_pp"), [128, 1024], F32))
        wg = SB("wg", [128, 8, 3072], BF16)
        wbr = SB("wbr", [128, 3, 4, 1024], BF16)
        wo = SB("wo", [128, 8, 1024], BF16)
        wq = SB("wq", [128, 8, 512], BF16)
        cwo = SB("cwo", [128, 4, 1024], BF16)
        bg = SB("bg", [1, 3072], BF16)
        ones = SB("ones", [1, 128], BF16)
        gcol = SB("gcol", [128, 16], F32)
        kmT = SB("kmT", [128, 4, 256], BF16)
        vm = SB("vm", [128, 2, 512], BF16)
        win = w["w_in"][l].rearrange("(c p) n -> p c n", p=128)
        for c in range(8):
            P.dma("pool", out=wg[:, c, :], in_=win[:, c, 3360:6432])
        for g in range(3):
            P.dma("pool", out=wbr[:, g, :, :], in_=w["w_branch"][l, g].rearrange("(c p) n -> p c n", p=128))
        P.dma("pool", out=wo[:, :, :], in_=w["w_out"][l].rearrange("(c p) n -> p c n", p=128))
        P.dma("pool", out=cwo[:, :, :], in_=w["cross_wo"][l].rearrange("(c p) n -> p c n", p=128))
        P.dma("pool", out=bg[0:1, :], in_=w["b_gate"][l:l + 1].rearrange("o g n -> o (g n)"))
        P.G("memset", ap=ones[0:1, :], constant=1.0)
        for c in range(8):
            load_col(k, "sp", gcol[:, c:c + 1], w["norm_cross"][l, c * 128:(c + 1) * 128], 128)
            load_col(k, "sp", gcol[:, 8 + c:9 + c], w["norm_mem"][l, c * 128:(c + 1) * 128], 128)
        with contextlib.ExitStack() as st2:
            stg = st2.enter_context(nc.sbuf_tensor(un("p3_stg"), [128, 8, 1024], F32))
            wkv = st2.enter_context(nc.sbuf_tensor(un("p3_wkv"), [128, 8, 1024], BF16))
            mx = st2.enter_context(nc.sbuf_tensor(un("p3_mx"), [128, 2, 1024], F32))
            mhb = st2.enter_context(nc.sbuf_tensor(un("p3_mhb"), [128, 2, 1024], BF16))
            mhT = st2.enter_context(nc.sbuf_tensor(un("p3_mhT"), [128, 8, 256], BF16))
            mjunk = st2.enter_context(nc.sbuf_tensor(un("p3_mjunk"), [128, 1024], BF16))
            mst = st2.enter_context(nc.sbuf_tensor(un("p3_mst"), [128, 8], F32))
            P.dma("sp", out=stg[:, :, 0:512], in_=w["cross_wq"][l].rearrange("(c p) n -> p c n", p=128))
            for c in range(8):
                P.V("tensor_scalar", out=wq[:, c, :], in0=stg[:, c, 0:512], scalar1=gcol[:, c:c + 1], scalar2=None,
                    op0=ALU.mult)
            P.dma("sp", out=stg[:, :, :], in_=w["cross_wkv"][l].rearrange("(c p) n -> p c n", p=128))
            for c in range(8):
                P.V("tensor_scalar", out=wkv[:, c, :], in0=stg[:, c, :], scalar1=gcol[:, 8 + c:9 + c], scalar2=None,
                    op0=ALU.mult)
            P.dma("sp", out=mx[:, :, :], in_=k.mem.rearrange("(j p) n -> p j n", p=128))
            for j in range(2):
                P.A("activation", out=mjunk[:, :], in_=mx[:, j, :], func=AF.Square, accum_out=mst[:, j:j + 1])
            rstd_from_ss(k, mst[:, 2:4], mst[:, 0:2], mst[:, 4:6], 1024.0, EPS)
            for j in range(2):
                P.V("tensor_scalar", out=mhb[:, j, :], in0=mx[:, j, :], scalar1=mst[:, 2 + j:3 + j], scalar2=None,
                    op0=ALU.mult)
                pb = bank(k)
                pv = bfv(pb)
                for c in range(8):
                    P.M("transpose", out=pv[:, c * 128:(c + 1) * 128], in_=mhb[:, j, c * 128:(c + 1) * 128],
                        identity=k.ident[:, :], accum=(c > 0))
                P.A("activation", out=mhT[:, :, j * 128:(j + 1) * 128],
                    in_=pv[:, 0:1024].rearrange("p (c t) -> p c t", c=8), func=AF.Copy)
            for h in range(4):
                pb = bank(k)
                for c in range(8):
                    P.M("matmul", out=pb[:, 0:256], lhsT=wkv[:, c, h * 128:(h + 1) * 128], rhs=mhT[:, c, :],
                        start=(c == 0), stop=(c == 7), accum=(c > 0))
                P.A("activation", out=kmT[:, h, :], in_=pb[:, 0:256], func=AF.Copy)
            for j in range(2):
                pb = bank(k)
                for c in range(8):
                    P.M("matmul", out=pb[:, 0:512], lhsT=mhT[:, c, j * 128:(j + 1) * 128], rhs=wkv[:, c, 512:1024],
                        start=(c == 0), stop=(c == 7), accum=(c > 0))
                P.A("activation", out=vm[:, j, :], in_=pb[:, 0:512], func=AF.Copy)
        P.barrier()

        sets = []
        for s_ in range(2):
            d = {}
            for name, shape, dt in [
                ("x", [128, 1024], F32), ("hT", [128, 8, 128], BF16), ("ob", [128, 2, 512], BF16),
                ("oT", [128, 3, 4, 128], BF16), ("gt", [128, 512], F32), ("tmp", [128, 512], F32),
                ("m", [128, 1024], F32), ("mb", [128, 1024], BF16), ("mT", [128, 8, 128], BF16),
                ("x1", [128, 1024], F32), ("junk", [128, 1024], BF16), ("st", [128, 16], F32),
                ("h2", [128, 1024], BF16), ("h2T", [128, 8, 128], BF16), ("qT", [128, 4, 128], BF16),
                ("p", [128, 4, 256], BF16), ("pn", [128, 4, 256], BF16), ("pT", [128, 8, 128], BF16),
                ("ocT", [128, 4, 128], BF16), ("x2", [128, 1024], F32),
            ]:
                d[name] = SB("%s%d" % (name, s_), shape, dt)
            sets.append(d)
        have_rw = ("rb" in k.phases) or ("rw2" in k.phases)
        poolA = BankPool(k.banks[0:3])
        poolB = BankPool(k.banks[3:6])

        def tile_gen(i):
            pool = poolA
            d = sets[i % 2]
            t0 = i * 128
            x, hT, oT, stt = d["x"], d["hT"], d["oT"], d["st"]
            P.dma("sp", out=x[:, :], in_=xin[t0:t0 + 128, :])
            P.dma("sp", out=hT[:, :, :], in_=k.hT_t[i])
            P.dma("sp", out=d["ob"][:, 0, :], in_=k.o_mla[t0:t0 + 128, :])
            P.dma("sp", out=d["ob"][:, 1, :], in_=k.o_gqa[t0:t0 + 128, :])
            if have_rw:
                P.dma("sp", out=oT[:, 2, :, :], in_=k.oT_rw[i])
            else:
                P.G("memset", ap=oT[:, 2, :, :], constant=0.0)
            pb = pool.get()
            pv = bfv(pb)
            obf = d["ob"][:, :, :].rearrange("p g n -> p (g n)")
            for c in range(8):
                P.M("transpose", out=pv[:, c * 128:(c + 1) * 128], in_=obf[:, c * 128:(c + 1) * 128],
                    identity=k.ident[:, :], accum=(c > 0))
            P.A("activation", out=oT[:, 0:2, :, :].rearrange("p g c t -> p (g c t)"), in_=pv[:, 0:1024], func=AF.Copy)
            yield
            for g in range(3):
                for n in range(2):
                    cs = slice(n * 512, (n + 1) * 512)
                    pz = pool.get()
                    for c in range(8):
                        P.M("matmul", out=pz[:, 0:512], lhsT=hT[:, c, :], rhs=wg[:, c, g * 1024 + n * 512:g * 1024 + (n + 1) * 512],
                            start=(c == 0), stop=False, accum=(c > 0))
                    P.M("matmul", out=pz[:, 0:512], lhsT=ones[0:1, :], rhs=bg[0:1, g * 1024 + n * 512:g * 1024 + (n + 1) * 512],
                        start=False, stop=True, accum=True)
                    P.A("activation", out=d["gt"][:, :], in_=pz[:, 0:512], func=AF.Sigmoid)
                    pbr = pool.get()
                    for c in range(4):
                        P.M("matmul", out=pbr[:, 0:512], lhsT=oT[:, g, c, :], rhs=wbr[:, g, c, cs],
                            start=(c == 0), stop=(c == 3), accum=(c > 0))
                    if g == 0:
                        P.V("tensor_tensor", out=d["m"][:, cs], in0=d["gt"][:, :], in1=pbr[:, 0:512], op=ALU.mult)
                    else:
                        P.V("tensor_tensor", out=d["tmp"][:, :], in0=d["gt"][:, :], in1=pbr[:, 0:512], op=ALU.mult)
                        if g == 1:
                            P.G("tensor_tensor", out=d["m"][:, cs], in0=d["m"][:, cs], in1=d["tmp"][:, :], op=ALU.add)
                        else:
                            P.G("tensor_tensor", out=d["mb"][:, cs], in0=d["m"][:, cs], in1=d["tmp"][:, :], op=ALU.add)
                    yield
            pool = poolB
            yield "mid"
            pb = pool.get()
            pv = bfv(pb)
            for c in range(8):
                P.M("transpose", out=pv[:, c * 128:(c + 1) * 128], in_=d["mb"][:, c * 128:(c + 1) * 128],
                    identity=k.ident[:, :], accum=(c > 0))
            P.A("activation", out=d["mT"][:, :, :].rearrange("p c t -> p (c t)"), in_=pv[:, 0:1024], func=AF.Copy)
            yield
            for n in range(2):
                cs = slice(n * 512, (n + 1) * 512)
                pb = pool.get()
                for c in range(8):
                    P.M("matmul", out=pb[:, 0:512], lhsT=d["mT"][:, c, :], rhs=wo[:, c, cs],
                        start=(c == 0), stop=(c == 7), accum=(c > 0))
                P.V("tensor_tensor", out=d["x1"][:, cs], in0=x[:, cs], in1=pb[:, 0:512], op=ALU.add)
            yield
            x1 = d["x1"]
            P.A("activation", out=d["junk"][:, :], in_=x1[:, :], func=AF.Square, accum_out=stt[:, 0:1])
            rstd_from_ss(k, stt[:, 1:2], stt[:, 0:1], stt[:, 2:3], 1024.0, EPS)
            P.V("tensor_scalar", out=d["h2"][:, :], in0=x1[:, :], scalar1=stt[:, 1:2], scalar2=None, op0=ALU.mult)
            pb = pool.get()
            pv = bfv(pb)
            for c in range(8):
                P.M("transpose", out=pv[:, c * 128:(c + 1) * 128], in_=d["h2"][:, c * 128:(c + 1) * 128],
                    identity=k.ident[:, :], accum=(c > 0))
            P.A("activation", out=d["h2T"][:, :, :].rearrange("p c t -> p (c t)"), in_=pv[:, 0:1024], func=AF.Copy)
            yield
            pb = pool.get()
            for h in range(4):
                for c in range(8):
                    P.M("matmul", out=pb[:, h * 128:(h + 1) * 128], lhsT=wq[:, c, h * 128:(h + 1) * 128], rhs=d["h2T"][:, c, :],
                        start=(c == 0), stop=(c == 7), accum=(c > 0 or h > 0))
            P.A("activation", out=d["qT"][:, :, :].rearrange("p h t -> p (h t)"), in_=pb[:, 0:512], func=AF.Copy)
            yield
            for h in range(4):
                P.M("matmul", out=pp2[:, h * 256:(h + 1) * 256], lhsT=d["qT"][:, h, :], rhs=kmT[:, h, :],
                    start=True, stop=True, accum=(h > 0))
            for h in range(4):
                P.A("activation", out=d["p"][:, h, :], in_=pp2[:, h * 256:(h + 1) * 256], func=AF.Exp,
                    scale=128.0 ** -0.5, accum_out=stt[:, 4 + h:5 + h])
            P.V("reciprocal", out=stt[:, 8:12], in_=stt[:, 4:8])
            yield
            P.V("tensor_tensor", out=d["pn"][:, :, :], in0=d["p"][:, :, :], in1=bc(stt[:, 8:12], [128, 4, 256], 2),
                op=ALU.mult)
            pb = pool.get()
            pv = bfv(pb)
            pnf = d["pn"][:, :, :].rearrange("p h m -> p (h m)")
            for c in range(8):
                P.M("transpose", out=pv[:, c * 128:(c + 1) * 128], in_=pnf[:, c * 128:(c + 1) * 128],
                    identity=k.ident[:, :], accum=(c > 0))
            P.A("activation", out=d["pT"][:, :, :].rearrange("p c t -> p (c t)"), in_=pv[:, 0:1024], func=AF.Copy)
            yield
            pb = pool.get()
            for h in range(4):
                for mc in range(2):
                    P.M("matmul", out=pb[:, h * 128:(h + 1) * 128], lhsT=vm[:, mc, h * 128:(h + 1) * 128],
                        rhs=d["pT"][:, h * 2 + mc, :], start=(mc == 0), stop=(mc == 1), accum=(mc > 0 or h > 0))
            P.A("activation", out=d["ocT"][:, :, :].rearrange("p h t -> p (h t)"), in_=pb[:, 0:512], func=AF.Copy)
            yield
            for n in range(2):
                cs = slice(n * 512, (n + 1) * 512)
                pb = pool.get()
                for h in range(4):
                    P.M("matmul", out=pb[:, 0:512], lhsT=d["ocT"][:, h, :], rhs=cwo[:, h, cs],
                        start=(h == 0), stop=(h == 3), accum=(h > 0))
                P.V("tensor_tensor", out=d["x2"][:, cs], in0=x1[:, cs], in1=pb[:, 0:512], op=ALU.add)
            P.dma("pool", out=k.x2[t0:t0 + 128, :], in_=d["x2"][:, :])

        run_skewed([tile_gen(i) for i in range(NT)], ratio=P3_RATIO)


def phase_p3b(k, l, xout, last):
    nc, P, S = k.nc, k.P, k.S
    w = k.w
    TT = 256
    NTT = S // TT
    with contextlib.ExitStack() as st:
        def SB(name, shape, dt):
            return st.enter_context(nc.sbuf_tensor(un("p4_" + name), list(shape), dt))
        alloc_banks(k, st)
        w1 = SB("w1", [128, 8, 4096], BF16)
        w2 = SB("w2", [128, 32, 1024], BF16)
        gcol = SB("gcol", [128, 8], F32)
        for c in range(8):
            load_col(k, "sp", gcol[:, c:c + 1], w["norm_mlp"][l, c * 128:(c + 1) * 128], 128)
        with contextlib.ExitStack() as st2:
            stgs = [st2.enter_context(nc.sbuf_tensor(un("p4_stg%d" % i), [128, 4096], F32)) for i in range(2)]
            w1v = w["mlp_w1"][l].rearrange("(c p) n -> p c n", p=128)
            for c in range(8):
                sg = stgs[c % 2]
                P.dma("sp", out=sg[:, :], in_=w1v[:, c, :])
                for hf in range(2):
                    P.V("tensor_scalar", out=w1[:, c, hf * 2048:(hf + 1) * 2048], in0=sg[:, hf * 2048:(hf + 1) * 2048],
                        scalar1=gcol[:, c:c + 1], scalar2=None, op0=ALU.mult)
        P.barrier()
        w2v = w["mlp_w2"][l].rearrange("(f p) n -> p f n", p=128)
        for f0 in range(0, 32, 4):
            P.dma("pool", out=w2[:, f0:f0 + 4, :], in_=w2v[:, f0:f0 + 4, :])
        if last:
            gf = SB("gf", [128, 1024], F32)
            P.dma("sp", out=gf[:, :], in_=w["norm_final"][0:1, :].broadcast_to([128, 1024]))
        uT = SB("uT", [128, 32, TT], BF16)
        rbuf = [SB("r%d" % i, [128, TT], BF16) for i in range(3)]
        junk = SB("junk", [128, 1024], BF16)
        hm = [SB("hm%d" % i, [128, 1024], BF16) for i in range(2)]
        sets = []
        for s_ in range(2):
            d = {}
            for name, shape, dt in [("x", [128, 2, 1024], F32), ("hmT", [128, 8, TT], BF16), ("st", [128, 16], F32),
                                    ("y", [128, 2, 1024], F32)]:
                if name == "y" and not last:
                    continue
                d[name] = SB("%s%d" % (name, s_), shape, dt)
            sets.append(d)
        nr = 0
        for i in range(NTT):
            d = sets[i % 2]
            t0 = i * TT
            x, hmT, stt = d["x"], d["hmT"], d["st"]
            P.dma("sp", out=x[:, :, :], in_=k.x2[t0:t0 + TT, :].rearrange("(j p) n -> p j n", p=128))
            for j in range(2):
                P.A("activation", out=junk[:, :], in_=x[:, j, :], func=AF.Square, accum_out=stt[:, j:j + 1])
            rstd_from_ss(k, stt[:, 2:4], stt[:, 0:2], stt[:, 4:6], 1024.0, EPS)
            for j in range(2):
                P.V("tensor_scalar", out=hm[j][:, :], in0=x[:, j, :], scalar1=stt[:, 2 + j:3 + j], scalar2=None,
                    op0=ALU.mult)
                pb = bank(k)
                pv = bfv(pb)
                for c in range(8):
                    P.M("transpose", out=pv[:, c * 128:(c + 1) * 128], in_=hm[j][:, c * 128:(c + 1) * 128],
                        identity=k.ident[:, :], accum=(c > 0))
                P.A("activation", out=hmT[:, :, j * 128:(j + 1) * 128],
                    in_=pv[:, 0:1024].rearrange("p (c t) -> p c t", c=8), func=AF.Copy)
            for f in range(32):
                pb = bank(k)
                for c in range(8):
                    P.M("matmul", out=pb[:, 0:TT], lhsT=w1[:, c, f * 128:(f + 1) * 128], rhs=hmT[:, c, :],
                        start=(c == 0), stop=(c == 7), accum=(c > 0))
                r = rbuf[nr % 3]
                nr += 1
                P.A("activation", out=r[:, :], in_=pb[:, 0:TT], func=AF.Relu)
                P.G("tensor_tensor", out=uT[:, f, :], in0=r[:, :], in1=r[:, :], op=ALU.mult)
            for j in range(2):
                for n in range(2):
                    cs = slice(n * 512, (n + 1) * 512)
                    pb = bank(k)
                    for f in range(32):
                        P.M("matmul", out=pb[:, 0:512], lhsT=uT[:, f, j * 128:(j + 1) * 128], rhs=w2[:, f, cs],
                            start=(f == 0), stop=(f == 31), accum=(f > 0))
                    P.V("tensor_tensor", out=x[:, j, cs], in0=x[:, j, cs], in1=pb[:, 0:512], op=ALU.add)
            if not last:
                P.dma("pool", out=xout[t0:t0 + TT, :].rearrange("(j p) n -> p j n", p=128), in_=x[:, :, :])
            else:
                for j in range(2):
                    P.A("activation", out=junk[:, :], in_=x[:, j, :], func=AF.Square, accum_out=stt[:, 8 + j:9 + j])
                rstd_from_ss(k, stt[:, 10:12], stt[:, 8:10], stt[:, 12:14], 1024.0, EPS)
                for j in range(2):
                    P.V("scalar_tensor_tensor", out=d["y"][:, j, :], in0=x[:, j, :], scalar=stt[:, 10 + j:11 + j],
                        in1=gf[:, :], op0=ALU.mult, op1=ALU.mult)
                P.dma("pool", out=k.y[t0:t0 + TT, :].rearrange("(j p) n -> p j n", p=128), in_=d["y"][:, :, :])
                if "xs0" in k.dbg or "xs1" in k.dbg:
                    P.dma("pool", out=xout[t0:t0 + TT, :].rearrange("(j p) n -> p j n", p=128), in_=x[:, :, :])


def rope_tables(pos, dim):
    inv = (10000.0 ** (-np.arange(0, dim, 2, dtype=np.float32) / dim)).astype(np.float32)
    ang = pos.astype(np.float32)[:, None] * inv[None, :]
    return np.cos(ang).astype(np.float32), np.sin(ang).astype(np.float32)


def const_inputs(S):
    pos = np.arange(S)
    c1, s1 = rope_tables(pos, 32)
    cr, sr = rope_tables(pos // 64, 32)
    cc, sc = rope_tables(pos % 64, 32)
    tab1 = np.stack([c1, s1], 1).astype(np.float32)
    tab2 = np.stack([np.stack([cr, cc], 1), np.stack([sr, sc], 1)], 1).astype(np.float32)
    s_idx = np.arange(128)[:, None]
    t_idx = np.arange(128)[None, :]
    msk = np.zeros((2, 2, 128, 128), np.float32)
    msk[0, 0] = s_idx < t_idx
    msk[0, 1] = s_idx <= t_idx
    msk[1, 0] = s_idx > t_idx
    msk[1, 1] = s_idx >= t_idx
    bdm = np.zeros((128, 128), np.float32)
    bdm[:64, :64] = 1
    bdm[64:, 64:] = 1
    return dict(tab1=tab1, tab2=tab2, ident=np.eye(128, dtype=np.float32), msk=msk, bdm=bdm)


_NC_CACHE = {}


def kernel(**inputs):
    S, depth = 8192, 2
    key = (S, depth)
    if key not in _NC_CACHE:
        _NC_CACHE[key] = build(S, depth)
    nc = _NC_CACHE[key]
    xp, xs = np.asarray(inputs["x_prompt"]), np.asarray(inputs["x_sample"])
    mp, ms = np.asarray(inputs["mem_prompt"]), np.asarray(inputs["mem_sample"])
    seqs = [(xp[b], mp[b]) for b in range(xp.shape[0])] + [(xs[b], ms[b]) for b in range(xs.shape[0])]
    n_seq = len(seqs)
    consts = const_inputs(S)
    wts = {name: np.ascontiguousarray(np.asarray(inputs[name], np.float32)) for name, _ in W_SPECS}
    wts["norm_final"] = np.ascontiguousarray(np.asarray(inputs["norm_final"], np.float32).reshape(1, D))
    in_maps = []
    for c in range(8):
        x, mem = seqs[c % n_seq]
        m = dict(x=np.ascontiguousarray(x, np.float32), mem=np.ascontiguousarray(mem, np.float32))
        m.update(wts)
        m.update(consts)
        in_maps.append(m)
    res = run_bass_kernel_spmd(nc, in_maps, core_ids=list(range(8)))
    ys = [np.asarray(res.results[c]["y"], np.float32) for c in range(n_seq)]
    y_prompt = np.stack(ys[:xp.shape[0]], 0)
    y_sample = np.stack(ys[xp.shape[0]:], 0)
    return (y_prompt, y_sample)
```
